# Optimizing a Trainium2 kernel written in Bass

```python
import math
import jax
import jax.numpy as jnp
from jax import lax
import numpy as np


D_MODEL = 1024
BATCH = 1
SEQ = 16384
DEPTH = 2

CHUNK = 64
NORM_EPS = 1e-6
BRANCH_WIDTH = D_MODEL // 2
N_BRANCH = 3
GLA_HEADS = 4
GLA_DK = BRANCH_WIDTH // GLA_HEADS
GLA_DV = BRANCH_WIDTH // GLA_HEADS
GLA_RANK = 16
GLA_GATE_NORM = 16.0
S5_GROUP_CH = 16
S5_GROUPS = BRANCH_WIDTH // S5_GROUP_CH
S5_STATE = 64
S5_WIDTH = S5_GROUPS * S5_GROUP_CH
ML_HEADS = 4
ML_DH = BRANCH_WIDTH // ML_HEADS
ML_CONV = 4
D_FF = 2816
FFN_CONV = 3

GLA_QK = GLA_HEADS * GLA_DK
GLA_V = GLA_HEADS * GLA_DV
ML_W = ML_HEADS * ML_DH
IN_SPLITS = (GLA_QK, GLA_QK, GLA_V, GLA_RANK, GLA_V, S5_WIDTH,
             ML_W, ML_W, ML_W, ML_HEADS, ML_HEADS, ML_W, N_BRANCH * D_MODEL)
D_IN = sum(IN_SPLITS)
SPLIT_POINTS = tuple(int(v) for v in np.cumsum(IN_SPLITS)[:-1])

kernel_name = "hybrid_gla_s5_mlstm_convffn"


def rms_norm(x, gain):
    xf = x.astype(jnp.float32)
    y = xf * lax.rsqrt(jnp.mean(xf * xf, axis=-1, keepdims=True) + NORM_EPS)
    return (y * gain.astype(jnp.float32)).astype(x.dtype)


def causal_dwconv(x, w, b):
    k_width, ch = w.shape
    y = lax.conv_general_dilated(x, w.astype(x.dtype)[:, None, :], window_strides=(1,),
                                 padding=[(k_width - 1, 0)],
                                 dimension_numbers=('NWC', 'WIO', 'NWC'),
                                 feature_group_count=ch)
    return y + b.astype(x.dtype)


def _heads(t, n):
    return t.reshape(t.shape[0], t.shape[1], n, -1)


def _to_chunks(t):
    bsz, s, h, d = t.shape
    return t.reshape(bsz, s // CHUNK, CHUNK, h, d).transpose(0, 3, 1, 2, 4)


def _from_chunks(t):
    bsz, h, n, c, d = t.shape
    return t.transpose(0, 2, 3, 1, 4).reshape(bsz, n * c, h, d)


def gla_chunked(q, k, v, log_a):
    f32 = jnp.float32
    q = _to_chunks(q.astype(f32)) * GLA_DK ** -0.5
    k = _to_chunks(k.astype(f32))
    v = _to_chunks(v.astype(f32))
    b = jnp.cumsum(_to_chunks(log_a.astype(f32)), axis=3)
    b_last = b[:, :, :, -1:, :]
    q_dec = q * jnp.exp(b)
    k_inv = k * jnp.exp(-b)
    kv = jnp.einsum('bhnld,bhnle->bhnde', k * jnp.exp(b_last - b), v)
    decay = jnp.exp(b_last[:, :, :, 0, :])

    def step(state, inp):
        dec, kv_c = inp
        return dec[..., None] * state + kv_c, state

    s0 = jnp.zeros(kv.shape[:2] + kv.shape[3:], f32)
    _, s_prev = lax.scan(step, s0, (jnp.moveaxis(decay, 2, 0), jnp.moveaxis(kv, 2, 0)))
    s_prev = jnp.moveaxis(s_prev, 0, 2)
    causal = jnp.tril(jnp.ones((CHUNK, CHUNK), bool))
    scores = jnp.where(causal, jnp.einsum('bhnld,bhnmd->bhnlm', q_dec, k_inv), 0.0)
    o = (jnp.einsum('bhnld,bhnde->bhnle', q_dec, s_prev)
         + jnp.einsum('bhnlm,bhnme->bhnle', scores, v))
    return _from_chunks(o)


def s5_scan(u, a_re, a_im, log_dt, b_re, b_im, c_re, c_im, d):
    f32 = jnp.float32
    bsz, s, _ = u.shape
    ug = u.astype(f32).reshape(bsz, s, S5_GROUPS, S5_GROUP_CH)
    a_re = a_re.astype(f32)
    a_im = a_im.astype(f32)
    dt = jnp.exp(log_dt.astype(f32))[:, None]
    mag = jnp.exp(dt * a_re)
    ab_re = mag * jnp.cos(dt * a_im)
    ab_im = mag * jnp.sin(dt * a_im)
    den = a_re * a_re + a_im * a_im
    zr = ab_re - 1.0
    f_re = (zr * a_re + ab_im * a_im) / den
    f_im = (ab_im * a_re - zr * a_im) / den
    b_re = b_re.astype(f32)
    b_im = b_im.astype(f32)
    bb_re = f_re[..., None] * b_re - f_im[..., None] * b_im
    bb_im = f_re[..., None] * b_im + f_im[..., None] * b_re
    bu_re = jnp.einsum('bsgh,gph->bsgp', ug, bb_re)
    bu_im = jnp.einsum('bsgh,gph->bsgp', ug, bb_im)
    ar = jnp.broadcast_to(ab_re, bu_re.shape)
    ai = jnp.broadcast_to(ab_im, bu_re.shape)

    def combine(e1, e2):
        a1r, a1i, b1r, b1i = e1
        a2r, a2i, b2r, b2i = e2
        return (a2r * a1r - a2i * a1i, a2r * a1i + a2i * a1r,
                a2r * b1r - a2i * b1i + b2r, a2r * b1i + a2i * b1r + b2i)

    _, _, xr, xi = lax.associative_scan(combine, (ar, ai, bu_re, bu_im), axis=1)
    y = (jnp.einsum('bsgp,ghp->bsgh', xr, c_re.astype(f32))
         - jnp.einsum('bsgp,ghp->bsgh', xi, c_im.astype(f32))
         + d.astype(f32) * ug)
    return y.reshape(bsz, s, S5_WIDTH)


def mlstm_chunked(q, k, v, i_pre, f_pre):
    f32 = jnp.float32
    bsz, s, h, _ = q.shape
    n_chunks = s // CHUNK
    q = _to_chunks(q.astype(f32))
    k = _to_chunks(k.astype(f32)) * ML_DH ** -0.5
    v = _to_chunks(v.astype(f32))

    def gate_chunks(t):
        return t.astype(f32).reshape(bsz, n_chunks, CHUNK, h).transpose(0, 3, 1, 2)

    log_i = gate_chunks(i_pre)
    b = jnp.cumsum(jax.nn.log_sigmoid(gate_chunks(f_pre)), axis=-1)
    b_last = b[..., -1]
    g_end = b_last[..., None] - b + log_i

    def step(carry, inp):
        c_st, n_st, m_st = carry
        k_c, v_c, g_c, bl = inp
        m_new = jnp.maximum(bl + m_st, jnp.max(g_c, axis=-1))
        w = jnp.exp(g_c - m_new[..., None])
        sc = jnp.exp(bl + m_st - m_new)
        c_new = sc[..., None, None] * c_st + jnp.einsum('bhl,bhld,bhle->bhde', w, k_c, v_c)
        n_new = sc[..., None] * n_st + jnp.einsum('bhl,bhld->bhd', w, k_c)
        return (c_new, n_new, m_new), (c_st, n_st, m_st)

    init = (jnp.zeros((bsz, h, ML_DH, ML_DH), f32), jnp.zeros((bsz, h, ML_DH), f32),
            jnp.zeros((bsz, h), f32))
    xs = (jnp.moveaxis(k, 2, 0), jnp.moveaxis(v, 2, 0), jnp.moveaxis(g_end, 2, 0),
          jnp.moveaxis(b_last, 2, 0))
    _, (c_prev, n_prev, m_prev) = lax.scan(step, init, xs)
    c_prev = jnp.moveaxis(c_prev, 0, 2)
    n_prev = jnp.moveaxis(n_prev, 0, 2)
    m_prev = jnp.moveaxis(m_prev, 0, 2)
    causal = jnp.tril(jnp.ones((CHUNK, CHUNK), bool))
    log_w = jnp.where(causal, b[..., :, None] - b[..., None, :] + log_i[..., None, :], -jnp.inf)
    m_inter = b + m_prev[..., None]
    m = jnp.maximum(m_inter, jnp.max(log_w, axis=-1))
    s_inter = jnp.exp(m_inter - m)
    qk = jnp.einsum('bhnld,bhnmd->bhnlm', q, k) * jnp.exp(log_w - m[..., None])
    num = (s_inter[..., None] * jnp.einsum('bhnld,bhnde->bhnle', q, c_prev)
           + jnp.einsum('bhnlm,bhnme->bhnle', qk, v))
    den = s_inter * jnp.einsum('bhnld,bhnd->bhnl', q, n_prev) + jnp.sum(qk, axis=-1)
    hid = num / jnp.maximum(jnp.abs(den), jnp.exp(-m))[..., None]
    return _from_chunks(hid)


def hybrid_mixer(u, w_in, gla_w_gk, gla_b_gk, gla_norm, gla_w_proj,
                 s5_a_re, s5_a_im, s5_log_dt, s5_b_re, s5_b_im, s5_c_re, s5_c_im, s5_d,
                 s5_w_glu, s5_b_glu, s5_w_proj, ml_conv_w, ml_conv_b, ml_b_i, ml_b_f,
                 ml_w_proj, w_out):
    bsz, s, _ = u.shape
    z = u @ w_in
    (g_q, g_k, g_v, g_lr, g_r, s_u, m_q, m_k, m_v, m_i, m_f, m_o,
     gate_pre) = jnp.split(z, SPLIT_POINTS, axis=-1)
    log_a = jax.nn.log_sigmoid((g_lr @ gla_w_gk + gla_b_gk).astype(jnp.float32)) / GLA_GATE_NORM
    o = gla_chunked(_heads(g_q, GLA_HEADS), _heads(g_k, GLA_HEADS), _heads(g_v, GLA_HEADS),
                    _heads(log_a, GLA_HEADS))
    o = rms_norm(o, gla_norm) * jax.nn.silu(_heads(g_r, GLA_HEADS).astype(jnp.float32))
    y_gla = o.reshape(bsz, s, GLA_V).astype(u.dtype) @ gla_w_proj
    y = jax.nn.gelu(s5_scan(s_u, s5_a_re, s5_a_im, s5_log_dt, s5_b_re, s5_b_im,
                            s5_c_re, s5_c_im, s5_d)).astype(u.dtype)
    y = y * jax.nn.sigmoid(y @ s5_w_glu + s5_b_glu)
    y_s5 = y @ s5_w_proj
    qk = jax.nn.silu(causal_dwconv(jnp.concatenate([m_q, m_k], axis=-1), ml_conv_w, ml_conv_b))
    q_ml, k_ml = jnp.split(qk, 2, axis=-1)
    hid = mlstm_chunked(_heads(q_ml, ML_HEADS), _heads(k_ml, ML_HEADS), _heads(m_v, ML_HEADS),
                        m_i + ml_b_i, m_f + ml_b_f)
    hid = jax.nn.sigmoid(_heads(m_o, ML_HEADS).astype(jnp.float32)) * hid
    y_ml = hid.reshape(bsz, s, ML_W).astype(u.dtype) @ ml_w_proj
    gates = jax.nn.sigmoid(gate_pre).reshape(bsz, s, N_BRANCH, D_MODEL)
    merged = gates[:, :, 0] * y_gla + gates[:, :, 1] * y_s5 + gates[:, :, 2] * y_ml
    return merged @ w_out


def conv_ffn(u, w_up, w_gate, conv_w, conv_b, w_down):
    a = causal_dwconv(u @ w_up, conv_w, conv_b)
    return (jax.nn.gelu(a) * (u @ w_gate)) @ w_down


def setup_inputs(seed: int = 0) -> dict:
    key = jax.random.key(seed)
    ks = iter(list(jax.random.split(key, 40)))
    L = DEPTH
    f32 = jnp.float32

    def nrm(shape, scale):
        return scale * jax.random.normal(next(ks), shape, f32)

    def gain(dim):
        return 1.0 + nrm((L, dim), 0.02)

    n_idx = jnp.arange(S5_STATE, dtype=f32)
    return {
        'x': nrm((BATCH, SEQ, D_MODEL), 1.0),
        'norm_mix_pre': gain(D_MODEL),
        'norm_mix_post': gain(D_MODEL),
        'norm_ffn_pre': gain(D_MODEL),
        'norm_ffn_post': gain(D_MODEL),
        'w_in': nrm((L, D_MODEL, D_IN), D_MODEL ** -0.5),
        'gla_w_gk': nrm((L, GLA_RANK, GLA_QK), GLA_RANK ** -0.5),
        'gla_b_gk': nrm((L, GLA_QK), 0.01),
        'gla_norm': gain(GLA_DV),
        'gla_w_proj': nrm((L, GLA_V, D_MODEL), GLA_V ** -0.5),
        's5_a_re': -0.5 + nrm((L, S5_GROUPS, S5_STATE), 0.01),
        's5_a_im': math.pi * n_idx + nrm((L, S5_GROUPS, S5_STATE), 0.01),
        's5_log_dt': jax.random.uniform(next(ks), (L, S5_GROUPS), f32,
                                        minval=math.log(1e-3), maxval=math.log(1e-1)),
        's5_b_re': nrm((L, S5_GROUPS, S5_STATE, S5_GROUP_CH), S5_GROUP_CH ** -0.5),
        's5_b_im': nrm((L, S5_GROUPS, S5_STATE, S5_GROUP_CH), S5_GROUP_CH ** -0.5),
        's5_c_re': nrm((L, S5_GROUPS, S5_GROUP_CH, S5_STATE), S5_STATE ** -0.5),
        's5_c_im': nrm((L, S5_GROUPS, S5_GROUP_CH, S5_STATE), S5_STATE ** -0.5),
        's5_d': nrm((L, S5_GROUPS, S5_GROUP_CH), 1.0),
        's5_w_glu': nrm((L, S5_WIDTH, S5_WIDTH), S5_WIDTH ** -0.5),
        's5_b_glu': nrm((L, S5_WIDTH), 0.01),
        's5_w_proj': nrm((L, S5_WIDTH, D_MODEL), S5_WIDTH ** -0.5),
        'ml_conv_w': nrm((L, ML_CONV, 2 * ML_W), ML_CONV ** -0.5),
        'ml_conv_b': nrm((L, 2 * ML_W), 0.01),
        'ml_b_i': nrm((L, ML_HEADS), 0.1),
        'ml_b_f': jnp.linspace(3.0, 6.0, ML_HEADS, dtype=f32) + nrm((L, ML_HEADS), 0.1),
        'ml_w_proj': nrm((L, ML_W, D_MODEL), ML_W ** -0.5),
        'w_out': nrm((L, D_MODEL, D_MODEL), D_MODEL ** -0.5),
        'ffn_w_up': nrm((L, D_MODEL, D_FF), D_MODEL ** -0.5),
        'ffn_w_gate': nrm((L, D_MODEL, D_FF), D_MODEL ** -0.5),
        'ffn_conv_w': nrm((L, FFN_CONV, D_FF), FFN_CONV ** -0.5),
        'ffn_conv_b': nrm((L, D_FF), 0.01),
        'ffn_w_down': nrm((L, D_FF, D_MODEL), D_FF ** -0.5),
    }


def reference(x, norm_mix_pre, norm_mix_post, norm_ffn_pre, norm_ffn_post, w_in,
              gla_w_gk, gla_b_gk, gla_norm, gla_w_proj,
              s5_a_re, s5_a_im, s5_log_dt, s5_b_re, s5_b_im, s5_c_re, s5_c_im, s5_d,
              s5_w_glu, s5_b_glu, s5_w_proj,
              ml_conv_w, ml_conv_b, ml_b_i, ml_b_f, ml_w_proj, w_out,
              ffn_w_up, ffn_w_gate, ffn_conv_w, ffn_conv_b, ffn_w_down):
    for l in range(DEPTH):
        u = rms_norm(x, norm_mix_pre[l])
        mix = hybrid_mixer(u, w_in[l], gla_w_gk[l], gla_b_gk[l], gla_norm[l], gla_w_proj[l],
                           s5_a_re[l], s5_a_im[l], s5_log_dt[l], s5_b_re[l], s5_b_im[l],
                           s5_c_re[l], s5_c_im[l], s5_d[l], s5_w_glu[l], s5_b_glu[l],
                           s5_w_proj[l], ml_conv_w[l], ml_conv_b[l], ml_b_i[l], ml_b_f[l],
                           ml_w_proj[l], w_out[l])
        x = x + rms_norm(mix, norm_mix_post[l])
        u = rms_norm(x, norm_ffn_pre[l])
        ffn = conv_ffn(u, ffn_w_up[l], ffn_w_gate[l], ffn_conv_w[l], ffn_conv_b[l], ffn_w_down[l])
        x = x + rms_norm(ffn, norm_ffn_post[l])
    return x
```

```python
import math
import numpy as np
import concourse.bass as bass
import concourse.mybir as mybir
from concourse.bass_utils import run_bass_kernel_spmd

F32 = mybir.dt.float32
BF16 = mybir.dt.bfloat16
ALU = mybir.AluOpType
AF = mybir.ActivationFunctionType

NCORE = 8
D = 1024
DIN = 7704
DFF = 2816
SEQ = 16384
OWN = SEQ // NCORE
HALO = 64
PRE = 3
NT = OWN + HALO
NW = NT + PRE
EPS = 1e-6
O_GQ, O_GK, O_GV, O_LR, O_GR, O_SU, O_MQ, O_MK, O_MV, O_MI, O_MF, O_MO, O_GATE = (
    0, 512, 1024, 1536, 1552, 2064, 2576, 3088, 3600, 4112, 4116, 4120, 4632)
TILES = [(0, 64), (64, 512), (576, 512), (1088, 512), (1600, 512)]
SNAP_TOK = OWN
LS5 = 8
GC = 1.5957691216057308

WNAMES = ['norm_mix_pre', 'norm_mix_post', 'norm_ffn_pre', 'norm_ffn_post', 'w_in',
          'gla_w_gk', 'gla_b_gk', 'gla_norm', 'gla_w_proj',
          's5_a_re', 's5_a_im', 's5_log_dt', 's5_b_re', 's5_b_im', 's5_c_re', 's5_c_im', 's5_d',
          's5_w_glu', 's5_b_glu', 's5_w_proj', 'ml_conv_w', 'ml_conv_b', 'ml_b_i', 'ml_b_f',
          'ml_w_proj', 'w_out', 'ffn_w_up', 'ffn_w_gate', 'ffn_conv_w', 'ffn_conv_b', 'ffn_w_down']
WSHAPES = {'norm_mix_pre': [D], 'norm_mix_post': [D], 'norm_ffn_pre': [D], 'norm_ffn_post': [D],
           'w_in': [D, DIN], 'gla_w_gk': [16, 512], 'gla_b_gk': [512], 'gla_norm': [128],
           'gla_w_proj': [512, D], 's5_a_re': [32, 64], 's5_a_im': [32, 64], 's5_log_dt': [32],
           's5_b_re': [32, 64, 16], 's5_b_im': [32, 64, 16], 's5_c_re': [32, 16, 64],
           's5_c_im': [32, 16, 64], 's5_d': [32, 16], 's5_w_glu': [512, 512], 's5_b_glu': [512],
           's5_w_proj': [512, D], 'ml_conv_w': [4, D], 'ml_conv_b': [D], 'ml_b_i': [4], 'ml_b_f': [4],
           'ml_w_proj': [512, D], 'w_out': [D, D], 'ffn_w_up': [D, DFF], 'ffn_w_gate': [D, DFF],
           'ffn_conv_w': [3, DFF], 'ffn_conv_b': [DFF], 'ffn_w_down': [DFF, D]}


class Sched:
    ENGS = ('pe', 'act', 'dve', 'pool', 'sp')
    SELF_SYNC = ('act', 'dve', 'pool')

    def __init__(self, nslots=16):
        self.streams = {e: [] for e in self.ENGS}
        self.ops = []
        self.lastw = {}
        self.rd = {}
        self.nslots = nslots
        self.slot_cnt = [0] * nslots
        self.slot_last = [None] * nslots
        self.dma_n = 0

    def add(self, eng, fn, r=(), w=(), dma=False):
        oid = len(self.ops)
        deps = set()
        raw = set()
        for k in r:
            p = self.lastw.get(k)
            if p is not None:
                deps.add(p)
                raw.add(p)
        for k in w:
            p = self.lastw.get(k)
            if p is not None:
                deps.add(p)
            rr = self.rd.get(k)
            if rr:
                deps.update(rr[0].values())
                deps.update(rr[1])
        o = {'id': oid, 'eng': eng, 'fn': fn, 'dma': dma, 'flag': False, 'raw': raw}
        if dma:
            s = self.dma_n % self.nslots
            self.dma_n += 1
            if self.slot_last[s] is not None:
                deps.add(self.slot_last[s])
            self.slot_cnt[s] += 1
            o['slot'] = s
            o['slotval'] = 16 * self.slot_cnt[s]
            self.slot_last[s] = oid
        for k in w:
            self.lastw[k] = oid
            self.rd[k] = ({}, [])
        for k in r:
            rr = self.rd.setdefault(k, ({}, []))
            if dma:
                rr[1].append(oid)
            else:
                rr[0][eng] = oid
        deps.discard(oid)
        o['deps'] = deps
        self.ops.append(o)
        self.streams[eng].append(o)
        return o

    def barrier(self):
        last = {}
        for e in self.ENGS:
            st = self.streams[e]
            for o in reversed(st):
                if o['fn'] is not None and not o['dma']:
                    last[e] = o['id']
                    break
        dmas = [x for x in self.slot_last if x is not None]
        for e in self.ENGS:
            oid = len(self.ops)
            deps = set(v for k, v in last.items() if k != e) | set(dmas)
            o = {'id': oid, 'eng': e, 'fn': None, 'dma': False, 'flag': False, 'deps': deps}
            self.ops.append(o)
            self.streams[e].append(o)

    def emit(self, nc, block, esem, ssem):
        ops = self.ops
        for o in ops:
            for d in o['deps']:
                dd = ops[d]
                if dd['dma']:
                    continue
                if dd['eng'] != o['eng'] or (d in o.get('raw', ()) and o['eng'] in self.SELF_SYNC):
                    dd['flag'] = True
        for e in self.ENGS:
            c = 0
            for o in self.streams[e]:
                if o['flag'] and not o['dma']:
                    c += 1
                    o['fidx'] = c
        sched = self

        def run(ename, eng):
            waited = {}
            for o in sched.streams[ename]:
                need = {}
                for d in o['deps']:
                    dd = ops[d]
                    if dd['dma']:
                        key = ('s', dd['slot'])
                        val = dd['slotval']
                    elif dd['eng'] == ename and not (d in o.get('raw', ()) and ename in sched.SELF_SYNC):
                        continue
                    else:
                        key = ('e', dd['eng'])
                        val = dd['fidx']
                    if val > waited.get(key, 0) and val > need.get(key, 0):
                        need[key] = val
                for key, val in need.items():
                    sem = ssem[key[1]] if key[0] == 's' else esem[key[1]]
                    eng.wait_ge(sem, val)
                    waited[key] = val
                if o['fn'] is None:
                    continue
                ins = o['fn'](eng)
                if o['dma']:
                    ins.then_inc(ssem[o['slot']], 16)
                elif o['flag']:
                    ins.then_inc(esem[ename], 1)
            if ename == 'sp':
                for s in range(sched.nslots):
                    if sched.slot_cnt[s]:
                        eng.wait_ge(ssem[s], 16 * sched.slot_cnt[s])

        @block.tensor
        def _(e):
            run('pe', e)

        @block.scalar
        def _(e):
            run('act', e)

        @block.vector
        def _(e):
            run('dve', e)

        @block.gpsimd
        def _(e):
            run('pool', e)

        @block.sync
        def _(e):
            run('sp', e)


def rawap(t, offset, pat):
    return bass.AP(tensor=t, offset=offset, ap=[list(p) for p in pat])


SEGT = 2048
DBGT = 512
BAR = False
TILE = 512
PI = math.pi


def build(nseg=8, nlayer=2, dbg=False, skip=()):
    nc = bass.Bass("TRN2", target_bir_lowering=False)
    S = Sched()
    ntok = nseg * SEGT
    x_in = nc.dram_tensor('x', [ntok, D], F32, kind='ExternalInput')
    W = {n: nc.dram_tensor(n, [2] + WSHAPES[n], F32, kind='ExternalInput') for n in WNAMES}
    WSZ = {n: int(np.prod(WSHAPES[n])) for n in WNAMES}
    out_t = nc.dram_tensor('out', [ntok, D], F32, kind='ExternalOutput')
    x1 = nc.dram_tensor('x1', [ntok, D], F32)
    xmid = nc.dram_tensor('xmid', [SEGT, D], F32)

    sb_top = [16512]
    SB_END = 229376

    def alloc(name, shape, dt, at=None):
        nbytes = int(np.prod(shape[1:])) * (4 if dt == F32 else 2)
        nbytes = (nbytes + 63) // 64 * 64
        if at is None:
            off = sb_top[0]
            sb_top[0] += nbytes
        else:
            off = at
        assert off + nbytes <= SB_END, (name, off, nbytes)
        return nc.alloc_sbuf_tensor_at(name, list(shape), dt, offset=off)

    psum = nc.alloc_psum_tensor('psum', [128, 8, 512], F32)

    def ps(b, p=128, n=512):
        return psum[0:p, b, 0:n]

    bank_rr = [0]

    def nextbank():
        b = bank_rr[0] % 6
        bank_rr[0] += 1
        return b

    ident = alloc('ident', [128, 128], BF16)
    onesbf = alloc('onesbf', [128, 128], BF16)
    ones32 = alloc('ones32', [128, 512], F32)
    maskut = alloc('maskut', [64, 64], F32)
    eps_t = alloc('eps_t', [128, 1], F32)
    negpi = alloc('negpi', [128, 1], F32)
    gains2 = [alloc('gain%d' % i, [128, D], F32) for i in range(2)]
    gains = [gains2[0], gains2[1], gains2[0], gains2[1]]
    xt0 = alloc('xt0', [128, D], F32)
    xt = [xt0, xt0]
    un = alloc('un', [128, D], BF16)
    junk = alloc('junk', [128, D], BF16)
    small = alloc('small', [128, 64], F32)
    stage = [alloc('stage%d' % i, [128, 512], F32) for i in range(2)]
    St32 = alloc('St32', [128, 4, 128], F32)
    Ct32 = alloc('Ct32', [128, 4, 128], F32)
    Nt32 = alloc('Nt32', [128, 4, 128], F32)
    Stbf = alloc('Stbf', [128, 4, 128], BF16)
    Ctbf = alloc('Ctbf', [128, 4, 128], BF16)
    Ntbf = alloc('Ntbf', [128, 4, 128], BF16)
    XXst = alloc('XXst', [128, 2, 16], F32)
    zch = alloc('zch', [128, 8, 3], F32)
    ahist = alloc('ahist', [128, 22, 2], F32)
    BBT = alloc('BBT', [32, 16, 2, 128], BF16)
    Ec = alloc('Ec', [128, 2, 16, 32], BF16)
    AA1 = alloc('AA1', [128, 2, 16], F32)
    A1i = alloc('A1i', [128, 16], F32)
    nA1i = alloc('nA1i', [128, 16], F32)
    Dd = alloc('Dd', [32, 16], F32)
    gsel = alloc('gsel', [32, 4, 128], BF16)
    selF = alloc('selF', [8, 4, 128], F32)
    selIF = alloc('selIF', [8, 4, 128], F32)
    UT0 = sb_top[0]
    uT = alloc('uT', [128, 8, SEGT], BF16)
    merged = alloc('merged', [128, 8, SEGT], BF16)
    PH0 = sb_top[0]
    arena = [PH0]
    an = [0]

    amax = {}

    def aalloc(name, shape, dt):
        an[0] += 1
        t = alloc('%s_%d' % (name, an[0]), shape, dt, at=arena[0])
        nb = int(np.prod(shape[1:])) * (4 if dt == F32 else 2)
        arena[0] += (nb + 63) // 64 * 64
        amax['top'] = max(amax.get('top', 0), arena[0])
        return t

    cnt = {'stage': 0}
    dumped = {}

    def dma(out, in_, r=(), w=(), slow=False):
        if slow:
            return S.add('sp', lambda e: e.dma_start(out=out, in_=in_, allow_slow_non_contiguous=True),
                         r=r, w=w, dma=True)
        return S.add('sp', lambda e: e.dma_start(out=out, in_=in_), r=r, w=w, dma=True)

    def act(out, in_, func, r, w, bias=None, scale=None, accum=None):
        kw = {}
        if bias is not None:
            kw['bias'] = bias
        if scale is not None:
            kw['scale'] = scale
        if accum is not None:
            kw['accum_out'] = accum
        return S.add('act', lambda e: e.activation(out=out, in_=in_, func=func, **kw), r=r, w=w)

    def tt(eng, out, in0, in1, op, r, w):
        return S.add(eng, lambda e: e.tensor_tensor(out=out, in0=in0, in1=in1, op=op), r=r, w=w)

    def ts(eng, out, in0, s1, s2, op0, op1, r, w):
        if s2 is None:
            return S.add(eng, lambda e: e.tensor_single_scalar(out=out, in_=in0, scalar=s1, op=op0), r=r, w=w)
        return S.add(eng, lambda e: e.tensor_scalar(out=out, in0=in0, scalar1=s1, scalar2=s2, op0=op0, op1=op1),
                     r=r, w=w)

    def stt(eng, out, in0, sc, in1, op0, op1, r, w):
        return S.add(eng, lambda e: e.scalar_tensor_tensor(out=out, in0=in0, scalar=sc, in1=in1, op0=op0, op1=op1),
                     r=r, w=w)

    def cp(eng, out, in_, r, w):
        if eng == 'act':
            return S.add('act', lambda e: e.copy(out=out, in_=in_), r=r, w=w)
        return S.add(eng, lambda e: e.tensor_copy(out=out, in_=in_), r=r, w=w)

    def mm(out, lhsT, rhs, start, stop, r, w):
        return S.add('pe', lambda e: e.matmul(out, lhsT, rhs, start=start, stop=stop, skip_group_check=True),
                     r=r, w=w)

    def tr(out, in_, idn, r, w):
        return S.add('pe', lambda e: e.transpose(out, in_, idn), r=r, w=w)

    def memset(eng, ap, val, w):
        return S.add(eng, lambda e: e.memset(ap, val), r=(), w=w)

    dbg_names = []

    def dump(name, ap, rkeys):
        if not dbg:
            return
        t = nc.dram_tensor('dbg_' + name, list(ap.shape), ap.dtype, kind='ExternalOutput')
        dbg_names.append('dbg_' + name)
        full = t.ap()
        dma(full, ap, r=rkeys, w=[('dbg', name)])

    scr = nc.dram_tensor('scr_bar', [64, 64], F32)
    barn = [0]

    def lbar(items):
        for ap_, key in items:
            i = barn[0] % 64
            barn[0] += 1
            if ap_.dtype != F32:
                continue
            dma(scr.ap()[i:i + 1, 0:1], ap_, r=[key], w=[('scr', i)])

    def recip(out, in_, r, w):
        return S.add('dve', lambda e: e.reciprocal(out=out, in_=in_), r=r, w=w)

    def load_w(dst, key, name, L, row0, kt, col0, ncols, rows=128):
        src_t = W[name]
        ld = WSHAPES[name][1] if len(WSHAPES[name]) > 1 else 1
        base = L * WSZ[name]
        c0 = 0
        while c0 < ncols:
            cw_ = min(ncols - c0, max(1, 512 // kt))
            i = cnt['stage'] % 2
            cnt['stage'] += 1
            stv = stage[i][0:rows, 0:kt * cw_].rearrange('p (k c) -> p k c', k=kt)
            src = rawap(src_t, base + row0 * ld + col0 + c0, [[ld, rows], [128 * ld, kt], [1, cw_]])
            dma(stv, src, r=(), w=[('stage', i)])
            cp('pool', dst[:, :, c0:c0 + cw_], stv, r=[('stage', i)], w=[key])
            c0 += cw_

    def vec_load(dst, name, L, pat, key, off=0, slow=True):
        dma(dst, rawap(W[name], L * WSZ[name] + off, pat), w=[key], slow=slow)

    memset('pool', ones32[:, :], 1.0, [('c', 'ones32')])
    memset('pool', eps_t[:, :], EPS, [('c', 'eps')])
    memset('pool', negpi[:, :], -PI, [('c', 'negpi')])
    memset('pool', maskut[:, :], 1.0, [('c', 'mask')])
    S.add('pool', lambda e: e.affine_select(out=maskut[:, :], in_=maskut[:, :], pattern=[[1, 64]],
                                            compare_op=ALU.is_ge, fill=0.0, base=0, channel_multiplier=-1),
          r=[('c', 'mask')], w=[('c', 'mask')])
    cp('pool', onesbf[:, :], ones32[:, 0:128], r=[('c', 'ones32')], w=[('c', 'onesbf')])
    identf = stage[1]
    memset('pool', identf[:, 0:128], 1.0, [('stage', 1)])
    S.add('pool', lambda e: e.affine_select(out=identf[:, 0:128], in_=identf[:, 0:128], pattern=[[1, 128]],
                                            compare_op=ALU.is_equal, fill=0.0, base=0, channel_multiplier=-1),
          r=[('stage', 1)], w=[('stage', 1)])
    cp('pool', ident[:, :], identf[:, 0:128], r=[('stage', 1)], w=[('c', 'ident')])
    for k in range(4):
        memset('pool', identf[0:32, 128:256], 1.0, [('stage', 1)])
        S.add('pool', lambda e, k=k: e.affine_select(out=identf[0:32, 128:256], in_=identf[0:32, 128:256],
                                                     pattern=[[1, 128]], compare_op=ALU.is_equal, fill=0.0,
                                                     base=-32 * k, channel_multiplier=-1),
              r=[('stage', 1)], w=[('stage', 1)])
        cp('pool', gsel[:, k, :], identf[0:32, 128:256], r=[('stage', 1)], w=[('c', 'gsel')])
    for h in range(4):
        for (dst, rows_) in ((selF, (4 + h,)), (selIF, (h, 4 + h))):
            memset('pool', dst[:, h, :], 0.0, [('c', 'sel')])
            for rr in rows_:
                memset('pool', identf[0:8, 256:384], 1.0, [('stage', 1)])
                S.add('pool', lambda e, rr=rr: e.affine_select(out=identf[0:8, 256:384], in_=identf[0:8, 256:384],
                                                               pattern=[[0, 128]], compare_op=ALU.is_equal, fill=0.0,
                                                               base=-rr, channel_multiplier=1),
                      r=[('stage', 1)], w=[('stage', 1)])
                tt('pool', dst[:, h, :], dst[:, h, :], identf[0:8, 256:384], ALU.add, r=[('stage', 1), ('c', 'sel')],
                   w=[('c', 'sel')])

    def norm_block(src_ap, nb, gain_t, gkey, dstT, dkey, dcol0, xi, extra_r=()):
        xb = xt[xi]
        dma(xb[0:nb, :], src_ap, r=extra_r, w=[('xt', 0)])
        act(junk[0:nb, :], xb[0:nb, :], AF.Square, r=[('xt', 0)], w=[('junk',)])
        S.add('dve', lambda e: e.reduce_sum(out=small[0:nb, 0:1], in_=junk[0:nb, :], axis=mybir.AxisListType.X),
              r=[('junk',)], w=[('ss',)])
        act(small[0:nb, 1:2], small[0:nb, 0:1], AF.Sqrt, r=[('ss',), ('c', 'eps')], w=[('sd',)],
            bias=eps_t[0:nb, :], scale=1.0 / D)
        recip(small[0:nb, 2:3], small[0:nb, 1:2], r=[('sd',)], w=[('rs',)])
        stt('dve', un[0:nb, :], xb[0:nb, :], small[0:nb, 2:3], gain_t[0:nb, :], ALU.mult, ALU.mult,
            r=[('xt', 0), ('rs',), gkey], w=[('un',)])
        pbf = psum[:, 7, :].bitcast(BF16)
        for k in range(8):
            tr(pbf[:, k * 128:k * 128 + nb], un[0:nb, k * 128:(k + 1) * 128], ident[0:nb, 0:nb],
               r=[('un',), ('c', 'ident')], w=[('ps', 7)])
        cp('act', dstT[:, :, dcol0:dcol0 + nb], pbf.rearrange('p (k t) -> p k t', k=8)[:, :, 0:nb], r=[('ps', 7)],
           w=[dkey])

    def rowsum_rstd(banks, nb, col):
        for hf, b in enumerate(banks):
            act(junk[0:nb, hf * 512:(hf + 1) * 512], ps(b, nb, 512), AF.Square, r=[('ps', b)], w=[('junk',)])
            S.add('dve', lambda e, hf=hf: e.reduce_sum(out=small[0:nb, 4 + hf:5 + hf],
                                                       in_=junk[0:nb, hf * 512:(hf + 1) * 512],
                                                       axis=mybir.AxisListType.X), r=[('junk',)], w=[('ssx', hf)])
        tt('dve', small[0:nb, 6:7], small[0:nb, 4:5], small[0:nb, 5:6], ALU.add, r=[('ssx', 0), ('ssx', 1)],
           w=[('ssx', 2)])
        act(small[0:nb, 7:8], small[0:nb, 6:7], AF.Sqrt, r=[('ssx', 2), ('c', 'eps')], w=[('ssx', 3)],
            bias=eps_t[0:nb, :], scale=1.0 / D)
        recip(small[0:nb, col:col + 1], small[0:nb, 7:8], r=[('ssx', 3)], w=[('rsx',)])

    def proj(bank, wt, wkey, c0, M, src, skey, tok0, nt, kt=8):
        for k in range(kt):
            mm(ps(bank, M, nt), wt[:, k, c0:c0 + M], src[:, k, tok0:tok0 + nt], k == 0, k == kt - 1,
               r=[wkey, skey], w=[('ps', bank)])

    def gelu(dst, v, n, vkey, dkey, tmpa, tmpb, P=128):
        act(tmpa[0:P, 0:n], v, AF.Square, r=[vkey], w=[('ga',)])
        ts('dve', tmpa[0:P, 0:n], tmpa[0:P, 0:n], 0.044715, 1.0, ALU.mult, ALU.add, r=[('ga',)], w=[('ga',)])
        tt('dve', tmpa[0:P, 0:n], tmpa[0:P, 0:n], v, ALU.mult, r=[('ga',), vkey], w=[('ga',)])
        act(tmpb[0:P, 0:n], tmpa[0:P, 0:n], AF.Sigmoid, r=[('ga',)], w=[('gb',)], scale=GC)
        tt('pool', dst, tmpb[0:P, 0:n], v, ALU.mult, r=[('gb',), vkey], w=[dkey])

    def s5_setup(L):
        arena[0] = PH0
        aR = aalloc('aR', [128, 16], F32); aI = aalloc('aI', [128, 16], F32); ldt = aalloc('ldt', [128, 16], F32)
        t = [aalloc('s5t', [128, 16], F32) for _ in range(12)]
        bR = aalloc('bR', [128, 16, 16], F32); bI = aalloc('bI', [128, 16, 16], F32)
        cR = aalloc('cR', [128, 16, 16], F32); cI = aalloc('cI', [128, 16, 16], F32)
        u1 = aalloc('u1', [128, 16, 16], F32); u2 = aalloc('u2', [128, 16, 16], F32)
        bsrc = [aalloc('bsrc', [128, 16, 32], BF16) for _ in range(2)]
        K = ('s5c', L)
        vec_load(aR[:, :], 's5_a_re', L, [[1, 128], [128, 16]], K)
        vec_load(aI[:, :], 's5_a_im', L, [[1, 128], [128, 16]], K)
        for e in range(2):
            vec_load(ldt[e * 64:(e + 1) * 64, :], 's5_log_dt', L, [[0, 64], [2, 16]], K, off=e)
        vec_load(bR[:, :, :], 's5_b_re', L, [[16, 128], [2048, 16], [1, 16]], K, slow=False)
        vec_load(bI[:, :, :], 's5_b_im', L, [[16, 128], [2048, 16], [1, 16]], K, slow=False)
        for e in range(2):
            for pr in range(16):
                vec_load(cR[e * 64:(e + 1) * 64, pr, :], 's5_c_re', L, [[1, 64], [64, 16]], K, off=e * 1024 + pr * 2048)
                vec_load(cI[e * 64:(e + 1) * 64, pr, :], 's5_c_im', L, [[1, 64], [64, 16]], K, off=e * 1024 + pr * 2048)
        vec_load(Dd[:, :], 's5_d', L, [[1, 32], [32, 16]], K)
        R = [K]
        dt_, dre, dim, mag, sn, cs, A1r, den, zr, fre, fim, tmp = t
        act(dt_[:, :], ldt[:, :], AF.Exp, r=R, w=R)
        tt('dve', dre[:, :], dt_[:, :], aR[:, :], ALU.mult, r=R, w=R)
        tt('dve', dim[:, :], dt_[:, :], aI[:, :], ALU.mult, r=R, w=R)
        act(mag[:, :], dre[:, :], AF.Exp, r=R, w=R)
        cp('dve', sn[:, :], dim[:, :], r=R, w=R)
        ts('dve', cs[:, :], dim[:, :], 0.5 * PI, None, ALU.add, None, r=R, w=R)
        for arr in (sn, cs):
            cp('dve', den[:, :], arr[:, :], r=R, w=R)
            for th in (PI, 3 * PI, 5 * PI, 7 * PI):
                ts('dve', tmp[:, :], den[:, :], th, None, ALU.is_ge, None, r=R, w=R)
                stt('dve', arr[:, :], tmp[:, :], -2 * PI, arr[:, :], ALU.mult, ALU.add, r=R, w=R)
            act(arr[:, :], arr[:, :], AF.Sin, r=R, w=R)
        tt('dve', A1r[:, :], mag[:, :], cs[:, :], ALU.mult, r=R, w=R)
        tt('dve', A1i[:, :], mag[:, :], sn[:, :], ALU.mult, r=R, w=R)
        ts('dve', nA1i[:, :], A1i[:, :], -1.0, None, ALU.mult, None, r=R, w=R)
        cp('dve', AA1[:, 0, :], A1r[:, :], r=R, w=R)
        cp('dve', AA1[:, 1, :], A1r[:, :], r=R, w=R)
        tt('dve', den[:, :], aR[:, :], aR[:, :], ALU.mult, r=R, w=R)
        tt('dve', tmp[:, :], aI[:, :], aI[:, :], ALU.mult, r=R, w=R)
        tt('dve', den[:, :], den[:, :], tmp[:, :], ALU.add, r=R, w=R)
        recip(den[:, :], den[:, :], r=R, w=R)
        ts('dve', zr[:, :], A1r[:, :], -1.0, None, ALU.add, None, r=R, w=R)
        tt('dve', fre[:, :], zr[:, :], aR[:, :], ALU.mult, r=R, w=R)
        tt('dve', tmp[:, :], A1i[:, :], aI[:, :], ALU.mult, r=R, w=R)
        tt('dve', fre[:, :], fre[:, :], tmp[:, :], ALU.add, r=R, w=R)
        tt('dve', fre[:, :], fre[:, :], den[:, :], ALU.mult, r=R, w=R)
        tt('dve', fim[:, :], A1i[:, :], aR[:, :], ALU.mult, r=R, w=R)
        tt('dve', tmp[:, :], zr[:, :], aI[:, :], ALU.mult, r=R, w=R)
        tt('dve', fim[:, :], fim[:, :], tmp[:, :], ALU.subtract, r=R, w=R)
        tt('dve', fim[:, :], fim[:, :], den[:, :], ALU.mult, r=R, w=R)
        frb = fre[:, :].unsqueeze(2).to_broadcast([128, 16, 16])
        fib = fim[:, :].unsqueeze(2).to_broadcast([128, 16, 16])
        tt('dve', u1[:, :, :], bR[:, :, :], frb, ALU.mult, r=R, w=R)
        tt('dve', u2[:, :, :], bI[:, :, :], fib, ALU.mult, r=R, w=R)
        tt('dve', u1[:, :, :], u1[:, :, :], u2[:, :, :], ALU.subtract, r=R, w=R)
        tt('dve', u2[:, :, :], bI[:, :, :], frb, ALU.mult, r=R, w=R)
        tt('dve', bI[:, :, :], bR[:, :, :], fib, ALU.mult, r=R, w=R)
        tt('dve', u2[:, :, :], u2[:, :, :], bI[:, :, :], ALU.add, r=R, w=R)
        ts('dve', cI[:, :, :], cI[:, :, :], -1.0, None, ALU.mult, None, r=R, w=R)
        for ri, (bsrc_, srcb, srcc) in enumerate(((bsrc[0], u1, cR), (bsrc[1], u2, cI))):
            memset('dve', bsrc_[:, :, :], 0.0, R)
            memset('dve', Ec[:, ri, :, :], 0.0, R)
            for e in range(2):
                cp('dve', bsrc_[e * 64:(e + 1) * 64, :, e * 16:(e + 1) * 16], srcb[e * 64:(e + 1) * 64, :, :], r=R, w=R)
                cp('dve', Ec[e * 64:(e + 1) * 64, ri, :, e * 16:(e + 1) * 16], srcc[e * 64:(e + 1) * 64, :, :], r=R, w=R)
        pbf = psum[:, 7, :].bitcast(BF16)
        for ri in range(2):
            for g4 in range(2):
                for j in range(8):
                    pr = g4 * 8 + j
                    tr(pbf[0:32, j * 128:(j + 1) * 128], bsrc[ri][:, pr, :], ident[:, :], r=R + [('c', 'ident')],
                       w=[('ps', 7)])
                cp('act', BBT[:, g4 * 8:(g4 + 1) * 8, ri, :], pbf[0:32, :].rearrange('p (j q) -> p j q', j=8),
                   r=[('ps', 7)], w=R)

    def chunk_engine(nt, h, qa, qkeys, ka, kkeys, qscale, kscale, vtok, vcol0, stl, bfl, use_n, o_bank, den_bank,
                     B):
        nch = nt // 64
        qd, ki, kd, kdT, sT, E1, E2, E3, dec = B
        stt('dve', qd[:, 0:nt], qa, qscale, E1[:, 0:nt], ALU.mult, ALU.mult, r=qkeys + [('E1',)], w=[('qd',)])
        stt('dve', ki[:, 0:nt], ka, kscale, E2[:, 0:nt], ALU.mult, ALU.mult, r=kkeys + [('E2',)], w=[('ki',)])
        stt('dve', kd[:, 0:nt], ka, kscale, E3[:, 0:nt], ALU.mult, ALU.mult, r=kkeys + [('E3',)], w=[('kd',)])
        S32 = stl[0]
        Sbf = bfl[0]
        pbf = psum[:, 7, :].bitcast(BF16)
        for c in range(nch):
            cs = c * 64
            i2 = c % 2
            tr(pbf[0:64, 0:128], kd[:, cs:cs + 64], ident[:, :], r=[('kd',), ('c', 'ident')], w=[('ps', 7)])
            cp('act', kdT[i2][:, :], pbf[0:64, 0:128], r=[('ps', 7)], w=[('kdT', i2)])
            mm(ps(6, 64, 64), ki[:, cs:cs + 64], qd[:, cs:cs + 64], True, True, r=[('ki',), ('qd',)], w=[('ps', 6)])
            tt('dve', sT[i2][:, :], ps(6, 64, 64), maskut[:, :], ALU.mult, r=[('ps', 6), ('c', 'mask')],
               w=[('sT', i2)])
            vv = vtok[:, c, vcol0 + h * 128: vcol0 + (h + 1) * 128]
            mm(psum[:, o_bank, cs:cs + 64], vv, sT[i2][:, :], True, False, r=[('vtok',), ('sT', i2)],
               w=[('ps', o_bank)])
            mm(psum[:, o_bank, cs:cs + 64], Sbf[:, h, :], qd[:, cs:cs + 64], False, True,
               r=[('stbf', id(Sbf), h), ('qd',)], w=[('ps', o_bank)])
            if use_n:
                Nbf = bfl[1]
                mm(psum[:, den_bank, cs:cs + 64], onesbf[0:64, :], sT[i2][:, :], True, False,
                   r=[('c', 'onesbf'), ('sT', i2)], w=[('ps', den_bank)])
                mm(psum[:, den_bank, cs:cs + 64], Nbf[:, h, :], qd[:, cs:cs + 64], False, True,
                   r=[('stbf', id(Nbf), h), ('qd',)], w=[('ps', den_bank)])
            mm(ps(6, 128, 128), kdT[i2][:, :], vv, True, True, r=[('kdT', i2), ('vtok',)], w=[('ps', 6)])
            stt('dve', S32[:, h, :], S32[:, h, :], dec[:, c:c + 1], ps(6, 128, 128), ALU.mult, ALU.add,
                r=[('st', id(S32), h), ('dec',), ('ps', 6)], w=[('st', id(S32), h)])
            cp('act', Sbf[:, h, :], S32[:, h, :], r=[('st', id(S32), h)], w=[('stbf', id(Sbf), h)])
            if use_n:
                N32 = stl[1]
                mm(ps(6, 128, 128), kdT[i2][:, :], onesbf[0:64, :], True, True, r=[('kdT', i2), ('c', 'onesbf')],
                   w=[('ps', 6)])
                stt('dve', N32[:, h, :], N32[:, h, :], dec[:, c:c + 1], ps(6, 128, 128), ALU.mult, ALU.add,
                    r=[('st', id(N32), h), ('dec',), ('ps', 6)], w=[('st', id(N32), h)])
                cp('act', Nbf[:, h, :], N32[:, h, :], r=[('st', id(N32), h)], w=[('stbf', id(Nbf), h)])

    def vproj(tok0, nt, wt, wkey, c0, vtok):
        for c in range(nt // 64):
            b = nextbank()
            for k in range(8):
                mm(ps(b, 64, 512), uT[:, k, tok0 + c * 64: tok0 + c * 64 + 64], wt[:, k, c0:c0 + 512],
                   k == 0, k == 7, r=[wkey, ('uT',)], w=[('ps', b)])
            cp('act', vtok[:, c, :], ps(b, 64, 512), r=[('ps', b)], w=[('vtok',)])

    def merge_out(tok0, nt, oT, okey, wpr, wprkey, wgate, wgkey, gc0, first, gsig, gtmp):
        for co in range(8):
            b = nextbank()
            for k in range(4):
                mm(ps(b, 128, nt), wpr[:, k, co * 128:(co + 1) * 128], oT[:, k, 0:nt], k == 0, k == 3,
                   r=[wprkey, okey], w=[('ps', b)])
            b2 = nextbank()
            proj(b2, wgate, wgkey, gc0 + co * 128, 128, uT, ('uT',), tok0, nt)
            act(gsig[:, 0:nt], ps(b2, 128, nt), AF.Sigmoid, r=[('ps', b2)], w=[('gsig',)])
            if first:
                tt('dve', merged[:, co, tok0:tok0 + nt], ps(b, 128, nt), gsig[:, 0:nt], ALU.mult,
                   r=[('ps', b), ('gsig',)], w=[('mg',)])
            else:
                tt('dve', gtmp[:, 0:nt], ps(b, 128, nt), gsig[:, 0:nt], ALU.mult, r=[('ps', b), ('gsig',)],
                   w=[('gtmp',)])
                tt('pool', merged[:, co, tok0:tok0 + nt], merged[:, co, tok0:tok0 + nt], gtmp[:, 0:nt], ALU.add,
                   r=[('gtmp',), ('mg',)], w=[('mg',)])

    def common_bufs():
        qd = aalloc('qd', [128, 512], BF16); ki = aalloc('ki', [128, 512], BF16); kd = aalloc('kd', [128, 512], BF16)
        kdT = [aalloc('kdT', [64, 128], BF16) for _ in range(2)]
        sT = [aalloc('sT', [64, 64], BF16) for _ in range(2)]
        E1 = aalloc('E1', [128, 512], F32); E2 = aalloc('E2', [128, 512], F32); E3 = aalloc('E3', [128, 512], F32)
        dec = aalloc('dec', [128, 8], F32)
        return (qd, ki, kd, kdT, sT, E1, E2, E3, dec)

    def phase_gla(L):
        arena[0] = PH0
        Wg = aalloc('Wg', [128, 8, 3088], BF16)
        Wgk = aalloc('Wgk', [16, 1, 512], BF16)
        Wpr = aalloc('Wpr', [128, 4, D], BF16)
        bgk = aalloc('bgk', [128, 4], F32); gln = aalloc('gln', [128, 1], F32)
        Bx = aalloc('Bx', [128, 513], F32)
        Drel = aalloc('Drel', [128, 512], F32); Drel2 = aalloc('Drel2', [128, 512], F32)
        B = common_bufs()
        qd, ki, kd, kdT, sT, E1, E2, E3, dec = B
        vtok = aalloc('vtok', [64, 8, 512], BF16)
        lrT = aalloc('lrT', [16, 512], BF16)
        sq = aalloc('sq', [128, 512], BF16)
        rstd = E1; t1 = E2; sr = E3
        ogT = aalloc('ogT', [128, 4, 512], BF16)
        gsig = aalloc('gsig', [128, 512], F32); gtmp = aalloc('gtmp', [128, 512], F32)
        spb = gtmp
        K = ('W', 'g')
        load_w(Wg[:, :, 0:2064], K, 'w_in', L, 0, 8, 0, 2064)
        load_w(Wg[:, :, 2064:3088], K, 'w_in', L, 0, 8, O_GATE, 1024)
        load_w(Wgk[:, :, :], K, 'gla_w_gk', L, 0, 1, 0, 512, rows=16)
        load_w(Wpr[:, :, :], K, 'gla_w_proj', L, 0, 4, 0, D)
        vec_load(bgk[:, :], 'gla_b_gk', L, [[1, 128], [128, 4]], K)
        ts('dve', bgk[:, :], bgk[:, :], -1.0, None, ALU.mult, None, r=[K], w=[K])
        vec_load(gln[:, :], 'gla_norm', L, [[1, 128], [1, 1]], K)
        memset('pool', Bx[:, 0:1], 0.0, [('Bx',)])
        for tok0 in range(0, SEGT, TILE):
            nt = TILE
            nch = nt // 64
            vproj(tok0, nt, Wg, K, O_GV, vtok)
            b = nextbank()
            proj(b, Wg, K, O_LR, 16, uT, ('uT',), tok0, nt)
            cp('act', lrT[:, 0:nt], ps(b, 16, nt), r=[('ps', b)], w=[('lrT',)])
            for h in range(4):
                b = nextbank()
                mm(ps(b, 128, nt), Wgk[:, 0, h * 128:(h + 1) * 128], lrT[:, 0:nt], True, True, r=[K, ('lrT',)],
                   w=[('ps', b)])
                act(spb[:, 0:nt], ps(b, 128, nt), AF.Exp, r=[('ps', b), K], w=[('gtmp',)], bias=bgk[:, h:h + 1],
                    scale=-1.0)
                act(spb[:, 0:nt], spb[:, 0:nt], AF.Ln, r=[('gtmp',)], w=[('gtmp',)], bias=1.0)
                S.add('dve', lambda e, nt=nt: e.tensor_tensor_scan(out=Bx[:, 1:1 + nt], data0=ones32[:, 0:nt],
                                                                   data1=spb[:, 0:nt], initial=0.0,
                                                                   op0=ALU.mult, op1=ALU.add),
                      r=[('gtmp',), ('c', 'ones32')], w=[('Bx',)])
                d3 = Drel[:, 0:nt].rearrange('p (c l) -> p c l', l=64)
                d23 = Drel2[:, 0:nt].rearrange('p (c l) -> p c l', l=64)
                bx3 = Bx[:, 1:1 + nt].rearrange('p (c l) -> p c l', l=64)
                br3 = Bx[:, 0:nt].rearrange('p (c l) -> p c l', l=64)[:, :, 0:1]
                tt('dve', d3, bx3, br3.to_broadcast([128, nch, 64]), ALU.subtract, r=[('Bx',)], w=[('Drel',)])
                tt('dve', d23, d3, d3[:, :, 63:64].to_broadcast([128, nch, 64]), ALU.subtract, r=[('Drel',)],
                   w=[('Drel2',)])
                act(E1[:, 0:nt], Drel[:, 0:nt], AF.Exp, r=[('Drel',)], w=[('E1',)], scale=-1.0 / 16)
                act(dec[:, 0:nch], Drel[:, 63:nt:64], AF.Exp, r=[('Drel',)], w=[('dec',)], scale=-1.0 / 16)
                act(E2[:, 0:nt], Drel[:, 0:nt], AF.Exp, r=[('Drel',)], w=[('E2',)], scale=1.0 / 16)
                act(E3[:, 0:nt], Drel2[:, 0:nt], AF.Exp, r=[('Drel2',)], w=[('E3',)], scale=1.0 / 16)
                bq = nextbank()
                proj(bq, Wg, K, O_GQ + h * 128, 128, uT, ('uT',), tok0, nt)
                bk = nextbank()
                proj(bk, Wg, K, O_GK + h * 128, 128, uT, ('uT',), tok0, nt)
                ob = nextbank()
                chunk_engine(nt, h, ps(bq, 128, nt), [('ps', bq)], ps(bk, 128, nt), [('ps', bk)], 128 ** -0.5, 1.0,
                             vtok, 0, [St32], [Stbf], False, ob, None, B)
                act(sq[:, 0:nt], psum[:, ob, 0:nt], AF.Square, r=[('ps', ob)], w=[('sq',)])
                b = nextbank()
                mm(ps(b, 128, nt), onesbf[:, :], sq[:, 0:nt], True, True, r=[('c', 'onesbf'), ('sq',)], w=[('ps', b)])
                act(rstd[:, 0:nt], ps(b, 128, nt), AF.Sqrt, r=[('ps', b), ('c', 'eps')], w=[('E1',)],
                    bias=eps_t[:, :], scale=1.0 / 128)
                recip(rstd[:, 0:nt], rstd[:, 0:nt], r=[('E1',)], w=[('E1',)])
                tt('dve', t1[:, 0:nt], psum[:, ob, 0:nt], rstd[:, 0:nt], ALU.mult, r=[('ps', ob), ('E1',)],
                   w=[('E2',)])
                b = nextbank()
                proj(b, Wg, K, O_GR + h * 128, 128, uT, ('uT',), tok0, nt)
                act(sr[:, 0:nt], ps(b, 128, nt), AF.Silu, r=[('ps', b)], w=[('E3',)])
                stt('dve', ogT[:, h, 0:nt], sr[:, 0:nt], gln[:, 0:1], t1[:, 0:nt], ALU.mult, ALU.mult,
                    r=[('E3',), K, ('E2',)], w=[('ogT',)])
            if tok0 == DBGT and L == 0 and not dumped.get('gla'):
                dumped['gla'] = 1
                dump('uT0', uT[:, :, DBGT:DBGT + 512], [('uT',)])
                dump('ident', ident[:, :], [('c', 'ident')])
                dump('mask', maskut[:, :], [('c', 'mask')])
                dump('Bx', Bx[:, :], [('Bx',)])
                dump('Drel', Drel[:, :], [('Drel',)])
                dump('dec', dec[:, :], [('dec',)])
                dump('qd', qd[:, :], [('qd',)])
                dump('ki', ki[:, :], [('ki',)])
                dump('kd', kd[:, :], [('kd',)])
                dump('vtok', vtok[:, :, :], [('vtok',)])
                dump('ogT', ogT[:, :, :], [('ogT',)])
                dump('St', St32[:, :, :], [('st', id(St32), h) for h in range(4)])
                dump('lrT', lrT[:, :], [('lrT',)])
                dump('Wg', Wg[:, :, 0:64], [K])
                dump('E1', E1[:, :], [('E1',)])
                dump('E2', E2[:, :], [('E2',)])
                dump('E3', E3[:, :], [('E3',)])
                dump('sq', sq[:, :], [('sq',)])
                dump('gln', gln[:, :], [K])
            merge_out(tok0, nt, ogT, ('ogT',), Wpr, K, Wg, K, 2064, True, gsig, gtmp)
            if BAR:
                lbar([(Bx[0:1, 0:1], ('Bx',)), (Drel[0:1, 0:1], ('Drel',)), (Drel2[0:1, 0:1], ('Drel2',)),
                      (dec[0:1, 0:1], ('dec',)), (E1[0:1, 0:1], ('E1',)), (E2[0:1, 0:1], ('E2',)), (E3[0:1, 0:1], ('E3',)),
                      (gsig[0:1, 0:1], ('gsig',)), (gtmp[0:1, 0:1], ('gtmp',))])
            if tok0 == DBGT and L == 0 and not dumped.get('gla2'):
                dumped['gla2'] = 1
                dump('mg0', merged[:, :, DBGT:DBGT + 512], [('mg',)])

    def phase_ml(L):
        arena[0] = PH0
        Wm = aalloc('Wm', [128, 8, 3080], BF16)
        Wpr = aalloc('Wmp', [128, 4, D], BF16)
        B = common_bufs()
        qd, ki, kd, kdT, sT, E1, E2, E3, dec = B
        vtok = aalloc('vtok', [64, 8, 512], BF16)
        zc = aalloc('zc', [128, 515], F32)
        qh = aalloc('qh', [128, 512], BF16); kh = aalloc('kh', [128, 512], BF16)
        B8 = aalloc('B8', [8, 513], F32)
        T2 = aalloc('T2', [8, 512], F32); T3 = aalloc('T3', [8, 512], F32)
        bif = aalloc('bif', [8, 1], F32); cw = aalloc('cw', [128, 8, 4], F32); cb = aalloc('cb', [128, 8], F32)
        dabs = E1; t1 = E2; so = E3
        ogT = aalloc('hT', [128, 4, 512], BF16)
        gsig = aalloc('gsig', [128, 512], F32); gtmp = aalloc('gtmp', [128, 512], F32)
        cacc = gtmp; gi8 = gsig[0:8, :]
        K = ('W', 'm')
        load_w(Wm[:, :, 0:2056], K, 'w_in', L, 0, 8, O_MQ, 2056)
        load_w(Wm[:, :, 2056:3080], K, 'w_in', L, 0, 8, O_GATE + 2048, 1024)
        load_w(Wpr[:, :, :], K, 'ml_w_proj', L, 0, 4, 0, D)
        vec_load(bif[0:4, :], 'ml_b_i', L, [[1, 4], [1, 1]], K)
        vec_load(bif[4:8, :], 'ml_b_f', L, [[1, 4], [1, 1]], K)
        for j in range(4):
            vec_load(cw[:, :, j], 'ml_conv_w', L, [[1, 128], [128, 8]], K, off=j * 1024)
        vec_load(cb[:, :], 'ml_conv_b', L, [[1, 128], [128, 8]], K)
        memset('pool', B8[:, 0:1], 0.0, [('B8',)])
        for tok0 in range(0, SEGT, TILE):
            nt = TILE
            nch = nt // 64
            vproj(tok0, nt, Wm, K, 1024, vtok)
            b = nextbank()
            proj(b, Wm, K, 1536, 8, uT, ('uT',), tok0, nt)
            act(gi8[:, 0:nt], ps(b, 8, nt), AF.Identity, r=[('ps', b), K], w=[('gsig',)], bias=bif[:, :])
            act(T3[:, 0:nt], gi8[:, 0:nt], AF.Exp, r=[('gsig',)], w=[('T3',)], scale=-1.0)
            act(T3[:, 0:nt], T3[:, 0:nt], AF.Ln, r=[('T3',)], w=[('T3',)], bias=1.0)
            S.add('dve', lambda e, nt=nt: e.tensor_tensor_scan(out=B8[:, 1:1 + nt], data0=ones32[0:8, 0:nt],
                                                               data1=T3[:, 0:nt], initial=0.0, op0=ALU.mult,
                                                               op1=ALU.add),
                  r=[('T3',), ('c', 'ones32')], w=[('B8',)])
            d3 = T2[:, 0:nt].rearrange('p (c l) -> p c l', l=64)
            d23 = T3[:, 0:nt].rearrange('p (c l) -> p c l', l=64)
            bx3 = B8[:, 1:1 + nt].rearrange('p (c l) -> p c l', l=64)
            br3 = B8[:, 0:nt].rearrange('p (c l) -> p c l', l=64)[:, :, 0:1]
            tt('dve', d3, bx3, br3.to_broadcast([8, nch, 64]), ALU.subtract, r=[('B8',)], w=[('T2',)])
            tt('dve', d23, d3, d3[:, :, 63:64].to_broadcast([8, nch, 64]), ALU.subtract, r=[('T2',), ('B8',)],
               w=[('T3',)])
            cp('dve', T2[0:4, 0:nt], gi8[0:4, 0:nt], r=[('gsig',), ('T2',)], w=[('T2',)])
            cp('dve', T3[0:4, 0:nt], gi8[0:4, 0:nt], r=[('gsig',), ('T3',)], w=[('T3',)])
            for h in range(4):
                b1 = nextbank()
                mm(ps(b1, 128, nt), selF[:, h, :], T2[:, 0:nt], True, True, r=[('c', 'sel'), ('T2',)], w=[('ps', b1)])
                act(E1[:, 0:nt], ps(b1, 128, nt), AF.Exp, r=[('ps', b1)], w=[('E1',)], scale=-1.0)
                act(dec[:, 0:nch], psum[:, b1, 63:nt:64], AF.Exp, r=[('ps', b1)], w=[('dec',)], scale=-1.0)
                b2 = nextbank()
                mm(ps(b2, 128, nt), selIF[:, h, :], T2[:, 0:nt], True, True, r=[('c', 'sel'), ('T2',)], w=[('ps', b2)])
                act(E2[:, 0:nt], ps(b2, 128, nt), AF.Exp, r=[('ps', b2)], w=[('E2',)])
                b3 = nextbank()
                mm(ps(b3, 128, nt), selIF[:, h, :], T3[:, 0:nt], True, True, r=[('c', 'sel'), ('T3',)], w=[('ps', b3)])
                act(E3[:, 0:nt], ps(b3, 128, nt), AF.Exp, r=[('ps', b3)], w=[('E3',)])
                for which, dstq in ((0, qh), (1, kh)):
                    ct = which * 4 + h
                    b = nextbank()
                    proj(b, Wm, K, ct * 128, 128, uT, ('uT',), tok0, nt)
                    cp('act', zc[:, 3:3 + nt], ps(b, 128, nt), r=[('ps', b)], w=[('zc',)])
                    cp('pool', zc[:, 0:3], zch[:, ct, :], r=[('zch', ct)], w=[('zc',)])
                    ts('dve', cacc[:, 0:nt], zc[:, 0:nt], cw[:, ct, 0:1], None, ALU.mult, None, r=[('zc',), K],
                       w=[('gtmp',)])
                    for j in range(1, 4):
                        stt('dve', cacc[:, 0:nt], zc[:, j:j + nt], cw[:, ct, j:j + 1], cacc[:, 0:nt], ALU.mult, ALU.add,
                            r=[('zc',), K, ('gtmp',)], w=[('gtmp',)])
                    cp('pool', zch[:, ct, :], zc[:, nt:nt + 3], r=[('zc',)], w=[('zch', ct)])
                    act(dstq[:, 0:nt], cacc[:, 0:nt], AF.Silu, r=[('gtmp',), K], w=[('qk', which)],
                        bias=cb[:, ct:ct + 1])
                ob = nextbank()
                db = nextbank()
                chunk_engine(nt, h, qh[:, 0:nt], [('qk', 0)], kh[:, 0:nt], [('qk', 1)], 1.0, 128 ** -0.5, vtok, 0,
                             [Ct32, Nt32], [Ctbf, Ntbf], True, ob, db, B)
                ts('dve', dabs[:, 0:nt], psum[:, db, 0:nt], -1.0, 1.0, ALU.mult, ALU.max, r=[('ps', db)], w=[('E1',)])
                tt('dve', dabs[:, 0:nt], dabs[:, 0:nt], psum[:, db, 0:nt], ALU.max, r=[('ps', db), ('E1',)], w=[('E1',)])
                recip(dabs[:, 0:nt], dabs[:, 0:nt], r=[('E1',)], w=[('E1',)])
                tt('dve', t1[:, 0:nt], psum[:, ob, 0:nt], dabs[:, 0:nt], ALU.mult, r=[('ps', ob), ('E1',)], w=[('E2',)])
                b = nextbank()
                proj(b, Wm, K, 1544 + h * 128, 128, uT, ('uT',), tok0, nt)
                act(so[:, 0:nt], ps(b, 128, nt), AF.Sigmoid, r=[('ps', b)], w=[('E3',)])
                tt('dve', ogT[:, h, 0:nt], t1[:, 0:nt], so[:, 0:nt], ALU.mult, r=[('E2',), ('E3',)], w=[('ogT',)])
            merge_out(tok0, nt, ogT, ('ogT',), Wpr, K, Wm, K, 2056, False, gsig, gtmp)

    def phase_s5(L):
        arena[0] = PH0
        Ws = aalloc('Ws', [128, 8, 1536], BF16)
        Wglu = aalloc('Wglu', [128, 4, 512], BF16)
        Wsp = aalloc('Wsp', [128, 4, D], BF16)
        bglu = aalloc('bglu', [128, 4], F32)
        su128 = aalloc('su128', [128, 4, 512], BF16)
        sup = aalloc('sup', [32, 16, 64], BF16)
        GU = aalloc('GU', [128, 2, 16, 64], F32)
        XXall = aalloc('XXall', [128, 2, 16, 65], F32)
        XXbf = aalloc('XXbf', [128, 2, 16, 64], BF16)
        P1 = aalloc('P1', [128, 2, 16], F32); P2 = aalloc('P2', [128, 2, 16], F32)
        yp = aalloc('yp', [32, 16, 64], BF16)
        yv = aalloc('yv', [128, 256], F32)
        ga = aalloc('ga', [128, 256], F32); gb = aalloc('gb', [128, 256], F32)
        yg = aalloc('yg', [128, 4, 512], BF16)
        y2 = aalloc('y2', [128, 4, 512], BF16)
        gsig = aalloc('gsig', [128, 512], F32); gtmp = aalloc('gtmp', [128, 512], F32)
        K = ('W', 's')
        KC = ('s5c', L)
        load_w(Ws[:, :, 0:512], K, 'w_in', L, 0, 8, O_SU, 512)
        load_w(Ws[:, :, 512:1536], K, 'w_in', L, 0, 8, O_GATE + 1024, 1024)
        load_w(Wglu[:, :, :], K, 's5_w_glu', L, 0, 4, 0, 512)
        load_w(Wsp[:, :, :], K, 's5_w_proj', L, 0, 4, 0, D)
        vec_load(bglu[:, :], 's5_b_glu', L, [[1, 128], [128, 4]], K)
        cp('dve', XXall[:, :, :, 0], XXst[:, :, :], r=[('xxst',)], w=[('XXall',)])
        for tok0 in range(0, SEGT, TILE):
            nt = TILE
            for blk in range(4):
                b = nextbank()
                proj(b, Ws, K, blk * 128, 128, uT, ('uT',), tok0, nt)
                cp('act', su128[:, blk, 0:nt], ps(b, 128, nt), r=[('ps', b)], w=[('su128',)])
            for tb in range(0, nt, 64):
                for blk in range(4):
                    b = nextbank()
                    for k in range(4):
                        mm(psum[0:32, b, k * 64:(k + 1) * 64], ident[:, 32 * k:32 * k + 32],
                           su128[:, blk, tb:tb + 64], True, True, r=[('c', 'ident'), ('su128',)], w=[('ps', b)])
                    cp('act', sup[:, blk * 4:(blk + 1) * 4, :], ps(b, 32, 256).rearrange('p (k t) -> p k t', k=4),
                       r=[('ps', b)], w=[('sup',)])
                for ri in range(2):
                    for g in range(4):
                        b = nextbank()
                        for k in range(4):
                            pr = g * 4 + k
                            mm(psum[:, b, k * 64:(k + 1) * 64], BBT[:, pr, ri, :], sup[:, pr, :], True, True,
                               r=[KC, ('sup',)], w=[('ps', b)])
                        cp('act', GU[:, ri, g * 4:(g + 1) * 4, :], ps(b, 128, 256).rearrange('p (k t) -> p k t', k=4),
                           r=[('ps', b)], w=[('GU',)])
                for t in range(64):
                    Xt = XXall[:, :, :, t]
                    tt('dve', P1[:, :, :], Xt, AA1[:, :, :], ALU.mult, r=[('XXall',), KC], w=[('P1',)])
                    tt('dve', P2[:, 0, :], XXall[:, 1, :, t], nA1i[:, :], ALU.mult, r=[('XXall',), KC], w=[('P2',)])
                    tt('dve', P2[:, 1, :], XXall[:, 0, :, t], A1i[:, :], ALU.mult, r=[('XXall',), KC], w=[('P2',)])
                    tt('dve', P1[:, :, :], P1[:, :, :], P2[:, :, :], ALU.add, r=[('P1',), ('P2',)], w=[('P1',)])
                    tt('dve', XXall[:, :, :, t + 1], P1[:, :, :], GU[:, :, :, t], ALU.add, r=[('P1',), ('GU',)],
                       w=[('XXall',)])
                cp('act', XXbf[:, :, :, :], XXall[:, :, :, 1:65], r=[('XXall',)], w=[('XXbf',)])
                cp('dve', XXall[:, :, :, 0], XXall[:, :, :, 64], r=[('XXall',)], w=[('XXall',)])
                for g in range(4):
                    b = nextbank()
                    for k in range(4):
                        pr = g * 4 + k
                        mm(psum[0:32, b, k * 64:(k + 1) * 64], Ec[:, 0, pr, :], XXbf[:, 0, pr, :], True, False,
                           r=[KC, ('XXbf',)], w=[('ps', b)])
                        mm(psum[0:32, b, k * 64:(k + 1) * 64], Ec[:, 1, pr, :], XXbf[:, 1, pr, :], False, True,
                           r=[KC, ('XXbf',)], w=[('ps', b)])
                    for k in range(4):
                        pr = g * 4 + k
                        stt('dve', yp[:, pr, :], sup[:, pr, :], Dd[:, pr:pr + 1], psum[0:32, b, k * 64:(k + 1) * 64],
                            ALU.mult, ALU.add, r=[('sup',), KC, ('ps', b)], w=[('yp',)])
                b = nextbank()
                for blk in range(4):
                    for k in range(4):
                        mm(psum[:, b, blk * 64:(blk + 1) * 64], gsel[:, k, :], yp[:, blk * 4 + k, :], k == 0, k == 3,
                           r=[('c', 'gsel'), ('yp',)], w=[('ps', b)])
                cp('act', yv[:, :], ps(b, 128, 256), r=[('ps', b)], w=[('yv',)])
                gelu(yg[:, :, tb:tb + 64], yv[:, :].rearrange('p (k t) -> p k t', k=4), 512, ('yv',), ('yg',), ga, gb) \
                    if False else None
                act(ga[:, :], yv[:, :], AF.Square, r=[('yv',)], w=[('ga',)])
                ts('dve', ga[:, :], ga[:, :], 0.044715, 1.0, ALU.mult, ALU.add, r=[('ga',)], w=[('ga',)])
                tt('dve', ga[:, :], ga[:, :], yv[:, :], ALU.mult, r=[('ga',), ('yv',)], w=[('ga',)])
                act(gb[:, :], ga[:, :], AF.Sigmoid, r=[('ga',)], w=[('gb',)], scale=GC)
                tt('pool', yg[:, :, tb:tb + 64], gb[:, :].rearrange('p (k t) -> p k t', k=4),
                   yv[:, :].rearrange('p (k t) -> p k t', k=4), ALU.mult, r=[('gb',), ('yv',)], w=[('yg',)])
            for co in range(4):
                b = nextbank()
                for k in range(4):
                    mm(ps(b, 128, nt), Wglu[:, k, co * 128:(co + 1) * 128], yg[:, k, 0:nt], k == 0, k == 3,
                       r=[K, ('yg',)], w=[('ps', b)])
                act(gsig[:, 0:nt], ps(b, 128, nt), AF.Sigmoid, r=[('ps', b), K], w=[('gsig',)], bias=bglu[:, co:co + 1])
                tt('dve', y2[:, co, 0:nt], yg[:, co, 0:nt], gsig[:, 0:nt], ALU.mult, r=[('yg',), ('gsig',)],
                   w=[('y2',)])
            merge_out(tok0, nt, y2, ('y2',), Wsp, K, Ws, K, 512, False, gsig, gtmp)
        cp('dve', XXst[:, :, :], XXall[:, :, :, 0], r=[('XXall',)], w=[('xxst',)])

    def phase_out(L, src_t, seg):
        arena[0] = PH0
        Wo = aalloc('Wo', [128, 8, D], BF16)
        t1 = aalloc('t1o', [128, D], F32)
        K = ('W', 'o')
        load_w(Wo[:, :, :], K, 'w_out', L, 0, 8, 0, D)
        if L == 0 and seg == 0:
            dump('mgall', merged[:, :, :], [('mg',)])
        for bi, tb in enumerate(range(0, SEGT, 128)):
            xi = bi % 2
            xb = xt[xi]
            dma(xb[:, :], rawap(src_t, (seg * SEGT + tb) * D, [[D, 128], [1, D]]), w=[('xt', 0)])
            banks = []
            for hf in range(2):
                b = nextbank()
                banks.append(b)
                for k in range(8):
                    mm(ps(b, 128, 512), merged[:, k, tb:tb + 128], Wo[:, k, hf * 512:(hf + 1) * 512], k == 0, k == 7,
                       r=[('mg',), K], w=[('ps', b)])
            rowsum_rstd(banks, 128, 8)
            for hf, b in enumerate(banks):
                tt('dve', t1[:, hf * 512:(hf + 1) * 512], ps(b, 128, 512), gains[1][:, hf * 512:(hf + 1) * 512], ALU.mult,
                   r=[('ps', b), ('c', 'gain')], w=[('t1o', hf)])
                stt('dve', xb[:, hf * 512:(hf + 1) * 512], t1[:, hf * 512:(hf + 1) * 512], small[:, 8:9],
                    xb[:, hf * 512:(hf + 1) * 512], ALU.mult, ALU.add, r=[('t1o', hf), ('rsx',), ('xt', 0)],
                    w=[('xt', 0)])
            if L == 0 and seg == 0 and tb == 128:
                dump('osmall', small[:, 0:16], [('rsx',), ('ssx', 0), ('ssx', 1), ('ssx', 2), ('ssx', 3)])
                dump('ot1', t1[:, :], [('t1o', 0), ('t1o', 1)])
                dump('ojunk', junk[:, :], [('junk',)])
            dma(rawap(xmid, tb * D, [[D, 128], [1, D]]), xb[:, :], r=[('xt', 0)], w=[('xmid', tb)])
            if L == 0 and seg == 0:
                dump('xmid%d' % tb, xb[:, :], [('xt', 0)])

    def phase_ffn(L, dst_t, seg):
        arena[0] = UT0
        Wup = aalloc('Wup', [128, 8, DFF], BF16)
        Wga = aalloc('Wga', [128, 8, DFF], BF16)
        Wd = aalloc('Wd', [128, 22, D], BF16)
        uF = aalloc('uF', [128, 8, 128], BF16)
        hb = aalloc('hb', [128, 22, 128], BF16)
        ab = aalloc('ab', [128, 130], F32)
        vv = aalloc('vv', [128, 128], F32)
        ga = aalloc('gaF', [128, 128], F32); gb = aalloc('gbF', [128, 128], F32); hv = aalloc('hv', [128, 128], F32)
        fw = aalloc('fw', [128, 22, 3], F32); fb = aalloc('fb', [128, 22], F32)
        t1 = aalloc('t1f', [128, D], F32)
        K = ('W', 'f')
        dep = [('uT',), ('mg',)]
        cp('pool', small[:, 50:51], small[:, 50:51], r=dep, w=[K] + dep)
        load_w(Wup[:, :, :], K, 'ffn_w_up', L, 0, 8, 0, DFF)
        load_w(Wga[:, :, :], K, 'ffn_w_gate', L, 0, 8, 0, DFF)
        load_w(Wd[:, :, :], K, 'ffn_w_down', L, 0, 22, 0, D)
        for j in range(3):
            vec_load(fw[:, :, j], 'ffn_conv_w', L, [[1, 128], [128, 22]], K, off=j * DFF)
        vec_load(fb[:, :], 'ffn_conv_b', L, [[1, 128], [128, 22]], K)
        dma(gains2[0][:, :], rawap(W['norm_ffn_pre'], L * D, [[0, 128], [1, D]]), w=[('c', 'gain')])
        dma(gains2[1][:, :], rawap(W['norm_ffn_post'], L * D, [[0, 128], [1, D]]), w=[('c', 'gain')])
        NTF = 128
        for t0 in range(0, SEGT, NTF):
            for j in range(1):
                tb = t0 + j * 128
                norm_block(rawap(xmid, tb * D, [[D, 128], [1, D]]), 128, gains[2], ('c', 'gain'), uF, ('uF',), j * 128,
                           j % 2, extra_r=[('xmid', tb)])
            for ct in range(22):
                b = nextbank()
                proj(b, Wup, K, ct * 128, 128, uF, ('uF',), 0, NTF)
                cp('act', ab[:, 2:2 + NTF], ps(b, 128, NTF), r=[('ps', b)], w=[('ab',)])
                cp('pool', ab[:, 0:2], ahist[:, ct, :], r=[('ah', ct)], w=[('ab',)])
                ts('dve', vv[:, :], ab[:, 0:NTF], fw[:, ct, 0:1], fb[:, ct:ct + 1], ALU.mult, ALU.add, r=[('ab',), K],
                   w=[('vv',)])
                for j in range(1, 3):
                    stt('dve', vv[:, :], ab[:, j:j + NTF], fw[:, ct, j:j + 1], vv[:, :], ALU.mult, ALU.add,
                        r=[('ab',), K, ('vv',)], w=[('vv',)])
                cp('pool', ahist[:, ct, :], ab[:, NTF:NTF + 2], r=[('ab',)], w=[('ah', ct)])
                gelu(hv[:, :], vv[:, :], NTF, ('vv',), ('hv',), ga, gb)
                b2 = nextbank()
                proj(b2, Wga, K, ct * 128, 128, uF, ('uF',), 0, NTF)
                tt('dve', hb[:, ct, :], hv[:, :], ps(b2, 128, NTF), ALU.mult, r=[('hv',), ('ps', b2)], w=[('hb',)])
            for j in range(1):
                tb = t0 + j * 128
                xi = j % 2
                xb = xt[xi]
                dma(xb[:, :], rawap(xmid, tb * D, [[D, 128], [1, D]]), r=[('xmid', tb)], w=[('xt', 0)])
                banks = []
                for hf in range(2):
                    b = nextbank()
                    banks.append(b)
                    for k in range(22):
                        mm(ps(b, 128, 512), hb[:, k, j * 128:(j + 1) * 128], Wd[:, k, hf * 512:(hf + 1) * 512], k == 0,
                           k == 21, r=[('hb',), K], w=[('ps', b)])
                rowsum_rstd(banks, 128, 9)
                for hf, b in enumerate(banks):
                    tt('dve', t1[:, hf * 512:(hf + 1) * 512], ps(b, 128, 512), gains[3][:, hf * 512:(hf + 1) * 512],
                       ALU.mult, r=[('ps', b), ('c', 'gain')], w=[('t1f', hf)])
                    stt('dve', xb[:, hf * 512:(hf + 1) * 512], t1[:, hf * 512:(hf + 1) * 512], small[:, 9:10],
                        xb[:, hf * 512:(hf + 1) * 512], ALU.mult, ALU.add, r=[('t1f', hf), ('rsx',), ('xt', 0)],
                        w=[('xt', 0)])
                dma(rawap(dst_t, (seg * SEGT + tb) * D, [[D, 128], [1, D]]), xb[:, :], r=[('xt', 0)],
                    w=[('dst', seg, tb)])
        cp('pool', small[:, 51:52], small[:, 51:52], r=[K, ('uF',), ('hb',)], w=[('uT',), ('mg',)])

    for L in range(nlayer):
        src_t = x_in if L == 0 else x1
        dst_t = out_t if L == nlayer - 1 else x1
        for stt_ in (St32, Ct32, Nt32):
            memset('pool', stt_[:, :, :], 0.0, [('st', id(stt_), h) for h in range(4)])
        for bf_, s32 in ((Stbf, St32), (Ctbf, Ct32), (Ntbf, Nt32)):
            for h in range(4):
                cp('pool', bf_[:, h, :], s32[:, h, :], r=[('st', id(s32), h)], w=[('stbf', id(bf_), h)])
        memset('pool', XXst[:, :, :], 0.0, [('xxst',)])
        for ct in range(8):
            memset('pool', zch[:, ct, :], 0.0, [('zch', ct)])
        for ct in range(22):
            memset('pool', ahist[:, ct, :], 0.0, [('ah', ct)])
        S.barrier()
        s5_setup(L)
        for seg in range(nseg):
            S.barrier()
            dma(gains2[0][:, :], rawap(W['norm_mix_pre'], L * D, [[0, 128], [1, D]]), w=[('c', 'gain')])
            dma(gains2[1][:, :], rawap(W['norm_mix_post'], L * D, [[0, 128], [1, D]]), w=[('c', 'gain')])
            for bi, tb in enumerate(range(0, SEGT, 128)):
                extra = [('dst', seg, tb)] if L > 0 else []
                norm_block(rawap(src_t, (seg * SEGT + tb) * D, [[D, 128], [1, D]]), 128, gains[0], ('c', 'gain'), uT,
                           ('uT',), tb, bi % 2, extra_r=extra)
            S.barrier()
            phase_gla(L)
            S.barrier()
            if 'ml' not in skip:
                phase_ml(L)
                S.barrier()
            if 's5' not in skip:
                phase_s5(L)
                S.barrier()
            phase_out(L, src_t, seg)
            S.barrier()
            phase_ffn(L, dst_t, seg)
            S.barrier()

    with nc.semaphore('e_pe') as s0, nc.semaphore('e_act') as s1, nc.semaphore('e_dve') as s2, \
            nc.semaphore('e_pool') as s3, nc.semaphore('e_sp') as s4:
        esem = {'pe': s0, 'act': s1, 'dve': s2, 'pool': s3, 'sp': s4}
        import contextlib
        with contextlib.ExitStack() as es:
            ssem = [es.enter_context(nc.semaphore('dslot%d' % i)) for i in range(S.nslots)]
            with nc.Block() as block:
                S.emit(nc, block, esem, ssem)
    build.dbg_names = dbg_names
    build.amax = amax
    build.ph0 = PH0
    return nc


def kernel(**inputs):
    nc = build()
    m = {'x': np.ascontiguousarray(inputs['x'].reshape(SEQ, D), dtype=np.float32)}
    for n in WNAMES:
        m[n] = np.ascontiguousarray(inputs[n], dtype=np.float32)
    res = run_bass_kernel_spmd(nc, [m], core_ids=[0])
    return res.results[0]['out'].reshape(1, SEQ, D).astype(np.float32)
```

```python
import math
import numpy as np
import concourse.bass as bass
import concourse.mybir as mybir
from concourse.bass_utils import run_bass_kernel_spmd

F32 = mybir.dt.float32
BF16 = mybir.dt.bfloat16
ALU = mybir.AluOpType
AF = mybir.ActivationFunctionType

NCORE = 8
D = 1024
DIN = 7704
DFF = 2816
SEQ = 16384
OWN = SEQ // NCORE
HALO = 64
PRE = 3
NT = OWN + HALO
NW = NT + PRE
EPS = 1e-6
O_GQ, O_GK, O_GV, O_LR, O_GR, O_SU, O_MQ, O_MK, O_MV, O_MI, O_MF, O_MO, O_GATE = (
    0, 512, 1024, 1536, 1552, 2064, 2576, 3088, 3600, 4112, 4116, 4120, 4632)
TILES = [(0, 64), (64, 512), (576, 512), (1088, 512), (1600, 512)]
SNAP_TOK = OWN
LS5 = 8
GC = 1.5957691216057308

WNAMES = ['norm_mix_pre', 'norm_mix_post', 'norm_ffn_pre', 'norm_ffn_post', 'w_in',
          'gla_w_gk', 'gla_b_gk', 'gla_norm', 'gla_w_proj',
          's5_a_re', 's5_a_im', 's5_log_dt', 's5_b_re', 's5_b_im', 's5_c_re', 's5_c_im', 's5_d',
          's5_w_glu', 's5_b_glu', 's5_w_proj', 'ml_conv_w', 'ml_conv_b', 'ml_b_i', 'ml_b_f',
          'ml_w_proj', 'w_out', 'ffn_w_up', 'ffn_w_gate', 'ffn_conv_w', 'ffn_conv_b', 'ffn_w_down']
WSHAPES = {'norm_mix_pre': [D], 'norm_mix_post': [D], 'norm_ffn_pre': [D], 'norm_ffn_post': [D],
           'w_in': [D, DIN], 'gla_w_gk': [16, 512], 'gla_b_gk': [512], 'gla_norm': [128],
           'gla_w_proj': [512, D], 's5_a_re': [32, 64], 's5_a_im': [32, 64], 's5_log_dt': [32],
           's5_b_re': [32, 64, 16], 's5_b_im': [32, 64, 16], 's5_c_re': [32, 16, 64],
           's5_c_im': [32, 16, 64], 's5_d': [32, 16], 's5_w_glu': [512, 512], 's5_b_glu': [512],
           's5_w_proj': [512, D], 'ml_conv_w': [4, D], 'ml_conv_b': [D], 'ml_b_i': [4], 'ml_b_f': [4],
           'ml_w_proj': [512, D], 'w_out': [D, D], 'ffn_w_up': [D, DFF], 'ffn_w_gate': [D, DFF],
           'ffn_conv_w': [3, DFF], 'ffn_conv_b': [DFF], 'ffn_w_down': [DFF, D]}


class Sched:
    ENGS = ('pe', 'act', 'dve', 'pool', 'sp')
    SELF_SYNC = ('act', 'dve', 'pool')

    def __init__(self, nslots=16):
        self.streams = {e: [] for e in self.ENGS}
        self.ops = []
        self.lastw = {}
        self.rd = {}
        self.nslots = nslots
        self.slot_cnt = [0] * nslots
        self.slot_last = [None] * nslots
        self.dma_n = 0

    def add(self, eng, fn, r=(), w=(), dma=False):
        oid = len(self.ops)
        deps = set()
        raw = set()
        for k in r:
            p = self.lastw.get(k)
            if p is not None:
                deps.add(p)
                raw.add(p)
        for k in w:
            p = self.lastw.get(k)
            if p is not None:
                deps.add(p)
            rr = self.rd.get(k)
            if rr:
                deps.update(rr[0].values())
                deps.update(rr[1])
        o = {'id': oid, 'eng': eng, 'fn': fn, 'dma': dma, 'flag': False, 'raw': raw}
        if dma:
            s = self.dma_n % self.nslots
            self.dma_n += 1
            if self.slot_last[s] is not None:
                deps.add(self.slot_last[s])
            self.slot_cnt[s] += 1
            o['slot'] = s
            o['slotval'] = 16 * self.slot_cnt[s]
            self.slot_last[s] = oid
        for k in w:
            self.lastw[k] = oid
            self.rd[k] = ({}, [])
        for k in r:
            rr = self.rd.setdefault(k, ({}, []))
            if dma:
                rr[1].append(oid)
            else:
                rr[0][eng] = oid
        deps.discard(oid)
        o['deps'] = deps
        self.ops.append(o)
        self.streams[eng].append(o)
        return o

    def barrier(self):
        last = {}
        for e in self.ENGS:
            st = self.streams[e]
            for o in reversed(st):
                if o['fn'] is not None and not o['dma']:
                    last[e] = o['id']
                    break
        dmas = [x for x in self.slot_last if x is not None]
        for e in self.ENGS:
            oid = len(self.ops)
            deps = set(v for k, v in last.items() if k != e) | set(dmas)
            o = {'id': oid, 'eng': e, 'fn': None, 'dma': False, 'flag': False, 'deps': deps}
            self.ops.append(o)
            self.streams[e].append(o)

    def emit(self, nc, block, esem, ssem):
        ops = self.ops
        for o in ops:
            for d in o['deps']:
                dd = ops[d]
                if dd['dma']:
                    continue
                if dd['eng'] != o['eng'] or (d in o.get('raw', ()) and o['eng'] in self.SELF_SYNC):
                    dd['flag'] = True
        for e in self.ENGS:
            c = 0
            for o in self.streams[e]:
                if o['flag'] and not o['dma']:
                    c += 1
                    o['fidx'] = c
        sched = self

        def run(ename, eng):
            waited = {}
            for o in sched.streams[ename]:
                need = {}
                for d in o['deps']:
                    dd = ops[d]
                    if dd['dma']:
                        key = ('s', dd['slot'])
                        val = dd['slotval']
                    elif dd['eng'] == ename and not (d in o.get('raw', ()) and ename in sched.SELF_SYNC):
                        continue
                    else:
                        key = ('e', dd['eng'])
                        val = dd['fidx']
                    if val > waited.get(key, 0) and val > need.get(key, 0):
                        need[key] = val
                for key, val in need.items():
                    sem = ssem[key[1]] if key[0] == 's' else esem[key[1]]
                    eng.wait_ge(sem, val)
                    waited[key] = val
                if o['fn'] is None:
                    continue
                ins = o['fn'](eng)
                if o['dma']:
                    ins.then_inc(ssem[o['slot']], 16)
                elif o['flag']:
                    ins.then_inc(esem[ename], 1)
            if ename == 'sp':
                for s in range(sched.nslots):
                    if sched.slot_cnt[s]:
                        eng.wait_ge(ssem[s], 16 * sched.slot_cnt[s])

        @block.tensor
        def _(e):
            run('pe', e)

        @block.scalar
        def _(e):
            run('act', e)

        @block.vector
        def _(e):
            run('dve', e)

        @block.gpsimd
        def _(e):
            run('pool', e)

        @block.sync
        def _(e):
            run('sp', e)


def rawap(t, offset, pat):
    return bass.AP(tensor=t, offset=offset, ap=[list(p) for p in pat])


SEGT = 2048
DBGT = 512
BAR = False
TILE = 512
PI = math.pi


def build(nseg=8, nlayer=2, dbg=False, skip=()):
    nc = bass.Bass("TRN2", target_bir_lowering=False)
    S = Sched()
    ntok = nseg * SEGT
    x_in = nc.dram_tensor('x', [ntok, D], F32, kind='ExternalInput')
    W = {n: nc.dram_tensor(n, [2] + WSHAPES[n], F32, kind='ExternalInput') for n in WNAMES}
    WSZ = {n: int(np.prod(WSHAPES[n])) for n in WNAMES}
    out_t = nc.dram_tensor('out', [ntok, D], F32, kind='ExternalOutput')
    x1 = nc.dram_tensor('x1', [ntok, D], F32)
    xmid = nc.dram_tensor('xmid', [SEGT, D], F32)

    sb_top = [16512]
    SB_END = 229376

    def alloc(name, shape, dt, at=None):
        nbytes = int(np.prod(shape[1:])) * (4 if dt == F32 else 2)
        nbytes = (nbytes + 63) // 64 * 64
        if at is None:
            off = sb_top[0]
            sb_top[0] += nbytes
        else:
            off = at
        assert off + nbytes <= SB_END, (name, off, nbytes)
        return nc.alloc_sbuf_tensor_at(name, list(shape), dt, offset=off)

    psum = nc.alloc_psum_tensor('psum', [128, 8, 512], F32)

    def ps(b, p=128, n=512):
        return psum[0:p, b, 0:n]

    bank_rr = [0]

    def nextbank():
        b = bank_rr[0] % 6
        bank_rr[0] += 1
        return b

    ident = alloc('ident', [128, 128], BF16)
    onesbf = alloc('onesbf', [128, 128], BF16)
    ones32 = alloc('ones32', [128, 512], F32)
    maskut = alloc('maskut', [64, 64], F32)
    eps_t = alloc('eps_t', [128, 1], F32)
    negpi = alloc('negpi', [128, 1], F32)
    gains2 = [alloc('gain%d' % i, [128, D], F32) for i in range(2)]
    gains = [gains2[0], gains2[1], gains2[0], gains2[1]]
    xt0 = alloc('xt0', [128, D], F32)
    xt = [xt0, xt0]
    un = alloc('un', [128, D], BF16)
    junk = alloc('junk', [128, D], BF16)
    small = alloc('small', [128, 64], F32)
    stage0 = alloc('stage0', [128, 512], F32)
    stage = [stage0, stage0]
    St32 = alloc('St32', [128, 4, 128], F32)
    Ct32 = alloc('Ct32', [128, 4, 128], F32)
    Nt32 = alloc('Nt32', [128, 4, 128], F32)
    Stbf = alloc('Stbf', [128, 4, 128], BF16)
    Ctbf = alloc('Ctbf', [128, 4, 128], BF16)
    Ntbf = alloc('Ntbf', [128, 4, 128], BF16)
    XXst = alloc('XXst', [128, 2, 16], F32)
    zch = alloc('zch', [128, 8, 3], F32)
    ahist = alloc('ahist', [128, 22, 2], F32)
    BBT = alloc('BBT', [32, 16, 2, 128], BF16)
    Ec = alloc('Ec', [128, 2, 16, 32], BF16)
    AA1 = alloc('AA1', [128, 2, 16], F32)
    A1i = alloc('A1i', [128, 16], F32)
    nA1i = alloc('nA1i', [128, 16], F32)
    Dd = alloc('Dd', [32, 16], F32)
    PW = alloc('PW', [128, 8, 2, 16], F32)
    nPWi = alloc('nPWi', [128, 8, 16], F32)
    AA8 = alloc('AA8', [128, 2, 16], F32)
    gsel = alloc('gsel', [32, 4, 128], BF16)
    selF = alloc('selF', [8, 4, 128], F32)
    selIF = alloc('selIF', [8, 4, 128], F32)
    UT0 = sb_top[0]
    uT = alloc('uT', [128, 8, SEGT], BF16)
    merged = alloc('merged', [128, 8, SEGT], BF16)
    PH0 = sb_top[0]
    arena = [PH0]
    an = [0]

    amax = {}

    def aalloc(name, shape, dt):
        an[0] += 1
        t = alloc('%s_%d' % (name, an[0]), shape, dt, at=arena[0])
        nb = int(np.prod(shape[1:])) * (4 if dt == F32 else 2)
        arena[0] += (nb + 63) // 64 * 64
        amax['top'] = max(amax.get('top', 0), arena[0])
        return t

    cnt = {'stage': 0}
    dumped = {}

    def dma(out, in_, r=(), w=(), slow=False):
        if slow:
            return S.add('sp', lambda e: e.dma_start(out=out, in_=in_, allow_slow_non_contiguous=True),
                         r=r, w=w, dma=True)
        return S.add('sp', lambda e: e.dma_start(out=out, in_=in_), r=r, w=w, dma=True)

    def act(out, in_, func, r, w, bias=None, scale=None, accum=None):
        kw = {}
        if bias is not None:
            kw['bias'] = bias
        if scale is not None:
            kw['scale'] = scale
        if accum is not None:
            kw['accum_out'] = accum
        return S.add('act', lambda e: e.activation(out=out, in_=in_, func=func, **kw), r=r, w=w)

    def tt(eng, out, in0, in1, op, r, w):
        return S.add(eng, lambda e: e.tensor_tensor(out=out, in0=in0, in1=in1, op=op), r=r, w=w)

    def ts(eng, out, in0, s1, s2, op0, op1, r, w):
        if s2 is None:
            return S.add(eng, lambda e: e.tensor_single_scalar(out=out, in_=in0, scalar=s1, op=op0), r=r, w=w)
        return S.add(eng, lambda e: e.tensor_scalar(out=out, in0=in0, scalar1=s1, scalar2=s2, op0=op0, op1=op1),
                     r=r, w=w)

    def stt(eng, out, in0, sc, in1, op0, op1, r, w):
        return S.add(eng, lambda e: e.scalar_tensor_tensor(out=out, in0=in0, scalar=sc, in1=in1, op0=op0, op1=op1),
                     r=r, w=w)

    def cp(eng, out, in_, r, w):
        if eng == 'act':
            return S.add('act', lambda e: e.copy(out=out, in_=in_), r=r, w=w)
        return S.add(eng, lambda e: e.tensor_copy(out=out, in_=in_), r=r, w=w)

    def mm(out, lhsT, rhs, start, stop, r, w):
        return S.add('pe', lambda e: e.matmul(out, lhsT, rhs, start=start, stop=stop, skip_group_check=True),
                     r=r, w=w)

    def tr(out, in_, idn, r, w):
        return S.add('pe', lambda e: e.transpose(out, in_, idn), r=r, w=w)

    def memset(eng, ap, val, w):
        return S.add(eng, lambda e: e.memset(ap, val), r=(), w=w)

    dbg_names = []

    def dump(name, ap, rkeys):
        if not dbg:
            return
        t = nc.dram_tensor('dbg_' + name, list(ap.shape), ap.dtype, kind='ExternalOutput')
        dbg_names.append('dbg_' + name)
        full = t.ap()
        dma(full, ap, r=rkeys, w=[('dbg', name)])

    scr = nc.dram_tensor('scr_bar', [64, 64], F32)
    barn = [0]

    def lbar(items):
        for ap_, key in items:
            i = barn[0] % 64
            barn[0] += 1
            if ap_.dtype != F32:
                continue
            dma(scr.ap()[i:i + 1, 0:1], ap_, r=[key], w=[('scr', i)])

    def recip(out, in_, r, w):
        return S.add('dve', lambda e: e.reciprocal(out=out, in_=in_), r=r, w=w)

    def load_w(dst, key, name, L, row0, kt, col0, ncols, rows=128):
        src_t = W[name]
        ld = WSHAPES[name][1] if len(WSHAPES[name]) > 1 else 1
        base = L * WSZ[name]
        kstep = max(1, 8 if kt <= 8 else 11)
        for k0 in range(0, kt, kstep):
            kn = min(kstep, kt - k0)
            src = rawap(src_t, base + (row0 + k0 * 128) * ld + col0, [[ld, rows], [128 * ld, kn], [1, ncols]])
            dd = dst[:, k0:k0 + kn, :]
            S.add('pool', lambda e, dd=dd, src=src: e.dma_start(out=dd, in_=src), r=(), w=[key], dma=True)

    def vec_load(dst, name, L, pat, key, off=0, slow=True):
        dma(dst, rawap(W[name], L * WSZ[name] + off, pat), w=[key], slow=slow)

    memset('pool', ones32[:, :], 1.0, [('c', 'ones32')])
    memset('pool', eps_t[:, :], EPS, [('c', 'eps')])
    memset('pool', negpi[:, :], -PI, [('c', 'negpi')])
    memset('pool', maskut[:, :], 1.0, [('c', 'mask')])
    S.add('pool', lambda e: e.affine_select(out=maskut[:, :], in_=maskut[:, :], pattern=[[1, 64]],
                                            compare_op=ALU.is_ge, fill=0.0, base=0, channel_multiplier=-1),
          r=[('c', 'mask')], w=[('c', 'mask')])
    cp('pool', onesbf[:, :], ones32[:, 0:128], r=[('c', 'ones32')], w=[('c', 'onesbf')])
    identf = stage[1]
    memset('pool', identf[:, 0:128], 1.0, [('stage', 1)])
    S.add('pool', lambda e: e.affine_select(out=identf[:, 0:128], in_=identf[:, 0:128], pattern=[[1, 128]],
                                            compare_op=ALU.is_equal, fill=0.0, base=0, channel_multiplier=-1),
          r=[('stage', 1)], w=[('stage', 1)])
    cp('pool', ident[:, :], identf[:, 0:128], r=[('stage', 1)], w=[('c', 'ident')])
    for k in range(4):
        memset('pool', identf[0:32, 128:256], 1.0, [('stage', 1)])
        S.add('pool', lambda e, k=k: e.affine_select(out=identf[0:32, 128:256], in_=identf[0:32, 128:256],
                                                     pattern=[[1, 128]], compare_op=ALU.is_equal, fill=0.0,
                                                     base=-32 * k, channel_multiplier=-1),
              r=[('stage', 1)], w=[('stage', 1)])
        cp('pool', gsel[:, k, :], identf[0:32, 128:256], r=[('stage', 1)], w=[('c', 'gsel')])
    for h in range(4):
        for (dst, rows_) in ((selF, (4 + h,)), (selIF, (h, 4 + h))):
            memset('pool', dst[:, h, :], 0.0, [('c', 'sel')])
            for rr in rows_:
                memset('pool', identf[0:8, 256:384], 1.0, [('stage', 1)])
                S.add('pool', lambda e, rr=rr: e.affine_select(out=identf[0:8, 256:384], in_=identf[0:8, 256:384],
                                                               pattern=[[0, 128]], compare_op=ALU.is_equal, fill=0.0,
                                                               base=-rr, channel_multiplier=1),
                      r=[('stage', 1)], w=[('stage', 1)])
                tt('pool', dst[:, h, :], dst[:, h, :], identf[0:8, 256:384], ALU.add, r=[('stage', 1), ('c', 'sel')],
                   w=[('c', 'sel')])

    def norm_block(src_ap, nb, gain_t, gkey, dstT, dkey, dcol0, xi, extra_r=()):
        xb = xt[xi]
        dma(xb[0:nb, :], src_ap, r=extra_r, w=[('xt', 0)])
        act(junk[0:nb, :], xb[0:nb, :], AF.Square, r=[('xt', 0)], w=[('junk',)])
        S.add('dve', lambda e: e.reduce_sum(out=small[0:nb, 0:1], in_=junk[0:nb, :], axis=mybir.AxisListType.X),
              r=[('junk',)], w=[('ss',)])
        act(small[0:nb, 1:2], small[0:nb, 0:1], AF.Sqrt, r=[('ss',), ('c', 'eps')], w=[('sd',)],
            bias=eps_t[0:nb, :], scale=1.0 / D)
        recip(small[0:nb, 2:3], small[0:nb, 1:2], r=[('sd',)], w=[('rs',)])
        stt('dve', un[0:nb, :], xb[0:nb, :], small[0:nb, 2:3], gain_t[0:nb, :], ALU.mult, ALU.mult,
            r=[('xt', 0), ('rs',), gkey], w=[('un',)])
        pbf = psum[:, 7, :].bitcast(BF16)
        for k in range(8):
            tr(pbf[:, k * 128:k * 128 + nb], un[0:nb, k * 128:(k + 1) * 128], ident[0:nb, 0:nb],
               r=[('un',), ('c', 'ident')], w=[('ps', 7)])
        cp('act', dstT[:, :, dcol0:dcol0 + nb], pbf.rearrange('p (k t) -> p k t', k=8)[:, :, 0:nb], r=[('ps', 7)],
           w=[dkey])

    def rowsum_rstd(banks, nb, col):
        for hf, b in enumerate(banks):
            act(junk[0:nb, hf * 512:(hf + 1) * 512], ps(b, nb, 512), AF.Square, r=[('ps', b)], w=[('junk',)])
            S.add('dve', lambda e, hf=hf: e.reduce_sum(out=small[0:nb, 4 + hf:5 + hf],
                                                       in_=junk[0:nb, hf * 512:(hf + 1) * 512],
                                                       axis=mybir.AxisListType.X), r=[('junk',)], w=[('ssx', hf)])
        tt('dve', small[0:nb, 6:7], small[0:nb, 4:5], small[0:nb, 5:6], ALU.add, r=[('ssx', 0), ('ssx', 1)],
           w=[('ssx', 2)])
        act(small[0:nb, 7:8], small[0:nb, 6:7], AF.Sqrt, r=[('ssx', 2), ('c', 'eps')], w=[('ssx', 3)],
            bias=eps_t[0:nb, :], scale=1.0 / D)
        recip(small[0:nb, col:col + 1], small[0:nb, 7:8], r=[('ssx', 3)], w=[('rsx',)])

    def proj(bank, wt, wkey, c0, M, src, skey, tok0, nt, kt=8):
        for k in range(kt):
            mm(ps(bank, M, nt), wt[:, k, c0:c0 + M], src[:, k, tok0:tok0 + nt], k == 0, k == kt - 1,
               r=[wkey, skey], w=[('ps', bank)])

    def gelu(dst, v, n, vkey, dkey, tmpa, tmpb, P=128):
        act(tmpa[0:P, 0:n], v, AF.Square, r=[vkey], w=[('ga',)])
        ts('dve', tmpa[0:P, 0:n], tmpa[0:P, 0:n], 0.044715, 1.0, ALU.mult, ALU.add, r=[('ga',)], w=[('ga',)])
        tt('dve', tmpa[0:P, 0:n], tmpa[0:P, 0:n], v, ALU.mult, r=[('ga',), vkey], w=[('ga',)])
        act(tmpb[0:P, 0:n], tmpa[0:P, 0:n], AF.Sigmoid, r=[('ga',)], w=[('gb',)], scale=GC)
        tt('pool', dst, tmpb[0:P, 0:n], v, ALU.mult, r=[('gb',), vkey], w=[dkey])

    def s5_setup(L):
        arena[0] = PH0
        aR = aalloc('aR', [128, 16], F32); aI = aalloc('aI', [128, 16], F32); ldt = aalloc('ldt', [128, 16], F32)
        t = [aalloc('s5t', [128, 16], F32) for _ in range(12)]
        bR = aalloc('bR', [128, 16, 16], F32); bI = aalloc('bI', [128, 16, 16], F32)
        cR = aalloc('cR', [128, 16, 16], F32); cI = aalloc('cI', [128, 16, 16], F32)
        u1 = aalloc('u1', [128, 16, 16], F32); u2 = aalloc('u2', [128, 16, 16], F32)
        bsrc = [aalloc('bsrc', [128, 16, 32], BF16) for _ in range(2)]
        K = ('s5c', L)
        vec_load(aR[:, :], 's5_a_re', L, [[1, 128], [128, 16]], K)
        vec_load(aI[:, :], 's5_a_im', L, [[1, 128], [128, 16]], K)
        for e in range(2):
            vec_load(ldt[e * 64:(e + 1) * 64, :], 's5_log_dt', L, [[0, 64], [2, 16]], K, off=e)
        vec_load(bR[:, :, :], 's5_b_re', L, [[16, 128], [2048, 16], [1, 16]], K, slow=False)
        vec_load(bI[:, :, :], 's5_b_im', L, [[16, 128], [2048, 16], [1, 16]], K, slow=False)
        for e in range(2):
            for pr in range(16):
                vec_load(cR[e * 64:(e + 1) * 64, pr, :], 's5_c_re', L, [[1, 64], [64, 16]], K, off=e * 1024 + pr * 2048)
                vec_load(cI[e * 64:(e + 1) * 64, pr, :], 's5_c_im', L, [[1, 64], [64, 16]], K, off=e * 1024 + pr * 2048)
        vec_load(Dd[:, :], 's5_d', L, [[1, 32], [32, 16]], K)
        R = [K]
        dt_, dre, dim, mag, sn, cs, A1r, den, zr, fre, fim, tmp = t
        act(dt_[:, :], ldt[:, :], AF.Exp, r=R, w=R)
        tt('dve', dre[:, :], dt_[:, :], aR[:, :], ALU.mult, r=R, w=R)
        tt('dve', dim[:, :], dt_[:, :], aI[:, :], ALU.mult, r=R, w=R)
        act(mag[:, :], dre[:, :], AF.Exp, r=R, w=R)
        cp('dve', sn[:, :], dim[:, :], r=R, w=R)
        ts('dve', cs[:, :], dim[:, :], 0.5 * PI, None, ALU.add, None, r=R, w=R)
        for arr in (sn, cs):
            cp('dve', den[:, :], arr[:, :], r=R, w=R)
            for th in (PI, 3 * PI, 5 * PI, 7 * PI):
                ts('dve', tmp[:, :], den[:, :], th, None, ALU.is_ge, None, r=R, w=R)
                stt('dve', arr[:, :], tmp[:, :], -2 * PI, arr[:, :], ALU.mult, ALU.add, r=R, w=R)
            act(arr[:, :], arr[:, :], AF.Sin, r=R, w=R)
        tt('dve', A1r[:, :], mag[:, :], cs[:, :], ALU.mult, r=R, w=R)
        tt('dve', A1i[:, :], mag[:, :], sn[:, :], ALU.mult, r=R, w=R)
        ts('dve', nA1i[:, :], A1i[:, :], -1.0, None, ALU.mult, None, r=R, w=R)
        cp('dve', AA1[:, 0, :], A1r[:, :], r=R, w=R)
        cp('dve', AA1[:, 1, :], A1r[:, :], r=R, w=R)
        cp('dve', PW[:, 0, 0, :], A1r[:, :], r=R, w=R)
        cp('dve', PW[:, 0, 1, :], A1i[:, :], r=R, w=R)
        for k in range(7):
            tt('dve', tmp[:, :], PW[:, k, 0, :], A1r[:, :], ALU.mult, r=R, w=R)
            tt('dve', den[:, :], PW[:, k, 1, :], A1i[:, :], ALU.mult, r=R, w=R)
            tt('dve', PW[:, k + 1, 0, :], tmp[:, :], den[:, :], ALU.subtract, r=R, w=R)
            tt('dve', tmp[:, :], PW[:, k, 0, :], A1i[:, :], ALU.mult, r=R, w=R)
            tt('dve', den[:, :], PW[:, k, 1, :], A1r[:, :], ALU.mult, r=R, w=R)
            tt('dve', PW[:, k + 1, 1, :], tmp[:, :], den[:, :], ALU.add, r=R, w=R)
        for k in range(8):
            ts('dve', nPWi[:, k, :], PW[:, k, 1, :], -1.0, None, ALU.mult, None, r=R, w=R)
        cp('dve', AA8[:, 0, :], PW[:, 7, 0, :], r=R, w=R)
        cp('dve', AA8[:, 1, :], PW[:, 7, 0, :], r=R, w=R)
        tt('dve', den[:, :], aR[:, :], aR[:, :], ALU.mult, r=R, w=R)
        tt('dve', tmp[:, :], aI[:, :], aI[:, :], ALU.mult, r=R, w=R)
        tt('dve', den[:, :], den[:, :], tmp[:, :], ALU.add, r=R, w=R)
        recip(den[:, :], den[:, :], r=R, w=R)
        ts('dve', zr[:, :], A1r[:, :], -1.0, None, ALU.add, None, r=R, w=R)
        tt('dve', fre[:, :], zr[:, :], aR[:, :], ALU.mult, r=R, w=R)
        tt('dve', tmp[:, :], A1i[:, :], aI[:, :], ALU.mult, r=R, w=R)
        tt('dve', fre[:, :], fre[:, :], tmp[:, :], ALU.add, r=R, w=R)
        tt('dve', fre[:, :], fre[:, :], den[:, :], ALU.mult, r=R, w=R)
        tt('dve', fim[:, :], A1i[:, :], aR[:, :], ALU.mult, r=R, w=R)
        tt('dve', tmp[:, :], zr[:, :], aI[:, :], ALU.mult, r=R, w=R)
        tt('dve', fim[:, :], fim[:, :], tmp[:, :], ALU.subtract, r=R, w=R)
        tt('dve', fim[:, :], fim[:, :], den[:, :], ALU.mult, r=R, w=R)
        frb = fre[:, :].unsqueeze(2).to_broadcast([128, 16, 16])
        fib = fim[:, :].unsqueeze(2).to_broadcast([128, 16, 16])
        tt('dve', u1[:, :, :], bR[:, :, :], frb, ALU.mult, r=R, w=R)
        tt('dve', u2[:, :, :], bI[:, :, :], fib, ALU.mult, r=R, w=R)
        tt('dve', u1[:, :, :], u1[:, :, :], u2[:, :, :], ALU.subtract, r=R, w=R)
        tt('dve', u2[:, :, :], bI[:, :, :], frb, ALU.mult, r=R, w=R)
        tt('dve', bI[:, :, :], bR[:, :, :], fib, ALU.mult, r=R, w=R)
        tt('dve', u2[:, :, :], u2[:, :, :], bI[:, :, :], ALU.add, r=R, w=R)
        ts('dve', cI[:, :, :], cI[:, :, :], -1.0, None, ALU.mult, None, r=R, w=R)
        for ri, (bsrc_, srcb, srcc) in enumerate(((bsrc[0], u1, cR), (bsrc[1], u2, cI))):
            memset('dve', bsrc_[:, :, :], 0.0, R)
            memset('dve', Ec[:, ri, :, :], 0.0, R)
            for e in range(2):
                cp('dve', bsrc_[e * 64:(e + 1) * 64, :, e * 16:(e + 1) * 16], srcb[e * 64:(e + 1) * 64, :, :], r=R, w=R)
                cp('dve', Ec[e * 64:(e + 1) * 64, ri, :, e * 16:(e + 1) * 16], srcc[e * 64:(e + 1) * 64, :, :], r=R, w=R)
        pbf = psum[:, 7, :].bitcast(BF16)
        for ri in range(2):
            for g4 in range(2):
                for j in range(8):
                    pr = g4 * 8 + j
                    tr(pbf[0:32, j * 128:(j + 1) * 128], bsrc[ri][:, pr, :], ident[:, :], r=R + [('c', 'ident')],
                       w=[('ps', 7)])
                cp('act', BBT[:, g4 * 8:(g4 + 1) * 8, ri, :], pbf[0:32, :].rearrange('p (j q) -> p j q', j=8),
                   r=[('ps', 7)], w=R)

    def chunk_engine(nt, h, qa, qkeys, ka, kkeys, qscale, kscale, vtok, vcol0, stl, bfl, use_n, o_bank, den_bank,
                     B):
        nch = nt // 64
        qd, ki, kd, kdT, sT, E1, E2, E3, dec = B
        stt('dve', qd[:, 0:nt], qa, qscale, E1[:, 0:nt], ALU.mult, ALU.mult, r=qkeys + [('E1',)], w=[('qd',)])
        stt('dve', ki[:, 0:nt], ka, kscale, E2[:, 0:nt], ALU.mult, ALU.mult, r=kkeys + [('E2',)], w=[('ki',)])
        stt('dve', kd[:, 0:nt], ka, kscale, E3[:, 0:nt], ALU.mult, ALU.mult, r=kkeys + [('E3',)], w=[('kd',)])
        S32 = stl[0]
        Sbf = bfl[0]
        pbf = psum[:, 7, :].bitcast(BF16)
        for c in range(nch):
            cs = c * 64
            i2 = c % 2
            tr(pbf[0:64, 0:128], kd[:, cs:cs + 64], ident[:, :], r=[('kd',), ('c', 'ident')], w=[('ps', 7)])
            cp('act', kdT[i2][:, :], pbf[0:64, 0:128], r=[('ps', 7)], w=[('kdT', i2)])
            mm(ps(6, 64, 64), ki[:, cs:cs + 64], qd[:, cs:cs + 64], True, True, r=[('ki',), ('qd',)], w=[('ps', 6)])
            tt('dve', sT[i2][:, :], ps(6, 64, 64), maskut[:, :], ALU.mult, r=[('ps', 6), ('c', 'mask')],
               w=[('sT', i2)])
            vv = vtok[:, c, vcol0 + h * 128: vcol0 + (h + 1) * 128]
            mm(psum[:, o_bank, cs:cs + 64], vv, sT[i2][:, :], True, False, r=[('vtok',), ('sT', i2)],
               w=[('ps', o_bank)])
            mm(psum[:, o_bank, cs:cs + 64], Sbf[:, h, :], qd[:, cs:cs + 64], False, True,
               r=[('stbf', id(Sbf), h), ('qd',)], w=[('ps', o_bank)])
            if use_n:
                Nbf = bfl[1]
                mm(psum[:, den_bank, cs:cs + 64], onesbf[0:64, :], sT[i2][:, :], True, False,
                   r=[('c', 'onesbf'), ('sT', i2)], w=[('ps', den_bank)])
                mm(psum[:, den_bank, cs:cs + 64], Nbf[:, h, :], qd[:, cs:cs + 64], False, True,
                   r=[('stbf', id(Nbf), h), ('qd',)], w=[('ps', den_bank)])
            mm(ps(6, 128, 128), kdT[i2][:, :], vv, True, True, r=[('kdT', i2), ('vtok',)], w=[('ps', 6)])
            stt('dve', S32[:, h, :], S32[:, h, :], dec[:, c:c + 1], ps(6, 128, 128), ALU.mult, ALU.add,
                r=[('st', id(S32), h), ('dec',), ('ps', 6)], w=[('st', id(S32), h)])
            cp('act', Sbf[:, h, :], S32[:, h, :], r=[('st', id(S32), h)], w=[('stbf', id(Sbf), h)])
            if use_n:
                N32 = stl[1]
                mm(ps(6, 128, 128), kdT[i2][:, :], onesbf[0:64, :], True, True, r=[('kdT', i2), ('c', 'onesbf')],
                   w=[('ps', 6)])
                stt('dve', N32[:, h, :], N32[:, h, :], dec[:, c:c + 1], ps(6, 128, 128), ALU.mult, ALU.add,
                    r=[('st', id(N32), h), ('dec',), ('ps', 6)], w=[('st', id(N32), h)])
                cp('act', Nbf[:, h, :], N32[:, h, :], r=[('st', id(N32), h)], w=[('stbf', id(Nbf), h)])

    def vproj(tok0, nt, wt, wkey, c0, vtok):
        for c in range(nt // 64):
            b = nextbank()
            for k in range(8):
                mm(ps(b, 64, 512), uT[:, k, tok0 + c * 64: tok0 + c * 64 + 64], wt[:, k, c0:c0 + 512],
                   k == 0, k == 7, r=[wkey, ('uT',)], w=[('ps', b)])
            cp('act', vtok[:, c, :], ps(b, 64, 512), r=[('ps', b)], w=[('vtok',)])

    def merge_out(tok0, nt, oT, okey, wpr, wprkey, wgate, wgkey, gc0, first, gsig, gtmp):
        for co in range(8):
            b = nextbank()
            for k in range(4):
                mm(ps(b, 128, nt), wpr[:, k, co * 128:(co + 1) * 128], oT[:, k, 0:nt], k == 0, k == 3,
                   r=[wprkey, okey], w=[('ps', b)])
            b2 = nextbank()
            proj(b2, wgate, wgkey, gc0 + co * 128, 128, uT, ('uT',), tok0, nt)
            act(gsig[:, 0:nt], ps(b2, 128, nt), AF.Sigmoid, r=[('ps', b2)], w=[('gsig',)])
            if first:
                tt('dve', merged[:, co, tok0:tok0 + nt], ps(b, 128, nt), gsig[:, 0:nt], ALU.mult,
                   r=[('ps', b), ('gsig',)], w=[('mg',)])
            else:
                tt('dve', gtmp[:, 0:nt], ps(b, 128, nt), gsig[:, 0:nt], ALU.mult, r=[('ps', b), ('gsig',)],
                   w=[('gtmp',)])
                tt('pool', merged[:, co, tok0:tok0 + nt], merged[:, co, tok0:tok0 + nt], gtmp[:, 0:nt], ALU.add,
                   r=[('gtmp',), ('mg',)], w=[('mg',)])

    def common_bufs():
        qd = aalloc('qd', [128, 512], BF16); ki = aalloc('ki', [128, 512], BF16); kd = aalloc('kd', [128, 512], BF16)
        kdT = [aalloc('kdT', [64, 128], BF16) for _ in range(2)]
        sT = [aalloc('sT', [64, 64], BF16) for _ in range(2)]
        E1 = aalloc('E1', [128, 512], F32); E2 = aalloc('E2', [128, 512], F32); E3 = aalloc('E3', [128, 512], F32)
        dec = aalloc('dec', [128, 8], F32)
        return (qd, ki, kd, kdT, sT, E1, E2, E3, dec)

    def phase_gla(L):
        arena[0] = PH0
        Wg = aalloc('Wg', [128, 8, 3088], BF16)
        Wgk = aalloc('Wgk', [16, 1, 512], BF16)
        Wpr = aalloc('Wpr', [128, 4, D], BF16)
        bgk = aalloc('bgk', [128, 4], F32); gln = aalloc('gln', [128, 1], F32)
        Bx = aalloc('Bx', [128, 513], F32)
        Drel = aalloc('Drel', [128, 512], F32); Drel2 = aalloc('Drel2', [128, 512], F32)
        B = common_bufs()
        qd, ki, kd, kdT, sT, E1, E2, E3, dec = B
        vtok = aalloc('vtok', [64, 8, 512], BF16)
        lrT = aalloc('lrT', [16, 512], BF16)
        sq = aalloc('sq', [128, 512], BF16)
        rstd = E1; t1 = E2; sr = E3
        ogT = aalloc('ogT', [128, 4, 512], BF16)
        gsig = aalloc('gsig', [128, 512], F32); gtmp = aalloc('gtmp', [128, 512], F32)
        spb = gtmp
        K = ('W', 'g')
        load_w(Wg[:, :, 0:2064], K, 'w_in', L, 0, 8, 0, 2064)
        load_w(Wg[:, :, 2064:3088], K, 'w_in', L, 0, 8, O_GATE, 1024)
        load_w(Wgk[:, :, :], K, 'gla_w_gk', L, 0, 1, 0, 512, rows=16)
        load_w(Wpr[:, :, :], K, 'gla_w_proj', L, 0, 4, 0, D)
        vec_load(bgk[:, :], 'gla_b_gk', L, [[1, 128], [128, 4]], K)
        ts('dve', bgk[:, :], bgk[:, :], -1.0, None, ALU.mult, None, r=[K], w=[K])
        vec_load(gln[:, :], 'gla_norm', L, [[1, 128], [1, 1]], K)
        memset('pool', Bx[:, 0:1], 0.0, [('Bx',)])
        for tok0 in range(0, SEGT, TILE):
            nt = TILE
            nch = nt // 64
            vproj(tok0, nt, Wg, K, O_GV, vtok)
            b = nextbank()
            proj(b, Wg, K, O_LR, 16, uT, ('uT',), tok0, nt)
            cp('act', lrT[:, 0:nt], ps(b, 16, nt), r=[('ps', b)], w=[('lrT',)])
            for h in range(4):
                b = nextbank()
                mm(ps(b, 128, nt), Wgk[:, 0, h * 128:(h + 1) * 128], lrT[:, 0:nt], True, True, r=[K, ('lrT',)],
                   w=[('ps', b)])
                act(spb[:, 0:nt], ps(b, 128, nt), AF.Exp, r=[('ps', b), K], w=[('gtmp',)], bias=bgk[:, h:h + 1],
                    scale=-1.0)
                act(spb[:, 0:nt], spb[:, 0:nt], AF.Ln, r=[('gtmp',)], w=[('gtmp',)], bias=1.0)
                S.add('dve', lambda e, nt=nt: e.tensor_tensor_scan(out=Bx[:, 1:1 + nt], data0=ones32[:, 0:nt],
                                                                   data1=spb[:, 0:nt], initial=0.0,
                                                                   op0=ALU.mult, op1=ALU.add),
                      r=[('gtmp',), ('c', 'ones32')], w=[('Bx',)])
                d3 = Drel[:, 0:nt].rearrange('p (c l) -> p c l', l=64)
                d23 = Drel2[:, 0:nt].rearrange('p (c l) -> p c l', l=64)
                bx3 = Bx[:, 1:1 + nt].rearrange('p (c l) -> p c l', l=64)
                br3 = Bx[:, 0:nt].rearrange('p (c l) -> p c l', l=64)[:, :, 0:1]
                tt('dve', d3, bx3, br3.to_broadcast([128, nch, 64]), ALU.subtract, r=[('Bx',)], w=[('Drel',)])
                tt('dve', d23, d3, d3[:, :, 63:64].to_broadcast([128, nch, 64]), ALU.subtract, r=[('Drel',)],
                   w=[('Drel2',)])
                act(E1[:, 0:nt], Drel[:, 0:nt], AF.Exp, r=[('Drel',)], w=[('E1',)], scale=-1.0 / 16)
                act(dec[:, 0:nch], Drel[:, 63:nt:64], AF.Exp, r=[('Drel',)], w=[('dec',)], scale=-1.0 / 16)
                act(E2[:, 0:nt], Drel[:, 0:nt], AF.Exp, r=[('Drel',)], w=[('E2',)], scale=1.0 / 16)
                act(E3[:, 0:nt], Drel2[:, 0:nt], AF.Exp, r=[('Drel2',)], w=[('E3',)], scale=1.0 / 16)
                bq = nextbank()
                proj(bq, Wg, K, O_GQ + h * 128, 128, uT, ('uT',), tok0, nt)
                bk = nextbank()
                proj(bk, Wg, K, O_GK + h * 128, 128, uT, ('uT',), tok0, nt)
                ob = nextbank()
                chunk_engine(nt, h, ps(bq, 128, nt), [('ps', bq)], ps(bk, 128, nt), [('ps', bk)], 128 ** -0.5, 1.0,
                             vtok, 0, [St32], [Stbf], False, ob, None, B)
                act(sq[:, 0:nt], psum[:, ob, 0:nt], AF.Square, r=[('ps', ob)], w=[('sq',)])
                b = nextbank()
                mm(ps(b, 128, nt), onesbf[:, :], sq[:, 0:nt], True, True, r=[('c', 'onesbf'), ('sq',)], w=[('ps', b)])
                act(rstd[:, 0:nt], ps(b, 128, nt), AF.Sqrt, r=[('ps', b), ('c', 'eps')], w=[('E1',)],
                    bias=eps_t[:, :], scale=1.0 / 128)
                recip(rstd[:, 0:nt], rstd[:, 0:nt], r=[('E1',)], w=[('E1',)])
                tt('dve', t1[:, 0:nt], psum[:, ob, 0:nt], rstd[:, 0:nt], ALU.mult, r=[('ps', ob), ('E1',)],
                   w=[('E2',)])
                b = nextbank()
                proj(b, Wg, K, O_GR + h * 128, 128, uT, ('uT',), tok0, nt)
                act(sr[:, 0:nt], ps(b, 128, nt), AF.Silu, r=[('ps', b)], w=[('E3',)])
                stt('dve', ogT[:, h, 0:nt], sr[:, 0:nt], gln[:, 0:1], t1[:, 0:nt], ALU.mult, ALU.mult,
                    r=[('E3',), K, ('E2',)], w=[('ogT',)])
            if tok0 == DBGT and L == 0 and not dumped.get('gla'):
                dumped['gla'] = 1
                dump('uT0', uT[:, :, DBGT:DBGT + 512], [('uT',)])
                dump('ident', ident[:, :], [('c', 'ident')])
                dump('mask', maskut[:, :], [('c', 'mask')])
                dump('Bx', Bx[:, :], [('Bx',)])
                dump('Drel', Drel[:, :], [('Drel',)])
                dump('dec', dec[:, :], [('dec',)])
                dump('qd', qd[:, :], [('qd',)])
                dump('ki', ki[:, :], [('ki',)])
                dump('kd', kd[:, :], [('kd',)])
                dump('vtok', vtok[:, :, :], [('vtok',)])
                dump('ogT', ogT[:, :, :], [('ogT',)])
                dump('St', St32[:, :, :], [('st', id(St32), h) for h in range(4)])
                dump('lrT', lrT[:, :], [('lrT',)])
                dump('Wg', Wg[:, :, 0:64], [K])
                dump('E1', E1[:, :], [('E1',)])
                dump('E2', E2[:, :], [('E2',)])
                dump('E3', E3[:, :], [('E3',)])
                dump('sq', sq[:, :], [('sq',)])
                dump('gln', gln[:, :], [K])
            merge_out(tok0, nt, ogT, ('ogT',), Wpr, K, Wg, K, 2064, True, gsig, gtmp)
            if BAR:
                lbar([(Bx[0:1, 0:1], ('Bx',)), (Drel[0:1, 0:1], ('Drel',)), (Drel2[0:1, 0:1], ('Drel2',)),
                      (dec[0:1, 0:1], ('dec',)), (E1[0:1, 0:1], ('E1',)), (E2[0:1, 0:1], ('E2',)), (E3[0:1, 0:1], ('E3',)),
                      (gsig[0:1, 0:1], ('gsig',)), (gtmp[0:1, 0:1], ('gtmp',))])
            if tok0 == DBGT and L == 0 and not dumped.get('gla2'):
                dumped['gla2'] = 1
                dump('mg0', merged[:, :, DBGT:DBGT + 512], [('mg',)])

    def phase_ml(L):
        arena[0] = PH0
        Wm = aalloc('Wm', [128, 8, 3080], BF16)
        Wpr = aalloc('Wmp', [128, 4, D], BF16)
        B = common_bufs()
        qd, ki, kd, kdT, sT, E1, E2, E3, dec = B
        vtok = aalloc('vtok', [64, 8, 512], BF16)
        zc = aalloc('zc', [128, 515], F32)
        qh = aalloc('qh', [128, 512], BF16); kh = aalloc('kh', [128, 512], BF16)
        B8 = aalloc('B8', [8, 513], F32)
        T2 = aalloc('T2', [8, 512], F32); T3 = aalloc('T3', [8, 512], F32)
        bif = aalloc('bif', [8, 1], F32); cw = aalloc('cw', [128, 8, 4], F32); cb = aalloc('cb', [128, 8], F32)
        dabs = E1; t1 = E2; so = E3
        ogT = aalloc('hT', [128, 4, 512], BF16)
        gsig = aalloc('gsig', [128, 512], F32); gtmp = aalloc('gtmp', [128, 512], F32)
        cacc = gtmp; gi8 = gsig[0:8, :]
        K = ('W', 'm')
        load_w(Wm[:, :, 0:2056], K, 'w_in', L, 0, 8, O_MQ, 2056)
        load_w(Wm[:, :, 2056:3080], K, 'w_in', L, 0, 8, O_GATE + 2048, 1024)
        load_w(Wpr[:, :, :], K, 'ml_w_proj', L, 0, 4, 0, D)
        vec_load(bif[0:4, :], 'ml_b_i', L, [[1, 4], [1, 1]], K)
        vec_load(bif[4:8, :], 'ml_b_f', L, [[1, 4], [1, 1]], K)
        for j in range(4):
            vec_load(cw[:, :, j], 'ml_conv_w', L, [[1, 128], [128, 8]], K, off=j * 1024)
        vec_load(cb[:, :], 'ml_conv_b', L, [[1, 128], [128, 8]], K)
        memset('pool', B8[:, 0:1], 0.0, [('B8',)])
        for tok0 in range(0, SEGT, TILE):
            nt = TILE
            nch = nt // 64
            vproj(tok0, nt, Wm, K, 1024, vtok)
            b = nextbank()
            proj(b, Wm, K, 1536, 8, uT, ('uT',), tok0, nt)
            act(gi8[:, 0:nt], ps(b, 8, nt), AF.Identity, r=[('ps', b), K], w=[('gsig',)], bias=bif[:, :])
            act(T3[:, 0:nt], gi8[:, 0:nt], AF.Exp, r=[('gsig',)], w=[('T3',)], scale=-1.0)
            act(T3[:, 0:nt], T3[:, 0:nt], AF.Ln, r=[('T3',)], w=[('T3',)], bias=1.0)
            S.add('dve', lambda e, nt=nt: e.tensor_tensor_scan(out=B8[:, 1:1 + nt], data0=ones32[0:8, 0:nt],
                                                               data1=T3[:, 0:nt], initial=0.0, op0=ALU.mult,
                                                               op1=ALU.add),
                  r=[('T3',), ('c', 'ones32')], w=[('B8',)])
            d3 = T2[:, 0:nt].rearrange('p (c l) -> p c l', l=64)
            d23 = T3[:, 0:nt].rearrange('p (c l) -> p c l', l=64)
            bx3 = B8[:, 1:1 + nt].rearrange('p (c l) -> p c l', l=64)
            br3 = B8[:, 0:nt].rearrange('p (c l) -> p c l', l=64)[:, :, 0:1]
            tt('dve', d3, bx3, br3.to_broadcast([8, nch, 64]), ALU.subtract, r=[('B8',)], w=[('T2',)])
            tt('dve', d23, d3, d3[:, :, 63:64].to_broadcast([8, nch, 64]), ALU.subtract, r=[('T2',), ('B8',)],
               w=[('T3',)])
            cp('dve', T2[0:4, 0:nt], gi8[0:4, 0:nt], r=[('gsig',), ('T2',)], w=[('T2',)])
            cp('dve', T3[0:4, 0:nt], gi8[0:4, 0:nt], r=[('gsig',), ('T3',)], w=[('T3',)])
            for h in range(4):
                b1 = nextbank()
                mm(ps(b1, 128, nt), selF[:, h, :], T2[:, 0:nt], True, True, r=[('c', 'sel'), ('T2',)], w=[('ps', b1)])
                act(E1[:, 0:nt], ps(b1, 128, nt), AF.Exp, r=[('ps', b1)], w=[('E1',)], scale=-1.0)
                act(dec[:, 0:nch], psum[:, b1, 63:nt:64], AF.Exp, r=[('ps', b1)], w=[('dec',)], scale=-1.0)
                b2 = nextbank()
                mm(ps(b2, 128, nt), selIF[:, h, :], T2[:, 0:nt], True, True, r=[('c', 'sel'), ('T2',)], w=[('ps', b2)])
                act(E2[:, 0:nt], ps(b2, 128, nt), AF.Exp, r=[('ps', b2)], w=[('E2',)])
                b3 = nextbank()
                mm(ps(b3, 128, nt), selIF[:, h, :], T3[:, 0:nt], True, True, r=[('c', 'sel'), ('T3',)], w=[('ps', b3)])
                act(E3[:, 0:nt], ps(b3, 128, nt), AF.Exp, r=[('ps', b3)], w=[('E3',)])
                for which, dstq in ((0, qh), (1, kh)):
                    ct = which * 4 + h
                    b = nextbank()
                    proj(b, Wm, K, ct * 128, 128, uT, ('uT',), tok0, nt)
                    cp('act', zc[:, 3:3 + nt], ps(b, 128, nt), r=[('ps', b)], w=[('zc',)])
                    cp('pool', zc[:, 0:3], zch[:, ct, :], r=[('zch', ct)], w=[('zc',)])
                    ts('dve', cacc[:, 0:nt], zc[:, 0:nt], cw[:, ct, 0:1], None, ALU.mult, None, r=[('zc',), K],
                       w=[('gtmp',)])
                    for j in range(1, 4):
                        stt('dve', cacc[:, 0:nt], zc[:, j:j + nt], cw[:, ct, j:j + 1], cacc[:, 0:nt], ALU.mult, ALU.add,
                            r=[('zc',), K, ('gtmp',)], w=[('gtmp',)])
                    cp('pool', zch[:, ct, :], zc[:, nt:nt + 3], r=[('zc',)], w=[('zch', ct)])
                    act(dstq[:, 0:nt], cacc[:, 0:nt], AF.Silu, r=[('gtmp',), K], w=[('qk', which)],
                        bias=cb[:, ct:ct + 1])
                ob = nextbank()
                db = nextbank()
                chunk_engine(nt, h, qh[:, 0:nt], [('qk', 0)], kh[:, 0:nt], [('qk', 1)], 1.0, 128 ** -0.5, vtok, 0,
                             [Ct32, Nt32], [Ctbf, Ntbf], True, ob, db, B)
                ts('dve', dabs[:, 0:nt], psum[:, db, 0:nt], -1.0, 1.0, ALU.mult, ALU.max, r=[('ps', db)], w=[('E1',)])
                tt('dve', dabs[:, 0:nt], dabs[:, 0:nt], psum[:, db, 0:nt], ALU.max, r=[('ps', db), ('E1',)], w=[('E1',)])
                recip(dabs[:, 0:nt], dabs[:, 0:nt], r=[('E1',)], w=[('E1',)])
                tt('dve', t1[:, 0:nt], psum[:, ob, 0:nt], dabs[:, 0:nt], ALU.mult, r=[('ps', ob), ('E1',)], w=[('E2',)])
                b = nextbank()
                proj(b, Wm, K, 1544 + h * 128, 128, uT, ('uT',), tok0, nt)
                act(so[:, 0:nt], ps(b, 128, nt), AF.Sigmoid, r=[('ps', b)], w=[('E3',)])
                tt('dve', ogT[:, h, 0:nt], t1[:, 0:nt], so[:, 0:nt], ALU.mult, r=[('E2',), ('E3',)], w=[('ogT',)])
            merge_out(tok0, nt, ogT, ('ogT',), Wpr, K, Wm, K, 2056, False, gsig, gtmp)

    def phase_s5(L):
        arena[0] = PH0
        TB = 128
        NCH = TB // 8
        Ws = aalloc('Ws', [128, 8, 1536], BF16)
        Wglu = aalloc('Wglu', [128, 4, 512], BF16)
        Wsp = aalloc('Wsp', [128, 4, D], BF16)
        bglu = aalloc('bglu', [128, 4], F32)
        su128 = aalloc('su128', [128, 4, 512], BF16)
        sup = aalloc('sup', [32, 16, TB], BF16)
        GU = aalloc('GU', [128, 2, 16, TB], F32)
        XC = aalloc('XC', [128, 2, 16, NCH + 1], F32)
        XXbf = aalloc('XXbf', [128, 2, 16, TB], BF16)
        P1 = aalloc('P1', [128, 2, 16, NCH], F32); P2 = aalloc('P2', [128, 2, 16, NCH], F32)
        P1s = aalloc('P1s', [128, 2, 16], F32); P2s = aalloc('P2s', [128, 2, 16], F32)
        yp = aalloc('yp', [32, 16, TB], BF16)
        yv = aalloc('yv', [128, 512], F32)
        ga = aalloc('ga', [128, 512], F32)
        yg = aalloc('yg', [128, 4, 512], BF16)
        y2 = aalloc('y2', [128, 4, 512], BF16)
        gsig = aalloc('gsig', [128, 512], F32); gtmp = aalloc('gtmp', [128, 512], F32)
        gb = gtmp
        K = ('W', 's')
        KC = ('s5c', L)
        load_w(Ws[:, :, 0:512], K, 'w_in', L, 0, 8, O_SU, 512)
        load_w(Ws[:, :, 512:1536], K, 'w_in', L, 0, 8, O_GATE + 1024, 1024)
        load_w(Wglu[:, :, :], K, 's5_w_glu', L, 0, 4, 0, 512)
        load_w(Wsp[:, :, :], K, 's5_w_proj', L, 0, 4, 0, D)
        vec_load(bglu[:, :], 's5_b_glu', L, [[1, 128], [128, 4]], K)
        cp('dve', XC[:, :, :, 0], XXst[:, :, :], r=[('xxst',)], w=[('XC',)])
        GU5 = GU[:, :, :, :].rearrange('p r g (n l) -> p r g n l', l=8)

        def G(tau):
            return GU5[:, :, :, :, tau]

        def cmul_acc(dst, src, k, dkey, skey):
            prb = PW[:, k, 0, :].unsqueeze(1).unsqueeze(3).to_broadcast([128, 2, 16, NCH])
            pib = PW[:, k, 1, :].unsqueeze(2).to_broadcast([128, 16, NCH])
            npib = nPWi[:, k, :].unsqueeze(2).to_broadcast([128, 16, NCH])
            tt('dve', P1[:, :, :, :], src, prb, ALU.mult, r=[skey, KC], w=[('P1',)])
            tt('pool', P2[:, 0, :, :], src[:, 1], npib, ALU.mult, r=[skey, KC], w=[('P2',)])
            tt('pool', P2[:, 1, :, :], src[:, 0], pib, ALU.mult, r=[skey, KC], w=[('P2',)])
            tt('dve', dst, dst, P1[:, :, :, :], ALU.add, r=[dkey, ('P1',)], w=[dkey])
            tt('dve', dst, dst, P2[:, :, :, :], ALU.add, r=[dkey, ('P2',)], w=[dkey])

        for tok0 in range(0, SEGT, TILE):
            nt = TILE
            for blk in range(4):
                b = nextbank()
                proj(b, Ws, K, blk * 128, 128, uT, ('uT',), tok0, nt)
                cp('act', su128[:, blk, 0:nt], ps(b, 128, nt), r=[('ps', b)], w=[('su128',)])
            for tb in range(0, nt, TB):
                for blk in range(4):
                    b = nextbank()
                    for k in range(4):
                        mm(psum[0:32, b, k * TB:(k + 1) * TB], ident[:, 32 * k:32 * k + 32],
                           su128[:, blk, tb:tb + TB], True, True, r=[('c', 'ident'), ('su128',)], w=[('ps', b)])
                    cp('act', sup[:, blk * 4:(blk + 1) * 4, :], ps(b, 32, 4 * TB).rearrange('p (k t) -> p k t', k=4),
                       r=[('ps', b)], w=[('sup',)])
                for ri in range(2):
                    for g in range(4):
                        b = nextbank()
                        for k in range(4):
                            pr = g * 4 + k
                            mm(psum[:, b, k * TB:(k + 1) * TB], BBT[:, pr, ri, :], sup[:, pr, :], True, True,
                               r=[KC, ('sup',)], w=[('ps', b)])
                        cp('act', GU[:, ri, g * 4:(g + 1) * 4, :], ps(b, 128, 4 * TB).rearrange('p (k t) -> p k t', k=4),
                           r=[('ps', b)], w=[('GU',)])
                for tau in range(1, 8):
                    cmul_acc(G(tau), G(tau - 1), 0, ('GU',), ('GU',))
                for n in range(NCH):
                    tt('dve', P1s[:, :, :], XC[:, :, :, n], AA8[:, :, :], ALU.mult, r=[('XC',), KC], w=[('P1s',)])
                    tt('pool', P2s[:, 0, :], XC[:, 1, :, n], nPWi[:, 7, :], ALU.mult, r=[('XC',), KC], w=[('P2s',)])
                    tt('pool', P2s[:, 1, :], XC[:, 0, :, n], PW[:, 7, 1, :], ALU.mult, r=[('XC',), KC], w=[('P2s',)])
                    tt('dve', P1s[:, :, :], P1s[:, :, :], P2s[:, :, :], ALU.add, r=[('P1s',), ('P2s',)], w=[('P1s',)])
                    tt('dve', XC[:, :, :, n + 1], P1s[:, :, :], GU5[:, :, :, n, 7], ALU.add, r=[('P1s',), ('GU',)],
                       w=[('XC',)])
                for tau in range(7):
                    cmul_acc(G(tau), XC[:, :, :, 0:NCH], tau, ('GU',), ('XC',))
                cp('dve', G(7), XC[:, :, :, 1:NCH + 1], r=[('XC',)], w=[('GU',)])
                cp('act', XXbf[:, :, :, :], GU[:, :, :, :], r=[('GU',)], w=[('XXbf',)])
                cp('dve', XC[:, :, :, 0], XC[:, :, :, NCH], r=[('XC',)], w=[('XC',)])
                for g in range(4):
                    b = nextbank()
                    for k in range(4):
                        pr = g * 4 + k
                        mm(psum[0:32, b, k * TB:(k + 1) * TB], Ec[:, 0, pr, :], XXbf[:, 0, pr, :], True, False,
                           r=[KC, ('XXbf',)], w=[('ps', b)])
                        mm(psum[0:32, b, k * TB:(k + 1) * TB], Ec[:, 1, pr, :], XXbf[:, 1, pr, :], False, True,
                           r=[KC, ('XXbf',)], w=[('ps', b)])
                    for k in range(4):
                        pr = g * 4 + k
                        stt('dve', yp[:, pr, :], sup[:, pr, :], Dd[:, pr:pr + 1], psum[0:32, b, k * TB:(k + 1) * TB],
                            ALU.mult, ALU.add, r=[('sup',), KC, ('ps', b)], w=[('yp',)])
                b = nextbank()
                for blk in range(4):
                    for k in range(4):
                        mm(psum[:, b, blk * TB:(blk + 1) * TB], gsel[:, k, :], yp[:, blk * 4 + k, :], k == 0, k == 3,
                           r=[('c', 'gsel'), ('yp',)], w=[('ps', b)])
                cp('act', yv[:, :], ps(b, 128, 4 * TB), r=[('ps', b)], w=[('yv',)])
                act(ga[:, :], yv[:, :], AF.Square, r=[('yv',)], w=[('ga',)])
                ts('dve', ga[:, :], ga[:, :], 0.044715, 1.0, ALU.mult, ALU.add, r=[('ga',)], w=[('ga',)])
                tt('dve', ga[:, :], ga[:, :], yv[:, :], ALU.mult, r=[('ga',), ('yv',)], w=[('ga',)])
                act(gb[:, :], ga[:, :], AF.Sigmoid, r=[('ga',)], w=[('gtmp',)], scale=GC)
                tt('pool', yg[:, :, tb:tb + TB], gb[:, :].rearrange('p (k t) -> p k t', k=4),
                   yv[:, :].rearrange('p (k t) -> p k t', k=4), ALU.mult, r=[('gtmp',), ('yv',)], w=[('yg',)])
            for co in range(4):
                b = nextbank()
                for k in range(4):
                    mm(ps(b, 128, nt), Wglu[:, k, co * 128:(co + 1) * 128], yg[:, k, 0:nt], k == 0, k == 3,
                       r=[K, ('yg',)], w=[('ps', b)])
                act(gsig[:, 0:nt], ps(b, 128, nt), AF.Sigmoid, r=[('ps', b), K], w=[('gsig',)], bias=bglu[:, co:co + 1])
                tt('dve', y2[:, co, 0:nt], yg[:, co, 0:nt], gsig[:, 0:nt], ALU.mult, r=[('yg',), ('gsig',)],
                   w=[('y2',)])
            merge_out(tok0, nt, y2, ('y2',), Wsp, K, Ws, K, 512, False, gsig, gtmp)
        cp('dve', XXst[:, :, :], XC[:, :, :, 0], r=[('XC',)], w=[('xxst',)])

    def phase_out(L, src_t, seg):
        arena[0] = PH0
        Wo = aalloc('Wo', [128, 8, D], BF16)
        t1 = aalloc('t1o', [128, D], F32)
        K = ('W', 'o')
        load_w(Wo[:, :, :], K, 'w_out', L, 0, 8, 0, D)
        if L == 0 and seg == 0:
            dump('mgall', merged[:, :, :], [('mg',)])
        for bi, tb in enumerate(range(0, SEGT, 128)):
            xi = bi % 2
            xb = xt[xi]
            dma(xb[:, :], rawap(src_t, (seg * SEGT + tb) * D, [[D, 128], [1, D]]), w=[('xt', 0)])
            banks = []
            for hf in range(2):
                b = nextbank()
                banks.append(b)
                for k in range(8):
                    mm(ps(b, 128, 512), merged[:, k, tb:tb + 128], Wo[:, k, hf * 512:(hf + 1) * 512], k == 0, k == 7,
                       r=[('mg',), K], w=[('ps', b)])
            rowsum_rstd(banks, 128, 8)
            for hf, b in enumerate(banks):
                tt('dve', t1[:, hf * 512:(hf + 1) * 512], ps(b, 128, 512), gains[1][:, hf * 512:(hf + 1) * 512], ALU.mult,
                   r=[('ps', b), ('c', 'gain')], w=[('t1o', hf)])
                stt('dve', xb[:, hf * 512:(hf + 1) * 512], t1[:, hf * 512:(hf + 1) * 512], small[:, 8:9],
                    xb[:, hf * 512:(hf + 1) * 512], ALU.mult, ALU.add, r=[('t1o', hf), ('rsx',), ('xt', 0)],
                    w=[('xt', 0)])
            if L == 0 and seg == 0 and tb == 128:
                dump('osmall', small[:, 0:16], [('rsx',), ('ssx', 0), ('ssx', 1), ('ssx', 2), ('ssx', 3)])
                dump('ot1', t1[:, :], [('t1o', 0), ('t1o', 1)])
                dump('ojunk', junk[:, :], [('junk',)])
            dma(rawap(xmid, tb * D, [[D, 128], [1, D]]), xb[:, :], r=[('xt', 0)], w=[('xmid', tb)])
            if L == 0 and seg == 0:
                dump('xmid%d' % tb, xb[:, :], [('xt', 0)])

    def phase_ffn(L, dst_t, seg):
        arena[0] = UT0
        Wup = aalloc('Wup', [128, 8, DFF], BF16)
        Wga = aalloc('Wga', [128, 8, DFF], BF16)
        Wd = aalloc('Wd', [128, 22, D], BF16)
        uF = aalloc('uF', [128, 8, 128], BF16)
        hb = aalloc('hb', [128, 22, 128], BF16)
        NB = 4
        fsets = [(aalloc('ab', [128, 130], F32), aalloc('vv', [128, 128], F32), aalloc('gaF', [128, 128], F32),
                  aalloc('vg', [128, 128], F32)) for _ in range(NB)]
        fw = aalloc('fw', [128, 22, 3], F32); fb = aalloc('fb', [128, 22], F32)
        t1 = aalloc('t1f', [128, D], F32)
        K = ('W', 'f')
        dep = [('uT',), ('mg',)]
        cp('pool', small[:, 50:51], small[:, 50:51], r=dep, w=[K] + dep)
        load_w(Wup[:, :, :], K, 'ffn_w_up', L, 0, 8, 0, DFF)
        load_w(Wga[:, :, :], K, 'ffn_w_gate', L, 0, 8, 0, DFF)
        load_w(Wd[:, :, :], K, 'ffn_w_down', L, 0, 22, 0, D)
        for j in range(3):
            vec_load(fw[:, :, j], 'ffn_conv_w', L, [[1, 128], [128, 22]], K, off=j * DFF)
        vec_load(fb[:, :], 'ffn_conv_b', L, [[1, 128], [128, 22]], K)
        dma(gains2[0][:, :], rawap(W['norm_ffn_pre'], L * D, [[0, 128], [1, D]]), w=[('c', 'gain')])
        dma(gains2[1][:, :], rawap(W['norm_ffn_post'], L * D, [[0, 128], [1, D]]), w=[('c', 'gain')])
        NTF = 128
        for t0 in range(0, SEGT, NTF):
            for j in range(1):
                tb = t0 + j * 128
                norm_block(rawap(xmid, tb * D, [[D, 128], [1, D]]), 128, gains[2], ('c', 'gain'), uF, ('uF',), j * 128,
                           j % 2, extra_r=[('xmid', tb)])
            for ct in range(22):
                si = ct % NB
                ab, vv, ga, vg = fsets[si]
                b = nextbank()
                proj(b, Wup, K, ct * 128, 128, uF, ('uF',), 0, NTF)
                cp('act', ab[:, 2:2 + NTF], ps(b, 128, NTF), r=[('ps', b)], w=[('ab', si)])
                cp('pool', ab[:, 0:2], ahist[:, ct, :], r=[('ah', ct)], w=[('ab', si)])
                ts('dve', vv[:, :], ab[:, 0:NTF], fw[:, ct, 0:1], fb[:, ct:ct + 1], ALU.mult, ALU.add,
                   r=[('ab', si), K], w=[('vv', si)])
                for j in range(1, 3):
                    stt('dve', vv[:, :], ab[:, j:j + NTF], fw[:, ct, j:j + 1], vv[:, :], ALU.mult, ALU.add,
                        r=[('ab', si), K, ('vv', si)], w=[('vv', si)])
                cp('pool', ahist[:, ct, :], ab[:, NTF:NTF + 2], r=[('ab', si)], w=[('ah', ct)])
                b2 = nextbank()
                proj(b2, Wga, K, ct * 128, 128, uF, ('uF',), 0, NTF)
                tt('dve', vg[:, :], vv[:, :], ps(b2, 128, NTF), ALU.mult, r=[('vv', si), ('ps', b2)], w=[('vg', si)])
                act(ga[:, :], vv[:, :], AF.Square, r=[('vv', si)], w=[('ga', si)], scale=math.sqrt(0.044715))
                stt('dve', ga[:, :], ga[:, :], 1.0, vv[:, :], ALU.add, ALU.mult, r=[('ga', si), ('vv', si)],
                    w=[('ga', si)])
                act(ga[:, :], ga[:, :], AF.Sigmoid, r=[('ga', si)], w=[('ga', si)], scale=GC)
                tt('pool', hb[:, ct, :], ga[:, :], vg[:, :], ALU.mult, r=[('ga', si), ('vg', si)], w=[('hb',)])
            for j in range(1):
                tb = t0 + j * 128
                xi = j % 2
                xb = xt[xi]
                dma(xb[:, :], rawap(xmid, tb * D, [[D, 128], [1, D]]), r=[('xmid', tb)], w=[('xt', 0)])
                banks = []
                for hf in range(2):
                    b = nextbank()
                    banks.append(b)
                    for k in range(22):
                        mm(ps(b, 128, 512), hb[:, k, j * 128:(j + 1) * 128], Wd[:, k, hf * 512:(hf + 1) * 512], k == 0,
                           k == 21, r=[('hb',), K], w=[('ps', b)])
                rowsum_rstd(banks, 128, 9)
                for hf, b in enumerate(banks):
                    tt('dve', t1[:, hf * 512:(hf + 1) * 512], ps(b, 128, 512), gains[3][:, hf * 512:(hf + 1) * 512],
                       ALU.mult, r=[('ps', b), ('c', 'gain')], w=[('t1f', hf)])
                    stt('dve', xb[:, hf * 512:(hf + 1) * 512], t1[:, hf * 512:(hf + 1) * 512], small[:, 9:10],
                        xb[:, hf * 512:(hf + 1) * 512], ALU.mult, ALU.add, r=[('t1f', hf), ('rsx',), ('xt', 0)],
                        w=[('xt', 0)])
                dma(rawap(dst_t, (seg * SEGT + tb) * D, [[D, 128], [1, D]]), xb[:, :], r=[('xt', 0)],
                    w=[('dst', seg, tb)])
        cp('pool', small[:, 51:52], small[:, 51:52], r=[K, ('uF',), ('hb',)], w=[('uT',), ('mg',)])

    for L in range(nlayer):
        src_t = x_in if L == 0 else x1
        dst_t = out_t if L == nlayer - 1 else x1
        for stt_ in (St32, Ct32, Nt32):
            memset('pool', stt_[:, :, :], 0.0, [('st', id(stt_), h) for h in range(4)])
        for bf_, s32 in ((Stbf, St32), (Ctbf, Ct32), (Ntbf, Nt32)):
            for h in range(4):
                cp('pool', bf_[:, h, :], s32[:, h, :], r=[('st', id(s32), h)], w=[('stbf', id(bf_), h)])
        memset('pool', XXst[:, :, :], 0.0, [('xxst',)])
        for ct in range(8):
            memset('pool', zch[:, ct, :], 0.0, [('zch', ct)])
        for ct in range(22):
            memset('pool', ahist[:, ct, :], 0.0, [('ah', ct)])
        S.barrier()
        s5_setup(L)
        for seg in range(nseg):
            S.barrier()
            dma(gains2[0][:, :], rawap(W['norm_mix_pre'], L * D, [[0, 128], [1, D]]), w=[('c', 'gain')])
            dma(gains2[1][:, :], rawap(W['norm_mix_post'], L * D, [[0, 128], [1, D]]), w=[('c', 'gain')])
            for bi, tb in enumerate(range(0, SEGT, 128)):
                extra = [('dst', seg, tb)] if L > 0 else []
                norm_block(rawap(src_t, (seg * SEGT + tb) * D, [[D, 128], [1, D]]), 128, gains[0], ('c', 'gain'), uT,
                           ('uT',), tb, bi % 2, extra_r=extra)
            S.barrier()
            if 'gla' not in skip:
                phase_gla(L)
                S.barrier()
            if 'ml' not in skip:
                phase_ml(L)
                S.barrier()
            if 's5' not in skip:
                phase_s5(L)
                S.barrier()
            if 'out' not in skip:
                phase_out(L, src_t, seg)
                S.barrier()
            if 'ffn' not in skip:
                phase_ffn(L, dst_t, seg)
                S.barrier()

    with nc.semaphore('e_pe') as s0, nc.semaphore('e_act') as s1, nc.semaphore('e_dve') as s2, \
            nc.semaphore('e_pool') as s3, nc.semaphore('e_sp') as s4:
        esem = {'pe': s0, 'act': s1, 'dve': s2, 'pool': s3, 'sp': s4}
        import contextlib
        with contextlib.ExitStack() as es:
            ssem = [es.enter_context(nc.semaphore('dslot%d' % i)) for i in range(S.nslots)]
            with nc.Block() as block:
                S.emit(nc, block, esem, ssem)
    build.dbg_names = dbg_names
    build.amax = amax
    build.ph0 = PH0
    return nc


def kernel(**inputs):
    nc = build()
    m = {'x': np.ascontiguousarray(inputs['x'].reshape(SEQ, D), dtype=np.float32)}
    for n in WNAMES:
        m[n] = np.ascontiguousarray(inputs[n], dtype=np.float32)
    res = run_bass_kernel_spmd(nc, [m], core_ids=[0])
    return res.results[0]['out'].reshape(1, SEQ, D).astype(np.float32)
```

```python
import math
import numpy as np
import concourse.bass as bass
import concourse.mybir as mybir
from concourse.bass_utils import run_bass_kernel_spmd

F32 = mybir.dt.float32
BF16 = mybir.dt.bfloat16
ALU = mybir.AluOpType
AF = mybir.ActivationFunctionType

NCORE = 8
D = 1024
DIN = 7704
DFF = 2816
SEQ = 16384
OWN = SEQ // NCORE
HALO = 64
PRE = 3
NT = OWN + HALO
NW = NT + PRE
EPS = 1e-6
O_GQ, O_GK, O_GV, O_LR, O_GR, O_SU, O_MQ, O_MK, O_MV, O_MI, O_MF, O_MO, O_GATE = (
    0, 512, 1024, 1536, 1552, 2064, 2576, 3088, 3600, 4112, 4116, 4120, 4632)
TILES = [(0, 64), (64, 512), (576, 512), (1088, 512), (1600, 512)]
SNAP_TOK = OWN
LS5 = 8
GC = 1.5957691216057308

WNAMES = ['norm_mix_pre', 'norm_mix_post', 'norm_ffn_pre', 'norm_ffn_post', 'w_in',
          'gla_w_gk', 'gla_b_gk', 'gla_norm', 'gla_w_proj',
          's5_a_re', 's5_a_im', 's5_log_dt', 's5_b_re', 's5_b_im', 's5_c_re', 's5_c_im', 's5_d',
          's5_w_glu', 's5_b_glu', 's5_w_proj', 'ml_conv_w', 'ml_conv_b', 'ml_b_i', 'ml_b_f',
          'ml_w_proj', 'w_out', 'ffn_w_up', 'ffn_w_gate', 'ffn_conv_w', 'ffn_conv_b', 'ffn_w_down']
WSHAPES = {'norm_mix_pre': [D], 'norm_mix_post': [D], 'norm_ffn_pre': [D], 'norm_ffn_post': [D],
           'w_in': [D, DIN], 'gla_w_gk': [16, 512], 'gla_b_gk': [512], 'gla_norm': [128],
           'gla_w_proj': [512, D], 's5_a_re': [32, 64], 's5_a_im': [32, 64], 's5_log_dt': [32],
           's5_b_re': [32, 64, 16], 's5_b_im': [32, 64, 16], 's5_c_re': [32, 16, 64],
           's5_c_im': [32, 16, 64], 's5_d': [32, 16], 's5_w_glu': [512, 512], 's5_b_glu': [512],
           's5_w_proj': [512, D], 'ml_conv_w': [4, D], 'ml_conv_b': [D], 'ml_b_i': [4], 'ml_b_f': [4],
           'ml_w_proj': [512, D], 'w_out': [D, D], 'ffn_w_up': [D, DFF], 'ffn_w_gate': [D, DFF],
           'ffn_conv_w': [3, DFF], 'ffn_conv_b': [DFF], 'ffn_w_down': [DFF, D]}


class Sched:
    ENGS = ('pe', 'act', 'dve', 'pool', 'sp')
    SELF_SYNC = ('act', 'dve', 'pool')

    def __init__(self, nslots=16):
        self.streams = {e: [] for e in self.ENGS}
        self.ops = []
        self.lastw = {}
        self.rd = {}
        self.nslots = nslots
        self.slot_cnt = [0] * nslots
        self.slot_last = [None] * nslots
        self.dma_n = 0

    def add(self, eng, fn, r=(), w=(), dma=False):
        oid = len(self.ops)
        deps = set()
        raw = set()
        for k in r:
            p = self.lastw.get(k)
            if p is not None:
                deps.add(p)
                raw.add(p)
        for k in w:
            p = self.lastw.get(k)
            if p is not None:
                deps.add(p)
            rr = self.rd.get(k)
            if rr:
                deps.update(rr[0].values())
                deps.update(rr[1])
        o = {'id': oid, 'eng': eng, 'fn': fn, 'dma': dma, 'flag': False, 'raw': raw}
        if dma:
            s = self.dma_n % self.nslots
            self.dma_n += 1
            if self.slot_last[s] is not None:
                deps.add(self.slot_last[s])
            self.slot_cnt[s] += 1
            o['slot'] = s
            o['slotval'] = 16 * self.slot_cnt[s]
            self.slot_last[s] = oid
        for k in w:
            self.lastw[k] = oid
            self.rd[k] = ({}, [])
        for k in r:
            rr = self.rd.setdefault(k, ({}, []))
            if dma:
                rr[1].append(oid)
            else:
                rr[0][eng] = oid
        deps.discard(oid)
        o['deps'] = deps
        self.ops.append(o)
        self.streams[eng].append(o)
        return o

    def barrier(self):
        last = {}
        for e in self.ENGS:
            st = self.streams[e]
            for o in reversed(st):
                if o['fn'] is not None and not o['dma']:
                    last[e] = o['id']
                    break
        dmas = [x for x in self.slot_last if x is not None]
        for e in self.ENGS:
            oid = len(self.ops)
            deps = set(v for k, v in last.items() if k != e) | set(dmas)
            o = {'id': oid, 'eng': e, 'fn': None, 'dma': False, 'flag': False, 'deps': deps}
            self.ops.append(o)
            self.streams[e].append(o)

    def emit(self, nc, block, esem, ssem):
        ops = self.ops
        for o in ops:
            for d in o['deps']:
                dd = ops[d]
                if dd['dma']:
                    continue
                if dd['eng'] != o['eng'] or (d in o.get('raw', ()) and o['eng'] in self.SELF_SYNC):
                    dd['flag'] = True
        for e in self.ENGS:
            c = 0
            for o in self.streams[e]:
                if o['flag'] and not o['dma']:
                    c += 1
                    o['fidx'] = c
        sched = self

        def run(ename, eng):
            waited = {}
            for o in sched.streams[ename]:
                need = {}
                for d in o['deps']:
                    dd = ops[d]
                    if dd['dma']:
                        key = ('s', dd['slot'])
                        val = dd['slotval']
                    elif dd['eng'] == ename and not (d in o.get('raw', ()) and ename in sched.SELF_SYNC):
                        continue
                    else:
                        key = ('e', dd['eng'])
                        val = dd['fidx']
                    if val > waited.get(key, 0) and val > need.get(key, 0):
                        need[key] = val
                for key, val in need.items():
                    sem = ssem[key[1]] if key[0] == 's' else esem[key[1]]
                    eng.wait_ge(sem, val)
                    waited[key] = val
                if o['fn'] is None:
                    continue
                ins = o['fn'](eng)
                if o['dma']:
                    ins.then_inc(ssem[o['slot']], 16)
                elif o['flag']:
                    ins.then_inc(esem[ename], 1)
            if ename == 'sp':
                for s in range(sched.nslots):
                    if sched.slot_cnt[s]:
                        eng.wait_ge(ssem[s], 16 * sched.slot_cnt[s])

        @block.tensor
        def _(e):
            run('pe', e)

        @block.scalar
        def _(e):
            run('act', e)

        @block.vector
        def _(e):
            run('dve', e)

        @block.gpsimd
        def _(e):
            run('pool', e)

        @block.sync
        def _(e):
            run('sp', e)


def rawap(t, offset, pat):
    return bass.AP(tensor=t, offset=offset, ap=[list(p) for p in pat])


SEGT = 2048
DBGT = 512
BAR = False
TILE = 512
PI = math.pi


def build(nseg=8, nlayer=2, dbg=False, skip=()):
    nc = bass.Bass("TRN2", target_bir_lowering=False)
    S = Sched()
    ntok = nseg * SEGT
    x_in = nc.dram_tensor('x', [ntok, D], F32, kind='ExternalInput')
    W = {n: nc.dram_tensor(n, [2] + WSHAPES[n], F32, kind='ExternalInput') for n in WNAMES}
    WSZ = {n: int(np.prod(WSHAPES[n])) for n in WNAMES}
    out_t = nc.dram_tensor('out', [ntok, D], F32, kind='ExternalOutput')
    x1 = nc.dram_tensor('x1', [ntok, D], F32)
    xmid = nc.dram_tensor('xmid', [SEGT, D], F32)

    sb_top = [16512]
    SB_END = 229376

    def alloc(name, shape, dt, at=None):
        nbytes = int(np.prod(shape[1:])) * (4 if dt == F32 else 2)
        nbytes = (nbytes + 63) // 64 * 64
        if at is None:
            off = sb_top[0]
            sb_top[0] += nbytes
        else:
            off = at
        assert off + nbytes <= SB_END, (name, off, nbytes)
        return nc.alloc_sbuf_tensor_at(name, list(shape), dt, offset=off)

    psum = nc.alloc_psum_tensor('psum', [128, 8, 512], F32)

    def ps(b, p=128, n=512):
        return psum[0:p, b, 0:n]

    bank_rr = [0]

    def nextbank():
        b = bank_rr[0] % 6
        bank_rr[0] += 1
        return b

    ident = alloc('ident', [128, 128], BF16)
    onesbf = alloc('onesbf', [128, 128], BF16)
    ones32 = alloc('ones32', [128, 512], F32)
    maskut = alloc('maskut', [64, 64], F32)
    eps_t = alloc('eps_t', [128, 1], F32)
    negpi = alloc('negpi', [128, 1], F32)
    gains2 = [alloc('gain%d' % i, [128, D], F32) for i in range(2)]
    gains = [gains2[0], gains2[1], gains2[0], gains2[1]]
    xt0 = alloc('xt0', [128, D], F32)
    xt = [xt0, xt0]
    un = alloc('un', [128, D], BF16)
    junk = alloc('junk', [128, D], BF16)
    small = alloc('small', [128, 64], F32)
    stage0 = alloc('stage0', [128, 512], F32)
    stage = [stage0, stage0]
    St32 = alloc('St32', [128, 4, 128], F32)
    Ct32 = alloc('Ct32', [128, 4, 128], F32)
    Nt32 = alloc('Nt32', [128, 4, 128], F32)
    Stbf = alloc('Stbf', [128, 4, 128], BF16)
    Ctbf = alloc('Ctbf', [128, 4, 128], BF16)
    Ntbf = alloc('Ntbf', [128, 4, 128], BF16)
    XXst = alloc('XXst', [128, 2, 16], F32)
    zch = alloc('zch', [128, 8, 3], F32)
    ahist = alloc('ahist', [128, 22, 2], F32)
    BBT = alloc('BBT', [32, 16, 2, 128], BF16)
    Ec = alloc('Ec', [128, 2, 16, 32], BF16)
    AA1 = alloc('AA1', [128, 2, 16], F32)
    A1i = alloc('A1i', [128, 16], F32)
    nA1i = alloc('nA1i', [128, 16], F32)
    Dd = alloc('Dd', [32, 16], F32)
    PW = alloc('PW', [128, 8, 2, 16], F32)
    nPWi = alloc('nPWi', [128, 8, 16], F32)
    AA8 = alloc('AA8', [128, 2, 16], F32)
    gsel = alloc('gsel', [32, 4, 128], BF16)
    selF = alloc('selF', [8, 4, 128], F32)
    selIF = alloc('selIF', [8, 4, 128], F32)
    UT0 = sb_top[0]
    uT = alloc('uT', [128, 8, SEGT], BF16)
    merged = alloc('merged', [128, 8, SEGT], BF16)
    PH0 = sb_top[0]
    arena = [PH0]
    an = [0]

    amax = {}

    def aalloc(name, shape, dt):
        an[0] += 1
        t = alloc('%s_%d' % (name, an[0]), shape, dt, at=arena[0])
        nb = int(np.prod(shape[1:])) * (4 if dt == F32 else 2)
        arena[0] += (nb + 63) // 64 * 64
        amax['top'] = max(amax.get('top', 0), arena[0])
        return t

    cnt = {'stage': 0}
    dumped = {}

    def dma(out, in_, r=(), w=(), slow=False):
        if slow:
            return S.add('sp', lambda e: e.dma_start(out=out, in_=in_, allow_slow_non_contiguous=True),
                         r=r, w=w, dma=True)
        return S.add('sp', lambda e: e.dma_start(out=out, in_=in_), r=r, w=w, dma=True)

    def act(out, in_, func, r, w, bias=None, scale=None, accum=None):
        kw = {}
        if bias is not None:
            kw['bias'] = bias
        if scale is not None:
            kw['scale'] = scale
        if accum is not None:
            kw['accum_out'] = accum
        return S.add('act', lambda e: e.activation(out=out, in_=in_, func=func, **kw), r=r, w=w)

    def tt(eng, out, in0, in1, op, r, w):
        return S.add(eng, lambda e: e.tensor_tensor(out=out, in0=in0, in1=in1, op=op), r=r, w=w)

    def ts(eng, out, in0, s1, s2, op0, op1, r, w):
        if s2 is None:
            return S.add(eng, lambda e: e.tensor_single_scalar(out=out, in_=in0, scalar=s1, op=op0), r=r, w=w)
        return S.add(eng, lambda e: e.tensor_scalar(out=out, in0=in0, scalar1=s1, scalar2=s2, op0=op0, op1=op1),
                     r=r, w=w)

    def stt(eng, out, in0, sc, in1, op0, op1, r, w):
        return S.add(eng, lambda e: e.scalar_tensor_tensor(out=out, in0=in0, scalar=sc, in1=in1, op0=op0, op1=op1),
                     r=r, w=w)

    def cp(eng, out, in_, r, w):
        if eng == 'act':
            return S.add('act', lambda e: e.copy(out=out, in_=in_), r=r, w=w)
        return S.add(eng, lambda e: e.tensor_copy(out=out, in_=in_), r=r, w=w)

    def mm(out, lhsT, rhs, start, stop, r, w):
        return S.add('pe', lambda e: e.matmul(out, lhsT, rhs, start=start, stop=stop, skip_group_check=True),
                     r=r, w=w)

    def tr(out, in_, idn, r, w):
        return S.add('pe', lambda e: e.transpose(out, in_, idn), r=r, w=w)

    def memset(eng, ap, val, w):
        return S.add(eng, lambda e: e.memset(ap, val), r=(), w=w)

    dbg_names = []

    def dump(name, ap, rkeys):
        if not dbg:
            return
        t = nc.dram_tensor('dbg_' + name, list(ap.shape), ap.dtype, kind='ExternalOutput')
        dbg_names.append('dbg_' + name)
        full = t.ap()
        dma(full, ap, r=rkeys, w=[('dbg', name)])

    scr = nc.dram_tensor('scr_bar', [64, 64], F32)
    barn = [0]

    def lbar(items):
        for ap_, key in items:
            i = barn[0] % 64
            barn[0] += 1
            if ap_.dtype != F32:
                continue
            dma(scr.ap()[i:i + 1, 0:1], ap_, r=[key], w=[('scr', i)])

    def recip(out, in_, r, w):
        return S.add('dve', lambda e: e.reciprocal(out=out, in_=in_), r=r, w=w)

    def load_w(dst, key, name, L, row0, kt, col0, ncols, rows=128):
        src_t = W[name]
        ld = WSHAPES[name][1] if len(WSHAPES[name]) > 1 else 1
        base = L * WSZ[name]
        kstep = max(1, 8 if kt <= 8 else 11)
        for k0 in range(0, kt, kstep):
            kn = min(kstep, kt - k0)
            src = rawap(src_t, base + (row0 + k0 * 128) * ld + col0, [[ld, rows], [128 * ld, kn], [1, ncols]])
            dd = dst[:, k0:k0 + kn, :]
            S.add('pool', lambda e, dd=dd, src=src: e.dma_start(out=dd, in_=src), r=(), w=[key], dma=True)

    def vec_load(dst, name, L, pat, key, off=0, slow=True):
        dma(dst, rawap(W[name], L * WSZ[name] + off, pat), w=[key], slow=slow)

    memset('pool', ones32[:, :], 1.0, [('c', 'ones32')])
    memset('pool', eps_t[:, :], EPS, [('c', 'eps')])
    memset('pool', negpi[:, :], -PI, [('c', 'negpi')])
    memset('pool', maskut[:, :], 1.0, [('c', 'mask')])
    S.add('pool', lambda e: e.affine_select(out=maskut[:, :], in_=maskut[:, :], pattern=[[1, 64]],
                                            compare_op=ALU.is_ge, fill=0.0, base=0, channel_multiplier=-1),
          r=[('c', 'mask')], w=[('c', 'mask')])
    cp('pool', onesbf[:, :], ones32[:, 0:128], r=[('c', 'ones32')], w=[('c', 'onesbf')])
    identf = stage[1]
    memset('pool', identf[:, 0:128], 1.0, [('stage', 1)])
    S.add('pool', lambda e: e.affine_select(out=identf[:, 0:128], in_=identf[:, 0:128], pattern=[[1, 128]],
                                            compare_op=ALU.is_equal, fill=0.0, base=0, channel_multiplier=-1),
          r=[('stage', 1)], w=[('stage', 1)])
    cp('pool', ident[:, :], identf[:, 0:128], r=[('stage', 1)], w=[('c', 'ident')])
    for k in range(4):
        memset('pool', identf[0:32, 128:256], 1.0, [('stage', 1)])
        S.add('pool', lambda e, k=k: e.affine_select(out=identf[0:32, 128:256], in_=identf[0:32, 128:256],
                                                     pattern=[[1, 128]], compare_op=ALU.is_equal, fill=0.0,
                                                     base=-32 * k, channel_multiplier=-1),
              r=[('stage', 1)], w=[('stage', 1)])
        cp('pool', gsel[:, k, :], identf[0:32, 128:256], r=[('stage', 1)], w=[('c', 'gsel')])
    for h in range(4):
        for (dst, rows_) in ((selF, (4 + h,)), (selIF, (h, 4 + h))):
            memset('pool', dst[:, h, :], 0.0, [('c', 'sel')])
            for rr in rows_:
                memset('pool', identf[0:8, 256:384], 1.0, [('stage', 1)])
                S.add('pool', lambda e, rr=rr: e.affine_select(out=identf[0:8, 256:384], in_=identf[0:8, 256:384],
                                                               pattern=[[0, 128]], compare_op=ALU.is_equal, fill=0.0,
                                                               base=-rr, channel_multiplier=1),
                      r=[('stage', 1)], w=[('stage', 1)])
                tt('pool', dst[:, h, :], dst[:, h, :], identf[0:8, 256:384], ALU.add, r=[('stage', 1), ('c', 'sel')],
                   w=[('c', 'sel')])

    def norm_block(src_ap, nb, gain_t, gkey, dstT, dkey, dcol0, xi, extra_r=()):
        xb = xt[xi]
        dma(xb[0:nb, :], src_ap, r=extra_r, w=[('xt', 0)])
        act(junk[0:nb, :], xb[0:nb, :], AF.Square, r=[('xt', 0)], w=[('junk',)])
        S.add('dve', lambda e: e.reduce_sum(out=small[0:nb, 0:1], in_=junk[0:nb, :], axis=mybir.AxisListType.X),
              r=[('junk',)], w=[('ss',)])
        act(small[0:nb, 1:2], small[0:nb, 0:1], AF.Sqrt, r=[('ss',), ('c', 'eps')], w=[('sd',)],
            bias=eps_t[0:nb, :], scale=1.0 / D)
        recip(small[0:nb, 2:3], small[0:nb, 1:2], r=[('sd',)], w=[('rs',)])
        stt('dve', un[0:nb, :], xb[0:nb, :], small[0:nb, 2:3], gain_t[0:nb, :], ALU.mult, ALU.mult,
            r=[('xt', 0), ('rs',), gkey], w=[('un',)])
        pbf = psum[:, 7, :].bitcast(BF16)
        for k in range(8):
            tr(pbf[:, k * 128:k * 128 + nb], un[0:nb, k * 128:(k + 1) * 128], ident[0:nb, 0:nb],
               r=[('un',), ('c', 'ident')], w=[('ps', 7)])
        cp('act', dstT[:, :, dcol0:dcol0 + nb], pbf.rearrange('p (k t) -> p k t', k=8)[:, :, 0:nb], r=[('ps', 7)],
           w=[dkey])

    def rowsum_rstd(banks, nb, col):
        for hf, b in enumerate(banks):
            act(junk[0:nb, hf * 512:(hf + 1) * 512], ps(b, nb, 512), AF.Square, r=[('ps', b)], w=[('junk',)])
            S.add('dve', lambda e, hf=hf: e.reduce_sum(out=small[0:nb, 4 + hf:5 + hf],
                                                       in_=junk[0:nb, hf * 512:(hf + 1) * 512],
                                                       axis=mybir.AxisListType.X), r=[('junk',)], w=[('ssx', hf)])
        tt('dve', small[0:nb, 6:7], small[0:nb, 4:5], small[0:nb, 5:6], ALU.add, r=[('ssx', 0), ('ssx', 1)],
           w=[('ssx', 2)])
        act(small[0:nb, 7:8], small[0:nb, 6:7], AF.Sqrt, r=[('ssx', 2), ('c', 'eps')], w=[('ssx', 3)],
            bias=eps_t[0:nb, :], scale=1.0 / D)
        recip(small[0:nb, col:col + 1], small[0:nb, 7:8], r=[('ssx', 3)], w=[('rsx',)])

    def proj(bank, wt, wkey, c0, M, src, skey, tok0, nt, kt=8):
        for k in range(kt):
            mm(ps(bank, M, nt), wt[:, k, c0:c0 + M], src[:, k, tok0:tok0 + nt], k == 0, k == kt - 1,
               r=[wkey, skey], w=[('ps', bank)])

    def gelu(dst, v, n, vkey, dkey, tmpa, tmpb, P=128):
        act(tmpa[0:P, 0:n], v, AF.Square, r=[vkey], w=[('ga',)])
        ts('dve', tmpa[0:P, 0:n], tmpa[0:P, 0:n], 0.044715, 1.0, ALU.mult, ALU.add, r=[('ga',)], w=[('ga',)])
        tt('dve', tmpa[0:P, 0:n], tmpa[0:P, 0:n], v, ALU.mult, r=[('ga',), vkey], w=[('ga',)])
        act(tmpb[0:P, 0:n], tmpa[0:P, 0:n], AF.Sigmoid, r=[('ga',)], w=[('gb',)], scale=GC)
        tt('pool', dst, tmpb[0:P, 0:n], v, ALU.mult, r=[('gb',), vkey], w=[dkey])

    def s5_setup(L):
        arena[0] = PH0
        aR = aalloc('aR', [128, 16], F32); aI = aalloc('aI', [128, 16], F32); ldt = aalloc('ldt', [128, 16], F32)
        t = [aalloc('s5t', [128, 16], F32) for _ in range(12)]
        bR = aalloc('bR', [128, 16, 16], F32); bI = aalloc('bI', [128, 16, 16], F32)
        cR = aalloc('cR', [128, 16, 16], F32); cI = aalloc('cI', [128, 16, 16], F32)
        u1 = aalloc('u1', [128, 16, 16], F32); u2 = aalloc('u2', [128, 16, 16], F32)
        bsrc = [aalloc('bsrc', [128, 16, 32], BF16) for _ in range(2)]
        K = ('s5c', L)
        vec_load(aR[:, :], 's5_a_re', L, [[1, 128], [128, 16]], K)
        vec_load(aI[:, :], 's5_a_im', L, [[1, 128], [128, 16]], K)
        for e in range(2):
            vec_load(ldt[e * 64:(e + 1) * 64, :], 's5_log_dt', L, [[0, 64], [2, 16]], K, off=e)
        vec_load(bR[:, :, :], 's5_b_re', L, [[16, 128], [2048, 16], [1, 16]], K, slow=False)
        vec_load(bI[:, :, :], 's5_b_im', L, [[16, 128], [2048, 16], [1, 16]], K, slow=False)
        for e in range(2):
            for pr in range(16):
                vec_load(cR[e * 64:(e + 1) * 64, pr, :], 's5_c_re', L, [[1, 64], [64, 16]], K, off=e * 1024 + pr * 2048)
                vec_load(cI[e * 64:(e + 1) * 64, pr, :], 's5_c_im', L, [[1, 64], [64, 16]], K, off=e * 1024 + pr * 2048)
        vec_load(Dd[:, :], 's5_d', L, [[1, 32], [32, 16]], K)
        R = [K]
        dt_, dre, dim, mag, sn, cs, A1r, den, zr, fre, fim, tmp = t
        act(dt_[:, :], ldt[:, :], AF.Exp, r=R, w=R)
        tt('dve', dre[:, :], dt_[:, :], aR[:, :], ALU.mult, r=R, w=R)
        tt('dve', dim[:, :], dt_[:, :], aI[:, :], ALU.mult, r=R, w=R)
        act(mag[:, :], dre[:, :], AF.Exp, r=R, w=R)
        cp('dve', sn[:, :], dim[:, :], r=R, w=R)
        ts('dve', cs[:, :], dim[:, :], 0.5 * PI, None, ALU.add, None, r=R, w=R)
        for arr in (sn, cs):
            cp('dve', den[:, :], arr[:, :], r=R, w=R)
            for th in (PI, 3 * PI, 5 * PI, 7 * PI):
                ts('dve', tmp[:, :], den[:, :], th, None, ALU.is_ge, None, r=R, w=R)
                stt('dve', arr[:, :], tmp[:, :], -2 * PI, arr[:, :], ALU.mult, ALU.add, r=R, w=R)
            act(arr[:, :], arr[:, :], AF.Sin, r=R, w=R)
        tt('dve', A1r[:, :], mag[:, :], cs[:, :], ALU.mult, r=R, w=R)
        tt('dve', A1i[:, :], mag[:, :], sn[:, :], ALU.mult, r=R, w=R)
        ts('dve', nA1i[:, :], A1i[:, :], -1.0, None, ALU.mult, None, r=R, w=R)
        cp('dve', AA1[:, 0, :], A1r[:, :], r=R, w=R)
        cp('dve', AA1[:, 1, :], A1r[:, :], r=R, w=R)
        cp('dve', PW[:, 0, 0, :], A1r[:, :], r=R, w=R)
        cp('dve', PW[:, 0, 1, :], A1i[:, :], r=R, w=R)
        for k in range(7):
            tt('dve', tmp[:, :], PW[:, k, 0, :], A1r[:, :], ALU.mult, r=R, w=R)
            tt('dve', den[:, :], PW[:, k, 1, :], A1i[:, :], ALU.mult, r=R, w=R)
            tt('dve', PW[:, k + 1, 0, :], tmp[:, :], den[:, :], ALU.subtract, r=R, w=R)
            tt('dve', tmp[:, :], PW[:, k, 0, :], A1i[:, :], ALU.mult, r=R, w=R)
            tt('dve', den[:, :], PW[:, k, 1, :], A1r[:, :], ALU.mult, r=R, w=R)
            tt('dve', PW[:, k + 1, 1, :], tmp[:, :], den[:, :], ALU.add, r=R, w=R)
        for k in range(8):
            ts('dve', nPWi[:, k, :], PW[:, k, 1, :], -1.0, None, ALU.mult, None, r=R, w=R)
        cp('dve', AA8[:, 0, :], PW[:, 7, 0, :], r=R, w=R)
        cp('dve', AA8[:, 1, :], PW[:, 7, 0, :], r=R, w=R)
        tt('dve', den[:, :], aR[:, :], aR[:, :], ALU.mult, r=R, w=R)
        tt('dve', tmp[:, :], aI[:, :], aI[:, :], ALU.mult, r=R, w=R)
        tt('dve', den[:, :], den[:, :], tmp[:, :], ALU.add, r=R, w=R)
        recip(den[:, :], den[:, :], r=R, w=R)
        ts('dve', zr[:, :], A1r[:, :], -1.0, None, ALU.add, None, r=R, w=R)
        tt('dve', fre[:, :], zr[:, :], aR[:, :], ALU.mult, r=R, w=R)
        tt('dve', tmp[:, :], A1i[:, :], aI[:, :], ALU.mult, r=R, w=R)
        tt('dve', fre[:, :], fre[:, :], tmp[:, :], ALU.add, r=R, w=R)
        tt('dve', fre[:, :], fre[:, :], den[:, :], ALU.mult, r=R, w=R)
        tt('dve', fim[:, :], A1i[:, :], aR[:, :], ALU.mult, r=R, w=R)
        tt('dve', tmp[:, :], zr[:, :], aI[:, :], ALU.mult, r=R, w=R)
        tt('dve', fim[:, :], fim[:, :], tmp[:, :], ALU.subtract, r=R, w=R)
        tt('dve', fim[:, :], fim[:, :], den[:, :], ALU.mult, r=R, w=R)
        frb = fre[:, :].unsqueeze(2).to_broadcast([128, 16, 16])
        fib = fim[:, :].unsqueeze(2).to_broadcast([128, 16, 16])
        tt('dve', u1[:, :, :], bR[:, :, :], frb, ALU.mult, r=R, w=R)
        tt('dve', u2[:, :, :], bI[:, :, :], fib, ALU.mult, r=R, w=R)
        tt('dve', u1[:, :, :], u1[:, :, :], u2[:, :, :], ALU.subtract, r=R, w=R)
        tt('dve', u2[:, :, :], bI[:, :, :], frb, ALU.mult, r=R, w=R)
        tt('dve', bI[:, :, :], bR[:, :, :], fib, ALU.mult, r=R, w=R)
        tt('dve', u2[:, :, :], u2[:, :, :], bI[:, :, :], ALU.add, r=R, w=R)
        ts('dve', cI[:, :, :], cI[:, :, :], -1.0, None, ALU.mult, None, r=R, w=R)
        for ri, (bsrc_, srcb, srcc) in enumerate(((bsrc[0], u1, cR), (bsrc[1], u2, cI))):
            memset('dve', bsrc_[:, :, :], 0.0, R)
            memset('dve', Ec[:, ri, :, :], 0.0, R)
            for e in range(2):
                cp('dve', bsrc_[e * 64:(e + 1) * 64, :, e * 16:(e + 1) * 16], srcb[e * 64:(e + 1) * 64, :, :], r=R, w=R)
                cp('dve', Ec[e * 64:(e + 1) * 64, ri, :, e * 16:(e + 1) * 16], srcc[e * 64:(e + 1) * 64, :, :], r=R, w=R)
        pbf = psum[:, 7, :].bitcast(BF16)
        for ri in range(2):
            for g4 in range(2):
                for j in range(8):
                    pr = g4 * 8 + j
                    tr(pbf[0:32, j * 128:(j + 1) * 128], bsrc[ri][:, pr, :], ident[:, :], r=R + [('c', 'ident')],
                       w=[('ps', 7)])
                cp('act', BBT[:, g4 * 8:(g4 + 1) * 8, ri, :], pbf[0:32, :].rearrange('p (j q) -> p j q', j=8),
                   r=[('ps', 7)], w=R)

    def chunk_engine(nt, h, qa, qkeys, ka, kkeys, qscale, kscale, vtok, vcol0, stl, bfl, use_n, o_bank, den_bank,
                     B):
        nch = nt // 64
        qd, ki, kd, kdT, sT, E1, E2, E3, dec = B
        stt('dve', qd[:, 0:nt], qa, qscale, E1[:, 0:nt], ALU.mult, ALU.mult, r=qkeys + [('E1',)], w=[('qd',)])
        stt('dve', ki[:, 0:nt], ka, kscale, E2[:, 0:nt], ALU.mult, ALU.mult, r=kkeys + [('E2',)], w=[('ki',)])
        stt('dve', kd[:, 0:nt], ka, kscale, E3[:, 0:nt], ALU.mult, ALU.mult, r=kkeys + [('E3',)], w=[('kd',)])
        S32 = stl[0]
        Sbf = bfl[0]
        pbf = psum[:, 7, :].bitcast(BF16)
        for c in range(nch):
            cs = c * 64
            i2 = c % 2
            tr(pbf[0:64, 0:128], kd[:, cs:cs + 64], ident[:, :], r=[('kd',), ('c', 'ident')], w=[('ps', 7)])
            cp('act', kdT[i2][:, :], pbf[0:64, 0:128], r=[('ps', 7)], w=[('kdT', i2)])
            mm(ps(6, 64, 64), ki[:, cs:cs + 64], qd[:, cs:cs + 64], True, True, r=[('ki',), ('qd',)], w=[('ps', 6)])
            tt('dve', sT[i2][:, :], ps(6, 64, 64), maskut[:, :], ALU.mult, r=[('ps', 6), ('c', 'mask')],
               w=[('sT', i2)])
            vv = vtok[:, c, vcol0 + h * 128: vcol0 + (h + 1) * 128]
            mm(psum[:, o_bank, cs:cs + 64], vv, sT[i2][:, :], True, False, r=[('vtok',), ('sT', i2)],
               w=[('ps', o_bank)])
            mm(psum[:, o_bank, cs:cs + 64], Sbf[:, h, :], qd[:, cs:cs + 64], False, True,
               r=[('stbf', id(Sbf), h), ('qd',)], w=[('ps', o_bank)])
            if use_n:
                Nbf = bfl[1]
                mm(psum[:, den_bank, cs:cs + 64], onesbf[0:64, :], sT[i2][:, :], True, False,
                   r=[('c', 'onesbf'), ('sT', i2)], w=[('ps', den_bank)])
                mm(psum[:, den_bank, cs:cs + 64], Nbf[:, h, :], qd[:, cs:cs + 64], False, True,
                   r=[('stbf', id(Nbf), h), ('qd',)], w=[('ps', den_bank)])
            mm(ps(6, 128, 128), kdT[i2][:, :], vv, True, True, r=[('kdT', i2), ('vtok',)], w=[('ps', 6)])
            stt('dve', S32[:, h, :], S32[:, h, :], dec[:, c:c + 1], ps(6, 128, 128), ALU.mult, ALU.add,
                r=[('st', id(S32), h), ('dec',), ('ps', 6)], w=[('st', id(S32), h)])
            cp('act', Sbf[:, h, :], S32[:, h, :], r=[('st', id(S32), h)], w=[('stbf', id(Sbf), h)])
            if use_n:
                N32 = stl[1]
                mm(ps(6, 128, 128), kdT[i2][:, :], onesbf[0:64, :], True, True, r=[('kdT', i2), ('c', 'onesbf')],
                   w=[('ps', 6)])
                stt('dve', N32[:, h, :], N32[:, h, :], dec[:, c:c + 1], ps(6, 128, 128), ALU.mult, ALU.add,
                    r=[('st', id(N32), h), ('dec',), ('ps', 6)], w=[('st', id(N32), h)])
                cp('act', Nbf[:, h, :], N32[:, h, :], r=[('st', id(N32), h)], w=[('stbf', id(Nbf), h)])

    def vproj(tok0, nt, wt, wkey, c0, vtok):
        for c in range(nt // 64):
            b = nextbank()
            for k in range(8):
                mm(ps(b, 64, 512), uT[:, k, tok0 + c * 64: tok0 + c * 64 + 64], wt[:, k, c0:c0 + 512],
                   k == 0, k == 7, r=[wkey, ('uT',)], w=[('ps', b)])
            cp('act', vtok[:, c, :], ps(b, 64, 512), r=[('ps', b)], w=[('vtok',)])

    def merge_out(tok0, nt, oT, okey, wpr, wprkey, wgate, wgkey, gc0, first, gsig, gtmp):
        for co in range(8):
            b = nextbank()
            for k in range(4):
                mm(ps(b, 128, nt), wpr[:, k, co * 128:(co + 1) * 128], oT[:, k, 0:nt], k == 0, k == 3,
                   r=[wprkey, okey], w=[('ps', b)])
            b2 = nextbank()
            proj(b2, wgate, wgkey, gc0 + co * 128, 128, uT, ('uT',), tok0, nt)
            act(gsig[:, 0:nt], ps(b2, 128, nt), AF.Sigmoid, r=[('ps', b2)], w=[('gsig',)])
            if first:
                tt('dve', merged[:, co, tok0:tok0 + nt], ps(b, 128, nt), gsig[:, 0:nt], ALU.mult,
                   r=[('ps', b), ('gsig',)], w=[('mg',)])
            else:
                tt('dve', gtmp[:, 0:nt], ps(b, 128, nt), gsig[:, 0:nt], ALU.mult, r=[('ps', b), ('gsig',)],
                   w=[('gtmp',)])
                tt('pool', merged[:, co, tok0:tok0 + nt], merged[:, co, tok0:tok0 + nt], gtmp[:, 0:nt], ALU.add,
                   r=[('gtmp',), ('mg',)], w=[('mg',)])

    def common_bufs():
        qd = aalloc('qd', [128, 512], BF16); ki = aalloc('ki', [128, 512], BF16); kd = aalloc('kd', [128, 512], BF16)
        kdT = [aalloc('kdT', [64, 128], BF16) for _ in range(2)]
        sT = [aalloc('sT', [64, 64], BF16) for _ in range(2)]
        E1 = aalloc('E1', [128, 512], F32); E2 = aalloc('E2', [128, 512], F32); E3 = aalloc('E3', [128, 512], F32)
        dec = aalloc('dec', [128, 8], F32)
        return (qd, ki, kd, kdT, sT, E1, E2, E3, dec)

    def phase_gla(L):
        arena[0] = PH0
        Wg = aalloc('Wg', [128, 8, 3088], BF16)
        Wgk = aalloc('Wgk', [16, 1, 512], BF16)
        Wpr = aalloc('Wpr', [128, 4, D], BF16)
        bgk = aalloc('bgk', [128, 4], F32); gln = aalloc('gln', [128, 1], F32)
        Bx = aalloc('Bx', [128, 513], F32)
        Drel = aalloc('Drel', [128, 512], F32); Drel2 = aalloc('Drel2', [128, 512], F32)
        B = common_bufs()
        qd, ki, kd, kdT, sT, E1, E2, E3, dec = B
        vtok = aalloc('vtok', [64, 8, 512], BF16)
        lrT = aalloc('lrT', [16, 512], BF16)
        sq = aalloc('sq', [128, 512], BF16)
        rstd = E1; t1 = E2; sr = E3
        ogT = aalloc('ogT', [128, 4, 512], BF16)
        gsig = aalloc('gsig', [128, 512], F32); gtmp = aalloc('gtmp', [128, 512], F32)
        spb = gtmp
        K = ('W', 'g')
        load_w(Wg[:, :, 0:2064], K, 'w_in', L, 0, 8, 0, 2064)
        load_w(Wg[:, :, 2064:3088], K, 'w_in', L, 0, 8, O_GATE, 1024)
        load_w(Wgk[:, :, :], K, 'gla_w_gk', L, 0, 1, 0, 512, rows=16)
        load_w(Wpr[:, :, :], K, 'gla_w_proj', L, 0, 4, 0, D)
        vec_load(bgk[:, :], 'gla_b_gk', L, [[1, 128], [128, 4]], K)
        ts('dve', bgk[:, :], bgk[:, :], -1.0, None, ALU.mult, None, r=[K], w=[K])
        vec_load(gln[:, :], 'gla_norm', L, [[1, 128], [1, 1]], K)
        memset('pool', Bx[:, 0:1], 0.0, [('Bx',)])
        for tok0 in range(0, SEGT, TILE):
            nt = TILE
            nch = nt // 64
            vproj(tok0, nt, Wg, K, O_GV, vtok)
            b = nextbank()
            proj(b, Wg, K, O_LR, 16, uT, ('uT',), tok0, nt)
            cp('act', lrT[:, 0:nt], ps(b, 16, nt), r=[('ps', b)], w=[('lrT',)])
            for h in range(4):
                b = nextbank()
                mm(ps(b, 128, nt), Wgk[:, 0, h * 128:(h + 1) * 128], lrT[:, 0:nt], True, True, r=[K, ('lrT',)],
                   w=[('ps', b)])
                act(spb[:, 0:nt], ps(b, 128, nt), AF.Exp, r=[('ps', b), K], w=[('gtmp',)], bias=bgk[:, h:h + 1],
                    scale=-1.0)
                act(spb[:, 0:nt], spb[:, 0:nt], AF.Ln, r=[('gtmp',)], w=[('gtmp',)], bias=1.0)
                S.add('dve', lambda e, nt=nt: e.tensor_tensor_scan(out=Bx[:, 1:1 + nt], data0=ones32[:, 0:nt],
                                                                   data1=spb[:, 0:nt], initial=0.0,
                                                                   op0=ALU.mult, op1=ALU.add),
                      r=[('gtmp',), ('c', 'ones32')], w=[('Bx',)])
                d3 = Drel[:, 0:nt].rearrange('p (c l) -> p c l', l=64)
                d23 = Drel2[:, 0:nt].rearrange('p (c l) -> p c l', l=64)
                bx3 = Bx[:, 1:1 + nt].rearrange('p (c l) -> p c l', l=64)
                br3 = Bx[:, 0:nt].rearrange('p (c l) -> p c l', l=64)[:, :, 0:1]
                tt('dve', d3, bx3, br3.to_broadcast([128, nch, 64]), ALU.subtract, r=[('Bx',)], w=[('Drel',)])
                tt('dve', d23, d3, d3[:, :, 63:64].to_broadcast([128, nch, 64]), ALU.subtract, r=[('Drel',)],
                   w=[('Drel2',)])
                act(E1[:, 0:nt], Drel[:, 0:nt], AF.Exp, r=[('Drel',)], w=[('E1',)], scale=-1.0 / 16)
                act(dec[:, 0:nch], Drel[:, 63:nt:64], AF.Exp, r=[('Drel',)], w=[('dec',)], scale=-1.0 / 16)
                act(E2[:, 0:nt], Drel[:, 0:nt], AF.Exp, r=[('Drel',)], w=[('E2',)], scale=1.0 / 16)
                act(E3[:, 0:nt], Drel2[:, 0:nt], AF.Exp, r=[('Drel2',)], w=[('E3',)], scale=1.0 / 16)
                bq = nextbank()
                proj(bq, Wg, K, O_GQ + h * 128, 128, uT, ('uT',), tok0, nt)
                bk = nextbank()
                proj(bk, Wg, K, O_GK + h * 128, 128, uT, ('uT',), tok0, nt)
                ob = nextbank()
                chunk_engine(nt, h, ps(bq, 128, nt), [('ps', bq)], ps(bk, 128, nt), [('ps', bk)], 128 ** -0.5, 1.0,
                             vtok, 0, [St32], [Stbf], False, ob, None, B)
                act(sq[:, 0:nt], psum[:, ob, 0:nt], AF.Square, r=[('ps', ob)], w=[('sq',)])
                b = nextbank()
                mm(ps(b, 128, nt), onesbf[:, :], sq[:, 0:nt], True, True, r=[('c', 'onesbf'), ('sq',)], w=[('ps', b)])
                act(rstd[:, 0:nt], ps(b, 128, nt), AF.Sqrt, r=[('ps', b), ('c', 'eps')], w=[('E1',)],
                    bias=eps_t[:, :], scale=1.0 / 128)
                recip(rstd[:, 0:nt], rstd[:, 0:nt], r=[('E1',)], w=[('E1',)])
                tt('dve', t1[:, 0:nt], psum[:, ob, 0:nt], rstd[:, 0:nt], ALU.mult, r=[('ps', ob), ('E1',)],
                   w=[('E2',)])
                b = nextbank()
                proj(b, Wg, K, O_GR + h * 128, 128, uT, ('uT',), tok0, nt)
                act(sr[:, 0:nt], ps(b, 128, nt), AF.Silu, r=[('ps', b)], w=[('E3',)])
                stt('dve', ogT[:, h, 0:nt], sr[:, 0:nt], gln[:, 0:1], t1[:, 0:nt], ALU.mult, ALU.mult,
                    r=[('E3',), K, ('E2',)], w=[('ogT',)])
            if tok0 == DBGT and L == 0 and not dumped.get('gla'):
                dumped['gla'] = 1
                dump('uT0', uT[:, :, DBGT:DBGT + 512], [('uT',)])
                dump('ident', ident[:, :], [('c', 'ident')])
                dump('mask', maskut[:, :], [('c', 'mask')])
                dump('Bx', Bx[:, :], [('Bx',)])
                dump('Drel', Drel[:, :], [('Drel',)])
                dump('dec', dec[:, :], [('dec',)])
                dump('qd', qd[:, :], [('qd',)])
                dump('ki', ki[:, :], [('ki',)])
                dump('kd', kd[:, :], [('kd',)])
                dump('vtok', vtok[:, :, :], [('vtok',)])
                dump('ogT', ogT[:, :, :], [('ogT',)])
                dump('St', St32[:, :, :], [('st', id(St32), h) for h in range(4)])
                dump('lrT', lrT[:, :], [('lrT',)])
                dump('Wg', Wg[:, :, 0:64], [K])
                dump('E1', E1[:, :], [('E1',)])
                dump('E2', E2[:, :], [('E2',)])
                dump('E3', E3[:, :], [('E3',)])
                dump('sq', sq[:, :], [('sq',)])
                dump('gln', gln[:, :], [K])
            merge_out(tok0, nt, ogT, ('ogT',), Wpr, K, Wg, K, 2064, True, gsig, gtmp)
            if BAR:
                lbar([(Bx[0:1, 0:1], ('Bx',)), (Drel[0:1, 0:1], ('Drel',)), (Drel2[0:1, 0:1], ('Drel2',)),
                      (dec[0:1, 0:1], ('dec',)), (E1[0:1, 0:1], ('E1',)), (E2[0:1, 0:1], ('E2',)), (E3[0:1, 0:1], ('E3',)),
                      (gsig[0:1, 0:1], ('gsig',)), (gtmp[0:1, 0:1], ('gtmp',))])
            if tok0 == DBGT and L == 0 and not dumped.get('gla2'):
                dumped['gla2'] = 1
                dump('mg0', merged[:, :, DBGT:DBGT + 512], [('mg',)])

    def phase_ml(L):
        arena[0] = PH0
        Wm = aalloc('Wm', [128, 8, 3080], BF16)
        Wpr = aalloc('Wmp', [128, 4, D], BF16)
        B = common_bufs()
        qd, ki, kd, kdT, sT, E1, E2, E3, dec = B
        vtok = aalloc('vtok', [64, 8, 512], BF16)
        zc = aalloc('zc', [128, 515], F32)
        qh = aalloc('qh', [128, 512], BF16); kh = aalloc('kh', [128, 512], BF16)
        B8 = aalloc('B8', [8, 513], F32)
        T2 = aalloc('T2', [8, 512], F32); T3 = aalloc('T3', [8, 512], F32)
        bif = aalloc('bif', [8, 1], F32); cw = aalloc('cw', [128, 8, 4], F32); cb = aalloc('cb', [128, 8], F32)
        dabs = E1; t1 = E2; so = E3
        ogT = aalloc('hT', [128, 4, 512], BF16)
        gsig = aalloc('gsig', [128, 512], F32); gtmp = aalloc('gtmp', [128, 512], F32)
        cacc = gtmp; gi8 = gsig[0:8, :]
        K = ('W', 'm')
        load_w(Wm[:, :, 0:2056], K, 'w_in', L, 0, 8, O_MQ, 2056)
        load_w(Wm[:, :, 2056:3080], K, 'w_in', L, 0, 8, O_GATE + 2048, 1024)
        load_w(Wpr[:, :, :], K, 'ml_w_proj', L, 0, 4, 0, D)
        vec_load(bif[0:4, :], 'ml_b_i', L, [[1, 4], [1, 1]], K)
        vec_load(bif[4:8, :], 'ml_b_f', L, [[1, 4], [1, 1]], K)
        for j in range(4):
            vec_load(cw[:, :, j], 'ml_conv_w', L, [[1, 128], [128, 8]], K, off=j * 1024)
        vec_load(cb[:, :], 'ml_conv_b', L, [[1, 128], [128, 8]], K)
        memset('pool', B8[:, 0:1], 0.0, [('B8',)])
        for tok0 in range(0, SEGT, TILE):
            nt = TILE
            nch = nt // 64
            vproj(tok0, nt, Wm, K, 1024, vtok)
            b = nextbank()
            proj(b, Wm, K, 1536, 8, uT, ('uT',), tok0, nt)
            act(gi8[:, 0:nt], ps(b, 8, nt), AF.Identity, r=[('ps', b), K], w=[('gsig',)], bias=bif[:, :])
            act(T3[:, 0:nt], gi8[:, 0:nt], AF.Exp, r=[('gsig',)], w=[('T3',)], scale=-1.0)
            act(T3[:, 0:nt], T3[:, 0:nt], AF.Ln, r=[('T3',)], w=[('T3',)], bias=1.0)
            S.add('dve', lambda e, nt=nt: e.tensor_tensor_scan(out=B8[:, 1:1 + nt], data0=ones32[0:8, 0:nt],
                                                               data1=T3[:, 0:nt], initial=0.0, op0=ALU.mult,
                                                               op1=ALU.add),
                  r=[('T3',), ('c', 'ones32')], w=[('B8',)])
            d3 = T2[:, 0:nt].rearrange('p (c l) -> p c l', l=64)
            d23 = T3[:, 0:nt].rearrange('p (c l) -> p c l', l=64)
            bx3 = B8[:, 1:1 + nt].rearrange('p (c l) -> p c l', l=64)
            br3 = B8[:, 0:nt].rearrange('p (c l) -> p c l', l=64)[:, :, 0:1]
            tt('dve', d3, bx3, br3.to_broadcast([8, nch, 64]), ALU.subtract, r=[('B8',)], w=[('T2',)])
            tt('dve', d23, d3, d3[:, :, 63:64].to_broadcast([8, nch, 64]), ALU.subtract, r=[('T2',), ('B8',)],
               w=[('T3',)])
            cp('dve', T2[0:4, 0:nt], gi8[0:4, 0:nt], r=[('gsig',), ('T2',)], w=[('T2',)])
            cp('dve', T3[0:4, 0:nt], gi8[0:4, 0:nt], r=[('gsig',), ('T3',)], w=[('T3',)])
            for h in range(4):
                b1 = nextbank()
                mm(ps(b1, 128, nt), selF[:, h, :], T2[:, 0:nt], True, True, r=[('c', 'sel'), ('T2',)], w=[('ps', b1)])
                act(E1[:, 0:nt], ps(b1, 128, nt), AF.Exp, r=[('ps', b1)], w=[('E1',)], scale=-1.0)
                act(dec[:, 0:nch], psum[:, b1, 63:nt:64], AF.Exp, r=[('ps', b1)], w=[('dec',)], scale=-1.0)
                b2 = nextbank()
                mm(ps(b2, 128, nt), selIF[:, h, :], T2[:, 0:nt], True, True, r=[('c', 'sel'), ('T2',)], w=[('ps', b2)])
                act(E2[:, 0:nt], ps(b2, 128, nt), AF.Exp, r=[('ps', b2)], w=[('E2',)])
                b3 = nextbank()
                mm(ps(b3, 128, nt), selIF[:, h, :], T3[:, 0:nt], True, True, r=[('c', 'sel'), ('T3',)], w=[('ps', b3)])
                act(E3[:, 0:nt], ps(b3, 128, nt), AF.Exp, r=[('ps', b3)], w=[('E3',)])
                for which, dstq in ((0, qh), (1, kh)):
                    ct = which * 4 + h
                    b = nextbank()
                    proj(b, Wm, K, ct * 128, 128, uT, ('uT',), tok0, nt)
                    cp('act', zc[:, 3:3 + nt], ps(b, 128, nt), r=[('ps', b)], w=[('zc',)])
                    cp('pool', zc[:, 0:3], zch[:, ct, :], r=[('zch', ct)], w=[('zc',)])
                    ts('dve', cacc[:, 0:nt], zc[:, 0:nt], cw[:, ct, 0:1], None, ALU.mult, None, r=[('zc',), K],
                       w=[('gtmp',)])
                    for j in range(1, 4):
                        stt('dve', cacc[:, 0:nt], zc[:, j:j + nt], cw[:, ct, j:j + 1], cacc[:, 0:nt], ALU.mult, ALU.add,
                            r=[('zc',), K, ('gtmp',)], w=[('gtmp',)])
                    cp('pool', zch[:, ct, :], zc[:, nt:nt + 3], r=[('zc',)], w=[('zch', ct)])
                    act(dstq[:, 0:nt], cacc[:, 0:nt], AF.Silu, r=[('gtmp',), K], w=[('qk', which)],
                        bias=cb[:, ct:ct + 1])
                ob = nextbank()
                db = nextbank()
                chunk_engine(nt, h, qh[:, 0:nt], [('qk', 0)], kh[:, 0:nt], [('qk', 1)], 1.0, 128 ** -0.5, vtok, 0,
                             [Ct32, Nt32], [Ctbf, Ntbf], True, ob, db, B)
                ts('dve', dabs[:, 0:nt], psum[:, db, 0:nt], -1.0, 1.0, ALU.mult, ALU.max, r=[('ps', db)], w=[('E1',)])
                tt('dve', dabs[:, 0:nt], dabs[:, 0:nt], psum[:, db, 0:nt], ALU.max, r=[('ps', db), ('E1',)], w=[('E1',)])
                recip(dabs[:, 0:nt], dabs[:, 0:nt], r=[('E1',)], w=[('E1',)])
                tt('dve', t1[:, 0:nt], psum[:, ob, 0:nt], dabs[:, 0:nt], ALU.mult, r=[('ps', ob), ('E1',)], w=[('E2',)])
                b = nextbank()
                proj(b, Wm, K, 1544 + h * 128, 128, uT, ('uT',), tok0, nt)
                act(so[:, 0:nt], ps(b, 128, nt), AF.Sigmoid, r=[('ps', b)], w=[('E3',)])
                tt('dve', ogT[:, h, 0:nt], t1[:, 0:nt], so[:, 0:nt], ALU.mult, r=[('E2',), ('E3',)], w=[('ogT',)])
            merge_out(tok0, nt, ogT, ('ogT',), Wpr, K, Wm, K, 2056, False, gsig, gtmp)

    def phase_s5(L):
        arena[0] = PH0
        TB = 128
        NCH = TB // 8
        Ws = aalloc('Ws', [128, 8, 1536], BF16)
        Wglu = aalloc('Wglu', [128, 4, 512], BF16)
        Wsp = aalloc('Wsp', [128, 4, D], BF16)
        bglu = aalloc('bglu', [128, 4], F32)
        su128 = aalloc('su128', [128, 4, 512], BF16)
        sup = aalloc('sup', [32, 16, TB], BF16)
        GU = aalloc('GU', [128, 2, 16, TB], F32)
        XC = aalloc('XC', [128, 2, 16, NCH + 1], F32)
        XXbf = aalloc('XXbf', [128, 2, 16, TB], BF16)
        P1 = aalloc('P1', [128, 2, 16, NCH], F32); P2 = aalloc('P2', [128, 2, 16, NCH], F32)
        P1s = aalloc('P1s', [128, 2, 16], F32); P2s = aalloc('P2s', [128, 2, 16], F32)
        yp = aalloc('yp', [32, 16, TB], BF16)
        yv = aalloc('yv', [128, 512], F32)
        ga = aalloc('ga', [128, 512], F32)
        yg = aalloc('yg', [128, 4, 512], BF16)
        y2 = aalloc('y2', [128, 4, 512], BF16)
        gsig = aalloc('gsig', [128, 512], F32); gtmp = aalloc('gtmp', [128, 512], F32)
        gb = gtmp
        K = ('W', 's')
        KC = ('s5c', L)
        load_w(Ws[:, :, 0:512], K, 'w_in', L, 0, 8, O_SU, 512)
        load_w(Ws[:, :, 512:1536], K, 'w_in', L, 0, 8, O_GATE + 1024, 1024)
        load_w(Wglu[:, :, :], K, 's5_w_glu', L, 0, 4, 0, 512)
        load_w(Wsp[:, :, :], K, 's5_w_proj', L, 0, 4, 0, D)
        vec_load(bglu[:, :], 's5_b_glu', L, [[1, 128], [128, 4]], K)
        cp('dve', XC[:, :, :, 0], XXst[:, :, :], r=[('xxst',)], w=[('XC',)])
        GU5 = GU[:, :, :, :].rearrange('p r g (n l) -> p r g n l', l=8)

        def G(tau):
            return GU5[:, :, :, :, tau]

        def cmul_acc(dst, src, k, dkey, skey):
            prb = PW[:, k, 0, :].unsqueeze(1).unsqueeze(3).to_broadcast([128, 2, 16, NCH])
            pib = PW[:, k, 1, :].unsqueeze(2).to_broadcast([128, 16, NCH])
            npib = nPWi[:, k, :].unsqueeze(2).to_broadcast([128, 16, NCH])
            tt('dve', P1[:, :, :, :], src, prb, ALU.mult, r=[skey, KC], w=[('P1',)])
            tt('pool', P2[:, 0, :, :], src[:, 1], npib, ALU.mult, r=[skey, KC], w=[('P2',)])
            tt('pool', P2[:, 1, :, :], src[:, 0], pib, ALU.mult, r=[skey, KC], w=[('P2',)])
            tt('dve', dst, dst, P1[:, :, :, :], ALU.add, r=[dkey, ('P1',)], w=[dkey])
            tt('dve', dst, dst, P2[:, :, :, :], ALU.add, r=[dkey, ('P2',)], w=[dkey])

        for tok0 in range(0, SEGT, TILE):
            nt = TILE
            for blk in range(4):
                b = nextbank()
                proj(b, Ws, K, blk * 128, 128, uT, ('uT',), tok0, nt)
                cp('act', su128[:, blk, 0:nt], ps(b, 128, nt), r=[('ps', b)], w=[('su128',)])
            for tb in range(0, nt, TB):
                for blk in range(4):
                    b = nextbank()
                    for k in range(4):
                        mm(psum[0:32, b, k * TB:(k + 1) * TB], ident[:, 32 * k:32 * k + 32],
                           su128[:, blk, tb:tb + TB], True, True, r=[('c', 'ident'), ('su128',)], w=[('ps', b)])
                    cp('act', sup[:, blk * 4:(blk + 1) * 4, :], ps(b, 32, 4 * TB).rearrange('p (k t) -> p k t', k=4),
                       r=[('ps', b)], w=[('sup',)])
                for ri in range(2):
                    for g in range(4):
                        b = nextbank()
                        for k in range(4):
                            pr = g * 4 + k
                            mm(psum[:, b, k * TB:(k + 1) * TB], BBT[:, pr, ri, :], sup[:, pr, :], True, True,
                               r=[KC, ('sup',)], w=[('ps', b)])
                        cp('act', GU[:, ri, g * 4:(g + 1) * 4, :], ps(b, 128, 4 * TB).rearrange('p (k t) -> p k t', k=4),
                           r=[('ps', b)], w=[('GU',)])
                for tau in range(1, 8):
                    cmul_acc(G(tau), G(tau - 1), 0, ('GU',), ('GU',))
                for n in range(NCH):
                    tt('dve', P1s[:, :, :], XC[:, :, :, n], AA8[:, :, :], ALU.mult, r=[('XC',), KC], w=[('P1s',)])
                    tt('pool', P2s[:, 0, :], XC[:, 1, :, n], nPWi[:, 7, :], ALU.mult, r=[('XC',), KC], w=[('P2s',)])
                    tt('pool', P2s[:, 1, :], XC[:, 0, :, n], PW[:, 7, 1, :], ALU.mult, r=[('XC',), KC], w=[('P2s',)])
                    tt('dve', P1s[:, :, :], P1s[:, :, :], P2s[:, :, :], ALU.add, r=[('P1s',), ('P2s',)], w=[('P1s',)])
                    tt('dve', XC[:, :, :, n + 1], P1s[:, :, :], GU5[:, :, :, n, 7], ALU.add, r=[('P1s',), ('GU',)],
                       w=[('XC',)])
                for tau in range(7):
                    cmul_acc(G(tau), XC[:, :, :, 0:NCH], tau, ('GU',), ('XC',))
                cp('dve', G(7), XC[:, :, :, 1:NCH + 1], r=[('XC',)], w=[('GU',)])
                cp('act', XXbf[:, :, :, :], GU[:, :, :, :], r=[('GU',)], w=[('XXbf',)])
                cp('dve', XC[:, :, :, 0], XC[:, :, :, NCH], r=[('XC',)], w=[('XC',)])
                for g in range(4):
                    b = nextbank()
                    for k in range(4):
                        pr = g * 4 + k
                        mm(psum[0:32, b, k * TB:(k + 1) * TB], Ec[:, 0, pr, :], XXbf[:, 0, pr, :], True, False,
                           r=[KC, ('XXbf',)], w=[('ps', b)])
                        mm(psum[0:32, b, k * TB:(k + 1) * TB], Ec[:, 1, pr, :], XXbf[:, 1, pr, :], False, True,
                           r=[KC, ('XXbf',)], w=[('ps', b)])
                    for k in range(4):
                        pr = g * 4 + k
                        stt('dve', yp[:, pr, :], sup[:, pr, :], Dd[:, pr:pr + 1], psum[0:32, b, k * TB:(k + 1) * TB],
                            ALU.mult, ALU.add, r=[('sup',), KC, ('ps', b)], w=[('yp',)])
                b = nextbank()
                for blk in range(4):
                    for k in range(4):
                        mm(psum[:, b, blk * TB:(blk + 1) * TB], gsel[:, k, :], yp[:, blk * 4 + k, :], k == 0, k == 3,
                           r=[('c', 'gsel'), ('yp',)], w=[('ps', b)])
                cp('act', yv[:, :], ps(b, 128, 4 * TB), r=[('ps', b)], w=[('yv',)])
                act(ga[:, :], yv[:, :], AF.Square, r=[('yv',)], w=[('ga',)])
                ts('dve', ga[:, :], ga[:, :], 0.044715, 1.0, ALU.mult, ALU.add, r=[('ga',)], w=[('ga',)])
                tt('dve', ga[:, :], ga[:, :], yv[:, :], ALU.mult, r=[('ga',), ('yv',)], w=[('ga',)])
                act(gb[:, :], ga[:, :], AF.Sigmoid, r=[('ga',)], w=[('gtmp',)], scale=GC)
                tt('pool', yg[:, :, tb:tb + TB], gb[:, :].rearrange('p (k t) -> p k t', k=4),
                   yv[:, :].rearrange('p (k t) -> p k t', k=4), ALU.mult, r=[('gtmp',), ('yv',)], w=[('yg',)])
            for co in range(4):
                b = nextbank()
                for k in range(4):
                    mm(ps(b, 128, nt), Wglu[:, k, co * 128:(co + 1) * 128], yg[:, k, 0:nt], k == 0, k == 3,
                       r=[K, ('yg',)], w=[('ps', b)])
                act(gsig[:, 0:nt], ps(b, 128, nt), AF.Sigmoid, r=[('ps', b), K], w=[('gsig',)], bias=bglu[:, co:co + 1])
                tt('dve', y2[:, co, 0:nt], yg[:, co, 0:nt], gsig[:, 0:nt], ALU.mult, r=[('yg',), ('gsig',)],
                   w=[('y2',)])
            merge_out(tok0, nt, y2, ('y2',), Wsp, K, Ws, K, 512, False, gsig, gtmp)
        cp('dve', XXst[:, :, :], XC[:, :, :, 0], r=[('XC',)], w=[('xxst',)])

    def phase_out(L, src_t, seg):
        arena[0] = PH0
        Wo = aalloc('Wo', [128, 8, D], BF16)
        t1 = aalloc('t1o', [128, D], F32)
        K = ('W', 'o')
        load_w(Wo[:, :, :], K, 'w_out', L, 0, 8, 0, D)
        if L == 0 and seg == 0:
            dump('mgall', merged[:, :, :], [('mg',)])
        for bi, tb in enumerate(range(0, SEGT, 128)):
            xi = bi % 2
            xb = xt[xi]
            dma(xb[:, :], rawap(src_t, (seg * SEGT + tb) * D, [[D, 128], [1, D]]), w=[('xt', 0)])
            banks = []
            for hf in range(2):
                b = nextbank()
                banks.append(b)
                for k in range(8):
                    mm(ps(b, 128, 512), merged[:, k, tb:tb + 128], Wo[:, k, hf * 512:(hf + 1) * 512], k == 0, k == 7,
                       r=[('mg',), K], w=[('ps', b)])
            rowsum_rstd(banks, 128, 8)
            for hf, b in enumerate(banks):
                tt('dve', t1[:, hf * 512:(hf + 1) * 512], ps(b, 128, 512), gains[1][:, hf * 512:(hf + 1) * 512], ALU.mult,
                   r=[('ps', b), ('c', 'gain')], w=[('t1o', hf)])
                stt('dve', xb[:, hf * 512:(hf + 1) * 512], t1[:, hf * 512:(hf + 1) * 512], small[:, 8:9],
                    xb[:, hf * 512:(hf + 1) * 512], ALU.mult, ALU.add, r=[('t1o', hf), ('rsx',), ('xt', 0)],
                    w=[('xt', 0)])
            if L == 0 and seg == 0 and tb == 128:
                dump('osmall', small[:, 0:16], [('rsx',), ('ssx', 0), ('ssx', 1), ('ssx', 2), ('ssx', 3)])
                dump('ot1', t1[:, :], [('t1o', 0), ('t1o', 1)])
                dump('ojunk', junk[:, :], [('junk',)])
            dma(rawap(xmid, tb * D, [[D, 128], [1, D]]), xb[:, :], r=[('xt', 0)], w=[('xmid', tb)])
            if L == 0 and seg == 0:
                dump('xmid%d' % tb, xb[:, :], [('xt', 0)])

    hs = nc.dram_tensor('hs', [22, 128, SEGT], BF16)

    def phase_ffn_a(L, seg):
        arena[0] = UT0
        Wup = aalloc('Wup', [128, 8, DFF], BF16)
        Wga = aalloc('Wga', [128, 8, DFF], BF16)
        NTF = 512
        uF = [aalloc('uF', [128, 8, NTF], BF16) for _ in range(2)]
        NB = 3
        fsets = [(aalloc('ab', [128, NTF + 2], F32), aalloc('vv', [128, NTF], F32), aalloc('gaF', [128, NTF], F32),
                  aalloc('vg', [128, NTF], F32), aalloc('ho', [128, NTF], BF16)) for _ in range(NB)]
        fw = aalloc('fw', [128, 22, 3], F32); fb = aalloc('fb', [128, 22], F32)
        K = ('W', 'f')
        load_w(Wup[:, :, :], K, 'ffn_w_up', L, 0, 8, 0, DFF)
        load_w(Wga[:, :, :], K, 'ffn_w_gate', L, 0, 8, 0, DFF)
        for j in range(3):
            vec_load(fw[:, :, j], 'ffn_conv_w', L, [[1, 128], [128, 22]], K, off=j * DFF)
        vec_load(fb[:, :], 'ffn_conv_b', L, [[1, 128], [128, 22]], K)
        dma(gains2[0][:, :], rawap(W['norm_ffn_pre'], L * D, [[0, 128], [1, D]]), w=[('c', 'gain')])
        gi = 0
        for ti, t0 in enumerate(range(0, SEGT, NTF)):
            uFt = uF[ti % 2]
            ukey = ('uF', ti % 2)
            for j in range(NTF // 128):
                tb = t0 + j * 128
                norm_block(rawap(xmid, tb * D, [[D, 128], [1, D]]), 128, gains[2], ('c', 'gain'), uFt, ukey, j * 128,
                           0, extra_r=[('xmid', tb)])
            for ct in range(22):
                si = gi % NB
                gi += 1
                ab, vv, ga, vg, ho = fsets[si]
                b = nextbank()
                proj(b, Wup, K, ct * 128, 128, uFt, ukey, 0, NTF)
                cp('act', ab[:, 2:2 + NTF], ps(b, 128, NTF), r=[('ps', b)], w=[('ab', si)])
                cp('pool', ab[:, 0:2], ahist[:, ct, :], r=[('ah', ct)], w=[('ab', si)])
                ts('dve', vv[:, :], ab[:, 0:NTF], fw[:, ct, 0:1], fb[:, ct:ct + 1], ALU.mult, ALU.add,
                   r=[('ab', si), K], w=[('vv', si)])
                for j in range(1, 3):
                    stt('dve', vv[:, :], ab[:, j:j + NTF], fw[:, ct, j:j + 1], vv[:, :], ALU.mult, ALU.add,
                        r=[('ab', si), K, ('vv', si)], w=[('vv', si)])
                cp('pool', ahist[:, ct, :], ab[:, NTF:NTF + 2], r=[('ab', si)], w=[('ah', ct)])
                b2 = nextbank()
                proj(b2, Wga, K, ct * 128, 128, uFt, ukey, 0, NTF)
                tt('dve', vg[:, :], vv[:, :], ps(b2, 128, NTF), ALU.mult, r=[('vv', si), ('ps', b2)], w=[('vg', si)])
                act(ga[:, :], vv[:, :], AF.Square, r=[('vv', si)], w=[('ga', si)], scale=math.sqrt(0.044715))
                stt('dve', ga[:, :], ga[:, :], 1.0, vv[:, :], ALU.add, ALU.mult, r=[('ga', si), ('vv', si)],
                    w=[('ga', si)])
                act(ga[:, :], ga[:, :], AF.Sigmoid, r=[('ga', si)], w=[('ga', si)], scale=GC)
                tt('pool', ho[:, :], ga[:, :], vg[:, :], ALU.mult, r=[('ga', si), ('vg', si)], w=[('ho', si)])
                dma(rawap(hs, ct * 128 * SEGT + t0, [[SEGT, 128], [1, NTF]]), ho[:, :], r=[('ho', si)],
                    w=[('hs', ct, t0)])

    def phase_ffn_b(L, dst_t, seg):
        arena[0] = UT0
        Wd = aalloc('Wd', [128, 22, D], BF16)
        hb = [aalloc('hb', [128, 22, 512], BF16) for _ in range(2)]
        t1 = aalloc('t1f', [128, D], F32)
        xb2 = [aalloc('xb2', [128, D], F32) for _ in range(2)]
        K = ('W', 'fd')
        load_w(Wd[:, :, :], K, 'ffn_w_down', L, 0, 22, 0, D)
        dma(gains2[1][:, :], rawap(W['norm_ffn_post'], L * D, [[0, 128], [1, D]]), w=[('c', 'gain')])
        bi = 0
        for ti, t0 in enumerate(range(0, SEGT, 512)):
            hbt = hb[ti % 2]
            hkey = ('hb', ti % 2)
            for c0 in (0, 11):
                dma(hbt[:, c0:c0 + 11, :], rawap(hs, c0 * 128 * SEGT + t0, [[SEGT, 128], [128 * SEGT, 11], [1, 512]]),
                    r=[('hs', ct, t0) for ct in range(c0, c0 + 11)], w=[hkey])
            for j in range(4):
                tb = t0 + j * 128
                xi = bi % 2
                bi += 1
                xb = xb2[xi]
                dma(xb[:, :], rawap(xmid, tb * D, [[D, 128], [1, D]]), r=[('xmid', tb)], w=[('xb2', xi)])
                banks = []
                for hf in range(2):
                    b = nextbank()
                    banks.append(b)
                    for k in range(22):
                        mm(ps(b, 128, 512), hbt[:, k, j * 128:(j + 1) * 128], Wd[:, k, hf * 512:(hf + 1) * 512], k == 0,
                           k == 21, r=[hkey, K], w=[('ps', b)])
                rowsum_rstd(banks, 128, 9)
                for hf, b in enumerate(banks):
                    tt('dve', t1[:, hf * 512:(hf + 1) * 512], ps(b, 128, 512), gains[3][:, hf * 512:(hf + 1) * 512],
                       ALU.mult, r=[('ps', b), ('c', 'gain')], w=[('t1f', hf)])
                    stt('dve', xb[:, hf * 512:(hf + 1) * 512], t1[:, hf * 512:(hf + 1) * 512], small[:, 9:10],
                        xb[:, hf * 512:(hf + 1) * 512], ALU.mult, ALU.add, r=[('t1f', hf), ('rsx',), ('xb2', xi)],
                        w=[('xb2', xi)])
                dma(rawap(dst_t, (seg * SEGT + tb) * D, [[D, 128], [1, D]]), xb[:, :], r=[('xb2', xi)],
                    w=[('dst', seg, tb)])

    for L in range(nlayer):
        src_t = x_in if L == 0 else x1
        dst_t = out_t if L == nlayer - 1 else x1
        for stt_ in (St32, Ct32, Nt32):
            memset('pool', stt_[:, :, :], 0.0, [('st', id(stt_), h) for h in range(4)])
        for bf_, s32 in ((Stbf, St32), (Ctbf, Ct32), (Ntbf, Nt32)):
            for h in range(4):
                cp('pool', bf_[:, h, :], s32[:, h, :], r=[('st', id(s32), h)], w=[('stbf', id(bf_), h)])
        memset('pool', XXst[:, :, :], 0.0, [('xxst',)])
        for ct in range(8):
            memset('pool', zch[:, ct, :], 0.0, [('zch', ct)])
        for ct in range(22):
            memset('pool', ahist[:, ct, :], 0.0, [('ah', ct)])
        S.barrier()
        s5_setup(L)
        for seg in range(nseg):
            S.barrier()
            dma(gains2[0][:, :], rawap(W['norm_mix_pre'], L * D, [[0, 128], [1, D]]), w=[('c', 'gain')])
            dma(gains2[1][:, :], rawap(W['norm_mix_post'], L * D, [[0, 128], [1, D]]), w=[('c', 'gain')])
            for bi, tb in enumerate(range(0, SEGT, 128)):
                extra = [('dst', seg, tb)] if L > 0 else []
                norm_block(rawap(src_t, (seg * SEGT + tb) * D, [[D, 128], [1, D]]), 128, gains[0], ('c', 'gain'), uT,
                           ('uT',), tb, bi % 2, extra_r=extra)
            S.barrier()
            if 'gla' not in skip:
                phase_gla(L)
                S.barrier()
            if 'ml' not in skip:
                phase_ml(L)
                S.barrier()
            if 's5' not in skip:
                phase_s5(L)
                S.barrier()
            if 'out' not in skip:
                phase_out(L, src_t, seg)
                S.barrier()
            if 'ffn' not in skip:
                phase_ffn_a(L, seg)
                S.barrier()
                phase_ffn_b(L, dst_t, seg)
                S.barrier()

    with nc.semaphore('e_pe') as s0, nc.semaphore('e_act') as s1, nc.semaphore('e_dve') as s2, \
            nc.semaphore('e_pool') as s3, nc.semaphore('e_sp') as s4:
        esem = {'pe': s0, 'act': s1, 'dve': s2, 'pool': s3, 'sp': s4}
        import contextlib
        with contextlib.ExitStack() as es:
            ssem = [es.enter_context(nc.semaphore('dslot%d' % i)) for i in range(S.nslots)]
            with nc.Block() as block:
                S.emit(nc, block, esem, ssem)
    build.dbg_names = dbg_names
    build.amax = amax
    build.ph0 = PH0
    return nc


def kernel(**inputs):
    nc = build()
    m = {'x': np.ascontiguousarray(inputs['x'].reshape(SEQ, D), dtype=np.float32)}
    for n in WNAMES:
        m[n] = np.ascontiguousarray(inputs[n], dtype=np.float32)
    res = run_bass_kernel_spmd(nc, [m], core_ids=[0])
    return res.results[0]['out'].reshape(1, SEQ, D).astype(np.float32)
```

```python
import math
import numpy as np
import concourse.bass as bass
import concourse.mybir as mybir
from concourse.bass_utils import run_bass_kernel_spmd

F32 = mybir.dt.float32
BF16 = mybir.dt.bfloat16
ALU = mybir.AluOpType
AF = mybir.ActivationFunctionType

NCORE = 8
D = 1024
DIN = 7704
DFF = 2816
SEQ = 16384
OWN = SEQ // NCORE
HALO = 64
PRE = 3
NT = OWN + HALO
NW = NT + PRE
EPS = 1e-6
O_GQ, O_GK, O_GV, O_LR, O_GR, O_SU, O_MQ, O_MK, O_MV, O_MI, O_MF, O_MO, O_GATE = (
    0, 512, 1024, 1536, 1552, 2064, 2576, 3088, 3600, 4112, 4116, 4120, 4632)
TILES = [(0, 64), (64, 512), (576, 512), (1088, 512), (1600, 512)]
SNAP_TOK = OWN
LS5 = 8
GC = 1.5957691216057308

WNAMES = ['norm_mix_pre', 'norm_mix_post', 'norm_ffn_pre', 'norm_ffn_post', 'w_in',
          'gla_w_gk', 'gla_b_gk', 'gla_norm', 'gla_w_proj',
          's5_a_re', 's5_a_im', 's5_log_dt', 's5_b_re', 's5_b_im', 's5_c_re', 's5_c_im', 's5_d',
          's5_w_glu', 's5_b_glu', 's5_w_proj', 'ml_conv_w', 'ml_conv_b', 'ml_b_i', 'ml_b_f',
          'ml_w_proj', 'w_out', 'ffn_w_up', 'ffn_w_gate', 'ffn_conv_w', 'ffn_conv_b', 'ffn_w_down']
WSHAPES = {'norm_mix_pre': [D], 'norm_mix_post': [D], 'norm_ffn_pre': [D], 'norm_ffn_post': [D],
           'w_in': [D, DIN], 'gla_w_gk': [16, 512], 'gla_b_gk': [512], 'gla_norm': [128],
           'gla_w_proj': [512, D], 's5_a_re': [32, 64], 's5_a_im': [32, 64], 's5_log_dt': [32],
           's5_b_re': [32, 64, 16], 's5_b_im': [32, 64, 16], 's5_c_re': [32, 16, 64],
           's5_c_im': [32, 16, 64], 's5_d': [32, 16], 's5_w_glu': [512, 512], 's5_b_glu': [512],
           's5_w_proj': [512, D], 'ml_conv_w': [4, D], 'ml_conv_b': [D], 'ml_b_i': [4], 'ml_b_f': [4],
           'ml_w_proj': [512, D], 'w_out': [D, D], 'ffn_w_up': [D, DFF], 'ffn_w_gate': [D, DFF],
           'ffn_conv_w': [3, DFF], 'ffn_conv_b': [DFF], 'ffn_w_down': [DFF, D]}


class Sched:
    ENGS = ('pe', 'act', 'dve', 'pool', 'sp')
    SELF_SYNC = ('act', 'dve', 'pool')

    def __init__(self, nslots=16):
        self.streams = {e: [] for e in self.ENGS}
        self.ops = []
        self.lastw = {}
        self.rd = {}
        self.nslots = nslots
        self.slot_cnt = [0] * nslots
        self.slot_last = [None] * nslots
        self.dma_n = 0

    def add(self, eng, fn, r=(), w=(), dma=False):
        oid = len(self.ops)
        deps = set()
        raw = set()
        for k in r:
            p = self.lastw.get(k)
            if p is not None:
                deps.add(p)
                raw.add(p)
        for k in w:
            p = self.lastw.get(k)
            if p is not None:
                deps.add(p)
            rr = self.rd.get(k)
            if rr:
                deps.update(rr[0].values())
                deps.update(rr[1])
        o = {'id': oid, 'eng': eng, 'fn': fn, 'dma': dma, 'flag': False, 'raw': raw}
        if dma:
            s = self.dma_n % self.nslots
            self.dma_n += 1
            if self.slot_last[s] is not None:
                deps.add(self.slot_last[s])
            self.slot_cnt[s] += 1
            o['slot'] = s
            o['slotval'] = 16 * self.slot_cnt[s]
            self.slot_last[s] = oid
        for k in w:
            self.lastw[k] = oid
            self.rd[k] = ({}, [])
        for k in r:
            rr = self.rd.setdefault(k, ({}, []))
            if dma:
                rr[1].append(oid)
            else:
                rr[0][eng] = oid
        deps.discard(oid)
        o['deps'] = deps
        self.ops.append(o)
        self.streams[eng].append(o)
        return o

    def barrier(self):
        last = {}
        for e in self.ENGS:
            st = self.streams[e]
            for o in reversed(st):
                if o['fn'] is not None and not o['dma']:
                    last[e] = o['id']
                    break
        dmas = [x for x in self.slot_last if x is not None]
        for e in self.ENGS:
            oid = len(self.ops)
            deps = set(v for k, v in last.items() if k != e) | set(dmas)
            o = {'id': oid, 'eng': e, 'fn': None, 'dma': False, 'flag': False, 'deps': deps}
            self.ops.append(o)
            self.streams[e].append(o)

    def emit(self, nc, block, esem, ssem):
        ops = self.ops
        for o in ops:
            for d in o['deps']:
                dd = ops[d]
                if dd['dma']:
                    continue
                if dd['eng'] != o['eng'] or (d in o.get('raw', ()) and o['eng'] in self.SELF_SYNC):
                    dd['flag'] = True
        for e in self.ENGS:
            c = 0
            for o in self.streams[e]:
                if o['flag'] and not o['dma']:
                    c += 1
                    o['fidx'] = c
        sched = self

        def run(ename, eng):
            waited = {}
            for o in sched.streams[ename]:
                need = {}
                for d in o['deps']:
                    dd = ops[d]
                    if dd['dma']:
                        key = ('s', dd['slot'])
                        val = dd['slotval']
                    elif dd['eng'] == ename and not (d in o.get('raw', ()) and ename in sched.SELF_SYNC):
                        continue
                    else:
                        key = ('e', dd['eng'])
                        val = dd['fidx']
                    if val > waited.get(key, 0) and val > need.get(key, 0):
                        need[key] = val
                for key, val in need.items():
                    sem = ssem[key[1]] if key[0] == 's' else esem[key[1]]
                    eng.wait_ge(sem, val)
                    waited[key] = val
                if o['fn'] is None:
                    continue
                ins = o['fn'](eng)
                if o['dma']:
                    ins.then_inc(ssem[o['slot']], 16)
                elif o['flag']:
                    ins.then_inc(esem[ename], 1)
            if ename == 'sp':
                for s in range(sched.nslots):
                    if sched.slot_cnt[s]:
                        eng.wait_ge(ssem[s], 16 * sched.slot_cnt[s])

        @block.tensor
        def _(e):
            run('pe', e)

        @block.scalar
        def _(e):
            run('act', e)

        @block.vector
        def _(e):
            run('dve', e)

        @block.gpsimd
        def _(e):
            run('pool', e)

        @block.sync
        def _(e):
            run('sp', e)


def rawap(t, offset, pat):
    return bass.AP(tensor=t, offset=offset, ap=[list(p) for p in pat])


SEGT = 2048
DBGT = 512
BAR = False
P2ENG = 'dve'
TILE = 512
PI = math.pi


def build(nseg=8, nlayer=2, dbg=False, skip=()):
    nc = bass.Bass("TRN2", target_bir_lowering=False)
    S = Sched()
    ntok = nseg * SEGT
    x_in = nc.dram_tensor('x', [ntok, D], F32, kind='ExternalInput')
    W = {n: nc.dram_tensor(n, [2] + WSHAPES[n], F32, kind='ExternalInput') for n in WNAMES}
    WSZ = {n: int(np.prod(WSHAPES[n])) for n in WNAMES}
    out_t = nc.dram_tensor('out', [ntok, D], F32, kind='ExternalOutput')
    x1 = nc.dram_tensor('x1', [ntok, D], F32)
    xmid = nc.dram_tensor('xmid', [SEGT, D], F32)

    sb_top = [16512]
    SB_END = 229376

    def alloc(name, shape, dt, at=None):
        nbytes = int(np.prod(shape[1:])) * (4 if dt == F32 else 2)
        nbytes = (nbytes + 63) // 64 * 64
        if at is None:
            off = sb_top[0]
            sb_top[0] += nbytes
        else:
            off = at
        assert off + nbytes <= SB_END, (name, off, nbytes)
        return nc.alloc_sbuf_tensor_at(name, list(shape), dt, offset=off)

    psum = nc.alloc_psum_tensor('psum', [128, 8, 512], F32)

    def ps(b, p=128, n=512):
        return psum[0:p, b, 0:n]

    bank_rr = [0]

    def nextbank():
        b = bank_rr[0] % 5
        bank_rr[0] += 1
        return b

    ident = alloc('ident', [128, 128], BF16)
    onesbf = alloc('onesbf', [128, 128], BF16)
    ones32 = alloc('ones32', [128, 512], F32)
    maskut = alloc('maskut', [64, 64], F32)
    eps_t = alloc('eps_t', [128, 1], F32)
    negpi = alloc('negpi', [128, 1], F32)
    gains2 = [alloc('gain%d' % i, [128, D], F32) for i in range(2)]
    gains = [gains2[0], gains2[1], gains2[0], gains2[1]]
    xt0 = alloc('xt0', [128, D], F32)
    xt = [xt0, xt0]
    un = alloc('un', [128, D], BF16)
    junk = alloc('junk', [128, D], BF16)
    small = alloc('small', [128, 64], F32)
    stage0 = alloc('stage0', [128, 512], F32)
    stage = [stage0, stage0]
    St32 = alloc('St32', [128, 4, 128], F32)
    Ct32 = alloc('Ct32', [128, 4, 128], F32)
    Nt32 = alloc('Nt32', [128, 4, 128], F32)
    Stbf = alloc('Stbf', [128, 4, 128], BF16)
    Ctbf = alloc('Ctbf', [128, 4, 128], BF16)
    Ntbf = alloc('Ntbf', [128, 4, 128], BF16)
    XXst = alloc('XXst', [128, 2, 16], F32)
    zch = alloc('zch', [128, 8, 3], F32)
    ahist = alloc('ahist', [128, 22, 2], F32)
    BBT = alloc('BBT', [32, 16, 2, 128], BF16)
    Ec = alloc('Ec', [128, 2, 16, 32], BF16)
    AA1 = alloc('AA1', [128, 2, 16], F32)
    A1i = alloc('A1i', [128, 16], F32)
    nA1i = alloc('nA1i', [128, 16], F32)
    Dd = alloc('Dd', [32, 16], F32)
    PW = alloc('PW', [128, 8, 2, 16], F32)
    nPWi = alloc('nPWi', [128, 8, 16], F32)
    AA8 = alloc('AA8', [128, 2, 16], F32)
    gsel = alloc('gsel', [32, 4, 128], BF16)
    selF = alloc('selF', [8, 4, 128], F32)
    selIF = alloc('selIF', [8, 4, 128], F32)
    UT0 = sb_top[0]
    uT = alloc('uT', [128, 8, SEGT], BF16)
    merged = alloc('merged', [128, 8, SEGT], BF16)
    PH0 = sb_top[0]
    arena = [PH0]
    an = [0]

    amax = {}

    def aalloc(name, shape, dt):
        an[0] += 1
        t = alloc('%s_%d' % (name, an[0]), shape, dt, at=arena[0])
        nb = int(np.prod(shape[1:])) * (4 if dt == F32 else 2)
        arena[0] += (nb + 63) // 64 * 64
        amax['top'] = max(amax.get('top', 0), arena[0])
        return t

    cnt = {'stage': 0}
    dumped = {}

    def dma(out, in_, r=(), w=(), slow=False):
        if slow:
            return S.add('sp', lambda e: e.dma_start(out=out, in_=in_, allow_slow_non_contiguous=True),
                         r=r, w=w, dma=True)
        return S.add('sp', lambda e: e.dma_start(out=out, in_=in_), r=r, w=w, dma=True)

    def act(out, in_, func, r, w, bias=None, scale=None, accum=None):
        kw = {}
        if bias is not None:
            kw['bias'] = bias
        if scale is not None:
            kw['scale'] = scale
        if accum is not None:
            kw['accum_out'] = accum
        return S.add('act', lambda e: e.activation(out=out, in_=in_, func=func, **kw), r=r, w=w)

    def tt(eng, out, in0, in1, op, r, w):
        return S.add(eng, lambda e: e.tensor_tensor(out=out, in0=in0, in1=in1, op=op), r=r, w=w)

    def ts(eng, out, in0, s1, s2, op0, op1, r, w):
        if s2 is None:
            return S.add(eng, lambda e: e.tensor_single_scalar(out=out, in_=in0, scalar=s1, op=op0), r=r, w=w)
        return S.add(eng, lambda e: e.tensor_scalar(out=out, in0=in0, scalar1=s1, scalar2=s2, op0=op0, op1=op1),
                     r=r, w=w)

    def stt(eng, out, in0, sc, in1, op0, op1, r, w):
        return S.add(eng, lambda e: e.scalar_tensor_tensor(out=out, in0=in0, scalar=sc, in1=in1, op0=op0, op1=op1),
                     r=r, w=w)

    def cp(eng, out, in_, r, w):
        if eng == 'act':
            return S.add('act', lambda e: e.copy(out=out, in_=in_), r=r, w=w)
        return S.add(eng, lambda e: e.tensor_copy(out=out, in_=in_), r=r, w=w)

    def mm(out, lhsT, rhs, start, stop, r, w):
        return S.add('pe', lambda e: e.matmul(out, lhsT, rhs, start=start, stop=stop, skip_group_check=True),
                     r=r, w=w)

    def tr(out, in_, idn, r, w):
        return S.add('pe', lambda e: e.transpose(out, in_, idn), r=r, w=w)

    def memset(eng, ap, val, w):
        return S.add(eng, lambda e: e.memset(ap, val), r=(), w=w)

    dbg_names = []

    def dump(name, ap, rkeys):
        if not dbg:
            return
        t = nc.dram_tensor('dbg_' + name, list(ap.shape), ap.dtype, kind='ExternalOutput')
        dbg_names.append('dbg_' + name)
        full = t.ap()
        dma(full, ap, r=rkeys, w=[('dbg', name)])

    scr = nc.dram_tensor('scr_bar', [64, 64], F32)
    barn = [0]

    def lbar(items):
        for ap_, key in items:
            i = barn[0] % 64
            barn[0] += 1
            if ap_.dtype != F32:
                continue
            dma(scr.ap()[i:i + 1, 0:1], ap_, r=[key], w=[('scr', i)])

    def recip(out, in_, r, w):
        return S.add('dve', lambda e: e.reciprocal(out=out, in_=in_), r=r, w=w)

    def load_w(dst, key, name, L, row0, kt, col0, ncols, rows=128):
        src_t = W[name]
        ld = WSHAPES[name][1] if len(WSHAPES[name]) > 1 else 1
        base = L * WSZ[name]
        kstep = max(1, 8 if kt <= 8 else 11)
        for k0 in range(0, kt, kstep):
            kn = min(kstep, kt - k0)
            src = rawap(src_t, base + (row0 + k0 * 128) * ld + col0, [[ld, rows], [128 * ld, kn], [1, ncols]])
            dd = dst[:, k0:k0 + kn, :]
            S.add('pool', lambda e, dd=dd, src=src: e.dma_start(out=dd, in_=src), r=(), w=[key], dma=True)

    class WK:
        def __init__(self, base, chunk=512):
            self.base = base
            self.chunk = chunk

        def keys(self, c0, n):
            return [(self.base, j) for j in range(c0 // self.chunk, (c0 + n - 1) // self.chunk + 1)]

    def wkeys(wkey, c0, n):
        return wkey.keys(c0, n) if isinstance(wkey, WK) else [wkey]

    def load_wc(dst_t, wk, name, L, kt, dcol0, scol0, ncols):
        c = dcol0
        end = dcol0 + ncols
        while c < end:
            ce = min(end, (c // wk.chunk + 1) * wk.chunk)
            load_w(dst_t[:, :, c:ce], (wk.base, c // wk.chunk), name, L, 0, kt, scol0 + (c - dcol0), ce - c)
            c = ce

    def vec_load(dst, name, L, pat, key, off=0, slow=True):
        dma(dst, rawap(W[name], L * WSZ[name] + off, pat), w=[key], slow=slow)

    memset('pool', ones32[:, :], 1.0, [('c', 'ones32')])
    memset('pool', eps_t[:, :], EPS, [('c', 'eps')])
    memset('pool', negpi[:, :], -PI, [('c', 'negpi')])
    memset('pool', maskut[:, :], 1.0, [('c', 'mask')])
    S.add('pool', lambda e: e.affine_select(out=maskut[:, :], in_=maskut[:, :], pattern=[[1, 64]],
                                            compare_op=ALU.is_ge, fill=0.0, base=0, channel_multiplier=-1),
          r=[('c', 'mask')], w=[('c', 'mask')])
    cp('pool', onesbf[:, :], ones32[:, 0:128], r=[('c', 'ones32')], w=[('c', 'onesbf')])
    identf = stage[1]
    memset('pool', identf[:, 0:128], 1.0, [('stage', 1)])
    S.add('pool', lambda e: e.affine_select(out=identf[:, 0:128], in_=identf[:, 0:128], pattern=[[1, 128]],
                                            compare_op=ALU.is_equal, fill=0.0, base=0, channel_multiplier=-1),
          r=[('stage', 1)], w=[('stage', 1)])
    cp('pool', ident[:, :], identf[:, 0:128], r=[('stage', 1)], w=[('c', 'ident')])
    for k in range(4):
        memset('pool', identf[0:32, 128:256], 1.0, [('stage', 1)])
        S.add('pool', lambda e, k=k: e.affine_select(out=identf[0:32, 128:256], in_=identf[0:32, 128:256],
                                                     pattern=[[1, 128]], compare_op=ALU.is_equal, fill=0.0,
                                                     base=-32 * k, channel_multiplier=-1),
              r=[('stage', 1)], w=[('stage', 1)])
        cp('pool', gsel[:, k, :], identf[0:32, 128:256], r=[('stage', 1)], w=[('c', 'gsel')])
    for h in range(4):
        for (dst, rows_) in ((selF, (4 + h,)), (selIF, (h, 4 + h))):
            memset('pool', dst[:, h, :], 0.0, [('c', 'sel')])
            for rr in rows_:
                memset('pool', identf[0:8, 256:384], 1.0, [('stage', 1)])
                S.add('pool', lambda e, rr=rr: e.affine_select(out=identf[0:8, 256:384], in_=identf[0:8, 256:384],
                                                               pattern=[[0, 128]], compare_op=ALU.is_equal, fill=0.0,
                                                               base=-rr, channel_multiplier=1),
                      r=[('stage', 1)], w=[('stage', 1)])
                tt('pool', dst[:, h, :], dst[:, h, :], identf[0:8, 256:384], ALU.add, r=[('stage', 1), ('c', 'sel')],
                   w=[('c', 'sel')])

    def norm_block(src_ap, nb, gain_t, gkey, dstT, dkey, dcol0, xi, extra_r=(), bufs=None):
        if bufs is None:
            xb, un_, junk_, sc, sid = xt[0], un, junk, 0, 'g'
            kx, ku, kj = ('xt', 0), ('un',), ('junk',)
            b7 = 7
        else:
            xb, un_, junk_, sc, sid = bufs
            kx, ku, kj = ('nbx', sid), ('nbu', sid), ('nbj', sid)
            b7 = nextbank()
        dma(xb[0:nb, :], src_ap, r=extra_r, w=[kx])
        act(junk_[0:nb, :], xb[0:nb, :], AF.Square, r=[kx], w=[kj])
        S.add('dve', lambda e: e.reduce_sum(out=small[0:nb, sc:sc + 1], in_=junk_[0:nb, :], axis=mybir.AxisListType.X),
              r=[kj], w=[('ss', sid)])
        act(small[0:nb, sc + 1:sc + 2], small[0:nb, sc:sc + 1], AF.Sqrt, r=[('ss', sid), ('c', 'eps')], w=[('sd', sid)],
            bias=eps_t[0:nb, :], scale=1.0 / D)
        recip(small[0:nb, sc + 2:sc + 3], small[0:nb, sc + 1:sc + 2], r=[('sd', sid)], w=[('rs', sid)])
        stt('dve', un_[0:nb, :], xb[0:nb, :], small[0:nb, sc + 2:sc + 3], gain_t[0:nb, :], ALU.mult, ALU.mult,
            r=[kx, ('rs', sid), gkey], w=[ku])
        pbf = psum[:, b7, :].bitcast(BF16)
        for k in range(8):
            tr(pbf[:, k * 128:k * 128 + nb], un_[0:nb, k * 128:(k + 1) * 128], ident[0:nb, 0:nb],
               r=[ku, ('c', 'ident')], w=[('ps', b7)])
        cp('act', dstT[:, :, dcol0:dcol0 + nb], pbf.rearrange('p (k t) -> p k t', k=8)[:, :, 0:nb], r=[('ps', b7)],
           w=[dkey])

    def norm_sets(n):
        out = []
        for i in range(n):
            out.append((aalloc('nbx', [128, D], F32), aalloc('nbu', [128, D], BF16), aalloc('nbj', [128, D], BF16),
                        20 + 3 * i, i))
        return out

    def rowsum_rstd(banks, nb, col):
        for hf, b in enumerate(banks):
            act(junk[0:nb, hf * 512:(hf + 1) * 512], ps(b, nb, 512), AF.Square, r=[('ps', b)], w=[('junk',)])
            S.add('dve', lambda e, hf=hf: e.reduce_sum(out=small[0:nb, 4 + hf:5 + hf],
                                                       in_=junk[0:nb, hf * 512:(hf + 1) * 512],
                                                       axis=mybir.AxisListType.X), r=[('junk',)], w=[('ssx', hf)])
        tt('dve', small[0:nb, 6:7], small[0:nb, 4:5], small[0:nb, 5:6], ALU.add, r=[('ssx', 0), ('ssx', 1)],
           w=[('ssx', 2)])
        act(small[0:nb, 7:8], small[0:nb, 6:7], AF.Sqrt, r=[('ssx', 2), ('c', 'eps')], w=[('ssx', 3)],
            bias=eps_t[0:nb, :], scale=1.0 / D)
        recip(small[0:nb, col:col + 1], small[0:nb, 7:8], r=[('ssx', 3)], w=[('rsx',)])

    def proj(bank, wt, wkey, c0, M, src, skey, tok0, nt, kt=8):
        for k in range(kt):
            mm(ps(bank, M, nt), wt[:, k, c0:c0 + M], src[:, k, tok0:tok0 + nt], k == 0, k == kt - 1,
               r=wkeys(wkey, c0, M) + [skey], w=[('ps', bank)])

    def gelu(dst, v, n, vkey, dkey, tmpa, tmpb, P=128):
        act(tmpa[0:P, 0:n], v, AF.Square, r=[vkey], w=[('ga',)])
        ts('dve', tmpa[0:P, 0:n], tmpa[0:P, 0:n], 0.044715, 1.0, ALU.mult, ALU.add, r=[('ga',)], w=[('ga',)])
        tt('dve', tmpa[0:P, 0:n], tmpa[0:P, 0:n], v, ALU.mult, r=[('ga',), vkey], w=[('ga',)])
        act(tmpb[0:P, 0:n], tmpa[0:P, 0:n], AF.Sigmoid, r=[('ga',)], w=[('gb',)], scale=GC)
        tt('pool', dst, tmpb[0:P, 0:n], v, ALU.mult, r=[('gb',), vkey], w=[dkey])

    def s5_setup(L):
        arena[0] = PH0
        aR = aalloc('aR', [128, 16], F32); aI = aalloc('aI', [128, 16], F32); ldt = aalloc('ldt', [128, 16], F32)
        t = [aalloc('s5t', [128, 16], F32) for _ in range(12)]
        bR = aalloc('bR', [128, 16, 16], F32); bI = aalloc('bI', [128, 16, 16], F32)
        cR = aalloc('cR', [128, 16, 16], F32); cI = aalloc('cI', [128, 16, 16], F32)
        u1 = aalloc('u1', [128, 16, 16], F32); u2 = aalloc('u2', [128, 16, 16], F32)
        bsrc = [aalloc('bsrc', [128, 16, 32], BF16) for _ in range(2)]
        K = ('s5c', L)
        vec_load(aR[:, :], 's5_a_re', L, [[1, 128], [128, 16]], K)
        vec_load(aI[:, :], 's5_a_im', L, [[1, 128], [128, 16]], K)
        for e in range(2):
            vec_load(ldt[e * 64:(e + 1) * 64, :], 's5_log_dt', L, [[0, 64], [2, 16]], K, off=e)
        vec_load(bR[:, :, :], 's5_b_re', L, [[16, 128], [2048, 16], [1, 16]], K, slow=False)
        vec_load(bI[:, :, :], 's5_b_im', L, [[16, 128], [2048, 16], [1, 16]], K, slow=False)
        for e in range(2):
            for pr in range(16):
                vec_load(cR[e * 64:(e + 1) * 64, pr, :], 's5_c_re', L, [[1, 64], [64, 16]], K, off=e * 1024 + pr * 2048)
                vec_load(cI[e * 64:(e + 1) * 64, pr, :], 's5_c_im', L, [[1, 64], [64, 16]], K, off=e * 1024 + pr * 2048)
        vec_load(Dd[:, :], 's5_d', L, [[1, 32], [32, 16]], K)
        R = [K]
        dt_, dre, dim, mag, sn, cs, A1r, den, zr, fre, fim, tmp = t
        act(dt_[:, :], ldt[:, :], AF.Exp, r=R, w=R)
        tt('dve', dre[:, :], dt_[:, :], aR[:, :], ALU.mult, r=R, w=R)
        tt('dve', dim[:, :], dt_[:, :], aI[:, :], ALU.mult, r=R, w=R)
        act(mag[:, :], dre[:, :], AF.Exp, r=R, w=R)
        cp('dve', sn[:, :], dim[:, :], r=R, w=R)
        ts('dve', cs[:, :], dim[:, :], 0.5 * PI, None, ALU.add, None, r=R, w=R)
        for arr in (sn, cs):
            cp('dve', den[:, :], arr[:, :], r=R, w=R)
            for th in (PI, 3 * PI, 5 * PI, 7 * PI):
                ts('dve', tmp[:, :], den[:, :], th, None, ALU.is_ge, None, r=R, w=R)
                stt('dve', arr[:, :], tmp[:, :], -2 * PI, arr[:, :], ALU.mult, ALU.add, r=R, w=R)
            act(arr[:, :], arr[:, :], AF.Sin, r=R, w=R)
        tt('dve', A1r[:, :], mag[:, :], cs[:, :], ALU.mult, r=R, w=R)
        tt('dve', A1i[:, :], mag[:, :], sn[:, :], ALU.mult, r=R, w=R)
        ts('dve', nA1i[:, :], A1i[:, :], -1.0, None, ALU.mult, None, r=R, w=R)
        cp('dve', AA1[:, 0, :], A1r[:, :], r=R, w=R)
        cp('dve', AA1[:, 1, :], A1r[:, :], r=R, w=R)
        cp('dve', PW[:, 0, 0, :], A1r[:, :], r=R, w=R)
        cp('dve', PW[:, 0, 1, :], A1i[:, :], r=R, w=R)
        for k in range(7):
            tt('dve', tmp[:, :], PW[:, k, 0, :], A1r[:, :], ALU.mult, r=R, w=R)
            tt('dve', den[:, :], PW[:, k, 1, :], A1i[:, :], ALU.mult, r=R, w=R)
            tt('dve', PW[:, k + 1, 0, :], tmp[:, :], den[:, :], ALU.subtract, r=R, w=R)
            tt('dve', tmp[:, :], PW[:, k, 0, :], A1i[:, :], ALU.mult, r=R, w=R)
            tt('dve', den[:, :], PW[:, k, 1, :], A1r[:, :], ALU.mult, r=R, w=R)
            tt('dve', PW[:, k + 1, 1, :], tmp[:, :], den[:, :], ALU.add, r=R, w=R)
        for k in range(8):
            ts('dve', nPWi[:, k, :], PW[:, k, 1, :], -1.0, None, ALU.mult, None, r=R, w=R)
        cp('dve', AA8[:, 0, :], PW[:, 7, 0, :], r=R, w=R)
        cp('dve', AA8[:, 1, :], PW[:, 7, 0, :], r=R, w=R)
        tt('dve', den[:, :], aR[:, :], aR[:, :], ALU.mult, r=R, w=R)
        tt('dve', tmp[:, :], aI[:, :], aI[:, :], ALU.mult, r=R, w=R)
        tt('dve', den[:, :], den[:, :], tmp[:, :], ALU.add, r=R, w=R)
        recip(den[:, :], den[:, :], r=R, w=R)
        ts('dve', zr[:, :], A1r[:, :], -1.0, None, ALU.add, None, r=R, w=R)
        tt('dve', fre[:, :], zr[:, :], aR[:, :], ALU.mult, r=R, w=R)
        tt('dve', tmp[:, :], A1i[:, :], aI[:, :], ALU.mult, r=R, w=R)
        tt('dve', fre[:, :], fre[:, :], tmp[:, :], ALU.add, r=R, w=R)
        tt('dve', fre[:, :], fre[:, :], den[:, :], ALU.mult, r=R, w=R)
        tt('dve', fim[:, :], A1i[:, :], aR[:, :], ALU.mult, r=R, w=R)
        tt('dve', tmp[:, :], zr[:, :], aI[:, :], ALU.mult, r=R, w=R)
        tt('dve', fim[:, :], fim[:, :], tmp[:, :], ALU.subtract, r=R, w=R)
        tt('dve', fim[:, :], fim[:, :], den[:, :], ALU.mult, r=R, w=R)
        frb = fre[:, :].unsqueeze(2).to_broadcast([128, 16, 16])
        fib = fim[:, :].unsqueeze(2).to_broadcast([128, 16, 16])
        tt('dve', u1[:, :, :], bR[:, :, :], frb, ALU.mult, r=R, w=R)
        tt('dve', u2[:, :, :], bI[:, :, :], fib, ALU.mult, r=R, w=R)
        tt('dve', u1[:, :, :], u1[:, :, :], u2[:, :, :], ALU.subtract, r=R, w=R)
        tt('dve', u2[:, :, :], bI[:, :, :], frb, ALU.mult, r=R, w=R)
        tt('dve', bI[:, :, :], bR[:, :, :], fib, ALU.mult, r=R, w=R)
        tt('dve', u2[:, :, :], u2[:, :, :], bI[:, :, :], ALU.add, r=R, w=R)
        ts('dve', cI[:, :, :], cI[:, :, :], -1.0, None, ALU.mult, None, r=R, w=R)
        for ri, (bsrc_, srcb, srcc) in enumerate(((bsrc[0], u1, cR), (bsrc[1], u2, cI))):
            memset('dve', bsrc_[:, :, :], 0.0, R)
            memset('dve', Ec[:, ri, :, :], 0.0, R)
            for e in range(2):
                cp('dve', bsrc_[e * 64:(e + 1) * 64, :, e * 16:(e + 1) * 16], srcb[e * 64:(e + 1) * 64, :, :], r=R, w=R)
                cp('dve', Ec[e * 64:(e + 1) * 64, ri, :, e * 16:(e + 1) * 16], srcc[e * 64:(e + 1) * 64, :, :], r=R, w=R)
        pbf = psum[:, 7, :].bitcast(BF16)
        for ri in range(2):
            for g4 in range(2):
                for j in range(8):
                    pr = g4 * 8 + j
                    tr(pbf[0:32, j * 128:(j + 1) * 128], bsrc[ri][:, pr, :], ident[:, :], r=R + [('c', 'ident')],
                       w=[('ps', 7)])
                cp('act', BBT[:, g4 * 8:(g4 + 1) * 8, ri, :], pbf[0:32, :].rearrange('p (j q) -> p j q', j=8),
                   r=[('ps', 7)], w=R)

    SBANK = (6, 5)

    def prep_head(nt, hi, qa, qkeys, ka, kkeys, qscale, kscale, B):
        qd2, ki2, kd2, kdT, sT, E1, E2, E3, dec2 = B
        stt('dve', qd2[hi][:, 0:nt], qa, qscale, E1[:, 0:nt], ALU.mult, ALU.mult, r=qkeys + [('E1',)], w=[('qd', hi)])
        stt('dve', ki2[hi][:, 0:nt], ka, kscale, E2[:, 0:nt], ALU.mult, ALU.mult, r=kkeys + [('E2',)], w=[('ki', hi)])
        stt('dve', kd2[hi][:, 0:nt], ka, kscale, E3[:, 0:nt], ALU.mult, ALU.mult, r=kkeys + [('E3',)], w=[('kd', hi)])

    def chunk_pair(nt, heads, vtok, vcol0, stl, bfl, use_n, o_banks, den_banks, B):
        nch = nt // 64
        qd2, ki2, kd2, kdT, sT, E1, E2, E3, dec2 = B
        S32 = stl[0]
        Sbf = bfl[0]
        pbf = psum[:, 7, :].bitcast(BF16)
        for c in range(nch):
            cs = c * 64
            for hi, h in enumerate(heads):
                qd, ki, kd, dec = qd2[hi], ki2[hi], kd2[hi], dec2[hi]
                i2 = hi * 2 + c % 2
                sb = SBANK[hi]
                o_bank = o_banks[hi]
                tr(pbf[0:64, i2 * 128:(i2 + 1) * 128], kd[:, cs:cs + 64], ident[:, :], r=[('kd', hi), ('c', 'ident')],
                   w=[('ps7', i2)])
                cp('act', kdT[i2][:, :], pbf[0:64, i2 * 128:(i2 + 1) * 128], r=[('ps7', i2)], w=[('kdT', i2)])
                mm(ps(sb, 64, 64), ki[:, cs:cs + 64], qd[:, cs:cs + 64], True, True, r=[('ki', hi), ('qd', hi)],
                   w=[('ps', sb)])
                tt('dve', sT[i2][:, :], ps(sb, 64, 64), maskut[:, :], ALU.mult, r=[('ps', sb), ('c', 'mask')],
                   w=[('sT', i2)])
                vv = vtok[:, c, vcol0 + h * 128: vcol0 + (h + 1) * 128]
                mm(psum[:, o_bank, cs:cs + 64], vv, sT[i2][:, :], True, False, r=[('vtok',), ('sT', i2)],
                   w=[('ps', o_bank)])
                mm(psum[:, o_bank, cs:cs + 64], Sbf[:, h, :], qd[:, cs:cs + 64], False, True,
                   r=[('stbf', id(Sbf), h), ('qd', hi)], w=[('ps', o_bank)])
                if use_n:
                    Nbf = bfl[1]
                    den_bank = den_banks[hi]
                    mm(psum[:, den_bank, cs:cs + 64], onesbf[0:64, :], sT[i2][:, :], True, False,
                       r=[('c', 'onesbf'), ('sT', i2)], w=[('ps', den_bank)])
                    mm(psum[:, den_bank, cs:cs + 64], Nbf[:, h, :], qd[:, cs:cs + 64], False, True,
                       r=[('stbf', id(Nbf), h), ('qd', hi)], w=[('ps', den_bank)])
                mm(psum[:, sb, 128:256], kdT[i2][:, :], vv, True, True, r=[('kdT', i2), ('vtok',)], w=[('ps', sb)])
                stt('dve', S32[:, h, :], S32[:, h, :], dec[:, c:c + 1], psum[:, sb, 128:256], ALU.mult, ALU.add,
                    r=[('st', id(S32), h), ('dec', hi), ('ps', sb)], w=[('st', id(S32), h)])
                cp('act', Sbf[:, h, :], S32[:, h, :], r=[('st', id(S32), h)], w=[('stbf', id(Sbf), h)])
                if use_n:
                    N32 = stl[1]
                    mm(psum[:, sb, 256:384], kdT[i2][:, :], onesbf[0:64, :], True, True,
                       r=[('kdT', i2), ('c', 'onesbf')], w=[('ps', sb)])
                    stt('dve', N32[:, h, :], N32[:, h, :], dec[:, c:c + 1], psum[:, sb, 256:384], ALU.mult, ALU.add,
                        r=[('st', id(N32), h), ('dec', hi), ('ps', sb)], w=[('st', id(N32), h)])
                    cp('act', Nbf[:, h, :], N32[:, h, :], r=[('st', id(N32), h)], w=[('stbf', id(Nbf), h)])

    def vproj(tok0, nt, wt, wkey, c0, vtok):
        for c in range(nt // 64):
            b = nextbank()
            for k in range(8):
                mm(ps(b, 64, 512), uT[:, k, tok0 + c * 64: tok0 + c * 64 + 64], wt[:, k, c0:c0 + 512],
                   k == 0, k == 7, r=wkeys(wkey, c0, 512) + [('uT',)], w=[('ps', b)])
            cp('act', vtok[:, c, :], ps(b, 64, 512), r=[('ps', b)], w=[('vtok',)])

    def merge_out(tok0, nt, oT, okey, wpr, wprkey, wgate, wgkey, gc0, first, gsig, gtmp):
        for co in range(8):
            b = nextbank()
            for k in range(4):
                mm(ps(b, 128, nt), wpr[:, k, co * 128:(co + 1) * 128], oT[:, k, 0:nt], k == 0, k == 3,
                   r=[wprkey, okey], w=[('ps', b)])
            b2 = nextbank()
            proj(b2, wgate, wgkey, gc0 + co * 128, 128, uT, ('uT',), tok0, nt)
            act(gsig[:, 0:nt], ps(b2, 128, nt), AF.Sigmoid, r=[('ps', b2)], w=[('gsig',)])
            if first:
                tt('dve', merged[:, co, tok0:tok0 + nt], ps(b, 128, nt), gsig[:, 0:nt], ALU.mult,
                   r=[('ps', b), ('gsig',)], w=[('mg',)])
            else:
                tt('dve', gtmp[:, 0:nt], ps(b, 128, nt), gsig[:, 0:nt], ALU.mult, r=[('ps', b), ('gsig',)],
                   w=[('gtmp',)])
                tt('pool', merged[:, co, tok0:tok0 + nt], merged[:, co, tok0:tok0 + nt], gtmp[:, 0:nt], ALU.add,
                   r=[('gtmp',), ('mg',)], w=[('mg',)])

    def common_bufs():
        qd2 = [aalloc('qd', [128, 512], BF16) for _ in range(2)]
        ki2 = [aalloc('ki', [128, 512], BF16) for _ in range(2)]
        kd2 = [aalloc('kd', [128, 512], BF16) for _ in range(2)]
        kdT = [aalloc('kdT', [64, 128], BF16) for _ in range(4)]
        sT = [aalloc('sT', [64, 64], BF16) for _ in range(4)]
        E1 = aalloc('E1', [128, 512], F32); E2 = aalloc('E2', [128, 512], F32); E3 = aalloc('E3', [128, 512], F32)
        dec2 = [aalloc('dec', [128, 8], F32) for _ in range(2)]
        return (qd2, ki2, kd2, kdT, sT, E1, E2, E3, dec2)

    def phase_gla(L):
        arena[0] = PH0
        Wg = aalloc('Wg', [128, 8, 3088], BF16)
        Wgk = aalloc('Wgk', [16, 1, 512], BF16)
        Wpr = aalloc('Wpr', [128, 4, D], BF16)
        bgk = aalloc('bgk', [128, 4], F32); gln = aalloc('gln', [128, 1], F32)
        Bx = aalloc('Bx', [128, 513], F32)
        Drel = aalloc('Drel', [128, 512], F32); Drel2 = aalloc('Drel2', [128, 512], F32)
        B = common_bufs()
        qd2, ki2, kd2, kdT, sT, E1, E2, E3, dec2 = B
        vtok = aalloc('vtok', [64, 8, 512], BF16)
        lrT = aalloc('lrT', [16, 512], BF16)
        sq = aalloc('sq', [128, 512], BF16)
        rstd = E1; t1 = E2; sr = E3
        ogT = aalloc('ogT', [128, 4, 512], BF16)
        gsig = aalloc('gsig', [128, 512], F32); gtmp = aalloc('gtmp', [128, 512], F32)
        spb = gtmp
        K = ('W', 'g')
        KW = WK(('Wc', 'g'))
        load_wc(Wg, KW, 'w_in', L, 8, 1024, 1024, 1040)
        load_wc(Wg, KW, 'w_in', L, 8, 0, 0, 1024)
        load_wc(Wg, KW, 'w_in', L, 8, 2064, O_GATE, 1024)
        load_w(Wgk[:, :, :], K, 'gla_w_gk', L, 0, 1, 0, 512, rows=16)
        load_w(Wpr[:, :, :], K, 'gla_w_proj', L, 0, 4, 0, D)
        vec_load(bgk[:, :], 'gla_b_gk', L, [[1, 128], [128, 4]], K)
        ts('dve', bgk[:, :], bgk[:, :], -1.0, None, ALU.mult, None, r=[K], w=[K])
        vec_load(gln[:, :], 'gla_norm', L, [[1, 128], [1, 1]], K)
        memset('pool', Bx[:, 0:1], 0.0, [('Bx',)])
        for tok0 in range(0, SEGT, TILE):
            nt = TILE
            nch = nt // 64
            vproj(tok0, nt, Wg, KW, O_GV, vtok)
            b = nextbank()
            proj(b, Wg, KW, O_LR, 16, uT, ('uT',), tok0, nt)
            cp('act', lrT[:, 0:nt], ps(b, 16, nt), r=[('ps', b)], w=[('lrT',)])
            for pair in ((0, 1), (2, 3)):
                for hi, h in enumerate(pair):
                    dec = dec2[hi]
                    b = nextbank()
                    mm(ps(b, 128, nt), Wgk[:, 0, h * 128:(h + 1) * 128], lrT[:, 0:nt], True, True, r=[K, ('lrT',)],
                       w=[('ps', b)])
                    act(spb[:, 0:nt], ps(b, 128, nt), AF.Exp, r=[('ps', b), K], w=[('gtmp',)], bias=bgk[:, h:h + 1],
                        scale=-1.0)
                    act(spb[:, 0:nt], spb[:, 0:nt], AF.Ln, r=[('gtmp',)], w=[('gtmp',)], bias=1.0)
                    S.add('dve', lambda e, nt=nt: e.tensor_tensor_scan(out=Bx[:, 1:1 + nt], data0=ones32[:, 0:nt],
                                                                       data1=spb[:, 0:nt], initial=0.0,
                                                                       op0=ALU.mult, op1=ALU.add),
                          r=[('gtmp',), ('c', 'ones32')], w=[('Bx',)])
                    d3 = Drel[:, 0:nt].rearrange('p (c l) -> p c l', l=64)
                    d23 = Drel2[:, 0:nt].rearrange('p (c l) -> p c l', l=64)
                    bx3 = Bx[:, 1:1 + nt].rearrange('p (c l) -> p c l', l=64)
                    br3 = Bx[:, 0:nt].rearrange('p (c l) -> p c l', l=64)[:, :, 0:1]
                    tt('dve', d3, bx3, br3.to_broadcast([128, nch, 64]), ALU.subtract, r=[('Bx',)], w=[('Drel',)])
                    tt('dve', d23, d3, d3[:, :, 63:64].to_broadcast([128, nch, 64]), ALU.subtract, r=[('Drel',)],
                       w=[('Drel2',)])
                    act(E1[:, 0:nt], Drel[:, 0:nt], AF.Exp, r=[('Drel',)], w=[('E1',)], scale=-1.0 / 16)
                    act(dec[:, 0:nch], Drel[:, 63:nt:64], AF.Exp, r=[('Drel',)], w=[('dec', hi)], scale=-1.0 / 16)
                    act(E2[:, 0:nt], Drel[:, 0:nt], AF.Exp, r=[('Drel',)], w=[('E2',)], scale=1.0 / 16)
                    act(E3[:, 0:nt], Drel2[:, 0:nt], AF.Exp, r=[('Drel2',)], w=[('E3',)], scale=1.0 / 16)
                    bq = nextbank()
                    proj(bq, Wg, KW, O_GQ + h * 128, 128, uT, ('uT',), tok0, nt)
                    bk = nextbank()
                    proj(bk, Wg, KW, O_GK + h * 128, 128, uT, ('uT',), tok0, nt)
                    prep_head(nt, hi, ps(bq, 128, nt), [('ps', bq)], ps(bk, 128, nt), [('ps', bk)], 128 ** -0.5, 1.0, B)
                obs = [nextbank(), nextbank()]
                chunk_pair(nt, pair, vtok, 0, [St32], [Stbf], False, obs, None, B)
                for hi, h in enumerate(pair):
                    ob = obs[hi]
                    act(sq[:, 0:nt], psum[:, ob, 0:nt], AF.Square, r=[('ps', ob)], w=[('sq',)])
                    b = nextbank()
                    mm(ps(b, 128, nt), onesbf[:, :], sq[:, 0:nt], True, True, r=[('c', 'onesbf'), ('sq',)],
                       w=[('ps', b)])
                    act(rstd[:, 0:nt], ps(b, 128, nt), AF.Sqrt, r=[('ps', b), ('c', 'eps')], w=[('E1',)],
                        bias=eps_t[:, :], scale=1.0 / 128)
                    recip(rstd[:, 0:nt], rstd[:, 0:nt], r=[('E1',)], w=[('E1',)])
                    tt('dve', t1[:, 0:nt], psum[:, ob, 0:nt], rstd[:, 0:nt], ALU.mult, r=[('ps', ob), ('E1',)],
                       w=[('E2',)])
                    b = nextbank()
                    proj(b, Wg, KW, O_GR + h * 128, 128, uT, ('uT',), tok0, nt)
                    act(sr[:, 0:nt], ps(b, 128, nt), AF.Silu, r=[('ps', b)], w=[('E3',)])
                    stt('dve', ogT[:, h, 0:nt], sr[:, 0:nt], gln[:, 0:1], t1[:, 0:nt], ALU.mult, ALU.mult,
                        r=[('E3',), K, ('E2',)], w=[('ogT',)])
            merge_out(tok0, nt, ogT, ('ogT',), Wpr, K, Wg, KW, 2064, True, gsig, gtmp)

    def phase_ml(L):
        arena[0] = PH0
        Wm = aalloc('Wm', [128, 8, 3080], BF16)
        Wpr = aalloc('Wmp', [128, 4, D], BF16)
        B = common_bufs()
        qd2, ki2, kd2, kdT, sT, E1, E2, E3, dec2 = B
        vtok = aalloc('vtok', [64, 8, 512], BF16)
        zc = aalloc('zc', [128, 515], F32)
        qh = aalloc('qh', [128, 512], BF16); kh = aalloc('kh', [128, 512], BF16)
        B8 = aalloc('B8', [8, 513], F32)
        T2 = aalloc('T2', [8, 512], F32); T3 = aalloc('T3', [8, 512], F32)
        bif = aalloc('bif', [8, 1], F32); cw = aalloc('cw', [128, 8, 4], F32); cb = aalloc('cb', [128, 8], F32)
        dabs = E1; t1 = E2; so = E3
        ogT = aalloc('hT', [128, 4, 512], BF16)
        gsig = aalloc('gsig', [128, 512], F32); gtmp = aalloc('gtmp', [128, 512], F32)
        cacc = gtmp; gi8 = gsig[0:8, :]
        K = ('W', 'm')
        KW = WK(('Wc', 'm'))
        load_wc(Wm, KW, 'w_in', L, 8, 1024, O_MQ + 1024, 1032)
        load_wc(Wm, KW, 'w_in', L, 8, 0, O_MQ, 1024)
        load_wc(Wm, KW, 'w_in', L, 8, 2056, O_GATE + 2048, 1024)
        load_w(Wpr[:, :, :], K, 'ml_w_proj', L, 0, 4, 0, D)
        vec_load(bif[0:4, :], 'ml_b_i', L, [[1, 4], [1, 1]], K)
        vec_load(bif[4:8, :], 'ml_b_f', L, [[1, 4], [1, 1]], K)
        for j in range(4):
            vec_load(cw[:, :, j], 'ml_conv_w', L, [[1, 128], [128, 8]], K, off=j * 1024)
        vec_load(cb[:, :], 'ml_conv_b', L, [[1, 128], [128, 8]], K)
        memset('pool', B8[:, 0:1], 0.0, [('B8',)])
        for tok0 in range(0, SEGT, TILE):
            nt = TILE
            nch = nt // 64
            vproj(tok0, nt, Wm, KW, 1024, vtok)
            b = nextbank()
            proj(b, Wm, KW, 1536, 8, uT, ('uT',), tok0, nt)
            act(gi8[:, 0:nt], ps(b, 8, nt), AF.Identity, r=[('ps', b), K], w=[('gsig',)], bias=bif[:, :])
            act(T3[:, 0:nt], gi8[:, 0:nt], AF.Exp, r=[('gsig',)], w=[('T3',)], scale=-1.0)
            act(T3[:, 0:nt], T3[:, 0:nt], AF.Ln, r=[('T3',)], w=[('T3',)], bias=1.0)
            S.add('dve', lambda e, nt=nt: e.tensor_tensor_scan(out=B8[:, 1:1 + nt], data0=ones32[0:8, 0:nt],
                                                               data1=T3[:, 0:nt], initial=0.0, op0=ALU.mult,
                                                               op1=ALU.add),
                  r=[('T3',), ('c', 'ones32')], w=[('B8',)])
            d3 = T2[:, 0:nt].rearrange('p (c l) -> p c l', l=64)
            d23 = T3[:, 0:nt].rearrange('p (c l) -> p c l', l=64)
            bx3 = B8[:, 1:1 + nt].rearrange('p (c l) -> p c l', l=64)
            br3 = B8[:, 0:nt].rearrange('p (c l) -> p c l', l=64)[:, :, 0:1]
            tt('dve', d3, bx3, br3.to_broadcast([8, nch, 64]), ALU.subtract, r=[('B8',)], w=[('T2',)])
            tt('dve', d23, d3, d3[:, :, 63:64].to_broadcast([8, nch, 64]), ALU.subtract, r=[('T2',), ('B8',)],
               w=[('T3',)])
            cp('dve', T2[0:4, 0:nt], gi8[0:4, 0:nt], r=[('gsig',), ('T2',)], w=[('T2',)])
            cp('dve', T3[0:4, 0:nt], gi8[0:4, 0:nt], r=[('gsig',), ('T3',)], w=[('T3',)])
            for pair in ((0, 1), (2, 3)):
                for hi, h in enumerate(pair):
                    dec = dec2[hi]
                    b1 = nextbank()
                    mm(ps(b1, 128, nt), selF[:, h, :], T2[:, 0:nt], True, True, r=[('c', 'sel'), ('T2',)],
                       w=[('ps', b1)])
                    act(E1[:, 0:nt], ps(b1, 128, nt), AF.Exp, r=[('ps', b1)], w=[('E1',)], scale=-1.0)
                    act(dec[:, 0:nch], psum[:, b1, 63:nt:64], AF.Exp, r=[('ps', b1)], w=[('dec', hi)], scale=-1.0)
                    b2 = nextbank()
                    mm(ps(b2, 128, nt), selIF[:, h, :], T2[:, 0:nt], True, True, r=[('c', 'sel'), ('T2',)],
                       w=[('ps', b2)])
                    act(E2[:, 0:nt], ps(b2, 128, nt), AF.Exp, r=[('ps', b2)], w=[('E2',)])
                    b3 = nextbank()
                    mm(ps(b3, 128, nt), selIF[:, h, :], T3[:, 0:nt], True, True, r=[('c', 'sel'), ('T3',)],
                       w=[('ps', b3)])
                    act(E3[:, 0:nt], ps(b3, 128, nt), AF.Exp, r=[('ps', b3)], w=[('E3',)])
                    for which, dstq in ((0, qh), (1, kh)):
                        ct = which * 4 + h
                        b = nextbank()
                        proj(b, Wm, KW, ct * 128, 128, uT, ('uT',), tok0, nt)
                        cp('act', zc[:, 3:3 + nt], ps(b, 128, nt), r=[('ps', b)], w=[('zc',)])
                        cp('pool', zc[:, 0:3], zch[:, ct, :], r=[('zch', ct)], w=[('zc',)])
                        ts('dve', cacc[:, 0:nt], zc[:, 0:nt], cw[:, ct, 0:1], None, ALU.mult, None, r=[('zc',), K],
                           w=[('gtmp',)])
                        for j in range(1, 4):
                            stt('dve', cacc[:, 0:nt], zc[:, j:j + nt], cw[:, ct, j:j + 1], cacc[:, 0:nt], ALU.mult,
                                ALU.add, r=[('zc',), K, ('gtmp',)], w=[('gtmp',)])
                        cp('pool', zch[:, ct, :], zc[:, nt:nt + 3], r=[('zc',)], w=[('zch', ct)])
                        act(dstq[:, 0:nt], cacc[:, 0:nt], AF.Silu, r=[('gtmp',), K], w=[('qk', which)],
                            bias=cb[:, ct:ct + 1])
                    prep_head(nt, hi, qh[:, 0:nt], [('qk', 0)], kh[:, 0:nt], [('qk', 1)], 1.0, 128 ** -0.5, B)
                obs = [nextbank(), nextbank()]
                dbs = [nextbank(), nextbank()]
                chunk_pair(nt, pair, vtok, 0, [Ct32, Nt32], [Ctbf, Ntbf], True, obs, dbs, B)
                for hi, h in enumerate(pair):
                    ob, db = obs[hi], dbs[hi]
                    ts('dve', dabs[:, 0:nt], psum[:, db, 0:nt], -1.0, 1.0, ALU.mult, ALU.max, r=[('ps', db)],
                       w=[('E1',)])
                    tt('dve', dabs[:, 0:nt], dabs[:, 0:nt], psum[:, db, 0:nt], ALU.max, r=[('ps', db), ('E1',)],
                       w=[('E1',)])
                    recip(dabs[:, 0:nt], dabs[:, 0:nt], r=[('E1',)], w=[('E1',)])
                    tt('dve', t1[:, 0:nt], psum[:, ob, 0:nt], dabs[:, 0:nt], ALU.mult, r=[('ps', ob), ('E1',)],
                       w=[('E2',)])
                    b = nextbank()
                    proj(b, Wm, KW, 1544 + h * 128, 128, uT, ('uT',), tok0, nt)
                    act(so[:, 0:nt], ps(b, 128, nt), AF.Sigmoid, r=[('ps', b)], w=[('E3',)])
                    tt('dve', ogT[:, h, 0:nt], t1[:, 0:nt], so[:, 0:nt], ALU.mult, r=[('E2',), ('E3',)], w=[('ogT',)])
            merge_out(tok0, nt, ogT, ('ogT',), Wpr, K, Wm, KW, 2056, False, gsig, gtmp)

    def phase_s5(L):
        arena[0] = PH0
        TB = 128
        NCH = TB // 8
        Ws = aalloc('Ws', [128, 8, 1536], BF16)
        Wglu = aalloc('Wglu', [128, 4, 512], BF16)
        Wsp = aalloc('Wsp', [128, 4, D], BF16)
        bglu = aalloc('bglu', [128, 4], F32)
        su128 = aalloc('su128', [128, 4, 512], BF16)
        sup = aalloc('sup', [32, 16, TB], BF16)
        GU = aalloc('GU', [128, 2, 16, TB], F32)
        XC = aalloc('XC', [128, 2, 16, NCH + 1], F32)
        XXbf = aalloc('XXbf', [128, 2, 16, TB], BF16)
        P1 = aalloc('P1', [128, 2, 16, NCH], F32); P2 = aalloc('P2', [128, 2, 16, NCH], F32)
        P1s = aalloc('P1s', [128, 2, 16], F32); P2s = aalloc('P2s', [128, 2, 16], F32)
        yp = aalloc('yp', [32, 16, TB], BF16)
        yv = aalloc('yv', [128, 512], F32)
        ga = aalloc('ga', [128, 512], F32)
        yg = aalloc('yg', [128, 4, 512], BF16)
        y2 = aalloc('y2', [128, 4, 512], BF16)
        gsig = aalloc('gsig', [128, 512], F32); gtmp = aalloc('gtmp', [128, 512], F32)
        gb = gtmp
        K = ('W', 's')
        KC = ('s5c', L)
        KW = WK(('Wc', 's'))
        load_wc(Ws, KW, 'w_in', L, 8, 0, O_SU, 512)
        load_wc(Ws, KW, 'w_in', L, 8, 512, O_GATE + 1024, 1024)
        load_w(Wglu[:, :, :], K, 's5_w_glu', L, 0, 4, 0, 512)
        load_w(Wsp[:, :, :], K, 's5_w_proj', L, 0, 4, 0, D)
        vec_load(bglu[:, :], 's5_b_glu', L, [[1, 128], [128, 4]], K)
        cp('dve', XC[:, :, :, 0], XXst[:, :, :], r=[('xxst',)], w=[('XC',)])
        GU5 = GU[:, :, :, :].rearrange('p r g (l n) -> p r g l n', l=8)

        def G(tau):
            return GU5[:, :, :, tau, :]

        def cmul_acc(dst, src, k, dkey, skey):
            prb = PW[:, k, 0, :].unsqueeze(1).unsqueeze(3).to_broadcast([128, 2, 16, NCH])
            pib = PW[:, k, 1, :].unsqueeze(2).to_broadcast([128, 16, NCH])
            npib = nPWi[:, k, :].unsqueeze(2).to_broadcast([128, 16, NCH])
            tt('dve', P1[:, :, :, :], src, prb, ALU.mult, r=[skey, KC], w=[('P1',)])
            tt(P2ENG, P2[:, 0, :, :], src[:, 1], npib, ALU.mult, r=[skey, KC], w=[('P2',)])
            tt(P2ENG, P2[:, 1, :, :], src[:, 0], pib, ALU.mult, r=[skey, KC], w=[('P2',)])
            tt('dve', dst, dst, P1[:, :, :, :], ALU.add, r=[dkey, ('P1',)], w=[dkey])
            tt('dve', dst, dst, P2[:, :, :, :], ALU.add, r=[dkey, ('P2',)], w=[dkey])

        for tok0 in range(0, SEGT, TILE):
            nt = TILE
            for blk in range(4):
                b = nextbank()
                proj(b, Ws, KW, blk * 128, 128, uT, ('uT',), tok0, nt)
                cp('act', su128[:, blk, 0:nt], ps(b, 128, nt), r=[('ps', b)], w=[('su128',)])
            for tb in range(0, nt, TB):
                for blk in range(4):
                    b = nextbank()
                    for k in range(4):
                        mm(psum[0:32, b, k * TB:(k + 1) * TB], ident[:, 32 * k:32 * k + 32],
                           su128[:, blk, tb:tb + TB].rearrange('p (n l) -> p l n', l=8), True, True,
                           r=[('c', 'ident'), ('su128',)], w=[('ps', b)])
                    cp('act', sup[:, blk * 4:(blk + 1) * 4, :], ps(b, 32, 4 * TB).rearrange('p (k t) -> p k t', k=4),
                       r=[('ps', b)], w=[('sup',)])
                for ri in range(2):
                    for g in range(4):
                        b = nextbank()
                        for k in range(4):
                            pr = g * 4 + k
                            mm(psum[:, b, k * TB:(k + 1) * TB], BBT[:, pr, ri, :], sup[:, pr, :], True, True,
                               r=[KC, ('sup',)], w=[('ps', b)])
                        cp('act', GU[:, ri, g * 4:(g + 1) * 4, :], ps(b, 128, 4 * TB).rearrange('p (k t) -> p k t', k=4),
                           r=[('ps', b)], w=[('GU',)])
                for tau in range(1, 8):
                    cmul_acc(G(tau), G(tau - 1), 0, ('GU',), ('GU',))
                for n in range(NCH):
                    tt('dve', P1s[:, :, :], XC[:, :, :, n], AA8[:, :, :], ALU.mult, r=[('XC',), KC], w=[('P1s',)])
                    tt('dve', P2s[:, 0, :], XC[:, 1, :, n], nPWi[:, 7, :], ALU.mult, r=[('XC',), KC], w=[('P2s',)])
                    tt('dve', P2s[:, 1, :], XC[:, 0, :, n], PW[:, 7, 1, :], ALU.mult, r=[('XC',), KC], w=[('P2s',)])
                    tt('dve', P1s[:, :, :], P1s[:, :, :], P2s[:, :, :], ALU.add, r=[('P1s',), ('P2s',)], w=[('P1s',)])
                    tt('dve', XC[:, :, :, n + 1], P1s[:, :, :], GU5[:, :, :, 7, n], ALU.add, r=[('P1s',), ('GU',)],
                       w=[('XC',)])
                for tau in range(7):
                    cmul_acc(G(tau), XC[:, :, :, 0:NCH], tau, ('GU',), ('XC',))
                cp('dve', G(7), XC[:, :, :, 1:NCH + 1], r=[('XC',)], w=[('GU',)])
                cp('act', XXbf[:, :, :, :], GU[:, :, :, :], r=[('GU',)], w=[('XXbf',)])
                cp('dve', XC[:, :, :, 0], XC[:, :, :, NCH], r=[('XC',)], w=[('XC',)])
                for g in range(4):
                    b = nextbank()
                    for k in range(4):
                        pr = g * 4 + k
                        mm(psum[0:32, b, k * TB:(k + 1) * TB], Ec[:, 0, pr, :], XXbf[:, 0, pr, :], True, False,
                           r=[KC, ('XXbf',)], w=[('ps', b)])
                        mm(psum[0:32, b, k * TB:(k + 1) * TB], Ec[:, 1, pr, :], XXbf[:, 1, pr, :], False, True,
                           r=[KC, ('XXbf',)], w=[('ps', b)])
                    for k in range(4):
                        pr = g * 4 + k
                        stt('dve', yp[:, pr, :], sup[:, pr, :], Dd[:, pr:pr + 1], psum[0:32, b, k * TB:(k + 1) * TB],
                            ALU.mult, ALU.add, r=[('sup',), KC, ('ps', b)], w=[('yp',)])
                b = nextbank()
                for blk in range(4):
                    for k in range(4):
                        mm(psum[:, b, blk * TB:(blk + 1) * TB], gsel[:, k, :],
                           yp[:, blk * 4 + k, :].rearrange('p (l n) -> p n l', l=8), k == 0, k == 3,
                           r=[('c', 'gsel'), ('yp',)], w=[('ps', b)])
                cp('act', yv[:, :], ps(b, 128, 4 * TB), r=[('ps', b)], w=[('yv',)])
                act(ga[:, :], yv[:, :], AF.Square, r=[('yv',)], w=[('ga',)])
                ts('dve', ga[:, :], ga[:, :], 0.044715, 1.0, ALU.mult, ALU.add, r=[('ga',)], w=[('ga',)])
                tt('dve', ga[:, :], ga[:, :], yv[:, :], ALU.mult, r=[('ga',), ('yv',)], w=[('ga',)])
                act(gb[:, :], ga[:, :], AF.Sigmoid, r=[('ga',)], w=[('gtmp',)], scale=GC)
                tt('pool', yg[:, :, tb:tb + TB], gb[:, :].rearrange('p (k t) -> p k t', k=4),
                   yv[:, :].rearrange('p (k t) -> p k t', k=4), ALU.mult, r=[('gtmp',), ('yv',)], w=[('yg',)])
            for co in range(4):
                b = nextbank()
                for k in range(4):
                    mm(ps(b, 128, nt), Wglu[:, k, co * 128:(co + 1) * 128], yg[:, k, 0:nt], k == 0, k == 3,
                       r=[K, ('yg',)], w=[('ps', b)])
                act(gsig[:, 0:nt], ps(b, 128, nt), AF.Sigmoid, r=[('ps', b), K], w=[('gsig',)], bias=bglu[:, co:co + 1])
                tt('dve', y2[:, co, 0:nt], yg[:, co, 0:nt], gsig[:, 0:nt], ALU.mult, r=[('yg',), ('gsig',)],
                   w=[('y2',)])
            merge_out(tok0, nt, y2, ('y2',), Wsp, K, Ws, KW, 512, False, gsig, gtmp)
        cp('dve', XXst[:, :, :], XC[:, :, :, 0], r=[('XC',)], w=[('xxst',)])

    def phase_out(L, src_t, seg):
        arena[0] = PH0
        Wo = aalloc('Wo', [128, 8, D], BF16)
        t1 = aalloc('t1o', [128, D], F32)
        K = ('W', 'o')
        load_w(Wo[:, :, :], K, 'w_out', L, 0, 8, 0, D)
        if L == 0 and seg == 0:
            dump('mgall', merged[:, :, :], [('mg',)])
        for bi, tb in enumerate(range(0, SEGT, 128)):
            xi = bi % 2
            xb = xt[xi]
            dma(xb[:, :], rawap(src_t, (seg * SEGT + tb) * D, [[D, 128], [1, D]]), w=[('xt', 0)])
            banks = []
            for hf in range(2):
                b = nextbank()
                banks.append(b)
                for k in range(8):
                    mm(ps(b, 128, 512), merged[:, k, tb:tb + 128], Wo[:, k, hf * 512:(hf + 1) * 512], k == 0, k == 7,
                       r=[('mg',), K], w=[('ps', b)])
            rowsum_rstd(banks, 128, 8)
            for hf, b in enumerate(banks):
                tt('dve', t1[:, hf * 512:(hf + 1) * 512], ps(b, 128, 512), gains[1][:, hf * 512:(hf + 1) * 512], ALU.mult,
                   r=[('ps', b), ('c', 'gain')], w=[('t1o', hf)])
                stt('dve', xb[:, hf * 512:(hf + 1) * 512], t1[:, hf * 512:(hf + 1) * 512], small[:, 8:9],
                    xb[:, hf * 512:(hf + 1) * 512], ALU.mult, ALU.add, r=[('t1o', hf), ('rsx',), ('xt', 0)],
                    w=[('xt', 0)])
            if L == 0 and seg == 0 and tb == 128:
                dump('osmall', small[:, 0:16], [('rsx',), ('ssx', 0), ('ssx', 1), ('ssx', 2), ('ssx', 3)])
                dump('ot1', t1[:, :], [('t1o', 0), ('t1o', 1)])
                dump('ojunk', junk[:, :], [('junk',)])
            dma(rawap(xmid, tb * D, [[D, 128], [1, D]]), xb[:, :], r=[('xt', 0)], w=[('xmid', tb)])
            if L == 0 and seg == 0:
                dump('xmid%d' % tb, xb[:, :], [('xt', 0)])

    hs = nc.dram_tensor('hs', [22, 128, SEGT], BF16)

    def phase_ffn_a(L, seg):
        arena[0] = UT0
        Wup = aalloc('Wup', [128, 8, DFF], BF16)
        Wga = aalloc('Wga', [128, 8, DFF], BF16)
        NTF = 512
        uF = [aalloc('uF', [128, 8, NTF], BF16) for _ in range(2)]
        NB = 3
        fsets = [(aalloc('ab', [128, NTF + 2], F32), aalloc('vv', [128, NTF], F32), aalloc('gaF', [128, NTF], F32),
                  aalloc('vg', [128, NTF], F32), aalloc('ho', [128, NTF], BF16)) for _ in range(NB)]
        fw = aalloc('fw', [128, 22, 3], F32); fb = aalloc('fb', [128, 22], F32)
        nsets = norm_sets(2)
        nbi = [0]
        K = ('W', 'f')
        KU = WK(('Wc', 'fu'), 704)
        KG = WK(('Wc', 'fg'), 704)
        for q4 in range(4):
            load_wc(Wup, KU, 'ffn_w_up', L, 8, q4 * 704, q4 * 704, 704)
            load_wc(Wga, KG, 'ffn_w_gate', L, 8, q4 * 704, q4 * 704, 704)
        for j in range(3):
            vec_load(fw[:, :, j], 'ffn_conv_w', L, [[1, 128], [128, 22]], K, off=j * DFF)
        vec_load(fb[:, :], 'ffn_conv_b', L, [[1, 128], [128, 22]], K)
        dma(gains2[0][:, :], rawap(W['norm_ffn_pre'], L * D, [[0, 128], [1, D]]), w=[('c', 'gain')])
        gi = 0
        for ti, t0 in enumerate(range(0, SEGT, NTF)):
            uFt = uF[ti % 2]
            ukey = ('uF', ti % 2)
            for j in range(NTF // 128):
                tb = t0 + j * 128
                norm_block(rawap(xmid, tb * D, [[D, 128], [1, D]]), 128, gains[2], ('c', 'gain'), uFt, ukey, j * 128,
                           0, extra_r=[('xmid', tb)], bufs=nsets[nbi[0] % 2])
                nbi[0] += 1
            for ct in range(22):
                si = gi % NB
                gi += 1
                ab, vv, ga, vg, ho = fsets[si]
                b = nextbank()
                proj(b, Wup, KU, ct * 128, 128, uFt, ukey, 0, NTF)
                cp('act', ab[:, 2:2 + NTF], ps(b, 128, NTF), r=[('ps', b)], w=[('ab', si)])
                cp('pool', ab[:, 0:2], ahist[:, ct, :], r=[('ah', ct)], w=[('ab', si)])
                ts('dve', vv[:, :], ab[:, 0:NTF], fw[:, ct, 0:1], fb[:, ct:ct + 1], ALU.mult, ALU.add,
                   r=[('ab', si), K], w=[('vv', si)])
                for j in range(1, 3):
                    stt('dve', vv[:, :], ab[:, j:j + NTF], fw[:, ct, j:j + 1], vv[:, :], ALU.mult, ALU.add,
                        r=[('ab', si), K, ('vv', si)], w=[('vv', si)])
                cp('pool', ahist[:, ct, :], ab[:, NTF:NTF + 2], r=[('ab', si)], w=[('ah', ct)])
                b2 = nextbank()
                proj(b2, Wga, KG, ct * 128, 128, uFt, ukey, 0, NTF)
                tt('dve', vg[:, :], vv[:, :], ps(b2, 128, NTF), ALU.mult, r=[('vv', si), ('ps', b2)], w=[('vg', si)])
                act(ga[:, :], vv[:, :], AF.Square, r=[('vv', si)], w=[('ga', si)], scale=math.sqrt(0.044715))
                stt('dve', ga[:, :], ga[:, :], 1.0, vv[:, :], ALU.add, ALU.mult, r=[('ga', si), ('vv', si)],
                    w=[('ga', si)])
                act(ga[:, :], ga[:, :], AF.Sigmoid, r=[('ga', si)], w=[('ga', si)], scale=GC)
                tt('pool', ho[:, :], ga[:, :], vg[:, :], ALU.mult, r=[('ga', si), ('vg', si)], w=[('ho', si)])
                dma(rawap(hs, ct * 128 * SEGT + t0, [[SEGT, 128], [1, NTF]]), ho[:, :], r=[('ho', si)],
                    w=[('hs', ct, t0)])

    def phase_ffn_b(L, dst_t, seg):
        arena[0] = UT0
        Wd = aalloc('Wd', [128, 22, D], BF16)
        hb = [aalloc('hb', [128, 22, 512], BF16) for _ in range(2)]
        t1 = aalloc('t1f', [128, D], F32)
        xb2 = [aalloc('xb2', [128, D], F32) for _ in range(2)]
        K = ('W', 'fd')
        KD = WK(('Wc', 'fd'), 512)
        for hf in range(2):
            load_w(Wd[:, :, hf * 512:(hf + 1) * 512], (KD.base, hf), 'ffn_w_down', L, 0, 22, hf * 512, 512)
        dma(gains2[1][:, :], rawap(W['norm_ffn_post'], L * D, [[0, 128], [1, D]]), w=[('c', 'gain')])
        bi = 0
        for ti, t0 in enumerate(range(0, SEGT, 512)):
            hbt = hb[ti % 2]
            hkey = ('hb', ti % 2)
            for c0 in (0, 11):
                dma(hbt[:, c0:c0 + 11, :], rawap(hs, c0 * 128 * SEGT + t0, [[SEGT, 128], [128 * SEGT, 11], [1, 512]]),
                    r=[('hs', ct, t0) for ct in range(c0, c0 + 11)], w=[hkey])
            for j in range(4):
                tb = t0 + j * 128
                xi = bi % 2
                bi += 1
                xb = xb2[xi]
                dma(xb[:, :], rawap(xmid, tb * D, [[D, 128], [1, D]]), r=[('xmid', tb)], w=[('xb2', xi)])
                banks = []
                for hf in range(2):
                    b = nextbank()
                    banks.append(b)
                    for k in range(22):
                        mm(ps(b, 128, 512), hbt[:, k, j * 128:(j + 1) * 128], Wd[:, k, hf * 512:(hf + 1) * 512], k == 0,
                           k == 21, r=[hkey, (KD.base, hf)], w=[('ps', b)])
                rowsum_rstd(banks, 128, 9)
                for hf, b in enumerate(banks):
                    tt('dve', t1[:, hf * 512:(hf + 1) * 512], ps(b, 128, 512), gains[3][:, hf * 512:(hf + 1) * 512],
                       ALU.mult, r=[('ps', b), ('c', 'gain')], w=[('t1f', hf)])
                    stt('dve', xb[:, hf * 512:(hf + 1) * 512], t1[:, hf * 512:(hf + 1) * 512], small[:, 9:10],
                        xb[:, hf * 512:(hf + 1) * 512], ALU.mult, ALU.add, r=[('t1f', hf), ('rsx',), ('xb2', xi)],
                        w=[('xb2', xi)])
                dma(rawap(dst_t, (seg * SEGT + tb) * D, [[D, 128], [1, D]]), xb[:, :], r=[('xb2', xi)],
                    w=[('dst', seg, tb)])

    for L in range(nlayer):
        src_t = x_in if L == 0 else x1
        dst_t = out_t if L == nlayer - 1 else x1
        for stt_ in (St32, Ct32, Nt32):
            memset('pool', stt_[:, :, :], 0.0, [('st', id(stt_), h) for h in range(4)])
        for bf_, s32 in ((Stbf, St32), (Ctbf, Ct32), (Ntbf, Nt32)):
            for h in range(4):
                cp('pool', bf_[:, h, :], s32[:, h, :], r=[('st', id(s32), h)], w=[('stbf', id(bf_), h)])
        memset('pool', XXst[:, :, :], 0.0, [('xxst',)])
        for ct in range(8):
            memset('pool', zch[:, ct, :], 0.0, [('zch', ct)])
        for ct in range(22):
            memset('pool', ahist[:, ct, :], 0.0, [('ah', ct)])
        S.barrier()
        s5_setup(L)
        for seg in range(nseg):
            S.barrier()
            dma(gains2[0][:, :], rawap(W['norm_mix_pre'], L * D, [[0, 128], [1, D]]), w=[('c', 'gain')])
            dma(gains2[1][:, :], rawap(W['norm_mix_post'], L * D, [[0, 128], [1, D]]), w=[('c', 'gain')])
            arena[0] = PH0
            nsets = norm_sets(3)
            for bi, tb in enumerate(range(0, SEGT, 128)):
                extra = [('dst', seg, tb)] if L > 0 else []
                norm_block(rawap(src_t, (seg * SEGT + tb) * D, [[D, 128], [1, D]]), 128, gains[0], ('c', 'gain'), uT,
                           ('uT',), tb, bi % 2, extra_r=extra, bufs=nsets[bi % 3])
            S.barrier()
            if 'gla' not in skip:
                phase_gla(L)
                S.barrier()
            if 'ml' not in skip:
                phase_ml(L)
                S.barrier()
            if 's5' not in skip:
                phase_s5(L)
                S.barrier()
            if 'out' not in skip:
                phase_out(L, src_t, seg)
                S.barrier()
            if 'ffn' not in skip:
                phase_ffn_a(L, seg)
                S.barrier()
                phase_ffn_b(L, dst_t, seg)
                S.barrier()

    with nc.semaphore('e_pe') as s0, nc.semaphore('e_act') as s1, nc.semaphore('e_dve') as s2, \
            nc.semaphore('e_pool') as s3, nc.semaphore('e_sp') as s4:
        esem = {'pe': s0, 'act': s1, 'dve': s2, 'pool': s3, 'sp': s4}
        import contextlib
        with contextlib.ExitStack() as es:
            ssem = [es.enter_context(nc.semaphore('dslot%d' % i)) for i in range(S.nslots)]
            with nc.Block() as block:
                S.emit(nc, block, esem, ssem)
    build.dbg_names = dbg_names
    build.amax = amax
    build.ph0 = PH0
    return nc


def kernel(**inputs):
    nc = build()
    m = {'x': np.ascontiguousarray(inputs['x'].reshape(SEQ, D), dtype=np.float32)}
    for n in WNAMES:
        m[n] = np.ascontiguousarray(inputs[n], dtype=np.float32)
    res = run_bass_kernel_spmd(nc, [m], core_ids=[0])
    return res.results[0]['out'].reshape(1, SEQ, D).astype(np.float32)
```

```python
import math
import numpy as np
import concourse.bass as bass
import concourse.mybir as mybir
from concourse.bass_utils import run_bass_kernel_spmd

F32 = mybir.dt.float32
BF16 = mybir.dt.bfloat16
ALU = mybir.AluOpType
AF = mybir.ActivationFunctionType

NCORE = 8
D = 1024
DIN = 7704
DFF = 2816
SEQ = 16384
OWN = SEQ // NCORE
HALO = 64
PRE = 3
NT = OWN + HALO
NW = NT + PRE
EPS = 1e-6
O_GQ, O_GK, O_GV, O_LR, O_GR, O_SU, O_MQ, O_MK, O_MV, O_MI, O_MF, O_MO, O_GATE = (
    0, 512, 1024, 1536, 1552, 2064, 2576, 3088, 3600, 4112, 4116, 4120, 4632)
TILES = [(0, 64), (64, 512), (576, 512), (1088, 512), (1600, 512)]
SNAP_TOK = OWN
LS5 = 8
GC = 1.5957691216057308

WNAMES = ['norm_mix_pre', 'norm_mix_post', 'norm_ffn_pre', 'norm_ffn_post', 'w_in',
          'gla_w_gk', 'gla_b_gk', 'gla_norm', 'gla_w_proj',
          's5_a_re', 's5_a_im', 's5_log_dt', 's5_b_re', 's5_b_im', 's5_c_re', 's5_c_im', 's5_d',
          's5_w_glu', 's5_b_glu', 's5_w_proj', 'ml_conv_w', 'ml_conv_b', 'ml_b_i', 'ml_b_f',
          'ml_w_proj', 'w_out', 'ffn_w_up', 'ffn_w_gate', 'ffn_conv_w', 'ffn_conv_b', 'ffn_w_down']
WSHAPES = {'norm_mix_pre': [D], 'norm_mix_post': [D], 'norm_ffn_pre': [D], 'norm_ffn_post': [D],
           'w_in': [D, DIN], 'gla_w_gk': [16, 512], 'gla_b_gk': [512], 'gla_norm': [128],
           'gla_w_proj': [512, D], 's5_a_re': [32, 64], 's5_a_im': [32, 64], 's5_log_dt': [32],
           's5_b_re': [32, 64, 16], 's5_b_im': [32, 64, 16], 's5_c_re': [32, 16, 64],
           's5_c_im': [32, 16, 64], 's5_d': [32, 16], 's5_w_glu': [512, 512], 's5_b_glu': [512],
           's5_w_proj': [512, D], 'ml_conv_w': [4, D], 'ml_conv_b': [D], 'ml_b_i': [4], 'ml_b_f': [4],
           'ml_w_proj': [512, D], 'w_out': [D, D], 'ffn_w_up': [D, DFF], 'ffn_w_gate': [D, DFF],
           'ffn_conv_w': [3, DFF], 'ffn_conv_b': [DFF], 'ffn_w_down': [DFF, D]}


class Sched:
    ENGS = ('pe', 'act', 'dve', 'pool', 'sp')
    SELF_SYNC = ('act', 'dve', 'pool')

    def __init__(self, nslots=16):
        self.streams = {e: [] for e in self.ENGS}
        self.ops = []
        self.lastw = {}
        self.rd = {}
        self.nslots = nslots
        self.slot_cnt = [0] * nslots
        self.slot_last = [None] * nslots
        self.dma_n = 0

    def add(self, eng, fn, r=(), w=(), dma=False):
        oid = len(self.ops)
        deps = set()
        raw = set()
        for k in r:
            p = self.lastw.get(k)
            if p is not None:
                deps.add(p)
                raw.add(p)
        for k in w:
            p = self.lastw.get(k)
            if p is not None:
                deps.add(p)
            rr = self.rd.get(k)
            if rr:
                deps.update(rr[0].values())
                deps.update(rr[1])
        o = {'id': oid, 'eng': eng, 'fn': fn, 'dma': dma, 'flag': False, 'raw': raw}
        if dma:
            s = self.dma_n % self.nslots
            self.dma_n += 1
            if self.slot_last[s] is not None:
                deps.add(self.slot_last[s])
            self.slot_cnt[s] += 1
            o['slot'] = s
            o['slotval'] = 16 * self.slot_cnt[s]
            self.slot_last[s] = oid
        for k in w:
            self.lastw[k] = oid
            self.rd[k] = ({}, [])
        for k in r:
            rr = self.rd.setdefault(k, ({}, []))
            if dma:
                rr[1].append(oid)
            else:
                rr[0][eng] = oid
        deps.discard(oid)
        o['deps'] = deps
        self.ops.append(o)
        self.streams[eng].append(o)
        return o

    def barrier(self):
        last = {}
        for e in self.ENGS:
            st = self.streams[e]
            for o in reversed(st):
                if o['fn'] is not None and not o['dma']:
                    last[e] = o['id']
                    break
        dmas = [x for x in self.slot_last if x is not None]
        for e in self.ENGS:
            oid = len(self.ops)
            deps = set(v for k, v in last.items() if k != e) | set(dmas)
            o = {'id': oid, 'eng': e, 'fn': None, 'dma': False, 'flag': False, 'deps': deps}
            self.ops.append(o)
            self.streams[e].append(o)

    def emit(self, nc, block, esem, ssem):
        ops = self.ops
        for o in ops:
            for d in o['deps']:
                dd = ops[d]
                if dd['dma']:
                    continue
                if dd['eng'] != o['eng'] or (d in o.get('raw', ()) and o['eng'] in self.SELF_SYNC):
                    dd['flag'] = True
        for e in self.ENGS:
            c = 0
            for o in self.streams[e]:
                if o['flag'] and not o['dma']:
                    c += 1
                    o['fidx'] = c
        sched = self

        def run(ename, eng):
            waited = {}
            for o in sched.streams[ename]:
                need = {}
                for d in o['deps']:
                    dd = ops[d]
                    if dd['dma']:
                        key = ('s', dd['slot'])
                        val = dd['slotval']
                    elif dd['eng'] == ename and not (d in o.get('raw', ()) and ename in sched.SELF_SYNC):
                        continue
                    else:
                        key = ('e', dd['eng'])
                        val = dd['fidx']
                    if val > waited.get(key, 0) and val > need.get(key, 0):
                        need[key] = val
                for key, val in need.items():
                    sem = ssem[key[1]] if key[0] == 's' else esem[key[1]]
                    eng.wait_ge(sem, val)
                    waited[key] = val
                if o['fn'] is None:
                    continue
                ins = o['fn'](eng)
                if o['dma']:
                    ins.then_inc(ssem[o['slot']], 16)
                elif o['flag']:
                    ins.then_inc(esem[ename], 1)
            if ename == 'sp':
                for s in range(sched.nslots):
                    if sched.slot_cnt[s]:
                        eng.wait_ge(ssem[s], 16 * sched.slot_cnt[s])

        @block.tensor
        def _(e):
            run('pe', e)

        @block.scalar
        def _(e):
            run('act', e)

        @block.vector
        def _(e):
            run('dve', e)

        @block.gpsimd
        def _(e):
            run('pool', e)

        @block.sync
        def _(e):
            run('sp', e)


def rawap(t, offset, pat):
    return bass.AP(tensor=t, offset=offset, ap=[list(p) for p in pat])


SEGT = 2048
DBGT = 512
BAR = False
P2ENG = 'dve'
TILE = 512
PI = math.pi


def build(nseg=8, nlayer=2, dbg=False, skip=()):
    nc = bass.Bass("TRN2", target_bir_lowering=False)
    S = Sched()
    ntok = nseg * SEGT
    x_in = nc.dram_tensor('x', [ntok, D], F32, kind='ExternalInput')
    W = {n: nc.dram_tensor(n, [2] + WSHAPES[n], F32, kind='ExternalInput') for n in WNAMES}
    WSZ = {n: int(np.prod(WSHAPES[n])) for n in WNAMES}
    out_t = nc.dram_tensor('out', [ntok, D], F32, kind='ExternalOutput')
    x1 = nc.dram_tensor('x1', [ntok, D], F32)
    xmid = nc.dram_tensor('xmid', [SEGT, D], F32)

    sb_top = [16512]
    SB_END = 229376

    def alloc(name, shape, dt, at=None):
        nbytes = int(np.prod(shape[1:])) * (4 if dt == F32 else 2)
        nbytes = (nbytes + 63) // 64 * 64
        if at is None:
            off = sb_top[0]
            sb_top[0] += nbytes
        else:
            off = at
        assert off + nbytes <= SB_END, (name, off, nbytes)
        return nc.alloc_sbuf_tensor_at(name, list(shape), dt, offset=off)

    psum = nc.alloc_psum_tensor('psum', [128, 8, 512], F32)

    def ps(b, p=128, n=512):
        return psum[0:p, b, 0:n]

    bank_rr = [0]

    def nextbank():
        b = bank_rr[0] % 5
        bank_rr[0] += 1
        return b

    ident = alloc('ident', [128, 128], BF16)
    onesbf = alloc('onesbf', [128, 128], BF16)
    ones32 = alloc('ones32', [128, 512], F32)
    maskut = alloc('maskut', [128, 64], F32)
    eps_t = alloc('eps_t', [128, 1], F32)
    negpi = alloc('negpi', [128, 1], F32)
    gains2 = [alloc('gain%d' % i, [128, D], F32) for i in range(2)]
    gains = [gains2[0], gains2[1], gains2[0], gains2[1]]
    xt0 = alloc('xt0', [128, D], F32)
    xt = [xt0, xt0]
    un = alloc('un', [128, D], BF16)
    junk = alloc('junk', [128, D], BF16)
    small = alloc('small', [128, 64], F32)
    stage0 = alloc('stage0', [128, 512], F32)
    stage = [stage0, stage0]
    St32 = alloc('St32', [128, 4, 128], F32)
    Ct32 = alloc('Ct32', [128, 4, 128], F32)
    Nt32 = alloc('Nt32', [128, 4, 128], F32)
    Stbf = alloc('Stbf', [128, 4, 128], BF16)
    Ctbf = alloc('Ctbf', [128, 4, 128], BF16)
    Ntbf = alloc('Ntbf', [128, 4, 128], BF16)
    XXst = alloc('XXst', [128, 2, 16], F32)
    zch = alloc('zch', [128, 8, 3], F32)
    ahist = alloc('ahist', [128, 22, 2], F32)
    BBT = alloc('BBT', [32, 16, 2, 128], BF16)
    Ec = alloc('Ec', [128, 2, 16, 32], BF16)
    AA1 = alloc('AA1', [128, 2, 16], F32)
    A1i = alloc('A1i', [128, 16], F32)
    nA1i = alloc('nA1i', [128, 16], F32)
    Dd = alloc('Dd', [32, 16], F32)
    PW = alloc('PW', [128, 8, 2, 16], F32)
    nPWi = alloc('nPWi', [128, 8, 16], F32)
    AA8 = alloc('AA8', [128, 2, 16], F32)
    gsel = alloc('gsel', [32, 4, 128], BF16)
    selF = alloc('selF', [8, 4, 128], F32)
    selIF = alloc('selIF', [8, 4, 128], F32)
    UT0 = sb_top[0]
    uT = alloc('uT', [128, 8, SEGT], BF16)
    merged = alloc('merged', [128, 8, SEGT], BF16)
    PH0 = sb_top[0]
    arena = [PH0]
    an = [0]

    amax = {}

    def aalloc(name, shape, dt):
        an[0] += 1
        t = alloc('%s_%d' % (name, an[0]), shape, dt, at=arena[0])
        nb = int(np.prod(shape[1:])) * (4 if dt == F32 else 2)
        arena[0] += (nb + 63) // 64 * 64
        amax['top'] = max(amax.get('top', 0), arena[0])
        return t

    cnt = {'stage': 0}
    dumped = {}

    def dma(out, in_, r=(), w=(), slow=False):
        if slow:
            return S.add('sp', lambda e: e.dma_start(out=out, in_=in_, allow_slow_non_contiguous=True),
                         r=r, w=w, dma=True)
        return S.add('sp', lambda e: e.dma_start(out=out, in_=in_), r=r, w=w, dma=True)

    def act(out, in_, func, r, w, bias=None, scale=None, accum=None):
        kw = {}
        if bias is not None:
            kw['bias'] = bias
        if scale is not None:
            kw['scale'] = scale
        if accum is not None:
            kw['accum_out'] = accum
        return S.add('act', lambda e: e.activation(out=out, in_=in_, func=func, **kw), r=r, w=w)

    def tt(eng, out, in0, in1, op, r, w):
        return S.add(eng, lambda e: e.tensor_tensor(out=out, in0=in0, in1=in1, op=op), r=r, w=w)

    def ts(eng, out, in0, s1, s2, op0, op1, r, w):
        if s2 is None:
            return S.add(eng, lambda e: e.tensor_single_scalar(out=out, in_=in0, scalar=s1, op=op0), r=r, w=w)
        return S.add(eng, lambda e: e.tensor_scalar(out=out, in0=in0, scalar1=s1, scalar2=s2, op0=op0, op1=op1),
                     r=r, w=w)

    def stt(eng, out, in0, sc, in1, op0, op1, r, w):
        return S.add(eng, lambda e: e.scalar_tensor_tensor(out=out, in0=in0, scalar=sc, in1=in1, op0=op0, op1=op1),
                     r=r, w=w)

    def cp(eng, out, in_, r, w):
        if eng == 'act':
            return S.add('act', lambda e: e.copy(out=out, in_=in_), r=r, w=w)
        return S.add(eng, lambda e: e.tensor_copy(out=out, in_=in_), r=r, w=w)

    def mm(out, lhsT, rhs, start, stop, r, w):
        return S.add('pe', lambda e: e.matmul(out, lhsT, rhs, start=start, stop=stop, skip_group_check=True),
                     r=r, w=w)

    def tr(out, in_, idn, r, w):
        return S.add('pe', lambda e: e.transpose(out, in_, idn), r=r, w=w)

    def memset(eng, ap, val, w):
        return S.add(eng, lambda e: e.memset(ap, val), r=(), w=w)

    dbg_names = []

    def dump(name, ap, rkeys):
        if not dbg:
            return
        t = nc.dram_tensor('dbg_' + name, list(ap.shape), ap.dtype, kind='ExternalOutput')
        dbg_names.append('dbg_' + name)
        full = t.ap()
        dma(full, ap, r=rkeys, w=[('dbg', name)])

    scr = nc.dram_tensor('scr_bar', [64, 64], F32)
    barn = [0]

    def lbar(items):
        for ap_, key in items:
            i = barn[0] % 64
            barn[0] += 1
            if ap_.dtype != F32:
                continue
            dma(scr.ap()[i:i + 1, 0:1], ap_, r=[key], w=[('scr', i)])

    def recip(out, in_, r, w):
        return S.add('dve', lambda e: e.reciprocal(out=out, in_=in_), r=r, w=w)

    def load_w(dst, key, name, L, row0, kt, col0, ncols, rows=128):
        src_t = W[name]
        ld = WSHAPES[name][1] if len(WSHAPES[name]) > 1 else 1
        base = L * WSZ[name]
        kstep = max(1, 8 if kt <= 8 else 11)
        for k0 in range(0, kt, kstep):
            kn = min(kstep, kt - k0)
            src = rawap(src_t, base + (row0 + k0 * 128) * ld + col0, [[ld, rows], [128 * ld, kn], [1, ncols]])
            dd = dst[:, k0:k0 + kn, :]
            S.add('pool', lambda e, dd=dd, src=src: e.dma_start(out=dd, in_=src), r=(), w=[key], dma=True)

    class WK:
        def __init__(self, base, chunk=512):
            self.base = base
            self.chunk = chunk

        def keys(self, c0, n):
            return [(self.base, j) for j in range(c0 // self.chunk, (c0 + n - 1) // self.chunk + 1)]

    def wkeys(wkey, c0, n):
        return wkey.keys(c0, n) if isinstance(wkey, WK) else [wkey]

    def load_wc(dst_t, wk, name, L, kt, dcol0, scol0, ncols):
        c = dcol0
        end = dcol0 + ncols
        while c < end:
            ce = min(end, (c // wk.chunk + 1) * wk.chunk)
            load_w(dst_t[:, :, c:ce], (wk.base, c // wk.chunk), name, L, 0, kt, scol0 + (c - dcol0), ce - c)
            c = ce

    def vec_load(dst, name, L, pat, key, off=0, slow=True):
        dma(dst, rawap(W[name], L * WSZ[name] + off, pat), w=[key], slow=slow)

    memset('pool', ones32[:, :], 1.0, [('c', 'ones32')])
    memset('pool', eps_t[:, :], EPS, [('c', 'eps')])
    memset('pool', negpi[:, :], -PI, [('c', 'negpi')])
    memset('pool', maskut[0:64, :], 1.0, [('c', 'mask')])
    S.add('pool', lambda e: e.affine_select(out=maskut[0:64, :], in_=maskut[0:64, :], pattern=[[1, 64]],
                                            compare_op=ALU.is_ge, fill=0.0, base=0, channel_multiplier=-1),
          r=[('c', 'mask')], w=[('c', 'mask')])
    cp('pool', onesbf[:, :], ones32[:, 0:128], r=[('c', 'ones32')], w=[('c', 'onesbf')])
    dma(maskut[64:128, :], maskut[0:64, :], r=[('c', 'mask')], w=[('c', 'mask')])
    identf = stage[1]
    memset('pool', identf[:, 0:128], 1.0, [('stage', 1)])
    S.add('pool', lambda e: e.affine_select(out=identf[:, 0:128], in_=identf[:, 0:128], pattern=[[1, 128]],
                                            compare_op=ALU.is_equal, fill=0.0, base=0, channel_multiplier=-1),
          r=[('stage', 1)], w=[('stage', 1)])
    cp('pool', ident[:, :], identf[:, 0:128], r=[('stage', 1)], w=[('c', 'ident')])
    for k in range(4):
        memset('pool', identf[0:32, 128:256], 1.0, [('stage', 1)])
        S.add('pool', lambda e, k=k: e.affine_select(out=identf[0:32, 128:256], in_=identf[0:32, 128:256],
                                                     pattern=[[1, 128]], compare_op=ALU.is_equal, fill=0.0,
                                                     base=-32 * k, channel_multiplier=-1),
              r=[('stage', 1)], w=[('stage', 1)])
        cp('pool', gsel[:, k, :], identf[0:32, 128:256], r=[('stage', 1)], w=[('c', 'gsel')])
    for h in range(4):
        for (dst, rows_) in ((selF, (4 + h,)), (selIF, (h, 4 + h))):
            memset('pool', dst[:, h, :], 0.0, [('c', 'sel')])
            for rr in rows_:
                memset('pool', identf[0:8, 256:384], 1.0, [('stage', 1)])
                S.add('pool', lambda e, rr=rr: e.affine_select(out=identf[0:8, 256:384], in_=identf[0:8, 256:384],
                                                               pattern=[[0, 128]], compare_op=ALU.is_equal, fill=0.0,
                                                               base=-rr, channel_multiplier=1),
                      r=[('stage', 1)], w=[('stage', 1)])
                tt('pool', dst[:, h, :], dst[:, h, :], identf[0:8, 256:384], ALU.add, r=[('stage', 1), ('c', 'sel')],
                   w=[('c', 'sel')])

    def norm_block(src_ap, nb, gain_t, gkey, dstT, dkey, dcol0, xi, extra_r=(), bufs=None):
        if bufs is None:
            xb, un_, junk_, sc, sid = xt[0], un, junk, 0, 'g'
            kx, ku, kj = ('xt', 0), ('un',), ('junk',)
            b7 = 7
        else:
            xb, un_, junk_, sc, sid = bufs
            kx, ku, kj = ('nbx', sid), ('nbu', sid), ('nbj', sid)
            b7 = nextbank()
        dma(xb[0:nb, :], src_ap, r=extra_r, w=[kx])
        act(junk_[0:nb, :], xb[0:nb, :], AF.Square, r=[kx], w=[kj])
        S.add('dve', lambda e: e.reduce_sum(out=small[0:nb, sc:sc + 1], in_=junk_[0:nb, :], axis=mybir.AxisListType.X),
              r=[kj], w=[('ss', sid)])
        act(small[0:nb, sc + 1:sc + 2], small[0:nb, sc:sc + 1], AF.Sqrt, r=[('ss', sid), ('c', 'eps')], w=[('sd', sid)],
            bias=eps_t[0:nb, :], scale=1.0 / D)
        recip(small[0:nb, sc + 2:sc + 3], small[0:nb, sc + 1:sc + 2], r=[('sd', sid)], w=[('rs', sid)])
        stt('dve', un_[0:nb, :], xb[0:nb, :], small[0:nb, sc + 2:sc + 3], gain_t[0:nb, :], ALU.mult, ALU.mult,
            r=[kx, ('rs', sid), gkey], w=[ku])
        pbf = psum[:, b7, :].bitcast(BF16)
        for k in range(8):
            tr(pbf[:, k * 128:k * 128 + nb], un_[0:nb, k * 128:(k + 1) * 128], ident[0:nb, 0:nb],
               r=[ku, ('c', 'ident')], w=[('ps', b7)])
        cp('act', dstT[:, :, dcol0:dcol0 + nb], pbf.rearrange('p (k t) -> p k t', k=8)[:, :, 0:nb], r=[('ps', b7)],
           w=[dkey])

    def norm_sets(n):
        out = []
        for i in range(n):
            out.append((aalloc('nbx', [128, D], F32), aalloc('nbu', [128, D], BF16), aalloc('nbj', [128, D], BF16),
                        20 + 3 * i, i))
        return out

    def rowsum_rstd(banks, nb, col):
        for hf, b in enumerate(banks):
            act(junk[0:nb, hf * 512:(hf + 1) * 512], ps(b, nb, 512), AF.Square, r=[('ps', b)], w=[('junk',)])
            S.add('dve', lambda e, hf=hf: e.reduce_sum(out=small[0:nb, 4 + hf:5 + hf],
                                                       in_=junk[0:nb, hf * 512:(hf + 1) * 512],
                                                       axis=mybir.AxisListType.X), r=[('junk',)], w=[('ssx', hf)])
        tt('dve', small[0:nb, 6:7], small[0:nb, 4:5], small[0:nb, 5:6], ALU.add, r=[('ssx', 0), ('ssx', 1)],
           w=[('ssx', 2)])
        act(small[0:nb, 7:8], small[0:nb, 6:7], AF.Sqrt, r=[('ssx', 2), ('c', 'eps')], w=[('ssx', 3)],
            bias=eps_t[0:nb, :], scale=1.0 / D)
        recip(small[0:nb, col:col + 1], small[0:nb, 7:8], r=[('ssx', 3)], w=[('rsx',)])

    def proj(bank, wt, wkey, c0, M, src, skey, tok0, nt, kt=8):
        for k in range(kt):
            mm(ps(bank, M, nt), wt[:, k, c0:c0 + M], src[:, k, tok0:tok0 + nt], k == 0, k == kt - 1,
               r=wkeys(wkey, c0, M) + [skey], w=[('ps', bank)])

    def gelu(dst, v, n, vkey, dkey, tmpa, tmpb, P=128):
        act(tmpa[0:P, 0:n], v, AF.Square, r=[vkey], w=[('ga',)])
        ts('dve', tmpa[0:P, 0:n], tmpa[0:P, 0:n], 0.044715, 1.0, ALU.mult, ALU.add, r=[('ga',)], w=[('ga',)])
        tt('dve', tmpa[0:P, 0:n], tmpa[0:P, 0:n], v, ALU.mult, r=[('ga',), vkey], w=[('ga',)])
        act(tmpb[0:P, 0:n], tmpa[0:P, 0:n], AF.Sigmoid, r=[('ga',)], w=[('gb',)], scale=GC)
        tt('pool', dst, tmpb[0:P, 0:n], v, ALU.mult, r=[('gb',), vkey], w=[dkey])

    def s5_setup(L):
        arena[0] = PH0
        aR = aalloc('aR', [128, 16], F32); aI = aalloc('aI', [128, 16], F32); ldt = aalloc('ldt', [128, 16], F32)
        t = [aalloc('s5t', [128, 16], F32) for _ in range(12)]
        bR = aalloc('bR', [128, 16, 16], F32); bI = aalloc('bI', [128, 16, 16], F32)
        cR = aalloc('cR', [128, 16, 16], F32); cI = aalloc('cI', [128, 16, 16], F32)
        u1 = aalloc('u1', [128, 16, 16], F32); u2 = aalloc('u2', [128, 16, 16], F32)
        bsrc = [aalloc('bsrc', [128, 16, 32], BF16) for _ in range(2)]
        K = ('s5c', L)
        vec_load(aR[:, :], 's5_a_re', L, [[1, 128], [128, 16]], K)
        vec_load(aI[:, :], 's5_a_im', L, [[1, 128], [128, 16]], K)
        for e in range(2):
            vec_load(ldt[e * 64:(e + 1) * 64, :], 's5_log_dt', L, [[0, 64], [2, 16]], K, off=e)
        vec_load(bR[:, :, :], 's5_b_re', L, [[16, 128], [2048, 16], [1, 16]], K, slow=False)
        vec_load(bI[:, :, :], 's5_b_im', L, [[16, 128], [2048, 16], [1, 16]], K, slow=False)
        for e in range(2):
            for pr in range(16):
                vec_load(cR[e * 64:(e + 1) * 64, pr, :], 's5_c_re', L, [[1, 64], [64, 16]], K, off=e * 1024 + pr * 2048)
                vec_load(cI[e * 64:(e + 1) * 64, pr, :], 's5_c_im', L, [[1, 64], [64, 16]], K, off=e * 1024 + pr * 2048)
        vec_load(Dd[:, :], 's5_d', L, [[1, 32], [32, 16]], K)
        R = [K]
        dt_, dre, dim, mag, sn, cs, A1r, den, zr, fre, fim, tmp = t
        act(dt_[:, :], ldt[:, :], AF.Exp, r=R, w=R)
        tt('dve', dre[:, :], dt_[:, :], aR[:, :], ALU.mult, r=R, w=R)
        tt('dve', dim[:, :], dt_[:, :], aI[:, :], ALU.mult, r=R, w=R)
        act(mag[:, :], dre[:, :], AF.Exp, r=R, w=R)
        cp('dve', sn[:, :], dim[:, :], r=R, w=R)
        ts('dve', cs[:, :], dim[:, :], 0.5 * PI, None, ALU.add, None, r=R, w=R)
        for arr in (sn, cs):
            cp('dve', den[:, :], arr[:, :], r=R, w=R)
            for th in (PI, 3 * PI, 5 * PI, 7 * PI):
                ts('dve', tmp[:, :], den[:, :], th, None, ALU.is_ge, None, r=R, w=R)
                stt('dve', arr[:, :], tmp[:, :], -2 * PI, arr[:, :], ALU.mult, ALU.add, r=R, w=R)
            act(arr[:, :], arr[:, :], AF.Sin, r=R, w=R)
        tt('dve', A1r[:, :], mag[:, :], cs[:, :], ALU.mult, r=R, w=R)
        tt('dve', A1i[:, :], mag[:, :], sn[:, :], ALU.mult, r=R, w=R)
        ts('dve', nA1i[:, :], A1i[:, :], -1.0, None, ALU.mult, None, r=R, w=R)
        cp('dve', AA1[:, 0, :], A1r[:, :], r=R, w=R)
        cp('dve', AA1[:, 1, :], A1r[:, :], r=R, w=R)
        cp('dve', PW[:, 0, 0, :], A1r[:, :], r=R, w=R)
        cp('dve', PW[:, 0, 1, :], A1i[:, :], r=R, w=R)
        for k in range(7):
            tt('dve', tmp[:, :], PW[:, k, 0, :], A1r[:, :], ALU.mult, r=R, w=R)
            tt('dve', den[:, :], PW[:, k, 1, :], A1i[:, :], ALU.mult, r=R, w=R)
            tt('dve', PW[:, k + 1, 0, :], tmp[:, :], den[:, :], ALU.subtract, r=R, w=R)
            tt('dve', tmp[:, :], PW[:, k, 0, :], A1i[:, :], ALU.mult, r=R, w=R)
            tt('dve', den[:, :], PW[:, k, 1, :], A1r[:, :], ALU.mult, r=R, w=R)
            tt('dve', PW[:, k + 1, 1, :], tmp[:, :], den[:, :], ALU.add, r=R, w=R)
        for k in range(8):
            ts('dve', nPWi[:, k, :], PW[:, k, 1, :], -1.0, None, ALU.mult, None, r=R, w=R)
        cp('dve', AA8[:, 0, :], PW[:, 7, 0, :], r=R, w=R)
        cp('dve', AA8[:, 1, :], PW[:, 7, 0, :], r=R, w=R)
        tt('dve', den[:, :], aR[:, :], aR[:, :], ALU.mult, r=R, w=R)
        tt('dve', tmp[:, :], aI[:, :], aI[:, :], ALU.mult, r=R, w=R)
        tt('dve', den[:, :], den[:, :], tmp[:, :], ALU.add, r=R, w=R)
        recip(den[:, :], den[:, :], r=R, w=R)
        ts('dve', zr[:, :], A1r[:, :], -1.0, None, ALU.add, None, r=R, w=R)
        tt('dve', fre[:, :], zr[:, :], aR[:, :], ALU.mult, r=R, w=R)
        tt('dve', tmp[:, :], A1i[:, :], aI[:, :], ALU.mult, r=R, w=R)
        tt('dve', fre[:, :], fre[:, :], tmp[:, :], ALU.add, r=R, w=R)
        tt('dve', fre[:, :], fre[:, :], den[:, :], ALU.mult, r=R, w=R)
        tt('dve', fim[:, :], A1i[:, :], aR[:, :], ALU.mult, r=R, w=R)
        tt('dve', tmp[:, :], zr[:, :], aI[:, :], ALU.mult, r=R, w=R)
        tt('dve', fim[:, :], fim[:, :], tmp[:, :], ALU.subtract, r=R, w=R)
        tt('dve', fim[:, :], fim[:, :], den[:, :], ALU.mult, r=R, w=R)
        frb = fre[:, :].unsqueeze(2).to_broadcast([128, 16, 16])
        fib = fim[:, :].unsqueeze(2).to_broadcast([128, 16, 16])
        tt('dve', u1[:, :, :], bR[:, :, :], frb, ALU.mult, r=R, w=R)
        tt('dve', u2[:, :, :], bI[:, :, :], fib, ALU.mult, r=R, w=R)
        tt('dve', u1[:, :, :], u1[:, :, :], u2[:, :, :], ALU.subtract, r=R, w=R)
        tt('dve', u2[:, :, :], bI[:, :, :], frb, ALU.mult, r=R, w=R)
        tt('dve', bI[:, :, :], bR[:, :, :], fib, ALU.mult, r=R, w=R)
        tt('dve', u2[:, :, :], u2[:, :, :], bI[:, :, :], ALU.add, r=R, w=R)
        ts('dve', cI[:, :, :], cI[:, :, :], -1.0, None, ALU.mult, None, r=R, w=R)
        for ri, (bsrc_, srcb, srcc) in enumerate(((bsrc[0], u1, cR), (bsrc[1], u2, cI))):
            memset('dve', bsrc_[:, :, :], 0.0, R)
            memset('dve', Ec[:, ri, :, :], 0.0, R)
            for e in range(2):
                cp('dve', bsrc_[e * 64:(e + 1) * 64, :, e * 16:(e + 1) * 16], srcb[e * 64:(e + 1) * 64, :, :], r=R, w=R)
                cp('dve', Ec[e * 64:(e + 1) * 64, ri, :, e * 16:(e + 1) * 16], srcc[e * 64:(e + 1) * 64, :, :], r=R, w=R)
        pbf = psum[:, 7, :].bitcast(BF16)
        for ri in range(2):
            for g4 in range(2):
                for j in range(8):
                    pr = g4 * 8 + j
                    tr(pbf[0:32, j * 128:(j + 1) * 128], bsrc[ri][:, pr, :], ident[:, :], r=R + [('c', 'ident')],
                       w=[('ps', 7)])
                cp('act', BBT[:, g4 * 8:(g4 + 1) * 8, ri, :], pbf[0:32, :].rearrange('p (j q) -> p j q', j=8),
                   r=[('ps', 7)], w=R)

    SBANK = (6, 5)

    def prep_head(nt, hi, qa, qkeys, ka, kkeys, qscale, kscale, B):
        qd2, ki2, kd2, kdT, sT, E1, E2, E3, dec2 = B
        stt('dve', qd2[hi][:, 0:nt], qa, qscale, E1[:, 0:nt], ALU.mult, ALU.mult, r=qkeys + [('E1',)], w=[('qd', hi)])
        stt('dve', ki2[hi][:, 0:nt], ka, kscale, E2[:, 0:nt], ALU.mult, ALU.mult, r=kkeys + [('E2',)], w=[('ki', hi)])
        stt('dve', kd2[hi][:, 0:nt], ka, kscale, E3[:, 0:nt], ALU.mult, ALU.mult, r=kkeys + [('E3',)], w=[('kd', hi)])

    def chunk_pair(nt, heads, vtok, vcol0, stl, bfl, use_n, o_banks, den_banks, B):
        nch = nt // 64
        qd2, ki2, kd2, kdT, sT, E1, E2, E3, dec2 = B
        S32 = stl[0]
        Sbf = bfl[0]
        pbf = psum[:, 7, :].bitcast(BF16)
        for c in range(nch):
            cs = c * 64
            for hi, h in enumerate(heads):
                qd, ki, kd, dec = qd2[hi], ki2[hi], kd2[hi], dec2[hi]
                i2 = hi * 2 + c % 2
                P0 = (c % 2) * 64
                PS = slice(P0, P0 + 64)
                sb = SBANK[hi]
                o_bank = o_banks[hi]
                tr(pbf[PS, i2 * 128:(i2 + 1) * 128], kd[:, cs:cs + 64], ident[:, :], r=[('kd', hi), ('c', 'ident')],
                   w=[('ps7', i2)])
                cp('act', kdT[i2][PS, :], pbf[PS, i2 * 128:(i2 + 1) * 128], r=[('ps7', i2)], w=[('kdT', i2)])
                mm(psum[PS, sb, 0:64], ki[:, cs:cs + 64], qd[:, cs:cs + 64], True, True, r=[('ki', hi), ('qd', hi)],
                   w=[('ps', sb)])
                tt('dve', sT[i2][PS, :], psum[PS, sb, 0:64], maskut[PS, :], ALU.mult, r=[('ps', sb), ('c', 'mask')],
                   w=[('sT', i2)])
                vv = vtok[PS, c // 2, vcol0 + h * 128: vcol0 + (h + 1) * 128]
                mm(psum[:, o_bank, cs:cs + 64], vv, sT[i2][PS, :], True, False, r=[('vtok',), ('sT', i2)],
                   w=[('ps', o_bank)])
                mm(psum[:, o_bank, cs:cs + 64], Sbf[:, h, :], qd[:, cs:cs + 64], False, True,
                   r=[('stbf', id(Sbf), h), ('qd', hi)], w=[('ps', o_bank)])
                if use_n:
                    Nbf = bfl[1]
                    den_bank = den_banks[hi]
                    mm(psum[:, den_bank, cs:cs + 64], onesbf[PS, :], sT[i2][PS, :], True, False,
                       r=[('c', 'onesbf'), ('sT', i2)], w=[('ps', den_bank)])
                    mm(psum[:, den_bank, cs:cs + 64], Nbf[:, h, :], qd[:, cs:cs + 64], False, True,
                       r=[('stbf', id(Nbf), h), ('qd', hi)], w=[('ps', den_bank)])
                mm(psum[:, sb, 128:256], kdT[i2][PS, :], vv, True, True, r=[('kdT', i2), ('vtok',)], w=[('ps', sb)])
                stt('dve', S32[:, h, :], S32[:, h, :], dec[:, c:c + 1], psum[:, sb, 128:256], ALU.mult, ALU.add,
                    r=[('st', id(S32), h), ('dec', hi), ('ps', sb)], w=[('st', id(S32), h)])
                cp('act', Sbf[:, h, :], S32[:, h, :], r=[('st', id(S32), h)], w=[('stbf', id(Sbf), h)])
                if use_n:
                    N32 = stl[1]
                    mm(psum[:, sb, 256:384], kdT[i2][PS, :], onesbf[PS, :], True, True,
                       r=[('kdT', i2), ('c', 'onesbf')], w=[('ps', sb)])
                    stt('dve', N32[:, h, :], N32[:, h, :], dec[:, c:c + 1], psum[:, sb, 256:384], ALU.mult, ALU.add,
                        r=[('st', id(N32), h), ('dec', hi), ('ps', sb)], w=[('st', id(N32), h)])
                    cp('act', Nbf[:, h, :], N32[:, h, :], r=[('st', id(N32), h)], w=[('stbf', id(Nbf), h)])

    def vproj(tok0, nt, wt, wkey, c0, vtok):
        for blk in range(nt // 128):
            b = nextbank()
            for k in range(8):
                mm(ps(b, 128, 512), uT[:, k, tok0 + blk * 128: tok0 + blk * 128 + 128], wt[:, k, c0:c0 + 512],
                   k == 0, k == 7, r=wkeys(wkey, c0, 512) + [('uT',)], w=[('ps', b)])
            cp('act', vtok[:, blk, :], ps(b, 128, 512), r=[('ps', b)], w=[('vtok',)])

    def merge_out(tok0, nt, oT, okey, wpr, wprkey, wgate, wgkey, gc0, first, gsig, gtmp):
        for co in range(8):
            b = nextbank()
            for k in range(4):
                mm(ps(b, 128, nt), wpr[:, k, co * 128:(co + 1) * 128], oT[:, k, 0:nt], k == 0, k == 3,
                   r=[wprkey, okey], w=[('ps', b)])
            b2 = nextbank()
            proj(b2, wgate, wgkey, gc0 + co * 128, 128, uT, ('uT',), tok0, nt)
            act(gsig[:, 0:nt], ps(b2, 128, nt), AF.Sigmoid, r=[('ps', b2)], w=[('gsig',)])
            if first:
                tt('dve', merged[:, co, tok0:tok0 + nt], ps(b, 128, nt), gsig[:, 0:nt], ALU.mult,
                   r=[('ps', b), ('gsig',)], w=[('mg',)])
            else:
                tt('dve', gtmp[:, 0:nt], ps(b, 128, nt), gsig[:, 0:nt], ALU.mult, r=[('ps', b), ('gsig',)],
                   w=[('gtmp',)])
                tt('pool', merged[:, co, tok0:tok0 + nt], merged[:, co, tok0:tok0 + nt], gtmp[:, 0:nt], ALU.add,
                   r=[('gtmp',), ('mg',)], w=[('mg',)])

    def common_bufs():
        qd2 = [aalloc('qd', [128, 512], BF16) for _ in range(2)]
        ki2 = [aalloc('ki', [128, 512], BF16) for _ in range(2)]
        kd2 = [aalloc('kd', [128, 512], BF16) for _ in range(2)]
        kdT = [aalloc('kdT', [128, 128], BF16) for _ in range(4)]
        sT = [aalloc('sT', [128, 64], BF16) for _ in range(4)]
        E1 = aalloc('E1', [128, 512], F32); E2 = aalloc('E2', [128, 512], F32); E3 = aalloc('E3', [128, 512], F32)
        dec2 = [aalloc('dec', [128, 8], F32) for _ in range(2)]
        return (qd2, ki2, kd2, kdT, sT, E1, E2, E3, dec2)

    def phase_gla(L):
        arena[0] = PH0
        Wg = aalloc('Wg', [128, 8, 3088], BF16)
        Wgk = aalloc('Wgk', [16, 1, 512], BF16)
        Wpr = aalloc('Wpr', [128, 4, D], BF16)
        bgk = aalloc('bgk', [128, 4], F32); gln = aalloc('gln', [128, 1], F32)
        Bx = aalloc('Bx', [128, 513], F32)
        Drel = aalloc('Drel', [128, 512], F32); Drel2 = aalloc('Drel2', [128, 512], F32)
        B = common_bufs()
        qd2, ki2, kd2, kdT, sT, E1, E2, E3, dec2 = B
        vtok = aalloc('vtok', [128, 4, 512], BF16)
        lrT = aalloc('lrT', [16, 512], BF16)
        sq = aalloc('sq', [128, 512], BF16)
        rstd = E1; t1 = E2; sr = E3
        ogT = aalloc('ogT', [128, 4, 512], BF16)
        gsig = aalloc('gsig', [128, 512], F32); gtmp = aalloc('gtmp', [128, 512], F32)
        spb = gtmp
        K = ('W', 'g')
        KW = WK(('Wc', 'g'))
        load_wc(Wg, KW, 'w_in', L, 8, 1024, 1024, 1040)
        load_wc(Wg, KW, 'w_in', L, 8, 0, 0, 1024)
        load_wc(Wg, KW, 'w_in', L, 8, 2064, O_GATE, 1024)
        load_w(Wgk[:, :, :], K, 'gla_w_gk', L, 0, 1, 0, 512, rows=16)
        load_w(Wpr[:, :, :], K, 'gla_w_proj', L, 0, 4, 0, D)
        vec_load(bgk[:, :], 'gla_b_gk', L, [[1, 128], [128, 4]], K)
        ts('dve', bgk[:, :], bgk[:, :], -1.0, None, ALU.mult, None, r=[K], w=[K])
        vec_load(gln[:, :], 'gla_norm', L, [[1, 128], [1, 1]], K)
        memset('pool', Bx[:, 0:1], 0.0, [('Bx',)])
        for tok0 in range(0, SEGT, TILE):
            nt = TILE
            nch = nt // 64
            vproj(tok0, nt, Wg, KW, O_GV, vtok)
            b = nextbank()
            proj(b, Wg, KW, O_LR, 16, uT, ('uT',), tok0, nt)
            cp('act', lrT[:, 0:nt], ps(b, 16, nt), r=[('ps', b)], w=[('lrT',)])
            for pair in ((0, 1), (2, 3)):
                for hi, h in enumerate(pair):
                    dec = dec2[hi]
                    b = nextbank()
                    mm(ps(b, 128, nt), Wgk[:, 0, h * 128:(h + 1) * 128], lrT[:, 0:nt], True, True, r=[K, ('lrT',)],
                       w=[('ps', b)])
                    act(spb[:, 0:nt], ps(b, 128, nt), AF.Exp, r=[('ps', b), K], w=[('gtmp',)], bias=bgk[:, h:h + 1],
                        scale=-1.0)
                    act(spb[:, 0:nt], spb[:, 0:nt], AF.Ln, r=[('gtmp',)], w=[('gtmp',)], bias=1.0)
                    S.add('dve', lambda e, nt=nt: e.tensor_tensor_scan(out=Bx[:, 1:1 + nt], data0=ones32[:, 0:nt],
                                                                       data1=spb[:, 0:nt], initial=0.0,
                                                                       op0=ALU.mult, op1=ALU.add),
                          r=[('gtmp',), ('c', 'ones32')], w=[('Bx',)])
                    d3 = Drel[:, 0:nt].rearrange('p (c l) -> p c l', l=64)
                    d23 = Drel2[:, 0:nt].rearrange('p (c l) -> p c l', l=64)
                    bx3 = Bx[:, 1:1 + nt].rearrange('p (c l) -> p c l', l=64)
                    br3 = Bx[:, 0:nt].rearrange('p (c l) -> p c l', l=64)[:, :, 0:1]
                    tt('dve', d3, bx3, br3.to_broadcast([128, nch, 64]), ALU.subtract, r=[('Bx',)], w=[('Drel',)])
                    tt('dve', d23, d3, d3[:, :, 63:64].to_broadcast([128, nch, 64]), ALU.subtract, r=[('Drel',)],
                       w=[('Drel2',)])
                    act(E1[:, 0:nt], Drel[:, 0:nt], AF.Exp, r=[('Drel',)], w=[('E1',)], scale=-1.0 / 16)
                    act(dec[:, 0:nch], Drel[:, 63:nt:64], AF.Exp, r=[('Drel',)], w=[('dec', hi)], scale=-1.0 / 16)
                    act(E2[:, 0:nt], Drel[:, 0:nt], AF.Exp, r=[('Drel',)], w=[('E2',)], scale=1.0 / 16)
                    act(E3[:, 0:nt], Drel2[:, 0:nt], AF.Exp, r=[('Drel2',)], w=[('E3',)], scale=1.0 / 16)
                    bq = nextbank()
                    proj(bq, Wg, KW, O_GQ + h * 128, 128, uT, ('uT',), tok0, nt)
                    bk = nextbank()
                    proj(bk, Wg, KW, O_GK + h * 128, 128, uT, ('uT',), tok0, nt)
                    prep_head(nt, hi, ps(bq, 128, nt), [('ps', bq)], ps(bk, 128, nt), [('ps', bk)], 128 ** -0.5, 1.0, B)
                obs = [nextbank(), nextbank()]
                chunk_pair(nt, pair, vtok, 0, [St32], [Stbf], False, obs, None, B)
                for hi, h in enumerate(pair):
                    ob = obs[hi]
                    act(sq[:, 0:nt], psum[:, ob, 0:nt], AF.Square, r=[('ps', ob)], w=[('sq',)])
                    b = nextbank()
                    mm(ps(b, 128, nt), onesbf[:, :], sq[:, 0:nt], True, True, r=[('c', 'onesbf'), ('sq',)],
                       w=[('ps', b)])
                    act(rstd[:, 0:nt], ps(b, 128, nt), AF.Sqrt, r=[('ps', b), ('c', 'eps')], w=[('E1',)],
                        bias=eps_t[:, :], scale=1.0 / 128)
                    recip(rstd[:, 0:nt], rstd[:, 0:nt], r=[('E1',)], w=[('E1',)])
                    tt('dve', t1[:, 0:nt], psum[:, ob, 0:nt], rstd[:, 0:nt], ALU.mult, r=[('ps', ob), ('E1',)],
                       w=[('E2',)])
                    b = nextbank()
                    proj(b, Wg, KW, O_GR + h * 128, 128, uT, ('uT',), tok0, nt)
                    act(sr[:, 0:nt], ps(b, 128, nt), AF.Silu, r=[('ps', b)], w=[('E3',)])
                    stt('dve', ogT[:, h, 0:nt], sr[:, 0:nt], gln[:, 0:1], t1[:, 0:nt], ALU.mult, ALU.mult,
                        r=[('E3',), K, ('E2',)], w=[('ogT',)])
            merge_out(tok0, nt, ogT, ('ogT',), Wpr, K, Wg, KW, 2064, True, gsig, gtmp)

    def phase_ml(L):
        arena[0] = PH0
        Wm = aalloc('Wm', [128, 8, 3080], BF16)
        Wpr = aalloc('Wmp', [128, 4, D], BF16)
        B = common_bufs()
        qd2, ki2, kd2, kdT, sT, E1, E2, E3, dec2 = B
        vtok = aalloc('vtok', [128, 4, 512], BF16)
        zc = aalloc('zc', [128, 515], F32)
        qh = aalloc('qh', [128, 512], BF16); kh = aalloc('kh', [128, 512], BF16)
        B8 = aalloc('B8', [8, 513], F32)
        T2 = aalloc('T2', [8, 512], F32); T3 = aalloc('T3', [8, 512], F32)
        bif = aalloc('bif', [8, 1], F32); cw = aalloc('cw', [128, 8, 4], F32); cb = aalloc('cb', [128, 8], F32)
        dabs = E1; t1 = E2; so = E3
        ogT = aalloc('hT', [128, 4, 512], BF16)
        gsig = aalloc('gsig', [128, 512], F32); gtmp = aalloc('gtmp', [128, 512], F32)
        cacc = gtmp; gi8 = gsig[0:8, :]
        K = ('W', 'm')
        KW = WK(('Wc', 'm'))
        load_wc(Wm, KW, 'w_in', L, 8, 1024, O_MQ + 1024, 1032)
        load_wc(Wm, KW, 'w_in', L, 8, 0, O_MQ, 1024)
        load_wc(Wm, KW, 'w_in', L, 8, 2056, O_GATE + 2048, 1024)
        load_w(Wpr[:, :, :], K, 'ml_w_proj', L, 0, 4, 0, D)
        vec_load(bif[0:4, :], 'ml_b_i', L, [[1, 4], [1, 1]], K)
        vec_load(bif[4:8, :], 'ml_b_f', L, [[1, 4], [1, 1]], K)
        for j in range(4):
            vec_load(cw[:, :, j], 'ml_conv_w', L, [[1, 128], [128, 8]], K, off=j * 1024)
        vec_load(cb[:, :], 'ml_conv_b', L, [[1, 128], [128, 8]], K)
        memset('pool', B8[:, 0:1], 0.0, [('B8',)])
        for tok0 in range(0, SEGT, TILE):
            nt = TILE
            nch = nt // 64
            vproj(tok0, nt, Wm, KW, 1024, vtok)
            b = nextbank()
            proj(b, Wm, KW, 1536, 8, uT, ('uT',), tok0, nt)
            act(gi8[:, 0:nt], ps(b, 8, nt), AF.Identity, r=[('ps', b), K], w=[('gsig',)], bias=bif[:, :])
            act(T3[:, 0:nt], gi8[:, 0:nt], AF.Exp, r=[('gsig',)], w=[('T3',)], scale=-1.0)
            act(T3[:, 0:nt], T3[:, 0:nt], AF.Ln, r=[('T3',)], w=[('T3',)], bias=1.0)
            S.add('dve', lambda e, nt=nt: e.tensor_tensor_scan(out=B8[:, 1:1 + nt], data0=ones32[0:8, 0:nt],
                                                               data1=T3[:, 0:nt], initial=0.0, op0=ALU.mult,
                                                               op1=ALU.add),
                  r=[('T3',), ('c', 'ones32')], w=[('B8',)])
            d3 = T2[:, 0:nt].rearrange('p (c l) -> p c l', l=64)
            d23 = T3[:, 0:nt].rearrange('p (c l) -> p c l', l=64)
            bx3 = B8[:, 1:1 + nt].rearrange('p (c l) -> p c l', l=64)
            br3 = B8[:, 0:nt].rearrange('p (c l) -> p c l', l=64)[:, :, 0:1]
            tt('dve', d3, bx3, br3.to_broadcast([8, nch, 64]), ALU.subtract, r=[('B8',)], w=[('T2',)])
            tt('dve', d23, d3, d3[:, :, 63:64].to_broadcast([8, nch, 64]), ALU.subtract, r=[('T2',), ('B8',)],
               w=[('T3',)])
            cp('dve', T2[0:4, 0:nt], gi8[0:4, 0:nt], r=[('gsig',), ('T2',)], w=[('T2',)])
            cp('dve', T3[0:4, 0:nt], gi8[0:4, 0:nt], r=[('gsig',), ('T3',)], w=[('T3',)])
            for pair in ((0, 1), (2, 3)):
                for hi, h in enumerate(pair):
                    dec = dec2[hi]
                    b1 = nextbank()
                    mm(ps(b1, 128, nt), selF[:, h, :], T2[:, 0:nt], True, True, r=[('c', 'sel'), ('T2',)],
                       w=[('ps', b1)])
                    act(E1[:, 0:nt], ps(b1, 128, nt), AF.Exp, r=[('ps', b1)], w=[('E1',)], scale=-1.0)
                    act(dec[:, 0:nch], psum[:, b1, 63:nt:64], AF.Exp, r=[('ps', b1)], w=[('dec', hi)], scale=-1.0)
                    b2 = nextbank()
                    mm(ps(b2, 128, nt), selIF[:, h, :], T2[:, 0:nt], True, True, r=[('c', 'sel'), ('T2',)],
                       w=[('ps', b2)])
                    act(E2[:, 0:nt], ps(b2, 128, nt), AF.Exp, r=[('ps', b2)], w=[('E2',)])
                    b3 = nextbank()
                    mm(ps(b3, 128, nt), selIF[:, h, :], T3[:, 0:nt], True, True, r=[('c', 'sel'), ('T3',)],
                       w=[('ps', b3)])
                    act(E3[:, 0:nt], ps(b3, 128, nt), AF.Exp, r=[('ps', b3)], w=[('E3',)])
                    for which, dstq in ((0, qh), (1, kh)):
                        ct = which * 4 + h
                        b = nextbank()
                        proj(b, Wm, KW, ct * 128, 128, uT, ('uT',), tok0, nt)
                        cp('act', zc[:, 3:3 + nt], ps(b, 128, nt), r=[('ps', b)], w=[('zc',)])
                        cp('pool', zc[:, 0:3], zch[:, ct, :], r=[('zch', ct)], w=[('zc',)])
                        ts('dve', cacc[:, 0:nt], zc[:, 0:nt], cw[:, ct, 0:1], None, ALU.mult, None, r=[('zc',), K],
                           w=[('gtmp',)])
                        for j in range(1, 4):
                            stt('dve', cacc[:, 0:nt], zc[:, j:j + nt], cw[:, ct, j:j + 1], cacc[:, 0:nt], ALU.mult,
                                ALU.add, r=[('zc',), K, ('gtmp',)], w=[('gtmp',)])
                        cp('pool', zch[:, ct, :], zc[:, nt:nt + 3], r=[('zc',)], w=[('zch', ct)])
                        act(dstq[:, 0:nt], cacc[:, 0:nt], AF.Silu, r=[('gtmp',), K], w=[('qk', which)],
                            bias=cb[:, ct:ct + 1])
                    prep_head(nt, hi, qh[:, 0:nt], [('qk', 0)], kh[:, 0:nt], [('qk', 1)], 1.0, 128 ** -0.5, B)
                obs = [nextbank(), nextbank()]
                dbs = [nextbank(), nextbank()]
                chunk_pair(nt, pair, vtok, 0, [Ct32, Nt32], [Ctbf, Ntbf], True, obs, dbs, B)
                for hi, h in enumerate(pair):
                    ob, db = obs[hi], dbs[hi]
                    ts('dve', dabs[:, 0:nt], psum[:, db, 0:nt], -1.0, 1.0, ALU.mult, ALU.max, r=[('ps', db)],
                       w=[('E1',)])
                    tt('dve', dabs[:, 0:nt], dabs[:, 0:nt], psum[:, db, 0:nt], ALU.max, r=[('ps', db), ('E1',)],
                       w=[('E1',)])
                    recip(dabs[:, 0:nt], dabs[:, 0:nt], r=[('E1',)], w=[('E1',)])
                    tt('dve', t1[:, 0:nt], psum[:, ob, 0:nt], dabs[:, 0:nt], ALU.mult, r=[('ps', ob), ('E1',)],
                       w=[('E2',)])
                    b = nextbank()
                    proj(b, Wm, KW, 1544 + h * 128, 128, uT, ('uT',), tok0, nt)
                    act(so[:, 0:nt], ps(b, 128, nt), AF.Sigmoid, r=[('ps', b)], w=[('E3',)])
                    tt('dve', ogT[:, h, 0:nt], t1[:, 0:nt], so[:, 0:nt], ALU.mult, r=[('E2',), ('E3',)], w=[('ogT',)])
            merge_out(tok0, nt, ogT, ('ogT',), Wpr, K, Wm, KW, 2056, False, gsig, gtmp)

    def phase_s5(L):
        arena[0] = PH0
        TB = 128
        NCH = TB // 8
        Ws = aalloc('Ws', [128, 8, 1536], BF16)
        Wglu = aalloc('Wglu', [128, 4, 512], BF16)
        Wsp = aalloc('Wsp', [128, 4, D], BF16)
        bglu = aalloc('bglu', [128, 4], F32)
        su128 = aalloc('su128', [128, 4, 512], BF16)
        sup = aalloc('sup', [32, 16, TB], BF16)
        GU = aalloc('GU', [128, 2, 16, TB], F32)
        XC = aalloc('XC', [128, 2, 16, NCH + 1], F32)
        XXbf = aalloc('XXbf', [128, 2, 16, TB], BF16)
        P1 = aalloc('P1', [128, 2, 16, NCH], F32); P2 = aalloc('P2', [128, 2, 16, NCH], F32)
        P1s = aalloc('P1s', [128, 2, 16], F32); P2s = aalloc('P2s', [128, 2, 16], F32)
        yp = aalloc('yp', [32, 16, TB], BF16)
        yv = aalloc('yv', [128, 512], F32)
        ga = aalloc('ga', [128, 512], F32)
        yg = aalloc('yg', [128, 4, 512], BF16)
        y2 = aalloc('y2', [128, 4, 512], BF16)
        gsig = aalloc('gsig', [128, 512], F32); gtmp = aalloc('gtmp', [128, 512], F32)
        gb = gtmp
        K = ('W', 's')
        KC = ('s5c', L)
        KW = WK(('Wc', 's'))
        load_wc(Ws, KW, 'w_in', L, 8, 0, O_SU, 512)
        load_wc(Ws, KW, 'w_in', L, 8, 512, O_GATE + 1024, 1024)
        load_w(Wglu[:, :, :], K, 's5_w_glu', L, 0, 4, 0, 512)
        load_w(Wsp[:, :, :], K, 's5_w_proj', L, 0, 4, 0, D)
        vec_load(bglu[:, :], 's5_b_glu', L, [[1, 128], [128, 4]], K)
        cp('dve', XC[:, :, :, 0], XXst[:, :, :], r=[('xxst',)], w=[('XC',)])
        GU5 = GU[:, :, :, :].rearrange('p r g (l n) -> p r g l n', l=8)

        def G(tau):
            return GU5[:, :, :, tau, :]

        def cmul_acc(dst, src, k, dkey, skey):
            prb = PW[:, k, 0, :].unsqueeze(1).unsqueeze(3).to_broadcast([128, 2, 16, NCH])
            pib = PW[:, k, 1, :].unsqueeze(2).to_broadcast([128, 16, NCH])
            npib = nPWi[:, k, :].unsqueeze(2).to_broadcast([128, 16, NCH])
            tt('dve', P1[:, :, :, :], src, prb, ALU.mult, r=[skey, KC], w=[('P1',)])
            tt(P2ENG, P2[:, 0, :, :], src[:, 1], npib, ALU.mult, r=[skey, KC], w=[('P2',)])
            tt(P2ENG, P2[:, 1, :, :], src[:, 0], pib, ALU.mult, r=[skey, KC], w=[('P2',)])
            tt('dve', dst, dst, P1[:, :, :, :], ALU.add, r=[dkey, ('P1',)], w=[dkey])
            tt('dve', dst, dst, P2[:, :, :, :], ALU.add, r=[dkey, ('P2',)], w=[dkey])

        for tok0 in range(0, SEGT, TILE):
            nt = TILE
            for blk in range(4):
                b = nextbank()
                proj(b, Ws, KW, blk * 128, 128, uT, ('uT',), tok0, nt)
                cp('act', su128[:, blk, 0:nt], ps(b, 128, nt), r=[('ps', b)], w=[('su128',)])
            for tb in range(0, nt, TB):
                for blk in range(4):
                    b = nextbank()
                    for k in range(4):
                        mm(psum[0:32, b, k * TB:(k + 1) * TB], ident[:, 32 * k:32 * k + 32],
                           su128[:, blk, tb:tb + TB].rearrange('p (n l) -> p l n', l=8), True, True,
                           r=[('c', 'ident'), ('su128',)], w=[('ps', b)])
                    cp('act', sup[:, blk * 4:(blk + 1) * 4, :], ps(b, 32, 4 * TB).rearrange('p (k t) -> p k t', k=4),
                       r=[('ps', b)], w=[('sup',)])
                for ri in range(2):
                    for g in range(4):
                        b = nextbank()
                        for k in range(4):
                            pr = g * 4 + k
                            mm(psum[:, b, k * TB:(k + 1) * TB], BBT[:, pr, ri, :], sup[:, pr, :], True, True,
                               r=[KC, ('sup',)], w=[('ps', b)])
                        cp('act', GU[:, ri, g * 4:(g + 1) * 4, :], ps(b, 128, 4 * TB).rearrange('p (k t) -> p k t', k=4),
                           r=[('ps', b)], w=[('GU',)])
                for tau in range(1, 8):
                    cmul_acc(G(tau), G(tau - 1), 0, ('GU',), ('GU',))
                for n in range(NCH):
                    tt('dve', P1s[:, :, :], XC[:, :, :, n], AA8[:, :, :], ALU.mult, r=[('XC',), KC], w=[('P1s',)])
                    tt('dve', P2s[:, 0, :], XC[:, 1, :, n], nPWi[:, 7, :], ALU.mult, r=[('XC',), KC], w=[('P2s',)])
                    tt('dve', P2s[:, 1, :], XC[:, 0, :, n], PW[:, 7, 1, :], ALU.mult, r=[('XC',), KC], w=[('P2s',)])
                    tt('dve', P1s[:, :, :], P1s[:, :, :], P2s[:, :, :], ALU.add, r=[('P1s',), ('P2s',)], w=[('P1s',)])
                    tt('dve', XC[:, :, :, n + 1], P1s[:, :, :], GU5[:, :, :, 7, n], ALU.add, r=[('P1s',), ('GU',)],
                       w=[('XC',)])
                for tau in range(7):
                    cmul_acc(G(tau), XC[:, :, :, 0:NCH], tau, ('GU',), ('XC',))
                cp('dve', G(7), XC[:, :, :, 1:NCH + 1], r=[('XC',)], w=[('GU',)])
                cp('act', XXbf[:, :, :, :], GU[:, :, :, :], r=[('GU',)], w=[('XXbf',)])
                cp('dve', XC[:, :, :, 0], XC[:, :, :, NCH], r=[('XC',)], w=[('XC',)])
                for g in range(4):
                    b = nextbank()
                    for k in range(4):
                        pr = g * 4 + k
                        mm(psum[0:32, b, k * TB:(k + 1) * TB], Ec[:, 0, pr, :], XXbf[:, 0, pr, :], True, False,
                           r=[KC, ('XXbf',)], w=[('ps', b)])
                        mm(psum[0:32, b, k * TB:(k + 1) * TB], Ec[:, 1, pr, :], XXbf[:, 1, pr, :], False, True,
                           r=[KC, ('XXbf',)], w=[('ps', b)])
                    for k in range(4):
                        pr = g * 4 + k
                        stt('dve', yp[:, pr, :], sup[:, pr, :], Dd[:, pr:pr + 1], psum[0:32, b, k * TB:(k + 1) * TB],
                            ALU.mult, ALU.add, r=[('sup',), KC, ('ps', b)], w=[('yp',)])
                b = nextbank()
                for blk in range(4):
                    for k in range(4):
                        mm(psum[:, b, blk * TB:(blk + 1) * TB], gsel[:, k, :],
                           yp[:, blk * 4 + k, :].rearrange('p (l n) -> p n l', l=8), k == 0, k == 3,
                           r=[('c', 'gsel'), ('yp',)], w=[('ps', b)])
                cp('act', yv[:, :], ps(b, 128, 4 * TB), r=[('ps', b)], w=[('yv',)])
                act(ga[:, :], yv[:, :], AF.Square, r=[('yv',)], w=[('ga',)])
                ts('dve', ga[:, :], ga[:, :], 0.044715, 1.0, ALU.mult, ALU.add, r=[('ga',)], w=[('ga',)])
                tt('dve', ga[:, :], ga[:, :], yv[:, :], ALU.mult, r=[('ga',), ('yv',)], w=[('ga',)])
                act(gb[:, :], ga[:, :], AF.Sigmoid, r=[('ga',)], w=[('gtmp',)], scale=GC)
                tt('pool', yg[:, :, tb:tb + TB], gb[:, :].rearrange('p (k t) -> p k t', k=4),
                   yv[:, :].rearrange('p (k t) -> p k t', k=4), ALU.mult, r=[('gtmp',), ('yv',)], w=[('yg',)])
            for co in range(4):
                b = nextbank()
                for k in range(4):
                    mm(ps(b, 128, nt), Wglu[:, k, co * 128:(co + 1) * 128], yg[:, k, 0:nt], k == 0, k == 3,
                       r=[K, ('yg',)], w=[('ps', b)])
                act(gsig[:, 0:nt], ps(b, 128, nt), AF.Sigmoid, r=[('ps', b), K], w=[('gsig',)], bias=bglu[:, co:co + 1])
                tt('dve', y2[:, co, 0:nt], yg[:, co, 0:nt], gsig[:, 0:nt], ALU.mult, r=[('yg',), ('gsig',)],
                   w=[('y2',)])
            merge_out(tok0, nt, y2, ('y2',), Wsp, K, Ws, KW, 512, False, gsig, gtmp)
        cp('dve', XXst[:, :, :], XC[:, :, :, 0], r=[('XC',)], w=[('xxst',)])

    def phase_out(L, src_t, seg):
        arena[0] = PH0
        Wo = aalloc('Wo', [128, 8, D], BF16)
        t1 = aalloc('t1o', [128, D], F32)
        K = ('W', 'o')
        load_w(Wo[:, :, :], K, 'w_out', L, 0, 8, 0, D)
        if L == 0 and seg == 0:
            dump('mgall', merged[:, :, :], [('mg',)])
        for bi, tb in enumerate(range(0, SEGT, 128)):
            xi = bi % 2
            xb = xt[xi]
            dma(xb[:, :], rawap(src_t, (seg * SEGT + tb) * D, [[D, 128], [1, D]]), w=[('xt', 0)])
            banks = []
            for hf in range(2):
                b = nextbank()
                banks.append(b)
                for k in range(8):
                    mm(ps(b, 128, 512), merged[:, k, tb:tb + 128], Wo[:, k, hf * 512:(hf + 1) * 512], k == 0, k == 7,
                       r=[('mg',), K], w=[('ps', b)])
            rowsum_rstd(banks, 128, 8)
            for hf, b in enumerate(banks):
                tt('dve', t1[:, hf * 512:(hf + 1) * 512], ps(b, 128, 512), gains[1][:, hf * 512:(hf + 1) * 512], ALU.mult,
                   r=[('ps', b), ('c', 'gain')], w=[('t1o', hf)])
                stt('dve', xb[:, hf * 512:(hf + 1) * 512], t1[:, hf * 512:(hf + 1) * 512], small[:, 8:9],
                    xb[:, hf * 512:(hf + 1) * 512], ALU.mult, ALU.add, r=[('t1o', hf), ('rsx',), ('xt', 0)],
                    w=[('xt', 0)])
            if L == 0 and seg == 0 and tb == 128:
                dump('osmall', small[:, 0:16], [('rsx',), ('ssx', 0), ('ssx', 1), ('ssx', 2), ('ssx', 3)])
                dump('ot1', t1[:, :], [('t1o', 0), ('t1o', 1)])
                dump('ojunk', junk[:, :], [('junk',)])
            dma(rawap(xmid, tb * D, [[D, 128], [1, D]]), xb[:, :], r=[('xt', 0)], w=[('xmid', tb)])
            if L == 0 and seg == 0:
                dump('xmid%d' % tb, xb[:, :], [('xt', 0)])

    hs = nc.dram_tensor('hs', [22, 128, SEGT], BF16)

    def phase_ffn_a(L, seg):
        arena[0] = UT0
        Wup = aalloc('Wup', [128, 8, DFF], BF16)
        Wga = aalloc('Wga', [128, 8, DFF], BF16)
        NTF = 512
        uF = [aalloc('uF', [128, 8, NTF], BF16) for _ in range(2)]
        NB = 3
        fsets = [(aalloc('ab', [128, NTF + 2], F32), aalloc('vv', [128, NTF], F32), aalloc('gaF', [128, NTF], F32),
                  aalloc('vg', [128, NTF], F32), aalloc('ho', [128, NTF], BF16)) for _ in range(NB)]
        fw = aalloc('fw', [128, 22, 3], F32); fb = aalloc('fb', [128, 22], F32)
        nsets = norm_sets(2)
        nbi = [0]
        K = ('W', 'f')
        KU = WK(('Wc', 'fu'), 704)
        KG = WK(('Wc', 'fg'), 704)
        for q4 in range(4):
            load_wc(Wup, KU, 'ffn_w_up', L, 8, q4 * 704, q4 * 704, 704)
            load_wc(Wga, KG, 'ffn_w_gate', L, 8, q4 * 704, q4 * 704, 704)
        for j in range(3):
            vec_load(fw[:, :, j], 'ffn_conv_w', L, [[1, 128], [128, 22]], K, off=j * DFF)
        vec_load(fb[:, :], 'ffn_conv_b', L, [[1, 128], [128, 22]], K)
        dma(gains2[0][:, :], rawap(W['norm_ffn_pre'], L * D, [[0, 128], [1, D]]), w=[('c', 'gain')])
        gi = 0
        for ti, t0 in enumerate(range(0, SEGT, NTF)):
            uFt = uF[ti % 2]
            ukey = ('uF', ti % 2)
            for j in range(NTF // 128):
                tb = t0 + j * 128
                norm_block(rawap(xmid, tb * D, [[D, 128], [1, D]]), 128, gains[2], ('c', 'gain'), uFt, ukey, j * 128,
                           0, extra_r=[('xmid', tb)], bufs=nsets[nbi[0] % 2])
                nbi[0] += 1
            for ct in range(22):
                si = gi % NB
                gi += 1
                ab, vv, ga, vg, ho = fsets[si]
                b = nextbank()
                proj(b, Wup, KU, ct * 128, 128, uFt, ukey, 0, NTF)
                cp('act', ab[:, 2:2 + NTF], ps(b, 128, NTF), r=[('ps', b)], w=[('ab', si)])
                cp('pool', ab[:, 0:2], ahist[:, ct, :], r=[('ah', ct)], w=[('ab', si)])
                ts('dve', vv[:, :], ab[:, 0:NTF], fw[:, ct, 0:1], fb[:, ct:ct + 1], ALU.mult, ALU.add,
                   r=[('ab', si), K], w=[('vv', si)])
                for j in range(1, 3):
                    stt('dve', vv[:, :], ab[:, j:j + NTF], fw[:, ct, j:j + 1], vv[:, :], ALU.mult, ALU.add,
                        r=[('ab', si), K, ('vv', si)], w=[('vv', si)])
                cp('pool', ahist[:, ct, :], ab[:, NTF:NTF + 2], r=[('ab', si)], w=[('ah', ct)])
                b2 = nextbank()
                proj(b2, Wga, KG, ct * 128, 128, uFt, ukey, 0, NTF)
                tt('dve', vg[:, :], vv[:, :], ps(b2, 128, NTF), ALU.mult, r=[('vv', si), ('ps', b2)], w=[('vg', si)])
                act(ga[:, :], vv[:, :], AF.Square, r=[('vv', si)], w=[('ga', si)], scale=math.sqrt(0.044715))
                stt('dve', ga[:, :], ga[:, :], 1.0, vv[:, :], ALU.add, ALU.mult, r=[('ga', si), ('vv', si)],
                    w=[('ga', si)])
                act(ga[:, :], ga[:, :], AF.Sigmoid, r=[('ga', si)], w=[('ga', si)], scale=GC)
                tt('pool', ho[:, :], ga[:, :], vg[:, :], ALU.mult, r=[('ga', si), ('vg', si)], w=[('ho', si)])
                dma(rawap(hs, ct * 128 * SEGT + t0, [[SEGT, 128], [1, NTF]]), ho[:, :], r=[('ho', si)],
                    w=[('hs', ct, t0)])

    def phase_ffn_b(L, dst_t, seg):
        arena[0] = UT0
        Wd = aalloc('Wd', [128, 22, D], BF16)
        hb = [aalloc('hb', [128, 22, 512], BF16) for _ in range(2)]
        t1 = aalloc('t1f', [128, D], F32)
        xb2 = [aalloc('xb2', [128, D], F32) for _ in range(2)]
        K = ('W', 'fd')
        KD = WK(('Wc', 'fd'), 512)
        for hf in range(2):
            load_w(Wd[:, :, hf * 512:(hf + 1) * 512], (KD.base, hf), 'ffn_w_down', L, 0, 22, hf * 512, 512)
        dma(gains2[1][:, :], rawap(W['norm_ffn_post'], L * D, [[0, 128], [1, D]]), w=[('c', 'gain')])
        bi = 0
        for ti, t0 in enumerate(range(0, SEGT, 512)):
            hbt = hb[ti % 2]
            hkey = ('hb', ti % 2)
            for c0 in (0, 11):
                dma(hbt[:, c0:c0 + 11, :], rawap(hs, c0 * 128 * SEGT + t0, [[SEGT, 128], [128 * SEGT, 11], [1, 512]]),
                    r=[('hs', ct, t0) for ct in range(c0, c0 + 11)], w=[hkey])
            for j in range(4):
                tb = t0 + j * 128
                xi = bi % 2
                bi += 1
                xb = xb2[xi]
                dma(xb[:, :], rawap(xmid, tb * D, [[D, 128], [1, D]]), r=[('xmid', tb)], w=[('xb2', xi)])
                banks = []
                for hf in range(2):
                    b = nextbank()
                    banks.append(b)
                    for k in range(22):
                        mm(ps(b, 128, 512), hbt[:, k, j * 128:(j + 1) * 128], Wd[:, k, hf * 512:(hf + 1) * 512], k == 0,
                           k == 21, r=[hkey, (KD.base, hf)], w=[('ps', b)])
                rowsum_rstd(banks, 128, 9)
                for hf, b in enumerate(banks):
                    tt('dve', t1[:, hf * 512:(hf + 1) * 512], ps(b, 128, 512), gains[3][:, hf * 512:(hf + 1) * 512],
                       ALU.mult, r=[('ps', b), ('c', 'gain')], w=[('t1f', hf)])
                    stt('dve', xb[:, hf * 512:(hf + 1) * 512], t1[:, hf * 512:(hf + 1) * 512], small[:, 9:10],
                        xb[:, hf * 512:(hf + 1) * 512], ALU.mult, ALU.add, r=[('t1f', hf), ('rsx',), ('xb2', xi)],
                        w=[('xb2', xi)])
                dma(rawap(dst_t, (seg * SEGT + tb) * D, [[D, 128], [1, D]]), xb[:, :], r=[('xb2', xi)],
                    w=[('dst', seg, tb)])

    for L in range(nlayer):
        src_t = x_in if L == 0 else x1
        dst_t = out_t if L == nlayer - 1 else x1
        for stt_ in (St32, Ct32, Nt32):
            memset('pool', stt_[:, :, :], 0.0, [('st', id(stt_), h) for h in range(4)])
        for bf_, s32 in ((Stbf, St32), (Ctbf, Ct32), (Ntbf, Nt32)):
            for h in range(4):
                cp('pool', bf_[:, h, :], s32[:, h, :], r=[('st', id(s32), h)], w=[('stbf', id(bf_), h)])
        memset('pool', XXst[:, :, :], 0.0, [('xxst',)])
        for ct in range(8):
            memset('pool', zch[:, ct, :], 0.0, [('zch', ct)])
        for ct in range(22):
            memset('pool', ahist[:, ct, :], 0.0, [('ah', ct)])
        S.barrier()
        s5_setup(L)
        for seg in range(nseg):
            S.barrier()
            dma(gains2[0][:, :], rawap(W['norm_mix_pre'], L * D, [[0, 128], [1, D]]), w=[('c', 'gain')])
            dma(gains2[1][:, :], rawap(W['norm_mix_post'], L * D, [[0, 128], [1, D]]), w=[('c', 'gain')])
            arena[0] = PH0
            nsets = norm_sets(3)
            for bi, tb in enumerate(range(0, SEGT, 128)):
                extra = [('dst', seg, tb)] if L > 0 else []
                norm_block(rawap(src_t, (seg * SEGT + tb) * D, [[D, 128], [1, D]]), 128, gains[0], ('c', 'gain'), uT,
                           ('uT',), tb, bi % 2, extra_r=extra, bufs=nsets[bi % 3])
            S.barrier()
            if 'gla' not in skip:
                phase_gla(L)
                S.barrier()
            if 'ml' not in skip:
                phase_ml(L)
                S.barrier()
            if 's5' not in skip:
                phase_s5(L)
                S.barrier()
            if 'out' not in skip:
                phase_out(L, src_t, seg)
                S.barrier()
            if 'ffn' not in skip:
                phase_ffn_a(L, seg)
                S.barrier()
                phase_ffn_b(L, dst_t, seg)
                S.barrier()

    with nc.semaphore('e_pe') as s0, nc.semaphore('e_act') as s1, nc.semaphore('e_dve') as s2, \
            nc.semaphore('e_pool') as s3, nc.semaphore('e_sp') as s4:
        esem = {'pe': s0, 'act': s1, 'dve': s2, 'pool': s3, 'sp': s4}
        import contextlib
        with contextlib.ExitStack() as es:
            ssem = [es.enter_context(nc.semaphore('dslot%d' % i)) for i in range(S.nslots)]
            with nc.Block() as block:
                S.emit(nc, block, esem, ssem)
    build.dbg_names = dbg_names
    build.amax = amax
    build.ph0 = PH0
    return nc


def kernel(**inputs):
    nc = build()
    m = {'x': np.ascontiguousarray(inputs['x'].reshape(SEQ, D), dtype=np.float32)}
    for n in WNAMES:
        m[n] = np.ascontiguousarray(inputs[n], dtype=np.float32)
    res = run_bass_kernel_spmd(nc, [m], core_ids=[0])
    return res.results[0]['out'].reshape(1, SEQ, D).astype(np.float32)
```

```python
import math
import numpy as np
import concourse.bass as bass
import concourse.mybir as mybir
from concourse.bass_utils import run_bass_kernel_spmd

F32 = mybir.dt.float32
BF16 = mybir.dt.bfloat16
ALU = mybir.AluOpType
AF = mybir.ActivationFunctionType

NCORE = 8
D = 1024
DIN = 7704
DFF = 2816
SEQ = 16384
OWN = SEQ // NCORE
HALO = 64
PRE = 3
NT = OWN + HALO
NW = NT + PRE
EPS = 1e-6
O_GQ, O_GK, O_GV, O_LR, O_GR, O_SU, O_MQ, O_MK, O_MV, O_MI, O_MF, O_MO, O_GATE = (
    0, 512, 1024, 1536, 1552, 2064, 2576, 3088, 3600, 4112, 4116, 4120, 4632)
TILES = [(0, 64), (64, 512), (576, 512), (1088, 512), (1600, 512)]
SNAP_TOK = OWN
LS5 = 8
GC = 1.5957691216057308

WNAMES = ['norm_mix_pre', 'norm_mix_post', 'norm_ffn_pre', 'norm_ffn_post', 'w_in',
          'gla_w_gk', 'gla_b_gk', 'gla_norm', 'gla_w_proj',
          's5_a_re', 's5_a_im', 's5_log_dt', 's5_b_re', 's5_b_im', 's5_c_re', 's5_c_im', 's5_d',
          's5_w_glu', 's5_b_glu', 's5_w_proj', 'ml_conv_w', 'ml_conv_b', 'ml_b_i', 'ml_b_f',
          'ml_w_proj', 'w_out', 'ffn_w_up', 'ffn_w_gate', 'ffn_conv_w', 'ffn_conv_b', 'ffn_w_down']
WSHAPES = {'norm_mix_pre': [D], 'norm_mix_post': [D], 'norm_ffn_pre': [D], 'norm_ffn_post': [D],
           'w_in': [D, DIN], 'gla_w_gk': [16, 512], 'gla_b_gk': [512], 'gla_norm': [128],
           'gla_w_proj': [512, D], 's5_a_re': [32, 64], 's5_a_im': [32, 64], 's5_log_dt': [32],
           's5_b_re': [32, 64, 16], 's5_b_im': [32, 64, 16], 's5_c_re': [32, 16, 64],
           's5_c_im': [32, 16, 64], 's5_d': [32, 16], 's5_w_glu': [512, 512], 's5_b_glu': [512],
           's5_w_proj': [512, D], 'ml_conv_w': [4, D], 'ml_conv_b': [D], 'ml_b_i': [4], 'ml_b_f': [4],
           'ml_w_proj': [512, D], 'w_out': [D, D], 'ffn_w_up': [D, DFF], 'ffn_w_gate': [D, DFF],
           'ffn_conv_w': [3, DFF], 'ffn_conv_b': [DFF], 'ffn_w_down': [DFF, D]}


class Sched:
    ENGS = ('pe', 'act', 'dve', 'pool', 'sp')
    SELF_SYNC = ('act', 'dve', 'pool')

    def __init__(self, nslots=16):
        self.streams = {e: [] for e in self.ENGS}
        self.ops = []
        self.lastw = {}
        self.rd = {}
        self.nslots = nslots
        self.slot_cnt = [0] * nslots
        self.slot_last = [None] * nslots
        self.dma_n = 0

    def add(self, eng, fn, r=(), w=(), dma=False):
        oid = len(self.ops)
        deps = set()
        raw = set()
        for k in r:
            p = self.lastw.get(k)
            if p is not None:
                deps.add(p)
                raw.add(p)
        for k in w:
            p = self.lastw.get(k)
            if p is not None:
                deps.add(p)
            rr = self.rd.get(k)
            if rr:
                deps.update(rr[0].values())
                deps.update(rr[1])
        o = {'id': oid, 'eng': eng, 'fn': fn, 'dma': dma, 'flag': False, 'raw': raw}
        if dma:
            s = self.dma_n % self.nslots
            self.dma_n += 1
            if self.slot_last[s] is not None:
                deps.add(self.slot_last[s])
            self.slot_cnt[s] += 1
            o['slot'] = s
            o['slotval'] = 16 * self.slot_cnt[s]
            self.slot_last[s] = oid
        for k in w:
            self.lastw[k] = oid
            self.rd[k] = ({}, [])
        for k in r:
            rr = self.rd.setdefault(k, ({}, []))
            if dma:
                rr[1].append(oid)
            else:
                rr[0][eng] = oid
        deps.discard(oid)
        o['deps'] = deps
        self.ops.append(o)
        self.streams[eng].append(o)
        return o

    def barrier(self):
        last = {}
        for e in self.ENGS:
            st = self.streams[e]
            for o in reversed(st):
                if o['fn'] is not None and not o['dma']:
                    last[e] = o['id']
                    break
        dmas = [x for x in self.slot_last if x is not None]
        for e in self.ENGS:
            oid = len(self.ops)
            deps = set(v for k, v in last.items() if k != e) | set(dmas)
            o = {'id': oid, 'eng': e, 'fn': None, 'dma': False, 'flag': False, 'deps': deps}
            self.ops.append(o)
            self.streams[e].append(o)

    def emit(self, nc, block, esem, ssem):
        ops = self.ops
        for o in ops:
            for d in o['deps']:
                dd = ops[d]
                if dd['dma']:
                    continue
                if dd['eng'] != o['eng'] or (d in o.get('raw', ()) and o['eng'] in self.SELF_SYNC):
                    dd['flag'] = True
        for e in self.ENGS:
            c = 0
            for o in self.streams[e]:
                if o['flag'] and not o['dma']:
                    c += 1
                    o['fidx'] = c
        sched = self

        def run(ename, eng):
            waited = {}
            for o in sched.streams[ename]:
                need = {}
                for d in o['deps']:
                    dd = ops[d]
                    if dd['dma']:
                        key = ('s', dd['slot'])
                        val = dd['slotval']
                    elif dd['eng'] == ename and not (d in o.get('raw', ()) and ename in sched.SELF_SYNC):
                        continue
                    else:
                        key = ('e', dd['eng'])
                        val = dd['fidx']
                    if val > waited.get(key, 0) and val > need.get(key, 0):
                        need[key] = val
                for key, val in need.items():
                    sem = ssem[key[1]] if key[0] == 's' else esem[key[1]]
                    eng.wait_ge(sem, val)
                    waited[key] = val
                if o['fn'] is None:
                    continue
                ins = o['fn'](eng)
                if o['dma']:
                    ins.then_inc(ssem[o['slot']], 16)
                elif o['flag']:
                    ins.then_inc(esem[ename], 1)
            if ename == 'sp':
                for s in range(sched.nslots):
                    if sched.slot_cnt[s]:
                        eng.wait_ge(ssem[s], 16 * sched.slot_cnt[s])

        @block.tensor
        def _(e):
            run('pe', e)

        @block.scalar
        def _(e):
            run('act', e)

        @block.vector
        def _(e):
            run('dve', e)

        @block.gpsimd
        def _(e):
            run('pool', e)

        @block.sync
        def _(e):
            run('sp', e)


def rawap(t, offset, pat):
    return bass.AP(tensor=t, offset=offset, ap=[list(p) for p in pat])


SEGT = 2048
DBGT = 512
BAR = False
P2ENG = 'dve'
TILE = 512
PI = math.pi


def build(nseg=8, nlayer=2, dbg=False, skip=()):
    nc = bass.Bass("TRN2", target_bir_lowering=False)
    S = Sched()
    ntok = nseg * SEGT
    x_in = nc.dram_tensor('x', [ntok, D], F32, kind='ExternalInput')
    W = {n: nc.dram_tensor(n, [2] + WSHAPES[n], F32, kind='ExternalInput') for n in WNAMES}
    WSZ = {n: int(np.prod(WSHAPES[n])) for n in WNAMES}
    out_t = nc.dram_tensor('out', [ntok, D], F32, kind='ExternalOutput')
    x1 = nc.dram_tensor('x1', [ntok, D], F32)
    xmid = nc.dram_tensor('xmid', [SEGT, D], F32)

    sb_top = [16512]
    SB_END = 229376

    def alloc(name, shape, dt, at=None):
        nbytes = int(np.prod(shape[1:])) * (4 if dt == F32 else 2)
        nbytes = (nbytes + 63) // 64 * 64
        if at is None:
            off = sb_top[0]
            sb_top[0] += nbytes
        else:
            off = at
        assert off + nbytes <= SB_END, (name, off, nbytes)
        return nc.alloc_sbuf_tensor_at(name, list(shape), dt, offset=off)

    psum = nc.alloc_psum_tensor('psum', [128, 8, 512], F32)

    def ps(b, p=128, n=512):
        return psum[0:p, b, 0:n]

    bank_rr = [0]

    def nextbank():
        b = bank_rr[0] % 5
        bank_rr[0] += 1
        return b

    ident = alloc('ident', [128, 128], BF16)
    onesbf = alloc('onesbf', [128, 128], BF16)
    ones32 = alloc('ones32', [128, 512], F32)
    maskut = alloc('maskut', [128, 64], F32)
    eps_t = alloc('eps_t', [128, 1], F32)
    negpi = alloc('negpi', [128, 1], F32)
    gainb = alloc('gainb', [128, D], F32)
    gains2 = [gainb, gainb]
    gains = [gainb, gainb, gainb, gainb]
    junk_ref = [None]
    small = alloc('small', [128, 64], F32)
    stage0 = alloc('stage0', [128, 512], F32)
    stage = [stage0, stage0]
    St32 = alloc('St32', [128, 4, 128], F32)
    Ct32 = alloc('Ct32', [128, 4, 128], F32)
    Nt32 = alloc('Nt32', [128, 4, 128], F32)
    Stbf = alloc('Stbf', [128, 4, 128], BF16)
    Ctbf = alloc('Ctbf', [128, 4, 128], BF16)
    Ntbf = alloc('Ntbf', [128, 4, 128], BF16)
    XXst = alloc('XXst', [128, 2, 16], F32)
    zch = alloc('zch', [128, 8, 3], F32)
    ahist = alloc('ahist', [128, 22, 2], F32)
    BBT = alloc('BBT', [32, 16, 2, 128], BF16)
    Ec = alloc('Ec', [128, 2, 16, 32], BF16)
    AA1 = alloc('AA1', [128, 2, 16], F32)
    A1i = alloc('A1i', [128, 16], F32)
    nA1i = alloc('nA1i', [128, 16], F32)
    Dd = alloc('Dd', [32, 16], F32)
    PW = alloc('PW', [128, 8, 2, 16], F32)
    nPWi = alloc('nPWi', [128, 8, 16], F32)
    AA8 = alloc('AA8', [128, 2, 16], F32)
    gsel = alloc('gsel', [32, 4, 128], BF16)
    Et = [alloc('Et%d' % i, [128, 8, 16, 32], BF16) for i in range(2)]
    sel_ref = [None, None]
    UT0 = sb_top[0]
    uT = alloc('uT', [128, 8, SEGT], BF16)
    merged = alloc('merged', [128, 8, SEGT], BF16)
    PH0 = sb_top[0]
    arena = [PH0]
    an = [0]

    amax = {}

    def aalloc(name, shape, dt):
        an[0] += 1
        t = alloc('%s_%d' % (name, an[0]), shape, dt, at=arena[0])
        nb = int(np.prod(shape[1:])) * (4 if dt == F32 else 2)
        arena[0] += (nb + 63) // 64 * 64
        amax['top'] = max(amax.get('top', 0), arena[0])
        return t

    cnt = {'stage': 0}
    dumped = {}

    def dma(out, in_, r=(), w=(), slow=False):
        if slow:
            return S.add('sp', lambda e: e.dma_start(out=out, in_=in_, allow_slow_non_contiguous=True),
                         r=r, w=w, dma=True)
        return S.add('sp', lambda e: e.dma_start(out=out, in_=in_), r=r, w=w, dma=True)

    def act(out, in_, func, r, w, bias=None, scale=None, accum=None):
        kw = {}
        if bias is not None:
            kw['bias'] = bias
        if scale is not None:
            kw['scale'] = scale
        if accum is not None:
            kw['accum_out'] = accum
        return S.add('act', lambda e: e.activation(out=out, in_=in_, func=func, **kw), r=r, w=w)

    def tt(eng, out, in0, in1, op, r, w):
        return S.add(eng, lambda e: e.tensor_tensor(out=out, in0=in0, in1=in1, op=op), r=r, w=w)

    def ts(eng, out, in0, s1, s2, op0, op1, r, w):
        if s2 is None:
            return S.add(eng, lambda e: e.tensor_single_scalar(out=out, in_=in0, scalar=s1, op=op0), r=r, w=w)
        return S.add(eng, lambda e: e.tensor_scalar(out=out, in0=in0, scalar1=s1, scalar2=s2, op0=op0, op1=op1),
                     r=r, w=w)

    def stt(eng, out, in0, sc, in1, op0, op1, r, w):
        return S.add(eng, lambda e: e.scalar_tensor_tensor(out=out, in0=in0, scalar=sc, in1=in1, op0=op0, op1=op1),
                     r=r, w=w)

    def cp(eng, out, in_, r, w):
        if eng == 'act':
            return S.add('act', lambda e: e.copy(out=out, in_=in_), r=r, w=w)
        return S.add(eng, lambda e: e.tensor_copy(out=out, in_=in_), r=r, w=w)

    def mm(out, lhsT, rhs, start, stop, r, w):
        return S.add('pe', lambda e: e.matmul(out, lhsT, rhs, start=start, stop=stop, skip_group_check=True),
                     r=r, w=w)

    def tr(out, in_, idn, r, w):
        return S.add('pe', lambda e: e.transpose(out, in_, idn), r=r, w=w)

    def memset(eng, ap, val, w):
        return S.add(eng, lambda e: e.memset(ap, val), r=(), w=w)

    dbg_names = []

    def dump(name, ap, rkeys):
        if not dbg:
            return
        t = nc.dram_tensor('dbg_' + name, list(ap.shape), ap.dtype, kind='ExternalOutput')
        dbg_names.append('dbg_' + name)
        full = t.ap()
        dma(full, ap, r=rkeys, w=[('dbg', name)])

    scr = nc.dram_tensor('scr_bar', [64, 64], F32)
    barn = [0]

    def lbar(items):
        for ap_, key in items:
            i = barn[0] % 64
            barn[0] += 1
            if ap_.dtype != F32:
                continue
            dma(scr.ap()[i:i + 1, 0:1], ap_, r=[key], w=[('scr', i)])

    def recip(out, in_, r, w):
        return S.add('dve', lambda e: e.reciprocal(out=out, in_=in_), r=r, w=w)

    def load_w(dst, key, name, L, row0, kt, col0, ncols, rows=128):
        src_t = W[name]
        ld = WSHAPES[name][1] if len(WSHAPES[name]) > 1 else 1
        base = L * WSZ[name]
        kstep = max(1, 8 if kt <= 8 else 11)
        for k0 in range(0, kt, kstep):
            kn = min(kstep, kt - k0)
            src = rawap(src_t, base + (row0 + k0 * 128) * ld + col0, [[ld, rows], [128 * ld, kn], [1, ncols]])
            dd = dst[:, k0:k0 + kn, :]
            S.add('pool', lambda e, dd=dd, src=src: e.dma_start(out=dd, in_=src), r=(), w=[key], dma=True)

    class WK:
        def __init__(self, base, chunk=512):
            self.base = base
            self.chunk = chunk

        def keys(self, c0, n):
            return [(self.base, j) for j in range(c0 // self.chunk, (c0 + n - 1) // self.chunk + 1)]

    def wkeys(wkey, c0, n):
        return wkey.keys(c0, n) if isinstance(wkey, WK) else [wkey]

    def load_wc(dst_t, wk, name, L, kt, dcol0, scol0, ncols):
        c = dcol0
        end = dcol0 + ncols
        while c < end:
            ce = min(end, (c // wk.chunk + 1) * wk.chunk)
            load_w(dst_t[:, :, c:ce], (wk.base, c // wk.chunk), name, L, 0, kt, scol0 + (c - dcol0), ce - c)
            c = ce

    def vec_load(dst, name, L, pat, key, off=0, slow=True):
        dma(dst, rawap(W[name], L * WSZ[name] + off, pat), w=[key], slow=slow)

    memset('pool', ones32[:, :], 1.0, [('c', 'ones32')])
    memset('pool', eps_t[:, :], EPS, [('c', 'eps')])
    memset('pool', negpi[:, :], -PI, [('c', 'negpi')])
    memset('pool', maskut[0:64, :], 1.0, [('c', 'mask')])
    S.add('pool', lambda e: e.affine_select(out=maskut[0:64, :], in_=maskut[0:64, :], pattern=[[1, 64]],
                                            compare_op=ALU.is_ge, fill=0.0, base=0, channel_multiplier=-1),
          r=[('c', 'mask')], w=[('c', 'mask')])
    cp('pool', onesbf[:, :], ones32[:, 0:128], r=[('c', 'ones32')], w=[('c', 'onesbf')])
    dma(maskut[64:128, :], maskut[0:64, :], r=[('c', 'mask')], w=[('c', 'mask')])
    identf = stage[1]
    memset('pool', identf[:, 0:128], 1.0, [('stage', 1)])
    S.add('pool', lambda e: e.affine_select(out=identf[:, 0:128], in_=identf[:, 0:128], pattern=[[1, 128]],
                                            compare_op=ALU.is_equal, fill=0.0, base=0, channel_multiplier=-1),
          r=[('stage', 1)], w=[('stage', 1)])
    cp('pool', ident[:, :], identf[:, 0:128], r=[('stage', 1)], w=[('c', 'ident')])
    for k in range(4):
        memset('pool', identf[0:32, 128:256], 1.0, [('stage', 1)])
        S.add('pool', lambda e, k=k: e.affine_select(out=identf[0:32, 128:256], in_=identf[0:32, 128:256],
                                                     pattern=[[1, 128]], compare_op=ALU.is_equal, fill=0.0,
                                                     base=-32 * k, channel_multiplier=-1),
              r=[('stage', 1)], w=[('stage', 1)])
        cp('pool', gsel[:, k, :], identf[0:32, 128:256], r=[('stage', 1)], w=[('c', 'gsel')])
    def norm_block(src_ap, nb, gain_t, gkey, dstT, dkey, dcol0, xi, extra_r=(), bufs=None):
        xb, un_, junk_, sc, sid = bufs
        kx, ku, kj = ('nbx', sid), ('nbu', sid), ('nbj', sid)
        b7 = nextbank()
        dma(xb[0:nb, :], src_ap, r=extra_r, w=[kx])
        act(junk_[0:nb, :], xb[0:nb, :], AF.Square, r=[kx], w=[kj])
        S.add('dve', lambda e: e.reduce_sum(out=small[0:nb, sc:sc + 1], in_=junk_[0:nb, :], axis=mybir.AxisListType.X),
              r=[kj], w=[('ss', sid)])
        act(small[0:nb, sc + 1:sc + 2], small[0:nb, sc:sc + 1], AF.Sqrt, r=[('ss', sid), ('c', 'eps')], w=[('sd', sid)],
            bias=eps_t[0:nb, :], scale=1.0 / D)
        recip(small[0:nb, sc + 2:sc + 3], small[0:nb, sc + 1:sc + 2], r=[('sd', sid)], w=[('rs', sid)])
        stt('dve', un_[0:nb, :], xb[0:nb, :], small[0:nb, sc + 2:sc + 3], gain_t[0:nb, :], ALU.mult, ALU.mult,
            r=[kx, ('rs', sid), gkey], w=[ku])
        pbf = psum[:, b7, :].bitcast(BF16)
        for k in range(8):
            tr(pbf[:, k * 128:k * 128 + nb], un_[0:nb, k * 128:(k + 1) * 128], ident[0:nb, 0:nb],
               r=[ku, ('c', 'ident')], w=[('ps', b7)])
        cp('act', dstT[:, :, dcol0:dcol0 + nb], pbf.rearrange('p (k t) -> p k t', k=8)[:, :, 0:nb], r=[('ps', b7)],
           w=[dkey])

    def norm_sets(n):
        out = []
        for i in range(n):
            out.append((aalloc('nbx', [128, D], F32), aalloc('nbu', [128, D], BF16), aalloc('nbj', [128, D], BF16),
                        20 + 3 * i, i))
        return out

    def rowsum_rstd(banks, nb, col):
        for hf, b in enumerate(banks):
            junk = junk_ref[0]
            act(junk[0:nb, hf * 512:(hf + 1) * 512], ps(b, nb, 512), AF.Square, r=[('ps', b)], w=[('junk',)])
            S.add('dve', lambda e, hf=hf: e.reduce_sum(out=small[0:nb, 4 + hf:5 + hf],
                                                       in_=junk[0:nb, hf * 512:(hf + 1) * 512],
                                                       axis=mybir.AxisListType.X), r=[('junk',)], w=[('ssx', hf)])
        tt('dve', small[0:nb, 6:7], small[0:nb, 4:5], small[0:nb, 5:6], ALU.add, r=[('ssx', 0), ('ssx', 1)],
           w=[('ssx', 2)])
        act(small[0:nb, 7:8], small[0:nb, 6:7], AF.Sqrt, r=[('ssx', 2), ('c', 'eps')], w=[('ssx', 3)],
            bias=eps_t[0:nb, :], scale=1.0 / D)
        recip(small[0:nb, col:col + 1], small[0:nb, 7:8], r=[('ssx', 3)], w=[('rsx',)])

    def proj(bank, wt, wkey, c0, M, src, skey, tok0, nt, kt=8):
        for k in range(kt):
            mm(ps(bank, M, nt), wt[:, k, c0:c0 + M], src[:, k, tok0:tok0 + nt], k == 0, k == kt - 1,
               r=wkeys(wkey, c0, M) + [skey], w=[('ps', bank)])

    def gelu(dst, v, n, vkey, dkey, tmpa, tmpb, P=128):
        act(tmpa[0:P, 0:n], v, AF.Square, r=[vkey], w=[('ga',)])
        ts('dve', tmpa[0:P, 0:n], tmpa[0:P, 0:n], 0.044715, 1.0, ALU.mult, ALU.add, r=[('ga',)], w=[('ga',)])
        tt('dve', tmpa[0:P, 0:n], tmpa[0:P, 0:n], v, ALU.mult, r=[('ga',), vkey], w=[('ga',)])
        act(tmpb[0:P, 0:n], tmpa[0:P, 0:n], AF.Sigmoid, r=[('ga',)], w=[('gb',)], scale=GC)
        tt('pool', dst, tmpb[0:P, 0:n], v, ALU.mult, r=[('gb',), vkey], w=[dkey])

    def s5_setup(L):
        arena[0] = PH0
        aR = aalloc('aR', [128, 16], F32); aI = aalloc('aI', [128, 16], F32); ldt = aalloc('ldt', [128, 16], F32)
        t = [aalloc('s5t', [128, 16], F32) for _ in range(12)]
        bR = aalloc('bR', [128, 16, 16], F32); bI = aalloc('bI', [128, 16, 16], F32)
        cR = aalloc('cR', [128, 16, 16], F32); cI = aalloc('cI', [128, 16, 16], F32)
        u1 = aalloc('u1', [128, 16, 16], F32); u2 = aalloc('u2', [128, 16, 16], F32)
        bsrc = [aalloc('bsrc', [128, 16, 32], BF16) for _ in range(2)]
        K = ('s5c', L)
        vec_load(aR[:, :], 's5_a_re', L, [[1, 128], [128, 16]], K)
        vec_load(aI[:, :], 's5_a_im', L, [[1, 128], [128, 16]], K)
        for e in range(2):
            vec_load(ldt[e * 64:(e + 1) * 64, :], 's5_log_dt', L, [[0, 64], [2, 16]], K, off=e)
        vec_load(bR[:, :, :], 's5_b_re', L, [[16, 128], [2048, 16], [1, 16]], K, slow=False)
        vec_load(bI[:, :, :], 's5_b_im', L, [[16, 128], [2048, 16], [1, 16]], K, slow=False)
        for e in range(2):
            for pr in range(16):
                vec_load(cR[e * 64:(e + 1) * 64, pr, :], 's5_c_re', L, [[1, 64], [64, 16]], K, off=e * 1024 + pr * 2048)
                vec_load(cI[e * 64:(e + 1) * 64, pr, :], 's5_c_im', L, [[1, 64], [64, 16]], K, off=e * 1024 + pr * 2048)
        vec_load(Dd[:, :], 's5_d', L, [[1, 32], [32, 16]], K)
        R = [K]
        dt_, dre, dim, mag, sn, cs, A1r, den, zr, fre, fim, tmp = t
        act(dt_[:, :], ldt[:, :], AF.Exp, r=R, w=R)
        tt('dve', dre[:, :], dt_[:, :], aR[:, :], ALU.mult, r=R, w=R)
        tt('dve', dim[:, :], dt_[:, :], aI[:, :], ALU.mult, r=R, w=R)
        act(mag[:, :], dre[:, :], AF.Exp, r=R, w=R)
        cp('dve', sn[:, :], dim[:, :], r=R, w=R)
        ts('dve', cs[:, :], dim[:, :], 0.5 * PI, None, ALU.add, None, r=R, w=R)
        for arr in (sn, cs):
            cp('dve', den[:, :], arr[:, :], r=R, w=R)
            for th in (PI, 3 * PI, 5 * PI, 7 * PI):
                ts('dve', tmp[:, :], den[:, :], th, None, ALU.is_ge, None, r=R, w=R)
                stt('dve', arr[:, :], tmp[:, :], -2 * PI, arr[:, :], ALU.mult, ALU.add, r=R, w=R)
            act(arr[:, :], arr[:, :], AF.Sin, r=R, w=R)
        tt('dve', A1r[:, :], mag[:, :], cs[:, :], ALU.mult, r=R, w=R)
        tt('dve', A1i[:, :], mag[:, :], sn[:, :], ALU.mult, r=R, w=R)
        ts('dve', nA1i[:, :], A1i[:, :], -1.0, None, ALU.mult, None, r=R, w=R)
        cp('dve', AA1[:, 0, :], A1r[:, :], r=R, w=R)
        cp('dve', AA1[:, 1, :], A1r[:, :], r=R, w=R)
        cp('dve', PW[:, 0, 0, :], A1r[:, :], r=R, w=R)
        cp('dve', PW[:, 0, 1, :], A1i[:, :], r=R, w=R)
        for k in range(7):
            tt('dve', tmp[:, :], PW[:, k, 0, :], A1r[:, :], ALU.mult, r=R, w=R)
            tt('dve', den[:, :], PW[:, k, 1, :], A1i[:, :], ALU.mult, r=R, w=R)
            tt('dve', PW[:, k + 1, 0, :], tmp[:, :], den[:, :], ALU.subtract, r=R, w=R)
            tt('dve', tmp[:, :], PW[:, k, 0, :], A1i[:, :], ALU.mult, r=R, w=R)
            tt('dve', den[:, :], PW[:, k, 1, :], A1r[:, :], ALU.mult, r=R, w=R)
            tt('dve', PW[:, k + 1, 1, :], tmp[:, :], den[:, :], ALU.add, r=R, w=R)
        for k in range(8):
            ts('dve', nPWi[:, k, :], PW[:, k, 1, :], -1.0, None, ALU.mult, None, r=R, w=R)
        cp('dve', AA8[:, 0, :], PW[:, 7, 0, :], r=R, w=R)
        cp('dve', AA8[:, 1, :], PW[:, 7, 0, :], r=R, w=R)
        tt('dve', den[:, :], aR[:, :], aR[:, :], ALU.mult, r=R, w=R)
        tt('dve', tmp[:, :], aI[:, :], aI[:, :], ALU.mult, r=R, w=R)
        tt('dve', den[:, :], den[:, :], tmp[:, :], ALU.add, r=R, w=R)
        recip(den[:, :], den[:, :], r=R, w=R)
        ts('dve', zr[:, :], A1r[:, :], -1.0, None, ALU.add, None, r=R, w=R)
        tt('dve', fre[:, :], zr[:, :], aR[:, :], ALU.mult, r=R, w=R)
        tt('dve', tmp[:, :], A1i[:, :], aI[:, :], ALU.mult, r=R, w=R)
        tt('dve', fre[:, :], fre[:, :], tmp[:, :], ALU.add, r=R, w=R)
        tt('dve', fre[:, :], fre[:, :], den[:, :], ALU.mult, r=R, w=R)
        tt('dve', fim[:, :], A1i[:, :], aR[:, :], ALU.mult, r=R, w=R)
        tt('dve', tmp[:, :], zr[:, :], aI[:, :], ALU.mult, r=R, w=R)
        tt('dve', fim[:, :], fim[:, :], tmp[:, :], ALU.subtract, r=R, w=R)
        tt('dve', fim[:, :], fim[:, :], den[:, :], ALU.mult, r=R, w=R)
        frb = fre[:, :].unsqueeze(2).to_broadcast([128, 16, 16])
        fib = fim[:, :].unsqueeze(2).to_broadcast([128, 16, 16])
        tt('dve', u1[:, :, :], bR[:, :, :], frb, ALU.mult, r=R, w=R)
        tt('dve', u2[:, :, :], bI[:, :, :], fib, ALU.mult, r=R, w=R)
        tt('dve', u1[:, :, :], u1[:, :, :], u2[:, :, :], ALU.subtract, r=R, w=R)
        tt('dve', u2[:, :, :], bI[:, :, :], frb, ALU.mult, r=R, w=R)
        tt('dve', bI[:, :, :], bR[:, :, :], fib, ALU.mult, r=R, w=R)
        tt('dve', u2[:, :, :], u2[:, :, :], bI[:, :, :], ALU.add, r=R, w=R)
        ts('dve', cI[:, :, :], cI[:, :, :], -1.0, None, ALU.mult, None, r=R, w=R)
        for ri, (bsrc_, srcb, srcc) in enumerate(((bsrc[0], u1, cR), (bsrc[1], u2, cI))):
            memset('dve', bsrc_[:, :, :], 0.0, R)
            memset('dve', Ec[:, ri, :, :], 0.0, R)
            for e in range(2):
                cp('dve', bsrc_[e * 64:(e + 1) * 64, :, e * 16:(e + 1) * 16], srcb[e * 64:(e + 1) * 64, :, :], r=R, w=R)
                cp('dve', Ec[e * 64:(e + 1) * 64, ri, :, e * 16:(e + 1) * 16], srcc[e * 64:(e + 1) * 64, :, :], r=R, w=R)
        for ri in range(2):
            memset('dve', Et[ri][:, :, :, :], 0.0, R)
        for tau in range(8):
            prb = PW[:, tau, 0, :].unsqueeze(2).to_broadcast([128, 16, 16])
            pib = PW[:, tau, 1, :].unsqueeze(2).to_broadcast([128, 16, 16])
            tt('dve', u1[:, :, :], cR[:, :, :], prb, ALU.mult, r=R, w=R)
            tt('dve', u2[:, :, :], cI[:, :, :], pib, ALU.mult, r=R, w=R)
            tt('dve', u1[:, :, :], u1[:, :, :], u2[:, :, :], ALU.add, r=R, w=R)
            tt('dve', u2[:, :, :], cI[:, :, :], prb, ALU.mult, r=R, w=R)
            tt('dve', bR[:, :, :], cR[:, :, :], pib, ALU.mult, r=R, w=R)
            tt('dve', u2[:, :, :], u2[:, :, :], bR[:, :, :], ALU.subtract, r=R, w=R)
            for ri, srcE in ((0, u1), (1, u2)):
                for e in range(2):
                    cp('dve', Et[ri][e * 64:(e + 1) * 64, tau, :, e * 16:(e + 1) * 16], srcE[e * 64:(e + 1) * 64, :, :],
                       r=R, w=R)
        pbf = psum[:, 7, :].bitcast(BF16)
        for ri in range(2):
            for g4 in range(2):
                for j in range(8):
                    pr = g4 * 8 + j
                    tr(pbf[0:32, j * 128:(j + 1) * 128], bsrc[ri][:, pr, :], ident[:, :], r=R + [('c', 'ident')],
                       w=[('ps', 7)])
                cp('act', BBT[:, g4 * 8:(g4 + 1) * 8, ri, :], pbf[0:32, :].rearrange('p (j q) -> p j q', j=8),
                   r=[('ps', 7)], w=R)

    SBANK = (6, 5)

    def prep_head(nt, hi, qa, qkeys, ka, kkeys, qscale, kscale, B):
        qd2, ki2, kd2, kdT, sT, E1, E2, E3, dec2 = B
        stt('dve', qd2[hi][:, 0:nt], qa, qscale, E1[:, 0:nt], ALU.mult, ALU.mult, r=qkeys + [('E1',)], w=[('qd', hi)])
        stt('dve', ki2[hi][:, 0:nt], ka, kscale, E2[:, 0:nt], ALU.mult, ALU.mult, r=kkeys + [('E2',)], w=[('ki', hi)])
        stt('dve', kd2[hi][:, 0:nt], ka, kscale, E3[:, 0:nt], ALU.mult, ALU.mult, r=kkeys + [('E3',)], w=[('kd', hi)])

    def chunk_pair(nt, heads, vtok, vcol0, stl, bfl, use_n, o_banks, den_banks, B):
        nch = nt // 64
        qd2, ki2, kd2, kdT, sT, E1, E2, E3, dec2 = B
        S32 = stl[0]
        Sbf = bfl[0]
        pbf = psum[:, 7, :].bitcast(BF16)
        for c in range(nch):
            cs = c * 64
            for hi, h in enumerate(heads):
                qd, ki, kd, dec = qd2[hi], ki2[hi], kd2[hi], dec2[hi]
                i2 = hi * 2 + c % 2
                P0 = (c % 2) * 64
                PS = slice(P0, P0 + 64)
                sb = SBANK[hi]
                o_bank = o_banks[hi]
                tr(pbf[PS, i2 * 128:(i2 + 1) * 128], kd[:, cs:cs + 64], ident[:, :], r=[('kd', hi), ('c', 'ident')],
                   w=[('ps7', i2)])
                cp('act', kdT[i2][PS, :], pbf[PS, i2 * 128:(i2 + 1) * 128], r=[('ps7', i2)], w=[('kdT', i2)])
                mm(psum[PS, sb, 0:64], ki[:, cs:cs + 64], qd[:, cs:cs + 64], True, True, r=[('ki', hi), ('qd', hi)],
                   w=[('ps', sb)])
                tt('dve', sT[i2][PS, :], psum[PS, sb, 0:64], maskut[PS, :], ALU.mult, r=[('ps', sb), ('c', 'mask')],
                   w=[('sT', i2)])
                vv = vtok[PS, c // 2, vcol0 + h * 128: vcol0 + (h + 1) * 128]
                mm(psum[:, o_bank, cs:cs + 64], vv, sT[i2][PS, :], True, False, r=[('vtok',), ('sT', i2)],
                   w=[('ps', o_bank)])
                mm(psum[:, o_bank, cs:cs + 64], Sbf[:, h, :], qd[:, cs:cs + 64], False, True,
                   r=[('stbf', id(Sbf), h), ('qd', hi)], w=[('ps', o_bank)])
                if use_n:
                    Nbf = bfl[1]
                    den_bank = den_banks[hi]
                    mm(psum[:, den_bank, cs:cs + 64], onesbf[PS, :], sT[i2][PS, :], True, False,
                       r=[('c', 'onesbf'), ('sT', i2)], w=[('ps', den_bank)])
                    mm(psum[:, den_bank, cs:cs + 64], Nbf[:, h, :], qd[:, cs:cs + 64], False, True,
                       r=[('stbf', id(Nbf), h), ('qd', hi)], w=[('ps', den_bank)])
                mm(psum[:, sb, 128:256], kdT[i2][PS, :], vv, True, True, r=[('kdT', i2), ('vtok',)], w=[('ps', sb)])
                stt('dve', S32[:, h, :], S32[:, h, :], dec[:, c:c + 1], psum[:, sb, 128:256], ALU.mult, ALU.add,
                    r=[('st', id(S32), h), ('dec', hi), ('ps', sb)], w=[('st', id(S32), h)])
                cp('act', Sbf[:, h, :], S32[:, h, :], r=[('st', id(S32), h)], w=[('stbf', id(Sbf), h)])
                if use_n:
                    N32 = stl[1]
                    mm(psum[:, sb, 256:384], kdT[i2][PS, :], onesbf[PS, :], True, True,
                       r=[('kdT', i2), ('c', 'onesbf')], w=[('ps', sb)])
                    stt('dve', N32[:, h, :], N32[:, h, :], dec[:, c:c + 1], psum[:, sb, 256:384], ALU.mult, ALU.add,
                        r=[('st', id(N32), h), ('dec', hi), ('ps', sb)], w=[('st', id(N32), h)])
                    cp('act', Nbf[:, h, :], N32[:, h, :], r=[('st', id(N32), h)], w=[('stbf', id(Nbf), h)])

    def vproj(tok0, nt, wt, wkey, c0, vtok):
        for blk in range(nt // 128):
            b = nextbank()
            for k in range(8):
                mm(ps(b, 128, 512), uT[:, k, tok0 + blk * 128: tok0 + blk * 128 + 128], wt[:, k, c0:c0 + 512],
                   k == 0, k == 7, r=wkeys(wkey, c0, 512) + [('uT',)], w=[('ps', b)])
            cp('act', vtok[:, blk, :], ps(b, 128, 512), r=[('ps', b)], w=[('vtok',)])

    def merge_out(tok0, nt, oT, okey, wpr, wprkey, wgate, wgkey, gc0, first, gsig, gtmp):
        for co in range(8):
            b = nextbank()
            for k in range(4):
                mm(ps(b, 128, nt), wpr[:, k, co * 128:(co + 1) * 128], oT[:, k, 0:nt], k == 0, k == 3,
                   r=[wprkey, okey], w=[('ps', b)])
            b2 = nextbank()
            proj(b2, wgate, wgkey, gc0 + co * 128, 128, uT, ('uT',), tok0, nt)
            act(gsig[:, 0:nt], ps(b2, 128, nt), AF.Sigmoid, r=[('ps', b2)], w=[('gsig',)])
            if first:
                tt('dve', merged[:, co, tok0:tok0 + nt], ps(b, 128, nt), gsig[:, 0:nt], ALU.mult,
                   r=[('ps', b), ('gsig',)], w=[('mg',)])
            else:
                tt('dve', gtmp[:, 0:nt], ps(b, 128, nt), gsig[:, 0:nt], ALU.mult, r=[('ps', b), ('gsig',)],
                   w=[('gtmp',)])
                tt('pool', merged[:, co, tok0:tok0 + nt], merged[:, co, tok0:tok0 + nt], gtmp[:, 0:nt], ALU.add,
                   r=[('gtmp',), ('mg',)], w=[('mg',)])

    def common_bufs():
        qd2 = [aalloc('qd', [128, 512], BF16) for _ in range(2)]
        ki2 = [aalloc('ki', [128, 512], BF16) for _ in range(2)]
        kd2 = [aalloc('kd', [128, 512], BF16) for _ in range(2)]
        kdT = [aalloc('kdT', [128, 128], BF16) for _ in range(4)]
        sT = [aalloc('sT', [128, 64], BF16) for _ in range(4)]
        E1 = aalloc('E1', [128, 512], F32); E2 = aalloc('E2', [128, 512], F32); E3 = aalloc('E3', [128, 512], F32)
        dec2 = [aalloc('dec', [128, 8], F32) for _ in range(2)]
        return (qd2, ki2, kd2, kdT, sT, E1, E2, E3, dec2)

    def phase_gla(L):
        arena[0] = PH0
        Wg = aalloc('Wg', [128, 8, 3088], BF16)
        Wgk = aalloc('Wgk', [16, 1, 512], BF16)
        Wpr = aalloc('Wpr', [128, 4, D], BF16)
        bgk = aalloc('bgk', [128, 4], F32); gln = aalloc('gln', [128, 1], F32)
        Bx = aalloc('Bx', [128, 513], F32)
        Drel = aalloc('Drel', [128, 512], F32); Drel2 = aalloc('Drel2', [128, 512], F32)
        B = common_bufs()
        qd2, ki2, kd2, kdT, sT, E1, E2, E3, dec2 = B
        vtok = aalloc('vtok', [128, 4, 512], BF16)
        lrT = aalloc('lrT', [16, 512], BF16)
        sq = aalloc('sq', [128, 512], BF16)
        rstd = E1; t1 = E2; sr = E3
        ogT = aalloc('ogT', [128, 4, 512], BF16)
        gsig = aalloc('gsig', [128, 512], F32); gtmp = aalloc('gtmp', [128, 512], F32)
        spb = gtmp
        K = ('W', 'g')
        KW = WK(('Wc', 'g'))
        load_wc(Wg, KW, 'w_in', L, 8, 1024, 1024, 1040)
        load_wc(Wg, KW, 'w_in', L, 8, 0, 0, 1024)
        load_wc(Wg, KW, 'w_in', L, 8, 2064, O_GATE, 1024)
        load_w(Wgk[:, :, :], K, 'gla_w_gk', L, 0, 1, 0, 512, rows=16)
        load_w(Wpr[:, :, :], K, 'gla_w_proj', L, 0, 4, 0, D)
        vec_load(bgk[:, :], 'gla_b_gk', L, [[1, 128], [128, 4]], K)
        ts('dve', bgk[:, :], bgk[:, :], -1.0, None, ALU.mult, None, r=[K], w=[K])
        vec_load(gln[:, :], 'gla_norm', L, [[1, 128], [1, 1]], K)
        memset('pool', Bx[:, 0:1], 0.0, [('Bx',)])
        for tok0 in range(0, SEGT, TILE):
            nt = TILE
            nch = nt // 64
            vproj(tok0, nt, Wg, KW, O_GV, vtok)
            b = nextbank()
            proj(b, Wg, KW, O_LR, 16, uT, ('uT',), tok0, nt)
            cp('act', lrT[:, 0:nt], ps(b, 16, nt), r=[('ps', b)], w=[('lrT',)])
            for pair in ((0, 1), (2, 3)):
                for hi, h in enumerate(pair):
                    dec = dec2[hi]
                    b = nextbank()
                    mm(ps(b, 128, nt), Wgk[:, 0, h * 128:(h + 1) * 128], lrT[:, 0:nt], True, True, r=[K, ('lrT',)],
                       w=[('ps', b)])
                    act(spb[:, 0:nt], ps(b, 128, nt), AF.Exp, r=[('ps', b), K], w=[('gtmp',)], bias=bgk[:, h:h + 1],
                        scale=-1.0)
                    act(spb[:, 0:nt], spb[:, 0:nt], AF.Ln, r=[('gtmp',)], w=[('gtmp',)], bias=1.0)
                    S.add('dve', lambda e, nt=nt: e.tensor_tensor_scan(out=Bx[:, 1:1 + nt], data0=ones32[:, 0:nt],
                                                                       data1=spb[:, 0:nt], initial=0.0,
                                                                       op0=ALU.mult, op1=ALU.add),
                          r=[('gtmp',), ('c', 'ones32')], w=[('Bx',)])
                    d3 = Drel[:, 0:nt].rearrange('p (c l) -> p c l', l=64)
                    d23 = Drel2[:, 0:nt].rearrange('p (c l) -> p c l', l=64)
                    bx3 = Bx[:, 1:1 + nt].rearrange('p (c l) -> p c l', l=64)
                    br3 = Bx[:, 0:nt].rearrange('p (c l) -> p c l', l=64)[:, :, 0:1]
                    tt('dve', d3, bx3, br3.to_broadcast([128, nch, 64]), ALU.subtract, r=[('Bx',)], w=[('Drel',)])
                    tt('dve', d23, d3, d3[:, :, 63:64].to_broadcast([128, nch, 64]), ALU.subtract, r=[('Drel',)],
                       w=[('Drel2',)])
                    act(E1[:, 0:nt], Drel[:, 0:nt], AF.Exp, r=[('Drel',)], w=[('E1',)], scale=-1.0 / 16)
                    act(dec[:, 0:nch], Drel[:, 63:nt:64], AF.Exp, r=[('Drel',)], w=[('dec', hi)], scale=-1.0 / 16)
                    act(E2[:, 0:nt], Drel[:, 0:nt], AF.Exp, r=[('Drel',)], w=[('E2',)], scale=1.0 / 16)
                    act(E3[:, 0:nt], Drel2[:, 0:nt], AF.Exp, r=[('Drel2',)], w=[('E3',)], scale=1.0 / 16)
                    bq = nextbank()
                    proj(bq, Wg, KW, O_GQ + h * 128, 128, uT, ('uT',), tok0, nt)
                    bk = nextbank()
                    proj(bk, Wg, KW, O_GK + h * 128, 128, uT, ('uT',), tok0, nt)
                    prep_head(nt, hi, ps(bq, 128, nt), [('ps', bq)], ps(bk, 128, nt), [('ps', bk)], 128 ** -0.5, 1.0, B)
                obs = [nextbank(), nextbank()]
                chunk_pair(nt, pair, vtok, 0, [St32], [Stbf], False, obs, None, B)
                for hi, h in enumerate(pair):
                    ob = obs[hi]
                    act(sq[:, 0:nt], psum[:, ob, 0:nt], AF.Square, r=[('ps', ob)], w=[('sq',)])
                    b = nextbank()
                    mm(ps(b, 128, nt), onesbf[:, :], sq[:, 0:nt], True, True, r=[('c', 'onesbf'), ('sq',)],
                       w=[('ps', b)])
                    act(rstd[:, 0:nt], ps(b, 128, nt), AF.Sqrt, r=[('ps', b), ('c', 'eps')], w=[('E1',)],
                        bias=eps_t[:, :], scale=1.0 / 128)
                    recip(rstd[:, 0:nt], rstd[:, 0:nt], r=[('E1',)], w=[('E1',)])
                    tt('dve', t1[:, 0:nt], psum[:, ob, 0:nt], rstd[:, 0:nt], ALU.mult, r=[('ps', ob), ('E1',)],
                       w=[('E2',)])
                    b = nextbank()
                    proj(b, Wg, KW, O_GR + h * 128, 128, uT, ('uT',), tok0, nt)
                    act(sr[:, 0:nt], ps(b, 128, nt), AF.Silu, r=[('ps', b)], w=[('E3',)])
                    stt('dve', ogT[:, h, 0:nt], sr[:, 0:nt], gln[:, 0:1], t1[:, 0:nt], ALU.mult, ALU.mult,
                        r=[('E3',), K, ('E2',)], w=[('ogT',)])
            merge_out(tok0, nt, ogT, ('ogT',), Wpr, K, Wg, KW, 2064, True, gsig, gtmp)

    def phase_ml(L):
        arena[0] = PH0
        Wm = aalloc('Wm', [128, 8, 3080], BF16)
        Wpr = aalloc('Wmp', [128, 4, D], BF16)
        B = common_bufs()
        qd2, ki2, kd2, kdT, sT, E1, E2, E3, dec2 = B
        vtok = aalloc('vtok', [128, 4, 512], BF16)
        zc = aalloc('zc', [128, 515], F32)
        qh = aalloc('qh', [128, 512], BF16); kh = aalloc('kh', [128, 512], BF16)
        B8 = aalloc('B8', [8, 513], F32)
        T2 = aalloc('T2', [8, 512], F32); T3 = aalloc('T3', [8, 512], F32)
        bif = aalloc('bif', [8, 1], F32); cw = aalloc('cw', [128, 8, 4], F32); cb = aalloc('cb', [128, 8], F32)
        dabs = E1; t1 = E2; so = E3
        ogT = aalloc('hT', [128, 4, 512], BF16)
        gsig = aalloc('gsig', [128, 512], F32); gtmp = aalloc('gtmp', [128, 512], F32)
        cacc = gtmp; gi8 = gsig[0:8, :]
        selF = aalloc('selF', [8, 4, 128], F32)
        selIF = aalloc('selIF', [8, 4, 128], F32)
        for h in range(4):
            for (dst, rows_) in ((selF, (4 + h,)), (selIF, (h, 4 + h))):
                memset('pool', dst[:, h, :], 0.0, [('c', 'sel')])
                for rr in rows_:
                    memset('pool', zc[0:8, 0:128], 1.0, [('zc',)])
                    S.add('pool', lambda e, rr=rr: e.affine_select(out=zc[0:8, 0:128], in_=zc[0:8, 0:128],
                                                                   pattern=[[0, 128]], compare_op=ALU.is_equal, fill=0.0,
                                                                   base=-rr, channel_multiplier=1),
                          r=[('zc',)], w=[('zc',)])
                    tt('pool', dst[:, h, :], dst[:, h, :], zc[0:8, 0:128], ALU.add, r=[('zc',), ('c', 'sel')],
                       w=[('c', 'sel')])
        K = ('W', 'm')
        KW = WK(('Wc', 'm'))
        load_wc(Wm, KW, 'w_in', L, 8, 1024, O_MQ + 1024, 1032)
        load_wc(Wm, KW, 'w_in', L, 8, 0, O_MQ, 1024)
        load_wc(Wm, KW, 'w_in', L, 8, 2056, O_GATE + 2048, 1024)
        load_w(Wpr[:, :, :], K, 'ml_w_proj', L, 0, 4, 0, D)
        vec_load(bif[0:4, :], 'ml_b_i', L, [[1, 4], [1, 1]], K)
        vec_load(bif[4:8, :], 'ml_b_f', L, [[1, 4], [1, 1]], K)
        for j in range(4):
            vec_load(cw[:, :, j], 'ml_conv_w', L, [[1, 128], [128, 8]], K, off=j * 1024)
        vec_load(cb[:, :], 'ml_conv_b', L, [[1, 128], [128, 8]], K)
        memset('pool', B8[:, 0:1], 0.0, [('B8',)])
        for tok0 in range(0, SEGT, TILE):
            nt = TILE
            nch = nt // 64
            vproj(tok0, nt, Wm, KW, 1024, vtok)
            b = nextbank()
            proj(b, Wm, KW, 1536, 8, uT, ('uT',), tok0, nt)
            act(gi8[:, 0:nt], ps(b, 8, nt), AF.Identity, r=[('ps', b), K], w=[('gsig',)], bias=bif[:, :])
            act(T3[:, 0:nt], gi8[:, 0:nt], AF.Exp, r=[('gsig',)], w=[('T3',)], scale=-1.0)
            act(T3[:, 0:nt], T3[:, 0:nt], AF.Ln, r=[('T3',)], w=[('T3',)], bias=1.0)
            S.add('dve', lambda e, nt=nt: e.tensor_tensor_scan(out=B8[:, 1:1 + nt], data0=ones32[0:8, 0:nt],
                                                               data1=T3[:, 0:nt], initial=0.0, op0=ALU.mult,
                                                               op1=ALU.add),
                  r=[('T3',), ('c', 'ones32')], w=[('B8',)])
            d3 = T2[:, 0:nt].rearrange('p (c l) -> p c l', l=64)
            d23 = T3[:, 0:nt].rearrange('p (c l) -> p c l', l=64)
            bx3 = B8[:, 1:1 + nt].rearrange('p (c l) -> p c l', l=64)
            br3 = B8[:, 0:nt].rearrange('p (c l) -> p c l', l=64)[:, :, 0:1]
            tt('dve', d3, bx3, br3.to_broadcast([8, nch, 64]), ALU.subtract, r=[('B8',)], w=[('T2',)])
            tt('dve', d23, d3, d3[:, :, 63:64].to_broadcast([8, nch, 64]), ALU.subtract, r=[('T2',), ('B8',)],
               w=[('T3',)])
            cp('dve', T2[0:4, 0:nt], gi8[0:4, 0:nt], r=[('gsig',), ('T2',)], w=[('T2',)])
            cp('dve', T3[0:4, 0:nt], gi8[0:4, 0:nt], r=[('gsig',), ('T3',)], w=[('T3',)])
            for pair in ((0, 1), (2, 3)):
                for hi, h in enumerate(pair):
                    dec = dec2[hi]
                    b1 = nextbank()
                    mm(ps(b1, 128, nt), selF[:, h, :], T2[:, 0:nt], True, True, r=[('c', 'sel'), ('T2',)],
                       w=[('ps', b1)])
                    act(E1[:, 0:nt], ps(b1, 128, nt), AF.Exp, r=[('ps', b1)], w=[('E1',)], scale=-1.0)
                    act(dec[:, 0:nch], psum[:, b1, 63:nt:64], AF.Exp, r=[('ps', b1)], w=[('dec', hi)], scale=-1.0)
                    b2 = nextbank()
                    mm(ps(b2, 128, nt), selIF[:, h, :], T2[:, 0:nt], True, True, r=[('c', 'sel'), ('T2',)],
                       w=[('ps', b2)])
                    act(E2[:, 0:nt], ps(b2, 128, nt), AF.Exp, r=[('ps', b2)], w=[('E2',)])
                    b3 = nextbank()
                    mm(ps(b3, 128, nt), selIF[:, h, :], T3[:, 0:nt], True, True, r=[('c', 'sel'), ('T3',)],
                       w=[('ps', b3)])
                    act(E3[:, 0:nt], ps(b3, 128, nt), AF.Exp, r=[('ps', b3)], w=[('E3',)])
                    for which, dstq in ((0, qh), (1, kh)):
                        ct = which * 4 + h
                        b = nextbank()
                        proj(b, Wm, KW, ct * 128, 128, uT, ('uT',), tok0, nt)
                        cp('act', zc[:, 3:3 + nt], ps(b, 128, nt), r=[('ps', b)], w=[('zc',)])
                        cp('pool', zc[:, 0:3], zch[:, ct, :], r=[('zch', ct)], w=[('zc',)])
                        ts('dve', cacc[:, 0:nt], zc[:, 0:nt], cw[:, ct, 0:1], None, ALU.mult, None, r=[('zc',), K],
                           w=[('gtmp',)])
                        for j in range(1, 4):
                            stt('dve', cacc[:, 0:nt], zc[:, j:j + nt], cw[:, ct, j:j + 1], cacc[:, 0:nt], ALU.mult,
                                ALU.add, r=[('zc',), K, ('gtmp',)], w=[('gtmp',)])
                        cp('pool', zch[:, ct, :], zc[:, nt:nt + 3], r=[('zc',)], w=[('zch', ct)])
                        act(dstq[:, 0:nt], cacc[:, 0:nt], AF.Silu, r=[('gtmp',), K], w=[('qk', which)],
                            bias=cb[:, ct:ct + 1])
                    prep_head(nt, hi, qh[:, 0:nt], [('qk', 0)], kh[:, 0:nt], [('qk', 1)], 1.0, 128 ** -0.5, B)
                obs = [nextbank(), nextbank()]
                dbs = [nextbank(), nextbank()]
                chunk_pair(nt, pair, vtok, 0, [Ct32, Nt32], [Ctbf, Ntbf], True, obs, dbs, B)
                for hi, h in enumerate(pair):
                    ob, db = obs[hi], dbs[hi]
                    ts('dve', dabs[:, 0:nt], psum[:, db, 0:nt], -1.0, 1.0, ALU.mult, ALU.max, r=[('ps', db)],
                       w=[('E1',)])
                    tt('dve', dabs[:, 0:nt], dabs[:, 0:nt], psum[:, db, 0:nt], ALU.max, r=[('ps', db), ('E1',)],
                       w=[('E1',)])
                    recip(dabs[:, 0:nt], dabs[:, 0:nt], r=[('E1',)], w=[('E1',)])
                    tt('dve', t1[:, 0:nt], psum[:, ob, 0:nt], dabs[:, 0:nt], ALU.mult, r=[('ps', ob), ('E1',)],
                       w=[('E2',)])
                    b = nextbank()
                    proj(b, Wm, KW, 1544 + h * 128, 128, uT, ('uT',), tok0, nt)
                    act(so[:, 0:nt], ps(b, 128, nt), AF.Sigmoid, r=[('ps', b)], w=[('E3',)])
                    tt('dve', ogT[:, h, 0:nt], t1[:, 0:nt], so[:, 0:nt], ALU.mult, r=[('E2',), ('E3',)], w=[('ogT',)])
            merge_out(tok0, nt, ogT, ('ogT',), Wpr, K, Wm, KW, 2056, False, gsig, gtmp)

    def phase_s5(L):
        arena[0] = PH0
        TB = 128
        NCH = TB // 8
        Ws = aalloc('Ws', [128, 8, 1536], BF16)
        Wglu = aalloc('Wglu', [128, 4, 512], BF16)
        Wsp = aalloc('Wsp', [128, 4, D], BF16)
        bglu = aalloc('bglu', [128, 4], F32)
        su128 = aalloc('su128', [128, 4, 512], BF16)
        sup2 = [aalloc('sup', [32, 16, TB], BF16) for _ in range(2)]
        GU = aalloc('GU', [128, 2, 16, TB], F32)
        XC = aalloc('XC', [128, 2, 16, NCH + 1], F32)
        XXbf = aalloc('XXbf', [128, 2, 16, TB], BF16)
        XCbf = aalloc('XCbf', [128, 2, 16, NCH], BF16)
        P1 = aalloc('P1', [128, 2, 16, NCH], F32); P2 = aalloc('P2', [128, 2, 16, NCH], F32)
        P1s = aalloc('P1s', [128, 2, 16], F32); P2s = aalloc('P2s', [128, 2, 16], F32)
        yp = aalloc('yp', [32, 16, TB], BF16)
        yv = aalloc('yv', [128, 512], F32)
        ga = aalloc('ga', [128, 512], F32)
        yg = aalloc('yg', [128, 4, 512], BF16)
        gsig = aalloc('gsig', [128, 512], F32); gtmp = aalloc('gtmp', [128, 512], F32)
        gb = gtmp
        K = ('W', 's')
        KC = ('s5c', L)
        y2 = yg
        KW = WK(('Wc', 's'))
        load_wc(Ws, KW, 'w_in', L, 8, 0, O_SU, 512)
        load_wc(Ws, KW, 'w_in', L, 8, 512, O_GATE + 1024, 1024)
        load_w(Wglu[:, :, :], K, 's5_w_glu', L, 0, 4, 0, 512)
        load_w(Wsp[:, :, :], K, 's5_w_proj', L, 0, 4, 0, D)
        vec_load(bglu[:, :], 's5_b_glu', L, [[1, 128], [128, 4]], K)
        cp('dve', XC[:, :, :, 0], XXst[:, :, :], r=[('xxst',)], w=[('XC',)])
        GU5 = GU[:, :, :, :].rearrange('p r g (l n) -> p r g l n', l=8)

        def G(tau):
            return GU5[:, :, :, tau, :]

        def cmul_acc(dst, src, k, dkey, skey):
            prb = PW[:, k, 0, :].unsqueeze(1).unsqueeze(3).to_broadcast([128, 2, 16, NCH])
            pib = PW[:, k, 1, :].unsqueeze(2).to_broadcast([128, 16, NCH])
            npib = nPWi[:, k, :].unsqueeze(2).to_broadcast([128, 16, NCH])
            tt('dve', P1[:, :, :, :], src, prb, ALU.mult, r=[skey, KC], w=[('P1',)])
            tt(P2ENG, P2[:, 0, :, :], src[:, 1], npib, ALU.mult, r=[skey, KC], w=[('P2',)])
            tt(P2ENG, P2[:, 1, :, :], src[:, 0], pib, ALU.mult, r=[skey, KC], w=[('P2',)])
            tt('dve', dst, dst, P1[:, :, :, :], ALU.add, r=[dkey, ('P1',)], w=[dkey])
            tt('dve', dst, dst, P2[:, :, :, :], ALU.add, r=[dkey, ('P2',)], w=[dkey])

        for tok0 in range(0, SEGT, TILE):
            nt = TILE
            for blk in range(4):
                b = nextbank()
                proj(b, Ws, KW, blk * 128, 128, uT, ('uT',), tok0, nt)
                cp('act', su128[:, blk, 0:nt], ps(b, 128, nt), r=[('ps', b)], w=[('su128',)])
            def front(tb, si):
                supb = sup2[si]
                for blk in range(4):
                    b = nextbank()
                    for k in range(4):
                        mm(psum[0:32, b, k * TB:(k + 1) * TB], ident[:, 32 * k:32 * k + 32],
                           su128[:, blk, tb:tb + TB].rearrange('p (n l) -> p l n', l=8), True, True,
                           r=[('c', 'ident'), ('su128',)], w=[('ps', b)])
                    cp('act', supb[:, blk * 4:(blk + 1) * 4, :], ps(b, 32, 4 * TB).rearrange('p (k t) -> p k t', k=4),
                       r=[('ps', b)], w=[('sup', si)])
                for ri in range(2):
                    for g in range(4):
                        b = nextbank()
                        for k in range(4):
                            pr = g * 4 + k
                            mm(psum[:, b, k * TB:(k + 1) * TB], BBT[:, pr, ri, :], supb[:, pr, :], True, True,
                               r=[KC, ('sup', si)], w=[('ps', b)])
                        cp('act', GU[:, ri, g * 4:(g + 1) * 4, :], ps(b, 128, 4 * TB).rearrange('p (k t) -> p k t', k=4),
                           r=[('ps', b)], w=[('GU',)])

            def mid(tb):
                for tau in range(1, 8):
                    cmul_acc(G(tau), G(tau - 1), 0, ('GU',), ('GU',))
                for n in range(NCH):
                    tt('dve', P1s[:, :, :], XC[:, :, :, n], AA8[:, :, :], ALU.mult, r=[('XC',), KC], w=[('P1s',)])
                    tt('dve', P2s[:, 0, :], XC[:, 1, :, n], nPWi[:, 7, :], ALU.mult, r=[('XC',), KC], w=[('P2s',)])
                    tt('dve', P2s[:, 1, :], XC[:, 0, :, n], PW[:, 7, 1, :], ALU.mult, r=[('XC',), KC], w=[('P2s',)])
                    tt('dve', P1s[:, :, :], P1s[:, :, :], P2s[:, :, :], ALU.add, r=[('P1s',), ('P2s',)], w=[('P1s',)])
                    tt('dve', XC[:, :, :, n + 1], P1s[:, :, :], GU5[:, :, :, 7, n], ALU.add, r=[('P1s',), ('GU',)],
                       w=[('XC',)])
                cp('act', XCbf[:, :, :, :], XC[:, :, :, 0:NCH], r=[('XC',)], w=[('XCbf',)])
                cp('act', XXbf[:, :, :, :], GU[:, :, :, :], r=[('GU',)], w=[('XXbf',)])
                cp('dve', XC[:, :, :, 0], XC[:, :, :, NCH], r=[('XC',)], w=[('XC',)])

            ybanks = {}

            def tail_y(tb):
                bl = []
                for g in range(4):
                    b = nextbank()
                    bl.append(b)
                    for k in range(4):
                        pr = g * 4 + k
                        mm(psum[0:32, b, k * TB:(k + 1) * TB], Ec[:, 0, pr, :], XXbf[:, 0, pr, :], True, False,
                           r=[KC, ('XXbf',)], w=[('ps', b)])
                        mm(psum[0:32, b, k * TB:(k + 1) * TB], Ec[:, 1, pr, :], XXbf[:, 1, pr, :], False, False,
                           r=[KC, ('XXbf',)], w=[('ps', b)])
                        for tau in range(8):
                            for ri in range(2):
                                mm(psum[0:32, b, k * TB + tau * NCH:k * TB + (tau + 1) * NCH], Et[ri][:, tau, pr, :],
                                   XCbf[:, ri, pr, :], False, tau == 7 and ri == 1, r=[KC, ('XCbf',)], w=[('ps', b)])
                ybanks[tb] = bl

            def tail_rest(tb, si):
                supb = sup2[si]
                for g in range(4):
                    b = ybanks[tb][g]
                    for k in range(4):
                        pr = g * 4 + k
                        stt('dve', yp[:, pr, :], supb[:, pr, :], Dd[:, pr:pr + 1], psum[0:32, b, k * TB:(k + 1) * TB],
                            ALU.mult, ALU.add, r=[('sup', si), KC, ('ps', b)], w=[('yp',)])
                b = nextbank()
                for blk in range(4):
                    for k in range(4):
                        mm(psum[:, b, blk * TB:(blk + 1) * TB], gsel[:, k, :],
                           yp[:, blk * 4 + k, :].rearrange('p (l n) -> p n l', l=8), k == 0, k == 3,
                           r=[('c', 'gsel'), ('yp',)], w=[('ps', b)])
                cp('act', yv[:, :], ps(b, 128, 4 * TB), r=[('ps', b)], w=[('yv',)])
                act(ga[:, :], yv[:, :], AF.Square, r=[('yv',)], w=[('ga',)])
                ts('pool', ga[:, :], ga[:, :], 0.044715, 1.0, ALU.mult, ALU.add, r=[('ga',)], w=[('ga',)])
                tt('pool', ga[:, :], ga[:, :], yv[:, :], ALU.mult, r=[('ga',), ('yv',)], w=[('ga',)])
                act(gb[:, :], ga[:, :], AF.Sigmoid, r=[('ga',)], w=[('gtmp',)], scale=GC)
                tt('pool', yg[:, :, tb:tb + TB], gb[:, :].rearrange('p (k t) -> p k t', k=4),
                   yv[:, :].rearrange('p (k t) -> p k t', k=4), ALU.mult, r=[('gtmp',), ('yv',)], w=[('yg',)])

            tbs = list(range(0, nt, TB))
            front(tbs[0], 0)
            mid(tbs[0])
            for bi_, tb in enumerate(tbs):
                if bi_ + 1 < len(tbs):
                    front(tbs[bi_ + 1], (bi_ + 1) % 2)
                tail_y(tb)
                if bi_ + 1 < len(tbs):
                    mid(tbs[bi_ + 1])
                tail_rest(tb, bi_ % 2)
            gbanks = []
            for co in range(4):
                b = nextbank()
                gbanks.append(b)
                for k in range(4):
                    mm(ps(b, 128, nt), Wglu[:, k, co * 128:(co + 1) * 128], yg[:, k, 0:nt], k == 0, k == 3,
                       r=[K, ('yg',)], w=[('ps', b)])
            for co in range(4):
                b = gbanks[co]
                act(gsig[:, 0:nt], ps(b, 128, nt), AF.Sigmoid, r=[('ps', b), K], w=[('gsig',)], bias=bglu[:, co:co + 1])
                tt('dve', y2[:, co, 0:nt], yg[:, co, 0:nt], gsig[:, 0:nt], ALU.mult, r=[('yg',), ('gsig',)],
                   w=[('yg',)])
            merge_out(tok0, nt, y2, ('yg',), Wsp, K, Ws, KW, 512, False, gsig, gtmp)
        cp('dve', XXst[:, :, :], XC[:, :, :, 0], r=[('XC',)], w=[('xxst',)])

    def phase_out(L, src_t, seg):
        arena[0] = PH0
        Wo = aalloc('Wo', [128, 8, D], BF16)
        t1 = aalloc('t1o', [128, D], F32)
        K = ('W', 'o')
        load_w(Wo[:, :, :], K, 'w_out', L, 0, 8, 0, D)
        dma(gainb[:, :], rawap(W['norm_mix_post'], L * D, [[0, 128], [1, D]]), w=[('c', 'gain')])
        junk_ref[0] = aalloc('junkO', [128, D], BF16)
        xbo = [aalloc('xbo', [128, D], F32) for _ in range(2)]
        if L == 0 and seg == 0:
            dump('mgall', merged[:, :, :], [('mg',)])
        for bi, tb in enumerate(range(0, SEGT, 128)):
            xi = bi % 2
            xb = xbo[xi]
            dma(xb[:, :], rawap(src_t, (seg * SEGT + tb) * D, [[D, 128], [1, D]]), w=[('xbo', xi)])
            banks = []
            for hf in range(2):
                b = nextbank()
                banks.append(b)
                for k in range(8):
                    mm(ps(b, 128, 512), merged[:, k, tb:tb + 128], Wo[:, k, hf * 512:(hf + 1) * 512], k == 0, k == 7,
                       r=[('mg',), K], w=[('ps', b)])
            rowsum_rstd(banks, 128, 8)
            for hf, b in enumerate(banks):
                tt('dve', t1[:, hf * 512:(hf + 1) * 512], ps(b, 128, 512), gains[1][:, hf * 512:(hf + 1) * 512], ALU.mult,
                   r=[('ps', b), ('c', 'gain')], w=[('t1o', hf)])
                stt('dve', xb[:, hf * 512:(hf + 1) * 512], t1[:, hf * 512:(hf + 1) * 512], small[:, 8:9],
                    xb[:, hf * 512:(hf + 1) * 512], ALU.mult, ALU.add, r=[('t1o', hf), ('rsx',), ('xbo', xi)],
                    w=[('xbo', xi)])
            if L == 0 and seg == 0 and tb == 128:
                dump('osmall', small[:, 0:16], [('rsx',), ('ssx', 0), ('ssx', 1), ('ssx', 2), ('ssx', 3)])
                dump('ot1', t1[:, :], [('t1o', 0), ('t1o', 1)])
                dump('ojunk', junk_ref[0][:, :], [('junk',)])
            dma(rawap(xmid, tb * D, [[D, 128], [1, D]]), xb[:, :], r=[('xbo', xi)], w=[('xmid', tb)])
            if L == 0 and seg == 0:
                dump('xmid%d' % tb, xb[:, :], [('xbo', xi)])

    hs = nc.dram_tensor('hs', [22, 128, SEGT], BF16)

    def phase_ffn_a(L, seg):
        arena[0] = UT0
        Wup = aalloc('Wup', [128, 8, DFF], BF16)
        Wga = aalloc('Wga', [128, 8, DFF], BF16)
        NTF = 512
        uF = [aalloc('uF', [128, 8, NTF], BF16) for _ in range(2)]
        NB = 3
        fsets = [(aalloc('ab', [128, NTF + 2], F32), aalloc('vv', [128, NTF], F32), aalloc('gaF', [128, NTF], F32),
                  aalloc('vg', [128, NTF], F32), aalloc('ho', [128, NTF], BF16)) for _ in range(NB)]
        fw = aalloc('fw', [128, 22, 3], F32); fb = aalloc('fb', [128, 22], F32)
        nsets = norm_sets(2)
        nbi = [0]
        K = ('W', 'f')
        KU = WK(('Wc', 'fu'), 704)
        KG = WK(('Wc', 'fg'), 704)
        for q4 in range(4):
            load_wc(Wup, KU, 'ffn_w_up', L, 8, q4 * 704, q4 * 704, 704)
            load_wc(Wga, KG, 'ffn_w_gate', L, 8, q4 * 704, q4 * 704, 704)
        for j in range(3):
            vec_load(fw[:, :, j], 'ffn_conv_w', L, [[1, 128], [128, 22]], K, off=j * DFF)
        vec_load(fb[:, :], 'ffn_conv_b', L, [[1, 128], [128, 22]], K)
        dma(gains2[0][:, :], rawap(W['norm_ffn_pre'], L * D, [[0, 128], [1, D]]), w=[('c', 'gain')])
        gi = 0
        for ti, t0 in enumerate(range(0, SEGT, NTF)):
            uFt = uF[ti % 2]
            ukey = ('uF', ti % 2)
            for j in range(NTF // 128):
                tb = t0 + j * 128
                norm_block(rawap(xmid, tb * D, [[D, 128], [1, D]]), 128, gains[2], ('c', 'gain'), uFt, ukey, j * 128,
                           0, extra_r=[('xmid', tb)], bufs=nsets[nbi[0] % 2])
                nbi[0] += 1
            for ct in range(22):
                si = gi % NB
                gi += 1
                ab, vv, ga, vg, ho = fsets[si]
                b = nextbank()
                proj(b, Wup, KU, ct * 128, 128, uFt, ukey, 0, NTF)
                cp('act', ab[:, 2:2 + NTF], ps(b, 128, NTF), r=[('ps', b)], w=[('ab', si)])
                cp('pool', ab[:, 0:2], ahist[:, ct, :], r=[('ah', ct)], w=[('ab', si)])
                ts('dve', vv[:, :], ab[:, 0:NTF], fw[:, ct, 0:1], fb[:, ct:ct + 1], ALU.mult, ALU.add,
                   r=[('ab', si), K], w=[('vv', si)])
                for j in range(1, 3):
                    stt('dve', vv[:, :], ab[:, j:j + NTF], fw[:, ct, j:j + 1], vv[:, :], ALU.mult, ALU.add,
                        r=[('ab', si), K, ('vv', si)], w=[('vv', si)])
                cp('pool', ahist[:, ct, :], ab[:, NTF:NTF + 2], r=[('ab', si)], w=[('ah', ct)])
                b2 = nextbank()
                proj(b2, Wga, KG, ct * 128, 128, uFt, ukey, 0, NTF)
                tt('dve', vg[:, :], vv[:, :], ps(b2, 128, NTF), ALU.mult, r=[('vv', si), ('ps', b2)], w=[('vg', si)])
                act(ga[:, :], vv[:, :], AF.Square, r=[('vv', si)], w=[('ga', si)], scale=math.sqrt(0.044715))
                stt('dve', ga[:, :], ga[:, :], 1.0, vv[:, :], ALU.add, ALU.mult, r=[('ga', si), ('vv', si)],
                    w=[('ga', si)])
                act(ga[:, :], ga[:, :], AF.Sigmoid, r=[('ga', si)], w=[('ga', si)], scale=GC)
                tt('pool', ho[:, :], ga[:, :], vg[:, :], ALU.mult, r=[('ga', si), ('vg', si)], w=[('ho', si)])
                dma(rawap(hs, ct * 128 * SEGT + t0, [[SEGT, 128], [1, NTF]]), ho[:, :], r=[('ho', si)],
                    w=[('hs', ct, t0)])

    def phase_ffn_b(L, dst_t, seg):
        arena[0] = UT0
        Wd = aalloc('Wd', [128, 22, D], BF16)
        hb = [aalloc('hb', [128, 22, 512], BF16) for _ in range(2)]
        t1 = aalloc('t1f', [128, D], F32)
        xb2 = [aalloc('xb2', [128, D], F32) for _ in range(2)]
        K = ('W', 'fd')
        junk_ref[0] = aalloc('junkF', [128, D], BF16)
        KD = WK(('Wc', 'fd'), 512)
        for hf in range(2):
            load_w(Wd[:, :, hf * 512:(hf + 1) * 512], (KD.base, hf), 'ffn_w_down', L, 0, 22, hf * 512, 512)
        dma(gains2[1][:, :], rawap(W['norm_ffn_post'], L * D, [[0, 128], [1, D]]), w=[('c', 'gain')])
        bi = 0
        for ti, t0 in enumerate(range(0, SEGT, 512)):
            hbt = hb[ti % 2]
            hkey = ('hb', ti % 2)
            for c0 in (0, 11):
                dma(hbt[:, c0:c0 + 11, :], rawap(hs, c0 * 128 * SEGT + t0, [[SEGT, 128], [128 * SEGT, 11], [1, 512]]),
                    r=[('hs', ct, t0) for ct in range(c0, c0 + 11)], w=[hkey])
            for j in range(4):
                tb = t0 + j * 128
                xi = bi % 2
                bi += 1
                xb = xb2[xi]
                dma(xb[:, :], rawap(xmid, tb * D, [[D, 128], [1, D]]), r=[('xmid', tb)], w=[('xb2', xi)])
                banks = []
                for hf in range(2):
                    b = nextbank()
                    banks.append(b)
                    for k in range(22):
                        mm(ps(b, 128, 512), hbt[:, k, j * 128:(j + 1) * 128], Wd[:, k, hf * 512:(hf + 1) * 512], k == 0,
                           k == 21, r=[hkey, (KD.base, hf)], w=[('ps', b)])
                rowsum_rstd(banks, 128, 9)
                for hf, b in enumerate(banks):
                    tt('dve', t1[:, hf * 512:(hf + 1) * 512], ps(b, 128, 512), gains[3][:, hf * 512:(hf + 1) * 512],
                       ALU.mult, r=[('ps', b), ('c', 'gain')], w=[('t1f', hf)])
                    stt('dve', xb[:, hf * 512:(hf + 1) * 512], t1[:, hf * 512:(hf + 1) * 512], small[:, 9:10],
                        xb[:, hf * 512:(hf + 1) * 512], ALU.mult, ALU.add, r=[('t1f', hf), ('rsx',), ('xb2', xi)],
                        w=[('xb2', xi)])
                dma(rawap(dst_t, (seg * SEGT + tb) * D, [[D, 128], [1, D]]), xb[:, :], r=[('xb2', xi)],
                    w=[('dst', seg, tb)])

    for L in range(nlayer):
        src_t = x_in if L == 0 else x1
        dst_t = out_t if L == nlayer - 1 else x1
        for stt_ in (St32, Ct32, Nt32):
            memset('pool', stt_[:, :, :], 0.0, [('st', id(stt_), h) for h in range(4)])
        for bf_, s32 in ((Stbf, St32), (Ctbf, Ct32), (Ntbf, Nt32)):
            for h in range(4):
                cp('pool', bf_[:, h, :], s32[:, h, :], r=[('st', id(s32), h)], w=[('stbf', id(bf_), h)])
        memset('pool', XXst[:, :, :], 0.0, [('xxst',)])
        for ct in range(8):
            memset('pool', zch[:, ct, :], 0.0, [('zch', ct)])
        for ct in range(22):
            memset('pool', ahist[:, ct, :], 0.0, [('ah', ct)])
        S.barrier()
        s5_setup(L)
        for seg in range(nseg):
            S.barrier()
            dma(gainb[:, :], rawap(W['norm_mix_pre'], L * D, [[0, 128], [1, D]]), w=[('c', 'gain')])
            arena[0] = PH0
            nsets = norm_sets(3)
            for bi, tb in enumerate(range(0, SEGT, 128)):
                extra = [('dst', seg, tb)] if L > 0 else []
                norm_block(rawap(src_t, (seg * SEGT + tb) * D, [[D, 128], [1, D]]), 128, gains[0], ('c', 'gain'), uT,
                           ('uT',), tb, bi % 2, extra_r=extra, bufs=nsets[bi % 3])
            S.barrier()
            if 'gla' not in skip:
                phase_gla(L)
                S.barrier()
            if 'ml' not in skip:
                phase_ml(L)
                S.barrier()
            if 's5' not in skip:
                phase_s5(L)
                S.barrier()
            if 'out' not in skip:
                phase_out(L, src_t, seg)
                S.barrier()
            if 'ffn' not in skip:
                phase_ffn_a(L, seg)
                S.barrier()
                phase_ffn_b(L, dst_t, seg)
                S.barrier()

    with nc.semaphore('e_pe') as s0, nc.semaphore('e_act') as s1, nc.semaphore('e_dve') as s2, \
            nc.semaphore('e_pool') as s3, nc.semaphore('e_sp') as s4:
        esem = {'pe': s0, 'act': s1, 'dve': s2, 'pool': s3, 'sp': s4}
        import contextlib
        with contextlib.ExitStack() as es:
            ssem = [es.enter_context(nc.semaphore('dslot%d' % i)) for i in range(S.nslots)]
            with nc.Block() as block:
                S.emit(nc, block, esem, ssem)
    build.dbg_names = dbg_names
    build.amax = amax
    build.ph0 = PH0
    return nc


def kernel(**inputs):
    nc = build()
    m = {'x': np.ascontiguousarray(inputs['x'].reshape(SEQ, D), dtype=np.float32)}
    for n in WNAMES:
        m[n] = np.ascontiguousarray(inputs[n], dtype=np.float32)
    res = run_bass_kernel_spmd(nc, [m], core_ids=[0])
    return res.results[0]['out'].reshape(1, SEQ, D).astype(np.float32)
```

```python
import math
import numpy as np
import concourse.bass as bass
import concourse.mybir as mybir
from concourse.bass_utils import run_bass_kernel_spmd

F32 = mybir.dt.float32
BF16 = mybir.dt.bfloat16
ALU = mybir.AluOpType
AF = mybir.ActivationFunctionType

NCORE = 8
D = 1024
DIN = 7704
DFF = 2816
SEQ = 16384
OWN = SEQ // NCORE
HALO = 64
PRE = 3
NT = OWN + HALO
NW = NT + PRE
EPS = 1e-6
O_GQ, O_GK, O_GV, O_LR, O_GR, O_SU, O_MQ, O_MK, O_MV, O_MI, O_MF, O_MO, O_GATE = (
    0, 512, 1024, 1536, 1552, 2064, 2576, 3088, 3600, 4112, 4116, 4120, 4632)
TILES = [(0, 64), (64, 512), (576, 512), (1088, 512), (1600, 512)]
SNAP_TOK = OWN
LS5 = 8
GC = 1.5957691216057308

WNAMES = ['norm_mix_pre', 'norm_mix_post', 'norm_ffn_pre', 'norm_ffn_post', 'w_in',
          'gla_w_gk', 'gla_b_gk', 'gla_norm', 'gla_w_proj',
          's5_a_re', 's5_a_im', 's5_log_dt', 's5_b_re', 's5_b_im', 's5_c_re', 's5_c_im', 's5_d',
          's5_w_glu', 's5_b_glu', 's5_w_proj', 'ml_conv_w', 'ml_conv_b', 'ml_b_i', 'ml_b_f',
          'ml_w_proj', 'w_out', 'ffn_w_up', 'ffn_w_gate', 'ffn_conv_w', 'ffn_conv_b', 'ffn_w_down']
WSHAPES = {'norm_mix_pre': [D], 'norm_mix_post': [D], 'norm_ffn_pre': [D], 'norm_ffn_post': [D],
           'w_in': [D, DIN], 'gla_w_gk': [16, 512], 'gla_b_gk': [512], 'gla_norm': [128],
           'gla_w_proj': [512, D], 's5_a_re': [32, 64], 's5_a_im': [32, 64], 's5_log_dt': [32],
           's5_b_re': [32, 64, 16], 's5_b_im': [32, 64, 16], 's5_c_re': [32, 16, 64],
           's5_c_im': [32, 16, 64], 's5_d': [32, 16], 's5_w_glu': [512, 512], 's5_b_glu': [512],
           's5_w_proj': [512, D], 'ml_conv_w': [4, D], 'ml_conv_b': [D], 'ml_b_i': [4], 'ml_b_f': [4],
           'ml_w_proj': [512, D], 'w_out': [D, D], 'ffn_w_up': [D, DFF], 'ffn_w_gate': [D, DFF],
           'ffn_conv_w': [3, DFF], 'ffn_conv_b': [DFF], 'ffn_w_down': [DFF, D]}


class Sched:
    ENGS = ('pe', 'act', 'dve', 'pool', 'sp')
    SELF_SYNC = ('act', 'dve', 'pool')

    def __init__(self, nslots=16):
        self.streams = {e: [] for e in self.ENGS}
        self.ops = []
        self.lastw = {}
        self.rd = {}
        self.nslots = nslots
        self.slot_cnt = [0] * nslots
        self.slot_last = [None] * nslots
        self.dma_n = 0

    def add(self, eng, fn, r=(), w=(), dma=False):
        oid = len(self.ops)
        deps = set()
        raw = set()
        for k in r:
            p = self.lastw.get(k)
            if p is not None:
                deps.add(p)
                raw.add(p)
        for k in w:
            p = self.lastw.get(k)
            if p is not None:
                deps.add(p)
            rr = self.rd.get(k)
            if rr:
                deps.update(rr[0].values())
                deps.update(rr[1])
        o = {'id': oid, 'eng': eng, 'fn': fn, 'dma': dma, 'flag': False, 'raw': raw}
        if dma:
            s = self.dma_n % self.nslots
            self.dma_n += 1
            if self.slot_last[s] is not None:
                deps.add(self.slot_last[s])
            self.slot_cnt[s] += 1
            o['slot'] = s
            o['slotval'] = 16 * self.slot_cnt[s]
            self.slot_last[s] = oid
        for k in w:
            self.lastw[k] = oid
            self.rd[k] = ({}, [])
        for k in r:
            rr = self.rd.setdefault(k, ({}, []))
            if dma:
                rr[1].append(oid)
            else:
                rr[0][eng] = oid
        deps.discard(oid)
        o['deps'] = deps
        self.ops.append(o)
        self.streams[eng].append(o)
        return o

    def barrier(self):
        last = {}
        for e in self.ENGS:
            st = self.streams[e]
            for o in reversed(st):
                if o['fn'] is not None and not o['dma']:
                    last[e] = o['id']
                    break
        dmas = [x for x in self.slot_last if x is not None]
        for e in self.ENGS:
            oid = len(self.ops)
            deps = set(v for k, v in last.items() if k != e) | set(dmas)
            o = {'id': oid, 'eng': e, 'fn': None, 'dma': False, 'flag': False, 'deps': deps}
            self.ops.append(o)
            self.streams[e].append(o)

    def emit(self, nc, block, esem, ssem):
        ops = self.ops
        for o in ops:
            for d in o['deps']:
                dd = ops[d]
                if dd['dma']:
                    continue
                if dd['eng'] != o['eng'] or (d in o.get('raw', ()) and o['eng'] in self.SELF_SYNC):
                    dd['flag'] = True
        for e in self.ENGS:
            c = 0
            for o in self.streams[e]:
                if o['flag'] and not o['dma']:
                    c += 1
                    o['fidx'] = c
        sched = self

        def run(ename, eng):
            waited = {}
            for o in sched.streams[ename]:
                need = {}
                for d in o['deps']:
                    dd = ops[d]
                    if dd['dma']:
                        key = ('s', dd['slot'])
                        val = dd['slotval']
                    elif dd['eng'] == ename and not (d in o.get('raw', ()) and ename in sched.SELF_SYNC):
                        continue
                    else:
                        key = ('e', dd['eng'])
                        val = dd['fidx']
                    if val > waited.get(key, 0) and val > need.get(key, 0):
                        need[key] = val
                for key, val in need.items():
                    sem = ssem[key[1]] if key[0] == 's' else esem[key[1]]
                    eng.wait_ge(sem, val)
                    waited[key] = val
                if o['fn'] is None:
                    continue
                ins = o['fn'](eng)
                if o['dma']:
                    ins.then_inc(ssem[o['slot']], 16)
                elif o['flag']:
                    ins.then_inc(esem[ename], 1)
            if ename == 'sp':
                for s in range(sched.nslots):
                    if sched.slot_cnt[s]:
                        eng.wait_ge(ssem[s], 16 * sched.slot_cnt[s])

        @block.tensor
        def _(e):
            run('pe', e)

        @block.scalar
        def _(e):
            run('act', e)

        @block.vector
        def _(e):
            run('dve', e)

        @block.gpsimd
        def _(e):
            run('pool', e)

        @block.sync
        def _(e):
            run('sp', e)


def rawap(t, offset, pat):
    return bass.AP(tensor=t, offset=offset, ap=[list(p) for p in pat])


SEGT = 2048
DBGT = 512
BAR = False
P2ENG = 'dve'
TILE = 512
PI = math.pi


def build(nseg=8, nlayer=2, dbg=False, skip=()):
    nc = bass.Bass("TRN2", target_bir_lowering=False)
    S = Sched()
    ntok = nseg * SEGT
    x_in = nc.dram_tensor('x', [ntok, D], F32, kind='ExternalInput')
    W = {n: nc.dram_tensor(n, [2] + WSHAPES[n], F32, kind='ExternalInput') for n in WNAMES}
    WSZ = {n: int(np.prod(WSHAPES[n])) for n in WNAMES}
    out_t = nc.dram_tensor('out', [ntok, D], F32, kind='ExternalOutput')
    x1 = nc.dram_tensor('x1', [ntok, D], F32)
    xmid = nc.dram_tensor('xmid', [SEGT, D], F32)

    sb_top = [16512]
    SB_END = 229376

    def alloc(name, shape, dt, at=None):
        nbytes = int(np.prod(shape[1:])) * (4 if dt == F32 else 2)
        nbytes = (nbytes + 63) // 64 * 64
        if at is None:
            off = sb_top[0]
            sb_top[0] += nbytes
        else:
            off = at
        assert off + nbytes <= SB_END, (name, off, nbytes)
        return nc.alloc_sbuf_tensor_at(name, list(shape), dt, offset=off)

    psum = nc.alloc_psum_tensor('psum', [128, 8, 512], F32)

    def ps(b, p=128, n=512):
        return psum[0:p, b, 0:n]

    bank_rr = [0]

    def nextbank():
        b = bank_rr[0] % 5
        bank_rr[0] += 1
        return b

    ident = alloc('ident', [128, 128], BF16)
    onesbf = alloc('onesbf', [128, 128], BF16)
    ones32 = alloc('ones32', [128, 512], F32)
    maskut = alloc('maskut', [128, 64], F32)
    eps_t = alloc('eps_t', [128, 1], F32)
    negpi = alloc('negpi', [128, 1], F32)
    gainb = alloc('gainb', [128, D], F32)
    gains2 = [gainb, gainb]
    gains = [gainb, gainb, gainb, gainb]
    junk_ref = [None]
    small = alloc('small', [128, 64], F32)
    stage0 = alloc('stage0', [128, 512], F32)
    stage = [stage0, stage0]
    St32 = alloc('St32', [128, 4, 128], F32)
    Ct32 = alloc('Ct32', [128, 4, 128], F32)
    Nt32 = alloc('Nt32', [128, 4, 128], F32)
    Stbf = alloc('Stbf', [128, 4, 128], BF16)
    Ctbf = alloc('Ctbf', [128, 4, 128], BF16)
    Ntbf = alloc('Ntbf', [128, 4, 128], BF16)
    XXst = alloc('XXst', [128, 2, 16], F32)
    zch = alloc('zch', [128, 8, 3], F32)
    ahist = alloc('ahist', [128, 22, 2], F32)
    BBT = alloc('BBT', [32, 16, 2, 128], BF16)
    Ec = alloc('Ec', [128, 2, 16, 32], BF16)
    AA1 = alloc('AA1', [128, 2, 16], F32)
    A1i = alloc('A1i', [128, 16], F32)
    nA1i = alloc('nA1i', [128, 16], F32)
    Dd = alloc('Dd', [32, 16], F32)
    PW = alloc('PW', [128, 8, 2, 16], F32)
    nPWi = alloc('nPWi', [128, 8, 16], F32)
    AA8 = alloc('AA8', [128, 2, 16], F32)
    gsel = alloc('gsel', [32, 4, 128], BF16)
    Et = [alloc('Et%d' % i, [128, 8, 16, 32], BF16) for i in range(2)]
    sel_ref = [None, None]
    UT0 = sb_top[0]
    uT = alloc('uT', [128, 8, SEGT], BF16)
    merged = alloc('merged', [128, 8, SEGT], BF16)
    PH0 = sb_top[0]
    arena = [PH0]
    an = [0]

    amax = {}

    def aalloc(name, shape, dt):
        an[0] += 1
        t = alloc('%s_%d' % (name, an[0]), shape, dt, at=arena[0])
        nb = int(np.prod(shape[1:])) * (4 if dt == F32 else 2)
        arena[0] += (nb + 63) // 64 * 64
        amax['top'] = max(amax.get('top', 0), arena[0])
        return t

    cnt = {'stage': 0}
    dumped = {}

    def dma(out, in_, r=(), w=(), slow=False):
        if slow:
            return S.add('sp', lambda e: e.dma_start(out=out, in_=in_, allow_slow_non_contiguous=True),
                         r=r, w=w, dma=True)
        return S.add('sp', lambda e: e.dma_start(out=out, in_=in_), r=r, w=w, dma=True)

    def act(out, in_, func, r, w, bias=None, scale=None, accum=None):
        kw = {}
        if bias is not None:
            kw['bias'] = bias
        if scale is not None:
            kw['scale'] = scale
        if accum is not None:
            kw['accum_out'] = accum
        return S.add('act', lambda e: e.activation(out=out, in_=in_, func=func, **kw), r=r, w=w)

    def tt(eng, out, in0, in1, op, r, w):
        return S.add(eng, lambda e: e.tensor_tensor(out=out, in0=in0, in1=in1, op=op), r=r, w=w)

    def ts(eng, out, in0, s1, s2, op0, op1, r, w):
        if s2 is None:
            return S.add(eng, lambda e: e.tensor_single_scalar(out=out, in_=in0, scalar=s1, op=op0), r=r, w=w)
        return S.add(eng, lambda e: e.tensor_scalar(out=out, in0=in0, scalar1=s1, scalar2=s2, op0=op0, op1=op1),
                     r=r, w=w)

    def stt(eng, out, in0, sc, in1, op0, op1, r, w):
        return S.add(eng, lambda e: e.scalar_tensor_tensor(out=out, in0=in0, scalar=sc, in1=in1, op0=op0, op1=op1),
                     r=r, w=w)

    def cp(eng, out, in_, r, w):
        if eng == 'act':
            return S.add('act', lambda e: e.copy(out=out, in_=in_), r=r, w=w)
        return S.add(eng, lambda e: e.tensor_copy(out=out, in_=in_), r=r, w=w)

    def mm(out, lhsT, rhs, start, stop, r, w):
        return S.add('pe', lambda e: e.matmul(out, lhsT, rhs, start=start, stop=stop, skip_group_check=True),
                     r=r, w=w)

    def tr(out, in_, idn, r, w):
        return S.add('pe', lambda e: e.transpose(out, in_, idn), r=r, w=w)

    def memset(eng, ap, val, w):
        return S.add(eng, lambda e: e.memset(ap, val), r=(), w=w)

    dbg_names = []

    def dump(name, ap, rkeys):
        if not dbg:
            return
        t = nc.dram_tensor('dbg_' + name, list(ap.shape), ap.dtype, kind='ExternalOutput')
        dbg_names.append('dbg_' + name)
        full = t.ap()
        dma(full, ap, r=rkeys, w=[('dbg', name)])

    scr = nc.dram_tensor('scr_bar', [64, 64], F32)
    barn = [0]

    def lbar(items):
        for ap_, key in items:
            i = barn[0] % 64
            barn[0] += 1
            if ap_.dtype != F32:
                continue
            dma(scr.ap()[i:i + 1, 0:1], ap_, r=[key], w=[('scr', i)])

    def recip(out, in_, r, w):
        return S.add('dve', lambda e: e.reciprocal(out=out, in_=in_), r=r, w=w)

    def load_w(dst, key, name, L, row0, kt, col0, ncols, rows=128):
        src_t = W[name]
        ld = WSHAPES[name][1] if len(WSHAPES[name]) > 1 else 1
        base = L * WSZ[name]
        kstep = max(1, 8 if kt <= 8 else 11)
        for k0 in range(0, kt, kstep):
            kn = min(kstep, kt - k0)
            src = rawap(src_t, base + (row0 + k0 * 128) * ld + col0, [[ld, rows], [128 * ld, kn], [1, ncols]])
            dd = dst[:, k0:k0 + kn, :]
            S.add('pool', lambda e, dd=dd, src=src: e.dma_start(out=dd, in_=src), r=(), w=[key], dma=True)

    class WK:
        def __init__(self, base, chunk=512):
            self.base = base
            self.chunk = chunk

        def keys(self, c0, n):
            return [(self.base, j) for j in range(c0 // self.chunk, (c0 + n - 1) // self.chunk + 1)]

    def wkeys(wkey, c0, n):
        return wkey.keys(c0, n) if isinstance(wkey, WK) else [wkey]

    def load_wc(dst_t, wk, name, L, kt, dcol0, scol0, ncols):
        c = dcol0
        end = dcol0 + ncols
        while c < end:
            ce = min(end, (c // wk.chunk + 1) * wk.chunk)
            load_w(dst_t[:, :, c:ce], (wk.base, c // wk.chunk), name, L, 0, kt, scol0 + (c - dcol0), ce - c)
            c = ce

    def vec_load(dst, name, L, pat, key, off=0, slow=True):
        dma(dst, rawap(W[name], L * WSZ[name] + off, pat), w=[key], slow=slow)

    memset('pool', ones32[:, :], 1.0, [('c', 'ones32')])
    memset('pool', eps_t[:, :], EPS, [('c', 'eps')])
    memset('pool', negpi[:, :], -PI, [('c', 'negpi')])
    memset('pool', maskut[0:64, :], 1.0, [('c', 'mask')])
    S.add('pool', lambda e: e.affine_select(out=maskut[0:64, :], in_=maskut[0:64, :], pattern=[[1, 64]],
                                            compare_op=ALU.is_ge, fill=0.0, base=0, channel_multiplier=-1),
          r=[('c', 'mask')], w=[('c', 'mask')])
    cp('pool', onesbf[:, :], ones32[:, 0:128], r=[('c', 'ones32')], w=[('c', 'onesbf')])
    dma(maskut[64:128, :], maskut[0:64, :], r=[('c', 'mask')], w=[('c', 'mask')])
    identf = stage[1]
    memset('pool', identf[:, 0:128], 1.0, [('stage', 1)])
    S.add('pool', lambda e: e.affine_select(out=identf[:, 0:128], in_=identf[:, 0:128], pattern=[[1, 128]],
                                            compare_op=ALU.is_equal, fill=0.0, base=0, channel_multiplier=-1),
          r=[('stage', 1)], w=[('stage', 1)])
    cp('pool', ident[:, :], identf[:, 0:128], r=[('stage', 1)], w=[('c', 'ident')])
    for k in range(4):
        memset('pool', identf[0:32, 128:256], 1.0, [('stage', 1)])
        S.add('pool', lambda e, k=k: e.affine_select(out=identf[0:32, 128:256], in_=identf[0:32, 128:256],
                                                     pattern=[[1, 128]], compare_op=ALU.is_equal, fill=0.0,
                                                     base=-32 * k, channel_multiplier=-1),
              r=[('stage', 1)], w=[('stage', 1)])
        cp('pool', gsel[:, k, :], identf[0:32, 128:256], r=[('stage', 1)], w=[('c', 'gsel')])
    def norm_block(src_ap, nb, gain_t, gkey, dstT, dkey, dcol0, xi, extra_r=(), bufs=None):
        xb, un_, junk_, sc, sid = bufs
        kx, ku, kj = ('nbx', sid), ('nbu', sid), ('nbj', sid)
        b7 = nextbank()
        dma(xb[0:nb, :], src_ap, r=extra_r, w=[kx])
        act(junk_[0:nb, :], xb[0:nb, :], AF.Square, r=[kx], w=[kj])
        S.add('dve', lambda e: e.reduce_sum(out=small[0:nb, sc:sc + 1], in_=junk_[0:nb, :], axis=mybir.AxisListType.X),
              r=[kj], w=[('ss', sid)])
        act(small[0:nb, sc + 1:sc + 2], small[0:nb, sc:sc + 1], AF.Sqrt, r=[('ss', sid), ('c', 'eps')], w=[('sd', sid)],
            bias=eps_t[0:nb, :], scale=1.0 / D)
        recip(small[0:nb, sc + 2:sc + 3], small[0:nb, sc + 1:sc + 2], r=[('sd', sid)], w=[('rs', sid)])
        stt('dve', un_[0:nb, :], xb[0:nb, :], small[0:nb, sc + 2:sc + 3], gain_t[0:nb, :], ALU.mult, ALU.mult,
            r=[kx, ('rs', sid), gkey], w=[ku])
        pbf = psum[:, b7, :].bitcast(BF16)
        for k in range(8):
            tr(pbf[:, k * 128:k * 128 + nb], un_[0:nb, k * 128:(k + 1) * 128], ident[0:nb, 0:nb],
               r=[ku, ('c', 'ident')], w=[('ps', b7)])
        cp('act', dstT[:, :, dcol0:dcol0 + nb], pbf.rearrange('p (k t) -> p k t', k=8)[:, :, 0:nb], r=[('ps', b7)],
           w=[dkey])

    def norm_sets(n):
        out = []
        for i in range(n):
            out.append((aalloc('nbx', [128, D], F32), aalloc('nbu', [128, D], BF16), aalloc('nbj', [128, D], BF16),
                        20 + 3 * i, i))
        return out

    def rowsum_rstd(banks, nb, col):
        for hf, b in enumerate(banks):
            junk = junk_ref[0]
            act(junk[0:nb, hf * 512:(hf + 1) * 512], ps(b, nb, 512), AF.Square, r=[('ps', b)], w=[('junk',)])
            S.add('dve', lambda e, hf=hf: e.reduce_sum(out=small[0:nb, 4 + hf:5 + hf],
                                                       in_=junk[0:nb, hf * 512:(hf + 1) * 512],
                                                       axis=mybir.AxisListType.X), r=[('junk',)], w=[('ssx', hf)])
        tt('dve', small[0:nb, 6:7], small[0:nb, 4:5], small[0:nb, 5:6], ALU.add, r=[('ssx', 0), ('ssx', 1)],
           w=[('ssx', 2)])
        act(small[0:nb, 7:8], small[0:nb, 6:7], AF.Sqrt, r=[('ssx', 2), ('c', 'eps')], w=[('ssx', 3)],
            bias=eps_t[0:nb, :], scale=1.0 / D)
        recip(small[0:nb, col:col + 1], small[0:nb, 7:8], r=[('ssx', 3)], w=[('rsx',)])

    def proj(bank, wt, wkey, c0, M, src, skey, tok0, nt, kt=8):
        for k in range(kt):
            mm(ps(bank, M, nt), wt[:, k, c0:c0 + M], src[:, k, tok0:tok0 + nt], k == 0, k == kt - 1,
               r=wkeys(wkey, c0, M) + [skey], w=[('ps', bank)])

    def gelu(dst, v, n, vkey, dkey, tmpa, tmpb, P=128):
        act(tmpa[0:P, 0:n], v, AF.Square, r=[vkey], w=[('ga',)])
        ts('dve', tmpa[0:P, 0:n], tmpa[0:P, 0:n], 0.044715, 1.0, ALU.mult, ALU.add, r=[('ga',)], w=[('ga',)])
        tt('dve', tmpa[0:P, 0:n], tmpa[0:P, 0:n], v, ALU.mult, r=[('ga',), vkey], w=[('ga',)])
        act(tmpb[0:P, 0:n], tmpa[0:P, 0:n], AF.Sigmoid, r=[('ga',)], w=[('gb',)], scale=GC)
        tt('pool', dst, tmpb[0:P, 0:n], v, ALU.mult, r=[('gb',), vkey], w=[dkey])

    def s5_setup(L):
        arena[0] = PH0
        aR = aalloc('aR', [128, 16], F32); aI = aalloc('aI', [128, 16], F32); ldt = aalloc('ldt', [128, 16], F32)
        t = [aalloc('s5t', [128, 16], F32) for _ in range(12)]
        bR = aalloc('bR', [128, 16, 16], F32); bI = aalloc('bI', [128, 16, 16], F32)
        cR = aalloc('cR', [128, 16, 16], F32); cI = aalloc('cI', [128, 16, 16], F32)
        u1 = aalloc('u1', [128, 16, 16], F32); u2 = aalloc('u2', [128, 16, 16], F32)
        bsrc = [aalloc('bsrc', [128, 16, 32], BF16) for _ in range(2)]
        K = ('s5c', L)
        vec_load(aR[:, :], 's5_a_re', L, [[1, 128], [128, 16]], K)
        vec_load(aI[:, :], 's5_a_im', L, [[1, 128], [128, 16]], K)
        for e in range(2):
            vec_load(ldt[e * 64:(e + 1) * 64, :], 's5_log_dt', L, [[0, 64], [2, 16]], K, off=e)
        vec_load(bR[:, :, :], 's5_b_re', L, [[16, 128], [2048, 16], [1, 16]], K, slow=False)
        vec_load(bI[:, :, :], 's5_b_im', L, [[16, 128], [2048, 16], [1, 16]], K, slow=False)
        for e in range(2):
            for pr in range(16):
                vec_load(cR[e * 64:(e + 1) * 64, pr, :], 's5_c_re', L, [[1, 64], [64, 16]], K, off=e * 1024 + pr * 2048)
                vec_load(cI[e * 64:(e + 1) * 64, pr, :], 's5_c_im', L, [[1, 64], [64, 16]], K, off=e * 1024 + pr * 2048)
        vec_load(Dd[:, :], 's5_d', L, [[1, 32], [32, 16]], K)
        R = [K]
        dt_, dre, dim, mag, sn, cs, A1r, den, zr, fre, fim, tmp = t
        act(dt_[:, :], ldt[:, :], AF.Exp, r=R, w=R)
        tt('dve', dre[:, :], dt_[:, :], aR[:, :], ALU.mult, r=R, w=R)
        tt('dve', dim[:, :], dt_[:, :], aI[:, :], ALU.mult, r=R, w=R)
        act(mag[:, :], dre[:, :], AF.Exp, r=R, w=R)
        cp('dve', sn[:, :], dim[:, :], r=R, w=R)
        ts('dve', cs[:, :], dim[:, :], 0.5 * PI, None, ALU.add, None, r=R, w=R)
        for arr in (sn, cs):
            cp('dve', den[:, :], arr[:, :], r=R, w=R)
            for th in (PI, 3 * PI, 5 * PI, 7 * PI):
                ts('dve', tmp[:, :], den[:, :], th, None, ALU.is_ge, None, r=R, w=R)
                stt('dve', arr[:, :], tmp[:, :], -2 * PI, arr[:, :], ALU.mult, ALU.add, r=R, w=R)
            act(arr[:, :], arr[:, :], AF.Sin, r=R, w=R)
        tt('dve', A1r[:, :], mag[:, :], cs[:, :], ALU.mult, r=R, w=R)
        tt('dve', A1i[:, :], mag[:, :], sn[:, :], ALU.mult, r=R, w=R)
        ts('dve', nA1i[:, :], A1i[:, :], -1.0, None, ALU.mult, None, r=R, w=R)
        cp('dve', AA1[:, 0, :], A1r[:, :], r=R, w=R)
        cp('dve', AA1[:, 1, :], A1r[:, :], r=R, w=R)
        cp('dve', PW[:, 0, 0, :], A1r[:, :], r=R, w=R)
        cp('dve', PW[:, 0, 1, :], A1i[:, :], r=R, w=R)
        for k in range(7):
            tt('dve', tmp[:, :], PW[:, k, 0, :], A1r[:, :], ALU.mult, r=R, w=R)
            tt('dve', den[:, :], PW[:, k, 1, :], A1i[:, :], ALU.mult, r=R, w=R)
            tt('dve', PW[:, k + 1, 0, :], tmp[:, :], den[:, :], ALU.subtract, r=R, w=R)
            tt('dve', tmp[:, :], PW[:, k, 0, :], A1i[:, :], ALU.mult, r=R, w=R)
            tt('dve', den[:, :], PW[:, k, 1, :], A1r[:, :], ALU.mult, r=R, w=R)
            tt('dve', PW[:, k + 1, 1, :], tmp[:, :], den[:, :], ALU.add, r=R, w=R)
        for k in range(8):
            ts('dve', nPWi[:, k, :], PW[:, k, 1, :], -1.0, None, ALU.mult, None, r=R, w=R)
        cp('dve', AA8[:, 0, :], PW[:, 7, 0, :], r=R, w=R)
        cp('dve', AA8[:, 1, :], PW[:, 7, 0, :], r=R, w=R)
        tt('dve', den[:, :], aR[:, :], aR[:, :], ALU.mult, r=R, w=R)
        tt('dve', tmp[:, :], aI[:, :], aI[:, :], ALU.mult, r=R, w=R)
        tt('dve', den[:, :], den[:, :], tmp[:, :], ALU.add, r=R, w=R)
        recip(den[:, :], den[:, :], r=R, w=R)
        ts('dve', zr[:, :], A1r[:, :], -1.0, None, ALU.add, None, r=R, w=R)
        tt('dve', fre[:, :], zr[:, :], aR[:, :], ALU.mult, r=R, w=R)
        tt('dve', tmp[:, :], A1i[:, :], aI[:, :], ALU.mult, r=R, w=R)
        tt('dve', fre[:, :], fre[:, :], tmp[:, :], ALU.add, r=R, w=R)
        tt('dve', fre[:, :], fre[:, :], den[:, :], ALU.mult, r=R, w=R)
        tt('dve', fim[:, :], A1i[:, :], aR[:, :], ALU.mult, r=R, w=R)
        tt('dve', tmp[:, :], zr[:, :], aI[:, :], ALU.mult, r=R, w=R)
        tt('dve', fim[:, :], fim[:, :], tmp[:, :], ALU.subtract, r=R, w=R)
        tt('dve', fim[:, :], fim[:, :], den[:, :], ALU.mult, r=R, w=R)
        frb = fre[:, :].unsqueeze(2).to_broadcast([128, 16, 16])
        fib = fim[:, :].unsqueeze(2).to_broadcast([128, 16, 16])
        tt('dve', u1[:, :, :], bR[:, :, :], frb, ALU.mult, r=R, w=R)
        tt('dve', u2[:, :, :], bI[:, :, :], fib, ALU.mult, r=R, w=R)
        tt('dve', u1[:, :, :], u1[:, :, :], u2[:, :, :], ALU.subtract, r=R, w=R)
        tt('dve', u2[:, :, :], bI[:, :, :], frb, ALU.mult, r=R, w=R)
        tt('dve', bI[:, :, :], bR[:, :, :], fib, ALU.mult, r=R, w=R)
        tt('dve', u2[:, :, :], u2[:, :, :], bI[:, :, :], ALU.add, r=R, w=R)
        ts('dve', cI[:, :, :], cI[:, :, :], -1.0, None, ALU.mult, None, r=R, w=R)
        for ri, (bsrc_, srcb, srcc) in enumerate(((bsrc[0], u1, cR), (bsrc[1], u2, cI))):
            memset('dve', bsrc_[:, :, :], 0.0, R)
            memset('dve', Ec[:, ri, :, :], 0.0, R)
            for e in range(2):
                cp('dve', bsrc_[e * 64:(e + 1) * 64, :, e * 16:(e + 1) * 16], srcb[e * 64:(e + 1) * 64, :, :], r=R, w=R)
                cp('dve', Ec[e * 64:(e + 1) * 64, ri, :, e * 16:(e + 1) * 16], srcc[e * 64:(e + 1) * 64, :, :], r=R, w=R)
        for ri in range(2):
            memset('dve', Et[ri][:, :, :, :], 0.0, R)
        for tau in range(8):
            prb = PW[:, tau, 0, :].unsqueeze(2).to_broadcast([128, 16, 16])
            pib = PW[:, tau, 1, :].unsqueeze(2).to_broadcast([128, 16, 16])
            tt('dve', u1[:, :, :], cR[:, :, :], prb, ALU.mult, r=R, w=R)
            tt('dve', u2[:, :, :], cI[:, :, :], pib, ALU.mult, r=R, w=R)
            tt('dve', u1[:, :, :], u1[:, :, :], u2[:, :, :], ALU.add, r=R, w=R)
            tt('dve', u2[:, :, :], cI[:, :, :], prb, ALU.mult, r=R, w=R)
            tt('dve', bR[:, :, :], cR[:, :, :], pib, ALU.mult, r=R, w=R)
            tt('dve', u2[:, :, :], u2[:, :, :], bR[:, :, :], ALU.subtract, r=R, w=R)
            for ri, srcE in ((0, u1), (1, u2)):
                for e in range(2):
                    cp('dve', Et[ri][e * 64:(e + 1) * 64, tau, :, e * 16:(e + 1) * 16], srcE[e * 64:(e + 1) * 64, :, :],
                       r=R, w=R)
        pbf = psum[:, 7, :].bitcast(BF16)
        for ri in range(2):
            for g4 in range(2):
                for j in range(8):
                    pr = g4 * 8 + j
                    tr(pbf[0:32, j * 128:(j + 1) * 128], bsrc[ri][:, pr, :], ident[:, :], r=R + [('c', 'ident')],
                       w=[('ps', 7)])
                cp('act', BBT[:, g4 * 8:(g4 + 1) * 8, ri, :], pbf[0:32, :].rearrange('p (j q) -> p j q', j=8),
                   r=[('ps', 7)], w=R)

    SBANK = (6, 5)

    def prep_head(nt, hi, qa, qkeys, ka, kkeys, qscale, kscale, B):
        qd2, ki2, kd2, kdT, sT, E1, E2, E3, dec2 = B
        stt('dve', qd2[hi][:, 0:nt], qa, qscale, E1[:, 0:nt], ALU.mult, ALU.mult, r=qkeys + [('E1',)], w=[('qd', hi)])
        stt('dve', ki2[hi][:, 0:nt], ka, kscale, E2[:, 0:nt], ALU.mult, ALU.mult, r=kkeys + [('E2',)], w=[('ki', hi)])
        stt('dve', kd2[hi][:, 0:nt], ka, kscale, E3[:, 0:nt], ALU.mult, ALU.mult, r=kkeys + [('E3',)], w=[('kd', hi)])

    def chunk_pair(nt, heads, vtok, vcol0, stl, bfl, use_n, o_banks, den_banks, B):
        nch = nt // 64
        qd2, ki2, kd2, kdT, sT, E1, E2, E3, dec2 = B
        S32 = stl[0]
        Sbf = bfl[0]
        pbf = psum[:, 7, :].bitcast(BF16)
        for c in range(nch):
            cs = c * 64
            for hi, h in enumerate(heads):
                qd, ki, kd, dec = qd2[hi], ki2[hi], kd2[hi], dec2[hi]
                i2 = hi * 2 + c % 2
                P0 = (c % 2) * 64
                PS = slice(P0, P0 + 64)
                sb = SBANK[hi]
                o_bank = o_banks[hi]
                tr(pbf[PS, i2 * 128:(i2 + 1) * 128], kd[:, cs:cs + 64], ident[:, :], r=[('kd', hi), ('c', 'ident')],
                   w=[('ps7', i2)])
                cp('act', kdT[i2][PS, :], pbf[PS, i2 * 128:(i2 + 1) * 128], r=[('ps7', i2)], w=[('kdT', i2)])
                mm(psum[PS, sb, 0:64], ki[:, cs:cs + 64], qd[:, cs:cs + 64], True, True, r=[('ki', hi), ('qd', hi)],
                   w=[('ps', sb)])
                tt('dve', sT[i2][PS, :], psum[PS, sb, 0:64], maskut[PS, :], ALU.mult, r=[('ps', sb), ('c', 'mask')],
                   w=[('sT', i2)])
                vv = vtok[PS, c // 2, vcol0 + h * 128: vcol0 + (h + 1) * 128]
                mm(psum[:, o_bank, cs:cs + 64], vv, sT[i2][PS, :], True, False, r=[('vtok',), ('sT', i2)],
                   w=[('ps', o_bank)])
                if use_n:
                    Nbf = bfl[1]
                    den_bank = den_banks[hi]
                    mm(psum[:, den_bank, cs:cs + 64], onesbf[PS, :], sT[i2][PS, :], True, False,
                       r=[('c', 'onesbf'), ('sT', i2)], w=[('ps', den_bank)])
                mm(psum[:, sb, 128:256], kdT[i2][PS, :], vv, True, True, r=[('kdT', i2), ('vtok',)], w=[('ps', sb)])
                if use_n:
                    mm(psum[:, sb, 256:384], kdT[i2][PS, :], onesbf[PS, :], True, True,
                       r=[('kdT', i2), ('c', 'onesbf')], w=[('ps', sb)])
                mm(psum[:, o_bank, cs:cs + 64], Sbf[:, h, :], qd[:, cs:cs + 64], False, True,
                   r=[('stbf', id(Sbf), h), ('qd', hi)], w=[('ps', o_bank)])
                if use_n:
                    mm(psum[:, den_bank, cs:cs + 64], Nbf[:, h, :], qd[:, cs:cs + 64], False, True,
                       r=[('stbf', id(Nbf), h), ('qd', hi)], w=[('ps', den_bank)])
                stt('dve', S32[:, h, :], S32[:, h, :], dec[:, c:c + 1], psum[:, sb, 128:256], ALU.mult, ALU.add,
                    r=[('st', id(S32), h), ('dec', hi), ('ps', sb)], w=[('st', id(S32), h)])
                cp('act', Sbf[:, h, :], S32[:, h, :], r=[('st', id(S32), h)], w=[('stbf', id(Sbf), h)])
                if use_n:
                    N32 = stl[1]
                    stt('dve', N32[:, h, :], N32[:, h, :], dec[:, c:c + 1], psum[:, sb, 256:384], ALU.mult, ALU.add,
                        r=[('st', id(N32), h), ('dec', hi), ('ps', sb)], w=[('st', id(N32), h)])
                    cp('act', Nbf[:, h, :], N32[:, h, :], r=[('st', id(N32), h)], w=[('stbf', id(Nbf), h)])

    def vproj(tok0, nt, wt, wkey, c0, vtok):
        for blk in range(nt // 128):
            b = nextbank()
            for k in range(8):
                mm(ps(b, 128, 512), uT[:, k, tok0 + blk * 128: tok0 + blk * 128 + 128], wt[:, k, c0:c0 + 512],
                   k == 0, k == 7, r=wkeys(wkey, c0, 512) + [('uT',)], w=[('ps', b)])
            cp('act', vtok[:, blk, :], ps(b, 128, 512), r=[('ps', b)], w=[('vtok',)])

    def merge_out(tok0, nt, oT, okey, wpr, wprkey, wgate, wgkey, gc0, first, gsig, gtmp):
        for co in range(8):
            b = nextbank()
            for k in range(4):
                mm(ps(b, 128, nt), wpr[:, k, co * 128:(co + 1) * 128], oT[:, k, 0:nt], k == 0, k == 3,
                   r=[wprkey, okey], w=[('ps', b)])
            b2 = nextbank()
            proj(b2, wgate, wgkey, gc0 + co * 128, 128, uT, ('uT',), tok0, nt)
            act(gsig[:, 0:nt], ps(b2, 128, nt), AF.Sigmoid, r=[('ps', b2)], w=[('gsig',)])
            if first:
                tt('dve', merged[:, co, tok0:tok0 + nt], ps(b, 128, nt), gsig[:, 0:nt], ALU.mult,
                   r=[('ps', b), ('gsig',)], w=[('mg',)])
            else:
                tt('dve', gtmp[:, 0:nt], ps(b, 128, nt), gsig[:, 0:nt], ALU.mult, r=[('ps', b), ('gsig',)],
                   w=[('gtmp',)])
                tt('pool', merged[:, co, tok0:tok0 + nt], merged[:, co, tok0:tok0 + nt], gtmp[:, 0:nt], ALU.add,
                   r=[('gtmp',), ('mg',)], w=[('mg',)])

    def common_bufs():
        qd2 = [aalloc('qd', [128, 512], BF16) for _ in range(2)]
        ki2 = [aalloc('ki', [128, 512], BF16) for _ in range(2)]
        kd2 = [aalloc('kd', [128, 512], BF16) for _ in range(2)]
        kdT = [aalloc('kdT', [128, 128], BF16) for _ in range(4)]
        sT = [aalloc('sT', [128, 64], BF16) for _ in range(4)]
        E1 = aalloc('E1', [128, 512], F32); E2 = aalloc('E2', [128, 512], F32); E3 = aalloc('E3', [128, 512], F32)
        dec2 = [aalloc('dec', [128, 8], F32) for _ in range(2)]
        return (qd2, ki2, kd2, kdT, sT, E1, E2, E3, dec2)

    def phase_gla(L):
        arena[0] = PH0
        Wg = aalloc('Wg', [128, 8, 3088], BF16)
        Wgk = aalloc('Wgk', [16, 1, 512], BF16)
        Wpr = aalloc('Wpr', [128, 4, D], BF16)
        bgk = aalloc('bgk', [128, 4], F32); gln = aalloc('gln', [128, 1], F32)
        Bx = aalloc('Bx', [128, 513], F32)
        Drel = aalloc('Drel', [128, 512], F32); Drel2 = aalloc('Drel2', [128, 512], F32)
        B = common_bufs()
        qd2, ki2, kd2, kdT, sT, E1, E2, E3, dec2 = B
        vtok = aalloc('vtok', [128, 4, 512], BF16)
        lrT = aalloc('lrT', [16, 512], BF16)
        sq = aalloc('sq', [128, 512], BF16)
        rstd = E1; t1 = E2; sr = E3
        ogT = aalloc('ogT', [128, 4, 512], BF16)
        gsig = aalloc('gsig', [128, 512], F32); gtmp = aalloc('gtmp', [128, 512], F32)
        spb = gtmp
        K = ('W', 'g')
        KW = WK(('Wc', 'g'))
        load_wc(Wg, KW, 'w_in', L, 8, 1024, 1024, 1040)
        load_wc(Wg, KW, 'w_in', L, 8, 0, 0, 1024)
        load_wc(Wg, KW, 'w_in', L, 8, 2064, O_GATE, 1024)
        load_w(Wgk[:, :, :], K, 'gla_w_gk', L, 0, 1, 0, 512, rows=16)
        load_w(Wpr[:, :, :], K, 'gla_w_proj', L, 0, 4, 0, D)
        vec_load(bgk[:, :], 'gla_b_gk', L, [[1, 128], [128, 4]], K)
        ts('dve', bgk[:, :], bgk[:, :], -1.0, None, ALU.mult, None, r=[K], w=[K])
        vec_load(gln[:, :], 'gla_norm', L, [[1, 128], [1, 1]], K)
        memset('pool', Bx[:, 0:1], 0.0, [('Bx',)])
        for tok0 in range(0, SEGT, TILE):
            nt = TILE
            nch = nt // 64
            vproj(tok0, nt, Wg, KW, O_GV, vtok)
            b = nextbank()
            proj(b, Wg, KW, O_LR, 16, uT, ('uT',), tok0, nt)
            cp('act', lrT[:, 0:nt], ps(b, 16, nt), r=[('ps', b)], w=[('lrT',)])
            for pair in ((0, 1), (2, 3)):
                for hi, h in enumerate(pair):
                    dec = dec2[hi]
                    b = nextbank()
                    mm(ps(b, 128, nt), Wgk[:, 0, h * 128:(h + 1) * 128], lrT[:, 0:nt], True, True, r=[K, ('lrT',)],
                       w=[('ps', b)])
                    act(spb[:, 0:nt], ps(b, 128, nt), AF.Exp, r=[('ps', b), K], w=[('gtmp',)], bias=bgk[:, h:h + 1],
                        scale=-1.0)
                    act(spb[:, 0:nt], spb[:, 0:nt], AF.Ln, r=[('gtmp',)], w=[('gtmp',)], bias=1.0)
                    S.add('dve', lambda e, nt=nt: e.tensor_tensor_scan(out=Bx[:, 1:1 + nt], data0=ones32[:, 0:nt],
                                                                       data1=spb[:, 0:nt], initial=0.0,
                                                                       op0=ALU.mult, op1=ALU.add),
                          r=[('gtmp',), ('c', 'ones32')], w=[('Bx',)])
                    d3 = Drel[:, 0:nt].rearrange('p (c l) -> p c l', l=64)
                    d23 = Drel2[:, 0:nt].rearrange('p (c l) -> p c l', l=64)
                    bx3 = Bx[:, 1:1 + nt].rearrange('p (c l) -> p c l', l=64)
                    br3 = Bx[:, 0:nt].rearrange('p (c l) -> p c l', l=64)[:, :, 0:1]
                    tt('dve', d3, bx3, br3.to_broadcast([128, nch, 64]), ALU.subtract, r=[('Bx',)], w=[('Drel',)])
                    tt('dve', d23, d3, d3[:, :, 63:64].to_broadcast([128, nch, 64]), ALU.subtract, r=[('Drel',)],
                       w=[('Drel2',)])
                    act(E1[:, 0:nt], Drel[:, 0:nt], AF.Exp, r=[('Drel',)], w=[('E1',)], scale=-1.0 / 16)
                    act(dec[:, 0:nch], Drel[:, 63:nt:64], AF.Exp, r=[('Drel',)], w=[('dec', hi)], scale=-1.0 / 16)
                    act(E2[:, 0:nt], Drel[:, 0:nt], AF.Exp, r=[('Drel',)], w=[('E2',)], scale=1.0 / 16)
                    act(E3[:, 0:nt], Drel2[:, 0:nt], AF.Exp, r=[('Drel2',)], w=[('E3',)], scale=1.0 / 16)
                    bq = nextbank()
                    proj(bq, Wg, KW, O_GQ + h * 128, 128, uT, ('uT',), tok0, nt)
                    bk = nextbank()
                    proj(bk, Wg, KW, O_GK + h * 128, 128, uT, ('uT',), tok0, nt)
                    prep_head(nt, hi, ps(bq, 128, nt), [('ps', bq)], ps(bk, 128, nt), [('ps', bk)], 128 ** -0.5, 1.0, B)
                obs = [nextbank(), nextbank()]
                chunk_pair(nt, pair, vtok, 0, [St32], [Stbf], False, obs, None, B)
                for hi, h in enumerate(pair):
                    ob = obs[hi]
                    act(sq[:, 0:nt], psum[:, ob, 0:nt], AF.Square, r=[('ps', ob)], w=[('sq',)])
                    b = nextbank()
                    mm(ps(b, 128, nt), onesbf[:, :], sq[:, 0:nt], True, True, r=[('c', 'onesbf'), ('sq',)],
                       w=[('ps', b)])
                    act(rstd[:, 0:nt], ps(b, 128, nt), AF.Sqrt, r=[('ps', b), ('c', 'eps')], w=[('E1',)],
                        bias=eps_t[:, :], scale=1.0 / 128)
                    recip(rstd[:, 0:nt], rstd[:, 0:nt], r=[('E1',)], w=[('E1',)])
                    tt('dve', t1[:, 0:nt], psum[:, ob, 0:nt], rstd[:, 0:nt], ALU.mult, r=[('ps', ob), ('E1',)],
                       w=[('E2',)])
                    b = nextbank()
                    proj(b, Wg, KW, O_GR + h * 128, 128, uT, ('uT',), tok0, nt)
                    act(sr[:, 0:nt], ps(b, 128, nt), AF.Silu, r=[('ps', b)], w=[('E3',)])
                    stt('dve', ogT[:, h, 0:nt], sr[:, 0:nt], gln[:, 0:1], t1[:, 0:nt], ALU.mult, ALU.mult,
                        r=[('E3',), K, ('E2',)], w=[('ogT',)])
            merge_out(tok0, nt, ogT, ('ogT',), Wpr, K, Wg, KW, 2064, True, gsig, gtmp)

    def phase_ml(L):
        arena[0] = PH0
        Wm = aalloc('Wm', [128, 8, 3080], BF16)
        Wpr = aalloc('Wmp', [128, 4, D], BF16)
        B = common_bufs()
        qd2, ki2, kd2, kdT, sT, E1, E2, E3, dec2 = B
        vtok = aalloc('vtok', [128, 4, 512], BF16)
        zc = aalloc('zc', [128, 515], F32)
        qh = aalloc('qh', [128, 512], BF16); kh = aalloc('kh', [128, 512], BF16)
        B8 = aalloc('B8', [8, 513], F32)
        T2 = aalloc('T2', [8, 512], F32); T3 = aalloc('T3', [8, 512], F32)
        bif = aalloc('bif', [8, 1], F32); cw = aalloc('cw', [128, 8, 4], F32); cb = aalloc('cb', [128, 8], F32)
        dabs = E1; t1 = E2; so = E3
        ogT = aalloc('hT', [128, 4, 512], BF16)
        gsig = aalloc('gsig', [128, 512], F32); gtmp = aalloc('gtmp', [128, 512], F32)
        cacc = gtmp; gi8 = gsig[0:8, :]
        selF = aalloc('selF', [8, 4, 128], F32)
        selIF = aalloc('selIF', [8, 4, 128], F32)
        for h in range(4):
            for (dst, rows_) in ((selF, (4 + h,)), (selIF, (h, 4 + h))):
                memset('pool', dst[:, h, :], 0.0, [('c', 'sel')])
                for rr in rows_:
                    memset('pool', zc[0:8, 0:128], 1.0, [('zc',)])
                    S.add('pool', lambda e, rr=rr: e.affine_select(out=zc[0:8, 0:128], in_=zc[0:8, 0:128],
                                                                   pattern=[[0, 128]], compare_op=ALU.is_equal, fill=0.0,
                                                                   base=-rr, channel_multiplier=1),
                          r=[('zc',)], w=[('zc',)])
                    tt('pool', dst[:, h, :], dst[:, h, :], zc[0:8, 0:128], ALU.add, r=[('zc',), ('c', 'sel')],
                       w=[('c', 'sel')])
        K = ('W', 'm')
        KW = WK(('Wc', 'm'))
        load_wc(Wm, KW, 'w_in', L, 8, 1024, O_MQ + 1024, 1032)
        load_wc(Wm, KW, 'w_in', L, 8, 0, O_MQ, 1024)
        load_wc(Wm, KW, 'w_in', L, 8, 2056, O_GATE + 2048, 1024)
        load_w(Wpr[:, :, :], K, 'ml_w_proj', L, 0, 4, 0, D)
        vec_load(bif[0:4, :], 'ml_b_i', L, [[1, 4], [1, 1]], K)
        vec_load(bif[4:8, :], 'ml_b_f', L, [[1, 4], [1, 1]], K)
        for j in range(4):
            vec_load(cw[:, :, j], 'ml_conv_w', L, [[1, 128], [128, 8]], K, off=j * 1024)
        vec_load(cb[:, :], 'ml_conv_b', L, [[1, 128], [128, 8]], K)
        memset('pool', B8[:, 0:1], 0.0, [('B8',)])
        for tok0 in range(0, SEGT, TILE):
            nt = TILE
            nch = nt // 64
            vproj(tok0, nt, Wm, KW, 1024, vtok)
            b = nextbank()
            proj(b, Wm, KW, 1536, 8, uT, ('uT',), tok0, nt)
            act(gi8[:, 0:nt], ps(b, 8, nt), AF.Identity, r=[('ps', b), K], w=[('gsig',)], bias=bif[:, :])
            act(T3[:, 0:nt], gi8[:, 0:nt], AF.Exp, r=[('gsig',)], w=[('T3',)], scale=-1.0)
            act(T3[:, 0:nt], T3[:, 0:nt], AF.Ln, r=[('T3',)], w=[('T3',)], bias=1.0)
            S.add('dve', lambda e, nt=nt: e.tensor_tensor_scan(out=B8[:, 1:1 + nt], data0=ones32[0:8, 0:nt],
                                                               data1=T3[:, 0:nt], initial=0.0, op0=ALU.mult,
                                                               op1=ALU.add),
                  r=[('T3',), ('c', 'ones32')], w=[('B8',)])
            d3 = T2[:, 0:nt].rearrange('p (c l) -> p c l', l=64)
            d23 = T3[:, 0:nt].rearrange('p (c l) -> p c l', l=64)
            bx3 = B8[:, 1:1 + nt].rearrange('p (c l) -> p c l', l=64)
            br3 = B8[:, 0:nt].rearrange('p (c l) -> p c l', l=64)[:, :, 0:1]
            tt('dve', d3, bx3, br3.to_broadcast([8, nch, 64]), ALU.subtract, r=[('B8',)], w=[('T2',)])
            tt('dve', d23, d3, d3[:, :, 63:64].to_broadcast([8, nch, 64]), ALU.subtract, r=[('T2',), ('B8',)],
               w=[('T3',)])
            cp('dve', T2[0:4, 0:nt], gi8[0:4, 0:nt], r=[('gsig',), ('T2',)], w=[('T2',)])
            cp('dve', T3[0:4, 0:nt], gi8[0:4, 0:nt], r=[('gsig',), ('T3',)], w=[('T3',)])
            for pair in ((0, 1), (2, 3)):
                for hi, h in enumerate(pair):
                    dec = dec2[hi]
                    b1 = nextbank()
                    mm(ps(b1, 128, nt), selF[:, h, :], T2[:, 0:nt], True, True, r=[('c', 'sel'), ('T2',)],
                       w=[('ps', b1)])
                    act(E1[:, 0:nt], ps(b1, 128, nt), AF.Exp, r=[('ps', b1)], w=[('E1',)], scale=-1.0)
                    act(dec[:, 0:nch], psum[:, b1, 63:nt:64], AF.Exp, r=[('ps', b1)], w=[('dec', hi)], scale=-1.0)
                    b2 = nextbank()
                    mm(ps(b2, 128, nt), selIF[:, h, :], T2[:, 0:nt], True, True, r=[('c', 'sel'), ('T2',)],
                       w=[('ps', b2)])
                    act(E2[:, 0:nt], ps(b2, 128, nt), AF.Exp, r=[('ps', b2)], w=[('E2',)])
                    b3 = nextbank()
                    mm(ps(b3, 128, nt), selIF[:, h, :], T3[:, 0:nt], True, True, r=[('c', 'sel'), ('T3',)],
                       w=[('ps', b3)])
                    act(E3[:, 0:nt], ps(b3, 128, nt), AF.Exp, r=[('ps', b3)], w=[('E3',)])
                    for which, dstq in ((0, qh), (1, kh)):
                        ct = which * 4 + h
                        b = nextbank()
                        proj(b, Wm, KW, ct * 128, 128, uT, ('uT',), tok0, nt)
                        cp('act', zc[:, 3:3 + nt], ps(b, 128, nt), r=[('ps', b)], w=[('zc',)])
                        cp('pool', zc[:, 0:3], zch[:, ct, :], r=[('zch', ct)], w=[('zc',)])
                        ts('dve', cacc[:, 0:nt], zc[:, 0:nt], cw[:, ct, 0:1], None, ALU.mult, None, r=[('zc',), K],
                           w=[('gtmp',)])
                        for j in range(1, 4):
                            stt('dve', cacc[:, 0:nt], zc[:, j:j + nt], cw[:, ct, j:j + 1], cacc[:, 0:nt], ALU.mult,
                                ALU.add, r=[('zc',), K, ('gtmp',)], w=[('gtmp',)])
                        cp('pool', zch[:, ct, :], zc[:, nt:nt + 3], r=[('zc',)], w=[('zch', ct)])
                        act(dstq[:, 0:nt], cacc[:, 0:nt], AF.Silu, r=[('gtmp',), K], w=[('qk', which)],
                            bias=cb[:, ct:ct + 1])
                    prep_head(nt, hi, qh[:, 0:nt], [('qk', 0)], kh[:, 0:nt], [('qk', 1)], 1.0, 128 ** -0.5, B)
                obs = [nextbank(), nextbank()]
                dbs = [nextbank(), nextbank()]
                chunk_pair(nt, pair, vtok, 0, [Ct32, Nt32], [Ctbf, Ntbf], True, obs, dbs, B)
                for hi, h in enumerate(pair):
                    ob, db = obs[hi], dbs[hi]
                    ts('dve', dabs[:, 0:nt], psum[:, db, 0:nt], -1.0, 1.0, ALU.mult, ALU.max, r=[('ps', db)],
                       w=[('E1',)])
                    tt('dve', dabs[:, 0:nt], dabs[:, 0:nt], psum[:, db, 0:nt], ALU.max, r=[('ps', db), ('E1',)],
                       w=[('E1',)])
                    recip(dabs[:, 0:nt], dabs[:, 0:nt], r=[('E1',)], w=[('E1',)])
                    tt('dve', t1[:, 0:nt], psum[:, ob, 0:nt], dabs[:, 0:nt], ALU.mult, r=[('ps', ob), ('E1',)],
                       w=[('E2',)])
                    b = nextbank()
                    proj(b, Wm, KW, 1544 + h * 128, 128, uT, ('uT',), tok0, nt)
                    act(so[:, 0:nt], ps(b, 128, nt), AF.Sigmoid, r=[('ps', b)], w=[('E3',)])
                    tt('dve', ogT[:, h, 0:nt], t1[:, 0:nt], so[:, 0:nt], ALU.mult, r=[('E2',), ('E3',)], w=[('ogT',)])
            merge_out(tok0, nt, ogT, ('ogT',), Wpr, K, Wm, KW, 2056, False, gsig, gtmp)

    def phase_s5(L):
        arena[0] = PH0
        TB = 128
        NCH = TB // 8
        Ws = aalloc('Ws', [128, 8, 1536], BF16)
        Wglu = aalloc('Wglu', [128, 4, 512], BF16)
        Wsp = aalloc('Wsp', [128, 4, D], BF16)
        bglu = aalloc('bglu', [128, 4], F32)
        su128 = aalloc('su128', [128, 4, 512], BF16)
        sup2 = [aalloc('sup', [32, 16, TB], BF16) for _ in range(2)]
        GU = aalloc('GU', [128, 2, 16, TB], F32)
        XC = aalloc('XC', [128, 2, 16, NCH + 1], F32)
        XXbf = aalloc('XXbf', [128, 2, 16, TB], BF16)
        XCbf = aalloc('XCbf', [128, 2, 16, NCH], BF16)
        P1 = aalloc('P1', [128, 2, 16, NCH], F32); P2 = aalloc('P2', [128, 2, 16, NCH], F32)
        P1s = aalloc('P1s', [128, 2, 16], F32); P2s = aalloc('P2s', [128, 2, 16], F32)
        yp = aalloc('yp', [32, 16, TB], BF16)
        yv = aalloc('yv', [128, 512], F32)
        ga = aalloc('ga', [128, 512], F32)
        yg = aalloc('yg', [128, 4, 512], BF16)
        gsig = aalloc('gsig', [128, 512], F32); gtmp = aalloc('gtmp', [128, 512], F32)
        gb = gtmp
        K = ('W', 's')
        KC = ('s5c', L)
        y2 = yg
        KW = WK(('Wc', 's'))
        load_wc(Ws, KW, 'w_in', L, 8, 0, O_SU, 512)
        load_wc(Ws, KW, 'w_in', L, 8, 512, O_GATE + 1024, 1024)
        load_w(Wglu[:, :, :], K, 's5_w_glu', L, 0, 4, 0, 512)
        load_w(Wsp[:, :, :], K, 's5_w_proj', L, 0, 4, 0, D)
        vec_load(bglu[:, :], 's5_b_glu', L, [[1, 128], [128, 4]], K)
        cp('dve', XC[:, :, :, 0], XXst[:, :, :], r=[('xxst',)], w=[('XC',)])
        GU5 = GU[:, :, :, :].rearrange('p r g (l n) -> p r g l n', l=8)

        def G(tau):
            return GU5[:, :, :, tau, :]

        def cmul_acc(dst, src, k, dkey, skey):
            prb = PW[:, k, 0, :].unsqueeze(1).unsqueeze(3).to_broadcast([128, 2, 16, NCH])
            pib = PW[:, k, 1, :].unsqueeze(2).to_broadcast([128, 16, NCH])
            npib = nPWi[:, k, :].unsqueeze(2).to_broadcast([128, 16, NCH])
            tt('dve', P1[:, :, :, :], src, prb, ALU.mult, r=[skey, KC], w=[('P1',)])
            tt(P2ENG, P2[:, 0, :, :], src[:, 1], npib, ALU.mult, r=[skey, KC], w=[('P2',)])
            tt(P2ENG, P2[:, 1, :, :], src[:, 0], pib, ALU.mult, r=[skey, KC], w=[('P2',)])
            tt('dve', dst, dst, P1[:, :, :, :], ALU.add, r=[dkey, ('P1',)], w=[dkey])
            tt('dve', dst, dst, P2[:, :, :, :], ALU.add, r=[dkey, ('P2',)], w=[dkey])

        for tok0 in range(0, SEGT, TILE):
            nt = TILE
            for blk in range(4):
                b = nextbank()
                proj(b, Ws, KW, blk * 128, 128, uT, ('uT',), tok0, nt)
                cp('act', su128[:, blk, 0:nt], ps(b, 128, nt), r=[('ps', b)], w=[('su128',)])
            def front(tb, si):
                supb = sup2[si]
                for blk in range(4):
                    b = nextbank()
                    for k in range(4):
                        mm(psum[0:32, b, k * TB:(k + 1) * TB], ident[:, 32 * k:32 * k + 32],
                           su128[:, blk, tb:tb + TB].rearrange('p (n l) -> p l n', l=8), True, True,
                           r=[('c', 'ident'), ('su128',)], w=[('ps', b)])
                    cp('act', supb[:, blk * 4:(blk + 1) * 4, :], ps(b, 32, 4 * TB).rearrange('p (k t) -> p k t', k=4),
                       r=[('ps', b)], w=[('sup', si)])
                for ri in range(2):
                    for g in range(4):
                        b = nextbank()
                        for k in range(4):
                            pr = g * 4 + k
                            mm(psum[:, b, k * TB:(k + 1) * TB], BBT[:, pr, ri, :], supb[:, pr, :], True, True,
                               r=[KC, ('sup', si)], w=[('ps', b)])
                        cp('act', GU[:, ri, g * 4:(g + 1) * 4, :], ps(b, 128, 4 * TB).rearrange('p (k t) -> p k t', k=4),
                           r=[('ps', b)], w=[('GU',)])

            def mid(tb):
                for tau in range(1, 8):
                    cmul_acc(G(tau), G(tau - 1), 0, ('GU',), ('GU',))
                cp('act', XXbf[:, :, :, :], GU[:, :, :, :], r=[('GU',)], w=[('XXbf',)])
                for n in range(NCH):
                    tt('dve', P1s[:, :, :], XC[:, :, :, n], AA8[:, :, :], ALU.mult, r=[('XC',), KC], w=[('P1s',)])
                    tt('dve', P2s[:, 0, :], XC[:, 1, :, n], nPWi[:, 7, :], ALU.mult, r=[('XC',), KC], w=[('P2s',)])
                    tt('dve', P2s[:, 1, :], XC[:, 0, :, n], PW[:, 7, 1, :], ALU.mult, r=[('XC',), KC], w=[('P2s',)])
                    tt('dve', P1s[:, :, :], P1s[:, :, :], P2s[:, :, :], ALU.add, r=[('P1s',), ('P2s',)], w=[('P1s',)])
                    tt('dve', XC[:, :, :, n + 1], P1s[:, :, :], GU5[:, :, :, 7, n], ALU.add, r=[('P1s',), ('GU',)],
                       w=[('XC',)])
                cp('act', XCbf[:, :, :, :], XC[:, :, :, 0:NCH], r=[('XC',)], w=[('XCbf',)])
                cp('dve', XC[:, :, :, 0], XC[:, :, :, NCH], r=[('XC',)], w=[('XC',)])

            ybanks = {}

            def tail_y(tb):
                bl = []
                for g in range(4):
                    b = nextbank()
                    bl.append(b)
                    for k in range(4):
                        pr = g * 4 + k
                        mm(psum[0:32, b, k * TB:(k + 1) * TB], Ec[:, 0, pr, :], XXbf[:, 0, pr, :], True, False,
                           r=[KC, ('XXbf',)], w=[('ps', b)])
                        mm(psum[0:32, b, k * TB:(k + 1) * TB], Ec[:, 1, pr, :], XXbf[:, 1, pr, :], False, False,
                           r=[KC, ('XXbf',)], w=[('ps', b)])
                        for tau in range(8):
                            for ri in range(2):
                                mm(psum[0:32, b, k * TB + tau * NCH:k * TB + (tau + 1) * NCH], Et[ri][:, tau, pr, :],
                                   XCbf[:, ri, pr, :], False, tau == 7 and ri == 1, r=[KC, ('XCbf',)], w=[('ps', b)])
                ybanks[tb] = bl

            def tail_rest(tb, si):
                supb = sup2[si]
                for g in range(4):
                    b = ybanks[tb][g]
                    for k in range(4):
                        pr = g * 4 + k
                        stt('dve', yp[:, pr, :], supb[:, pr, :], Dd[:, pr:pr + 1], psum[0:32, b, k * TB:(k + 1) * TB],
                            ALU.mult, ALU.add, r=[('sup', si), KC, ('ps', b)], w=[('yp',)])
                b = nextbank()
                for blk in range(4):
                    for k in range(4):
                        mm(psum[:, b, blk * TB:(blk + 1) * TB], gsel[:, k, :],
                           yp[:, blk * 4 + k, :].rearrange('p (l n) -> p n l', l=8), k == 0, k == 3,
                           r=[('c', 'gsel'), ('yp',)], w=[('ps', b)])
                cp('act', yv[:, :], ps(b, 128, 4 * TB), r=[('ps', b)], w=[('yv',)])
                act(ga[:, :], yv[:, :], AF.Square, r=[('yv',)], w=[('ga',)])
                ts('pool', ga[:, :], ga[:, :], 0.044715, 1.0, ALU.mult, ALU.add, r=[('ga',)], w=[('ga',)])
                tt('pool', ga[:, :], ga[:, :], yv[:, :], ALU.mult, r=[('ga',), ('yv',)], w=[('ga',)])
                act(gb[:, :], ga[:, :], AF.Sigmoid, r=[('ga',)], w=[('gtmp',)], scale=GC)
                tt('pool', yg[:, :, tb:tb + TB], gb[:, :].rearrange('p (k t) -> p k t', k=4),
                   yv[:, :].rearrange('p (k t) -> p k t', k=4), ALU.mult, r=[('gtmp',), ('yv',)], w=[('yg',)])

            tbs = list(range(0, nt, TB))
            front(tbs[0], 0)
            mid(tbs[0])
            for bi_, tb in enumerate(tbs):
                if bi_ + 1 < len(tbs):
                    front(tbs[bi_ + 1], (bi_ + 1) % 2)
                tail_y(tb)
                if bi_ + 1 < len(tbs):
                    mid(tbs[bi_ + 1])
                tail_rest(tb, bi_ % 2)
            gbanks = []
            for co in range(4):
                b = nextbank()
                gbanks.append(b)
                for k in range(4):
                    mm(ps(b, 128, nt), Wglu[:, k, co * 128:(co + 1) * 128], yg[:, k, 0:nt], k == 0, k == 3,
                       r=[K, ('yg',)], w=[('ps', b)])
            for co in range(4):
                b = gbanks[co]
                act(gsig[:, 0:nt], ps(b, 128, nt), AF.Sigmoid, r=[('ps', b), K], w=[('gsig',)], bias=bglu[:, co:co + 1])
                tt('dve', y2[:, co, 0:nt], yg[:, co, 0:nt], gsig[:, 0:nt], ALU.mult, r=[('yg',), ('gsig',)],
                   w=[('yg',)])
            merge_out(tok0, nt, y2, ('yg',), Wsp, K, Ws, KW, 512, False, gsig, gtmp)
        cp('dve', XXst[:, :, :], XC[:, :, :, 0], r=[('XC',)], w=[('xxst',)])

    def phase_out(L, src_t, seg):
        arena[0] = PH0
        Wo = aalloc('Wo', [128, 8, D], BF16)
        t1 = aalloc('t1o', [128, D], F32)
        K = ('W', 'o')
        load_w(Wo[:, :, :], K, 'w_out', L, 0, 8, 0, D)
        dma(gainb[:, :], rawap(W['norm_mix_post'], L * D, [[0, 128], [1, D]]), w=[('c', 'gain')])
        junk_ref[0] = aalloc('junkO', [128, D], BF16)
        xbo = [aalloc('xbo', [128, D], F32) for _ in range(2)]
        if L == 0 and seg == 0:
            dump('mgall', merged[:, :, :], [('mg',)])
        for bi, tb in enumerate(range(0, SEGT, 128)):
            xi = bi % 2
            xb = xbo[xi]
            dma(xb[:, :], rawap(src_t, (seg * SEGT + tb) * D, [[D, 128], [1, D]]), w=[('xbo', xi)])
            banks = []
            for hf in range(2):
                b = nextbank()
                banks.append(b)
                for k in range(8):
                    mm(ps(b, 128, 512), merged[:, k, tb:tb + 128], Wo[:, k, hf * 512:(hf + 1) * 512], k == 0, k == 7,
                       r=[('mg',), K], w=[('ps', b)])
            rowsum_rstd(banks, 128, 8)
            for hf, b in enumerate(banks):
                tt('dve', t1[:, hf * 512:(hf + 1) * 512], ps(b, 128, 512), gains[1][:, hf * 512:(hf + 1) * 512], ALU.mult,
                   r=[('ps', b), ('c', 'gain')], w=[('t1o', hf)])
                stt('dve', xb[:, hf * 512:(hf + 1) * 512], t1[:, hf * 512:(hf + 1) * 512], small[:, 8:9],
                    xb[:, hf * 512:(hf + 1) * 512], ALU.mult, ALU.add, r=[('t1o', hf), ('rsx',), ('xbo', xi)],
                    w=[('xbo', xi)])
            if L == 0 and seg == 0 and tb == 128:
                dump('osmall', small[:, 0:16], [('rsx',), ('ssx', 0), ('ssx', 1), ('ssx', 2), ('ssx', 3)])
                dump('ot1', t1[:, :], [('t1o', 0), ('t1o', 1)])
                dump('ojunk', junk_ref[0][:, :], [('junk',)])
            dma(rawap(xmid, tb * D, [[D, 128], [1, D]]), xb[:, :], r=[('xbo', xi)], w=[('xmid', tb)])
            if L == 0 and seg == 0:
                dump('xmid%d' % tb, xb[:, :], [('xbo', xi)])

    hs = nc.dram_tensor('hs', [22, 128, SEGT], BF16)

    def phase_ffn_a(L, seg):
        arena[0] = UT0
        Wup = aalloc('Wup', [128, 8, DFF], BF16)
        Wga = aalloc('Wga', [128, 8, DFF], BF16)
        NTF = 512
        uF = [aalloc('uF', [128, 8, NTF], BF16) for _ in range(2)]
        NB = 3
        fsets = [(aalloc('ab', [128, NTF + 2], F32), aalloc('vv', [128, NTF], F32), aalloc('gaF', [128, NTF], F32),
                  aalloc('vg', [128, NTF], F32), aalloc('ho', [128, NTF], BF16)) for _ in range(NB)]
        fw = aalloc('fw', [128, 22, 3], F32); fb = aalloc('fb', [128, 22], F32)
        nsets = norm_sets(2)
        nbi = [0]
        K = ('W', 'f')
        KU = WK(('Wc', 'fu'), 704)
        KG = WK(('Wc', 'fg'), 704)
        for q4 in range(4):
            load_wc(Wup, KU, 'ffn_w_up', L, 8, q4 * 704, q4 * 704, 704)
            load_wc(Wga, KG, 'ffn_w_gate', L, 8, q4 * 704, q4 * 704, 704)
        for j in range(3):
            vec_load(fw[:, :, j], 'ffn_conv_w', L, [[1, 128], [128, 22]], K, off=j * DFF)
        vec_load(fb[:, :], 'ffn_conv_b', L, [[1, 128], [128, 22]], K)
        dma(gains2[0][:, :], rawap(W['norm_ffn_pre'], L * D, [[0, 128], [1, D]]), w=[('c', 'gain')])
        gi = 0
        def ffn_norm(ti):
            t0_ = ti * NTF
            for j in range(NTF // 128):
                tb = t0_ + j * 128
                norm_block(rawap(xmid, tb * D, [[D, 128], [1, D]]), 128, gains[2], ('c', 'gain'), uF[ti % 2],
                           ('uF', ti % 2), j * 128, 0, extra_r=[('xmid', tb)], bufs=nsets[nbi[0] % 2])
                nbi[0] += 1

        ntile = SEGT // NTF
        ffn_norm(0)
        for ti, t0 in enumerate(range(0, SEGT, NTF)):
            uFt = uF[ti % 2]
            ukey = ('uF', ti % 2)
            for ct in range(22):
                if ct == 8 and ti + 1 < ntile:
                    ffn_norm(ti + 1)
                si = gi % NB
                gi += 1
                ab, vv, ga, vg, ho = fsets[si]
                b = nextbank()
                proj(b, Wup, KU, ct * 128, 128, uFt, ukey, 0, NTF)
                cp('act', ab[:, 2:2 + NTF], ps(b, 128, NTF), r=[('ps', b)], w=[('ab', si)])
                cp('pool', ab[:, 0:2], ahist[:, ct, :], r=[('ah', ct)], w=[('ab', si)])
                ts('dve', vv[:, :], ab[:, 0:NTF], fw[:, ct, 0:1], fb[:, ct:ct + 1], ALU.mult, ALU.add,
                   r=[('ab', si), K], w=[('vv', si)])
                for j in range(1, 3):
                    stt('dve', vv[:, :], ab[:, j:j + NTF], fw[:, ct, j:j + 1], vv[:, :], ALU.mult, ALU.add,
                        r=[('ab', si), K, ('vv', si)], w=[('vv', si)])
                cp('pool', ahist[:, ct, :], ab[:, NTF:NTF + 2], r=[('ab', si)], w=[('ah', ct)])
                b2 = nextbank()
                proj(b2, Wga, KG, ct * 128, 128, uFt, ukey, 0, NTF)
                tt('dve', vg[:, :], vv[:, :], ps(b2, 128, NTF), ALU.mult, r=[('vv', si), ('ps', b2)], w=[('vg', si)])
                act(ga[:, :], vv[:, :], AF.Square, r=[('vv', si)], w=[('ga', si)], scale=math.sqrt(0.044715))
                stt('dve', ga[:, :], ga[:, :], 1.0, vv[:, :], ALU.add, ALU.mult, r=[('ga', si), ('vv', si)],
                    w=[('ga', si)])
                act(ga[:, :], ga[:, :], AF.Sigmoid, r=[('ga', si)], w=[('ga', si)], scale=GC)
                tt('pool', ho[:, :], ga[:, :], vg[:, :], ALU.mult, r=[('ga', si), ('vg', si)], w=[('ho', si)])
                dma(rawap(hs, ct * 128 * SEGT + t0, [[SEGT, 128], [1, NTF]]), ho[:, :], r=[('ho', si)],
                    w=[('hs', ct, t0)])

    def phase_ffn_b(L, dst_t, seg):
        arena[0] = UT0
        Wd = aalloc('Wd', [128, 22, D], BF16)
        hb = [aalloc('hb', [128, 22, 512], BF16) for _ in range(2)]
        t1 = aalloc('t1f', [128, D], F32)
        xb2 = [aalloc('xb2', [128, D], F32) for _ in range(2)]
        K = ('W', 'fd')
        junk_ref[0] = aalloc('junkF', [128, D], BF16)
        KD = WK(('Wc', 'fd'), 512)
        for hf in range(2):
            load_w(Wd[:, :, hf * 512:(hf + 1) * 512], (KD.base, hf), 'ffn_w_down', L, 0, 22, hf * 512, 512)
        dma(gains2[1][:, :], rawap(W['norm_ffn_post'], L * D, [[0, 128], [1, D]]), w=[('c', 'gain')])
        bi = 0
        for ti, t0 in enumerate(range(0, SEGT, 512)):
            hbt = hb[ti % 2]
            hkey = ('hb', ti % 2)
            for c0 in (0, 11):
                dma(hbt[:, c0:c0 + 11, :], rawap(hs, c0 * 128 * SEGT + t0, [[SEGT, 128], [128 * SEGT, 11], [1, 512]]),
                    r=[('hs', ct, t0) for ct in range(c0, c0 + 11)], w=[hkey])
            for j in range(4):
                tb = t0 + j * 128
                xi = bi % 2
                bi += 1
                xb = xb2[xi]
                dma(xb[:, :], rawap(xmid, tb * D, [[D, 128], [1, D]]), r=[('xmid', tb)], w=[('xb2', xi)])
                banks = []
                for hf in range(2):
                    b = nextbank()
                    banks.append(b)
                    for k in range(22):
                        mm(ps(b, 128, 512), hbt[:, k, j * 128:(j + 1) * 128], Wd[:, k, hf * 512:(hf + 1) * 512], k == 0,
                           k == 21, r=[hkey, (KD.base, hf)], w=[('ps', b)])
                rowsum_rstd(banks, 128, 9)
                for hf, b in enumerate(banks):
                    tt('dve', t1[:, hf * 512:(hf + 1) * 512], ps(b, 128, 512), gains[3][:, hf * 512:(hf + 1) * 512],
                       ALU.mult, r=[('ps', b), ('c', 'gain')], w=[('t1f', hf)])
                    stt('dve', xb[:, hf * 512:(hf + 1) * 512], t1[:, hf * 512:(hf + 1) * 512], small[:, 9:10],
                        xb[:, hf * 512:(hf + 1) * 512], ALU.mult, ALU.add, r=[('t1f', hf), ('rsx',), ('xb2', xi)],
                        w=[('xb2', xi)])
                dma(rawap(dst_t, (seg * SEGT + tb) * D, [[D, 128], [1, D]]), xb[:, :], r=[('xb2', xi)],
                    w=[('dst', seg, tb)])

    for L in range(nlayer):
        src_t = x_in if L == 0 else x1
        dst_t = out_t if L == nlayer - 1 else x1
        for stt_ in (St32, Ct32, Nt32):
            memset('pool', stt_[:, :, :], 0.0, [('st', id(stt_), h) for h in range(4)])
        for bf_, s32 in ((Stbf, St32), (Ctbf, Ct32), (Ntbf, Nt32)):
            for h in range(4):
                cp('pool', bf_[:, h, :], s32[:, h, :], r=[('st', id(s32), h)], w=[('stbf', id(bf_), h)])
        memset('pool', XXst[:, :, :], 0.0, [('xxst',)])
        for ct in range(8):
            memset('pool', zch[:, ct, :], 0.0, [('zch', ct)])
        for ct in range(22):
            memset('pool', ahist[:, ct, :], 0.0, [('ah', ct)])
        S.barrier()
        s5_setup(L)
        for seg in range(nseg):
            S.barrier()
            dma(gainb[:, :], rawap(W['norm_mix_pre'], L * D, [[0, 128], [1, D]]), w=[('c', 'gain')])
            arena[0] = PH0
            nsets = norm_sets(3)
            for bi, tb in enumerate(range(0, SEGT, 128)):
                extra = [('dst', seg, tb)] if L > 0 else []
                norm_block(rawap(src_t, (seg * SEGT + tb) * D, [[D, 128], [1, D]]), 128, gains[0], ('c', 'gain'), uT,
                           ('uT',), tb, bi % 2, extra_r=extra, bufs=nsets[bi % 3])
            S.barrier()
            if 'gla' not in skip:
                phase_gla(L)
                S.barrier()
            if 'ml' not in skip:
                phase_ml(L)
                S.barrier()
            if 's5' not in skip:
                phase_s5(L)
                S.barrier()
            if 'out' not in skip:
                phase_out(L, src_t, seg)
                S.barrier()
            if 'ffn' not in skip:
                phase_ffn_a(L, seg)
                S.barrier()
                phase_ffn_b(L, dst_t, seg)
                S.barrier()

    with nc.semaphore('e_pe') as s0, nc.semaphore('e_act') as s1, nc.semaphore('e_dve') as s2, \
            nc.semaphore('e_pool') as s3, nc.semaphore('e_sp') as s4:
        esem = {'pe': s0, 'act': s1, 'dve': s2, 'pool': s3, 'sp': s4}
        import contextlib
        with contextlib.ExitStack() as es:
            ssem = [es.enter_context(nc.semaphore('dslot%d' % i)) for i in range(S.nslots)]
            with nc.Block() as block:
                S.emit(nc, block, esem, ssem)
    build.dbg_names = dbg_names
    build.amax = amax
    build.ph0 = PH0
    return nc


def kernel(**inputs):
    nc = build()
    m = {'x': np.ascontiguousarray(inputs['x'].reshape(SEQ, D), dtype=np.float32)}
    for n in WNAMES:
        m[n] = np.ascontiguousarray(inputs[n], dtype=np.float32)
    res = run_bass_kernel_spmd(nc, [m], core_ids=[0])
    return res.results[0]['out'].reshape(1, SEQ, D).astype(np.float32)
```

```python
import math
import numpy as np
import concourse.bass as bass
import concourse.mybir as mybir
from concourse.bass_utils import run_bass_kernel_spmd

F32 = mybir.dt.float32
BF16 = mybir.dt.bfloat16
ALU = mybir.AluOpType
AF = mybir.ActivationFunctionType

NCORE = 8
D = 1024
DIN = 7704
DFF = 2816
SEQ = 16384
OWN = SEQ // NCORE
HALO = 64
PRE = 3
NT = OWN + HALO
NW = NT + PRE
EPS = 1e-6
O_GQ, O_GK, O_GV, O_LR, O_GR, O_SU, O_MQ, O_MK, O_MV, O_MI, O_MF, O_MO, O_GATE = (
    0, 512, 1024, 1536, 1552, 2064, 2576, 3088, 3600, 4112, 4116, 4120, 4632)
TILES = [(0, 64), (64, 512), (576, 512), (1088, 512), (1600, 512)]
SNAP_TOK = OWN
LS5 = 8
GC = 1.5957691216057308

WNAMES = ['norm_mix_pre', 'norm_mix_post', 'norm_ffn_pre', 'norm_ffn_post', 'w_in',
          'gla_w_gk', 'gla_b_gk', 'gla_norm', 'gla_w_proj',
          's5_a_re', 's5_a_im', 's5_log_dt', 's5_b_re', 's5_b_im', 's5_c_re', 's5_c_im', 's5_d',
          's5_w_glu', 's5_b_glu', 's5_w_proj', 'ml_conv_w', 'ml_conv_b', 'ml_b_i', 'ml_b_f',
          'ml_w_proj', 'w_out', 'ffn_w_up', 'ffn_w_gate', 'ffn_conv_w', 'ffn_conv_b', 'ffn_w_down']
WSHAPES = {'norm_mix_pre': [D], 'norm_mix_post': [D], 'norm_ffn_pre': [D], 'norm_ffn_post': [D],
           'w_in': [D, DIN], 'gla_w_gk': [16, 512], 'gla_b_gk': [512], 'gla_norm': [128],
           'gla_w_proj': [512, D], 's5_a_re': [32, 64], 's5_a_im': [32, 64], 's5_log_dt': [32],
           's5_b_re': [32, 64, 16], 's5_b_im': [32, 64, 16], 's5_c_re': [32, 16, 64],
           's5_c_im': [32, 16, 64], 's5_d': [32, 16], 's5_w_glu': [512, 512], 's5_b_glu': [512],
           's5_w_proj': [512, D], 'ml_conv_w': [4, D], 'ml_conv_b': [D], 'ml_b_i': [4], 'ml_b_f': [4],
           'ml_w_proj': [512, D], 'w_out': [D, D], 'ffn_w_up': [D, DFF], 'ffn_w_gate': [D, DFF],
           'ffn_conv_w': [3, DFF], 'ffn_conv_b': [DFF], 'ffn_w_down': [DFF, D]}


class Sched:
    ENGS = ('pe', 'act', 'dve', 'pool', 'sp')
    SELF_SYNC = ('act', 'dve', 'pool')

    def __init__(self, nslots=16):
        self.streams = {e: [] for e in self.ENGS}
        self.ops = []
        self.lastw = {}
        self.rd = {}
        self.nslots = nslots
        self.slot_cnt = [0] * nslots
        self.slot_last = [None] * nslots
        self.dma_n = 0

    def add(self, eng, fn, r=(), w=(), dma=False):
        oid = len(self.ops)
        deps = set()
        raw = set()
        for k in r:
            p = self.lastw.get(k)
            if p is not None:
                deps.add(p)
                raw.add(p)
        for k in w:
            p = self.lastw.get(k)
            if p is not None:
                deps.add(p)
            rr = self.rd.get(k)
            if rr:
                deps.update(rr[0].values())
                deps.update(rr[1])
        o = {'id': oid, 'eng': eng, 'fn': fn, 'dma': dma, 'flag': False, 'raw': raw}
        if dma:
            s = self.dma_n % self.nslots
            self.dma_n += 1
            if self.slot_last[s] is not None:
                deps.add(self.slot_last[s])
            self.slot_cnt[s] += 1
            o['slot'] = s
            o['slotval'] = 16 * self.slot_cnt[s]
            self.slot_last[s] = oid
        for k in w:
            self.lastw[k] = oid
            self.rd[k] = ({}, [])
        for k in r:
            rr = self.rd.setdefault(k, ({}, []))
            if dma:
                rr[1].append(oid)
            else:
                rr[0][eng] = oid
        deps.discard(oid)
        o['deps'] = deps
        self.ops.append(o)
        self.streams[eng].append(o)
        return o

    def barrier(self):
        last = {}
        for e in self.ENGS:
            st = self.streams[e]
            for o in reversed(st):
                if o['fn'] is not None and not o['dma']:
                    last[e] = o['id']
                    break
        dmas = [x for x in self.slot_last if x is not None]
        for e in self.ENGS:
            oid = len(self.ops)
            deps = set(v for k, v in last.items() if k != e) | set(dmas)
            o = {'id': oid, 'eng': e, 'fn': None, 'dma': False, 'flag': False, 'deps': deps}
            self.ops.append(o)
            self.streams[e].append(o)

    def emit(self, nc, block, esem, ssem):
        ops = self.ops
        for o in ops:
            for d in o['deps']:
                dd = ops[d]
                if dd['dma']:
                    continue
                if dd['eng'] != o['eng'] or (d in o.get('raw', ()) and o['eng'] in self.SELF_SYNC):
                    dd['flag'] = True
        for e in self.ENGS:
            c = 0
            for o in self.streams[e]:
                if o['flag'] and not o['dma']:
                    c += 1
                    o['fidx'] = c
        sched = self

        def run(ename, eng):
            waited = {}
            for o in sched.streams[ename]:
                need = {}
                for d in o['deps']:
                    dd = ops[d]
                    if dd['dma']:
                        key = ('s', dd['slot'])
                        val = dd['slotval']
                    elif dd['eng'] == ename and not (d in o.get('raw', ()) and ename in sched.SELF_SYNC):
                        continue
                    else:
                        key = ('e', dd['eng'])
                        val = dd['fidx']
                    if val > waited.get(key, 0) and val > need.get(key, 0):
                        need[key] = val
                for key, val in need.items():
                    sem = ssem[key[1]] if key[0] == 's' else esem[key[1]]
                    eng.wait_ge(sem, val)
                    waited[key] = val
                if o['fn'] is None:
                    continue
                ins = o['fn'](eng)
                if o['dma']:
                    ins.then_inc(ssem[o['slot']], 16)
                elif o['flag']:
                    ins.then_inc(esem[ename], 1)
            if ename == 'sp':
                for s in range(sched.nslots):
                    if sched.slot_cnt[s]:
                        eng.wait_ge(ssem[s], 16 * sched.slot_cnt[s])

        @block.tensor
        def _(e):
            run('pe', e)

        @block.scalar
        def _(e):
            run('act', e)

        @block.vector
        def _(e):
            run('dve', e)

        @block.gpsimd
        def _(e):
            run('pool', e)

        @block.sync
        def _(e):
            run('sp', e)


def rawap(t, offset, pat):
    return bass.AP(tensor=t, offset=offset, ap=[list(p) for p in pat])


SEGT = 2048
DBGT = 512
BAR = False
P2ENG = 'dve'
TILE = 512
PI = math.pi


def build(nseg=8, nlayer=2, dbg=False, skip=()):
    nc = bass.Bass("TRN2", target_bir_lowering=False)
    S = Sched()
    ntok = nseg * SEGT
    x_in = nc.dram_tensor('x', [ntok, D], F32, kind='ExternalInput')
    W = {n: nc.dram_tensor(n, [2] + WSHAPES[n], F32, kind='ExternalInput') for n in WNAMES}
    WSZ = {n: int(np.prod(WSHAPES[n])) for n in WNAMES}
    out_t = nc.dram_tensor('out', [ntok, D], F32, kind='ExternalOutput')
    x1 = nc.dram_tensor('x1', [ntok, D], F32)
    xmid = nc.dram_tensor('xmid', [SEGT, D], F32)

    sb_top = [16512]
    SB_END = 229376

    def alloc(name, shape, dt, at=None):
        nbytes = int(np.prod(shape[1:])) * (4 if dt == F32 else 2)
        nbytes = (nbytes + 63) // 64 * 64
        if at is None:
            off = sb_top[0]
            sb_top[0] += nbytes
        else:
            off = at
        assert off + nbytes <= SB_END, (name, off, nbytes)
        return nc.alloc_sbuf_tensor_at(name, list(shape), dt, offset=off)

    psum = nc.alloc_psum_tensor('psum', [128, 8, 512], F32)

    def ps(b, p=128, n=512):
        return psum[0:p, b, 0:n]

    bank_rr = [0]

    def nextbank():
        b = bank_rr[0] % 5
        bank_rr[0] += 1
        return b

    ident = alloc('ident', [128, 128], BF16)
    onesbf = alloc('onesbf', [128, 128], BF16)
    ones32 = alloc('ones32', [128, 512], F32)
    maskut = alloc('maskut', [128, 64], F32)
    eps_t = alloc('eps_t', [128, 1], F32)
    negpi = alloc('negpi', [128, 1], F32)
    gainb = alloc('gainb', [128, D], F32)
    gains2 = [gainb, gainb]
    gains = [gainb, gainb, gainb, gainb]
    junk_ref = [None]
    small = alloc('small', [128, 64], F32)
    stage0 = alloc('stage0', [128, 512], F32)
    stage = [stage0, stage0]
    St32 = alloc('St32', [128, 4, 128], F32)
    Ct32 = alloc('Ct32', [128, 4, 128], F32)
    Nt32 = alloc('Nt32', [128, 4, 128], F32)
    Stbf = alloc('Stbf', [128, 4, 128], BF16)
    Ctbf = alloc('Ctbf', [128, 4, 128], BF16)
    Ntbf = alloc('Ntbf', [128, 4, 128], BF16)
    XXst = alloc('XXst', [128, 2, 16], F32)
    zch = alloc('zch', [128, 8, 3], F32)
    ahist = alloc('ahist', [128, 22, 2], F32)
    BBT = alloc('BBT', [32, 16, 2, 128], BF16)
    Ec = alloc('Ec', [128, 2, 16, 32], BF16)
    AA1 = alloc('AA1', [128, 2, 16], F32)
    A1i = alloc('A1i', [128, 16], F32)
    nA1i = alloc('nA1i', [128, 16], F32)
    Dd = alloc('Dd', [32, 16], F32)
    PW = alloc('PW', [128, 8, 2, 16], F32)
    nPWi = alloc('nPWi', [128, 8, 16], F32)
    AA8 = alloc('AA8', [128, 2, 16], F32)
    gsel = alloc('gsel', [32, 4, 128], BF16)
    Et = [alloc('Et%d' % i, [128, 8, 16, 32], BF16) for i in range(2)]
    sel_ref = [None, None]
    UT0 = sb_top[0]
    uT = alloc('uT', [128, 8, SEGT], BF16)
    merged = alloc('merged', [128, 8, SEGT], BF16)
    PH0 = sb_top[0]
    arena = [PH0]
    an = [0]

    amax = {}

    def aalloc(name, shape, dt):
        an[0] += 1
        t = alloc('%s_%d' % (name, an[0]), shape, dt, at=arena[0])
        nb = int(np.prod(shape[1:])) * (4 if dt == F32 else 2)
        arena[0] += (nb + 63) // 64 * 64
        amax['top'] = max(amax.get('top', 0), arena[0])
        return t

    cnt = {'stage': 0}
    dumped = {}

    def dma(out, in_, r=(), w=(), slow=False):
        if slow:
            return S.add('sp', lambda e: e.dma_start(out=out, in_=in_, allow_slow_non_contiguous=True),
                         r=r, w=w, dma=True)
        return S.add('sp', lambda e: e.dma_start(out=out, in_=in_), r=r, w=w, dma=True)

    def act(out, in_, func, r, w, bias=None, scale=None, accum=None):
        kw = {}
        if bias is not None:
            kw['bias'] = bias
        if scale is not None:
            kw['scale'] = scale
        if accum is not None:
            kw['accum_out'] = accum
        return S.add('act', lambda e: e.activation(out=out, in_=in_, func=func, **kw), r=r, w=w)

    def tt(eng, out, in0, in1, op, r, w):
        return S.add(eng, lambda e: e.tensor_tensor(out=out, in0=in0, in1=in1, op=op), r=r, w=w)

    def ts(eng, out, in0, s1, s2, op0, op1, r, w):
        if s2 is None:
            return S.add(eng, lambda e: e.tensor_single_scalar(out=out, in_=in0, scalar=s1, op=op0), r=r, w=w)
        return S.add(eng, lambda e: e.tensor_scalar(out=out, in0=in0, scalar1=s1, scalar2=s2, op0=op0, op1=op1),
                     r=r, w=w)

    def stt(eng, out, in0, sc, in1, op0, op1, r, w):
        return S.add(eng, lambda e: e.scalar_tensor_tensor(out=out, in0=in0, scalar=sc, in1=in1, op0=op0, op1=op1),
                     r=r, w=w)

    def cp(eng, out, in_, r, w):
        if eng == 'act':
            return S.add('act', lambda e: e.copy(out=out, in_=in_), r=r, w=w)
        return S.add(eng, lambda e: e.tensor_copy(out=out, in_=in_), r=r, w=w)

    def mm(out, lhsT, rhs, start, stop, r, w):
        return S.add('pe', lambda e: e.matmul(out, lhsT, rhs, start=start, stop=stop, skip_group_check=True),
                     r=r, w=w)

    def tr(out, in_, idn, r, w):
        return S.add('pe', lambda e: e.transpose(out, in_, idn), r=r, w=w)

    def memset(eng, ap, val, w):
        return S.add(eng, lambda e: e.memset(ap, val), r=(), w=w)

    dbg_names = []

    def dump(name, ap, rkeys):
        if not dbg:
            return
        t = nc.dram_tensor('dbg_' + name, list(ap.shape), ap.dtype, kind='ExternalOutput')
        dbg_names.append('dbg_' + name)
        full = t.ap()
        dma(full, ap, r=rkeys, w=[('dbg', name)])

    scr = nc.dram_tensor('scr_bar', [64, 64], F32)
    barn = [0]

    def lbar(items):
        for ap_, key in items:
            i = barn[0] % 64
            barn[0] += 1
            if ap_.dtype != F32:
                continue
            dma(scr.ap()[i:i + 1, 0:1], ap_, r=[key], w=[('scr', i)])

    def recip(out, in_, r, w):
        return S.add('dve', lambda e: e.reciprocal(out=out, in_=in_), r=r, w=w)

    def load_w(dst, key, name, L, row0, kt, col0, ncols, rows=128):
        src_t = W[name]
        ld = WSHAPES[name][1] if len(WSHAPES[name]) > 1 else 1
        base = L * WSZ[name]
        kstep = max(1, 8 if kt <= 8 else 11)
        for k0 in range(0, kt, kstep):
            kn = min(kstep, kt - k0)
            src = rawap(src_t, base + (row0 + k0 * 128) * ld + col0, [[ld, rows], [128 * ld, kn], [1, ncols]])
            dd = dst[:, k0:k0 + kn, :]
            S.add('pool', lambda e, dd=dd, src=src: e.dma_start(out=dd, in_=src), r=(), w=[key], dma=True)

    class WK:
        def __init__(self, base, chunk=512):
            self.base = base
            self.chunk = chunk

        def keys(self, c0, n):
            return [(self.base, j) for j in range(c0 // self.chunk, (c0 + n - 1) // self.chunk + 1)]

    def wkeys(wkey, c0, n):
        return wkey.keys(c0, n) if isinstance(wkey, WK) else [wkey]

    def load_wc(dst_t, wk, name, L, kt, dcol0, scol0, ncols):
        c = dcol0
        end = dcol0 + ncols
        while c < end:
            ce = min(end, (c // wk.chunk + 1) * wk.chunk)
            load_w(dst_t[:, :, c:ce], (wk.base, c // wk.chunk), name, L, 0, kt, scol0 + (c - dcol0), ce - c)
            c = ce

    def vec_load(dst, name, L, pat, key, off=0, slow=True):
        dma(dst, rawap(W[name], L * WSZ[name] + off, pat), w=[key], slow=slow)

    memset('pool', ones32[:, :], 1.0, [('c', 'ones32')])
    memset('pool', eps_t[:, :], EPS, [('c', 'eps')])
    memset('pool', negpi[:, :], -PI, [('c', 'negpi')])
    memset('pool', maskut[0:64, :], 1.0, [('c', 'mask')])
    S.add('pool', lambda e: e.affine_select(out=maskut[0:64, :], in_=maskut[0:64, :], pattern=[[1, 64]],
                                            compare_op=ALU.is_ge, fill=0.0, base=0, channel_multiplier=-1),
          r=[('c', 'mask')], w=[('c', 'mask')])
    cp('pool', onesbf[:, :], ones32[:, 0:128], r=[('c', 'ones32')], w=[('c', 'onesbf')])
    dma(maskut[64:128, :], maskut[0:64, :], r=[('c', 'mask')], w=[('c', 'mask')])
    identf = stage[1]
    memset('pool', identf[:, 0:128], 1.0, [('stage', 1)])
    S.add('pool', lambda e: e.affine_select(out=identf[:, 0:128], in_=identf[:, 0:128], pattern=[[1, 128]],
                                            compare_op=ALU.is_equal, fill=0.0, base=0, channel_multiplier=-1),
          r=[('stage', 1)], w=[('stage', 1)])
    cp('pool', ident[:, :], identf[:, 0:128], r=[('stage', 1)], w=[('c', 'ident')])
    for k in range(4):
        memset('pool', identf[0:32, 128:256], 1.0, [('stage', 1)])
        S.add('pool', lambda e, k=k: e.affine_select(out=identf[0:32, 128:256], in_=identf[0:32, 128:256],
                                                     pattern=[[1, 128]], compare_op=ALU.is_equal, fill=0.0,
                                                     base=-32 * k, channel_multiplier=-1),
              r=[('stage', 1)], w=[('stage', 1)])
        cp('pool', gsel[:, k, :], identf[0:32, 128:256], r=[('stage', 1)], w=[('c', 'gsel')])
    def norm_block(src_ap, nb, gain_t, gkey, dstT, dkey, dcol0, xi, extra_r=(), bufs=None):
        xb, un_, junk_, sc, sid = bufs
        kx, ku, kj = ('nbx', sid), ('nbu', sid), ('nbj', sid)
        b7 = nextbank()
        dma(xb[0:nb, :], src_ap, r=extra_r, w=[kx])
        act(junk_[0:nb, :], xb[0:nb, :], AF.Square, r=[kx], w=[kj])
        S.add('dve', lambda e: e.reduce_sum(out=small[0:nb, sc:sc + 1], in_=junk_[0:nb, :], axis=mybir.AxisListType.X),
              r=[kj], w=[('ss', sid)])
        act(small[0:nb, sc + 1:sc + 2], small[0:nb, sc:sc + 1], AF.Sqrt, r=[('ss', sid), ('c', 'eps')], w=[('sd', sid)],
            bias=eps_t[0:nb, :], scale=1.0 / D)
        recip(small[0:nb, sc + 2:sc + 3], small[0:nb, sc + 1:sc + 2], r=[('sd', sid)], w=[('rs', sid)])
        stt('dve', un_[0:nb, :], xb[0:nb, :], small[0:nb, sc + 2:sc + 3], gain_t[0:nb, :], ALU.mult, ALU.mult,
            r=[kx, ('rs', sid), gkey], w=[ku])
        pbf = psum[:, b7, :].bitcast(BF16)
        for k in range(8):
            tr(pbf[:, k * 128:k * 128 + nb], un_[0:nb, k * 128:(k + 1) * 128], ident[0:nb, 0:nb],
               r=[ku, ('c', 'ident')], w=[('ps', b7)])
        cp('act', dstT[:, :, dcol0:dcol0 + nb], pbf.rearrange('p (k t) -> p k t', k=8)[:, :, 0:nb], r=[('ps', b7)],
           w=[dkey])

    def norm_sets(n):
        out = []
        for i in range(n):
            out.append((aalloc('nbx', [128, D], F32), aalloc('nbu', [128, D], BF16), aalloc('nbj', [128, D], BF16),
                        20 + 3 * i, i))
        return out

    def rowsum_rstd(banks, nb, col):
        for hf, b in enumerate(banks):
            junk = junk_ref[0]
            act(junk[0:nb, hf * 512:(hf + 1) * 512], ps(b, nb, 512), AF.Square, r=[('ps', b)], w=[('junk',)])
            S.add('dve', lambda e, hf=hf: e.reduce_sum(out=small[0:nb, 4 + hf:5 + hf],
                                                       in_=junk[0:nb, hf * 512:(hf + 1) * 512],
                                                       axis=mybir.AxisListType.X), r=[('junk',)], w=[('ssx', hf)])
        tt('dve', small[0:nb, 6:7], small[0:nb, 4:5], small[0:nb, 5:6], ALU.add, r=[('ssx', 0), ('ssx', 1)],
           w=[('ssx', 2)])
        act(small[0:nb, 7:8], small[0:nb, 6:7], AF.Sqrt, r=[('ssx', 2), ('c', 'eps')], w=[('ssx', 3)],
            bias=eps_t[0:nb, :], scale=1.0 / D)
        recip(small[0:nb, col:col + 1], small[0:nb, 7:8], r=[('ssx', 3)], w=[('rsx',)])

    def proj(bank, wt, wkey, c0, M, src, skey, tok0, nt, kt=8):
        for k in range(kt):
            mm(ps(bank, M, nt), wt[:, k, c0:c0 + M], src[:, k, tok0:tok0 + nt], k == 0, k == kt - 1,
               r=wkeys(wkey, c0, M) + [skey], w=[('ps', bank)])

    def gelu(dst, v, n, vkey, dkey, tmpa, tmpb, P=128):
        act(tmpa[0:P, 0:n], v, AF.Square, r=[vkey], w=[('ga',)])
        ts('dve', tmpa[0:P, 0:n], tmpa[0:P, 0:n], 0.044715, 1.0, ALU.mult, ALU.add, r=[('ga',)], w=[('ga',)])
        tt('dve', tmpa[0:P, 0:n], tmpa[0:P, 0:n], v, ALU.mult, r=[('ga',), vkey], w=[('ga',)])
        act(tmpb[0:P, 0:n], tmpa[0:P, 0:n], AF.Sigmoid, r=[('ga',)], w=[('gb',)], scale=GC)
        tt('pool', dst, tmpb[0:P, 0:n], v, ALU.mult, r=[('gb',), vkey], w=[dkey])

    def s5_setup(L):
        arena[0] = PH0
        aR = aalloc('aR', [128, 16], F32); aI = aalloc('aI', [128, 16], F32); ldt = aalloc('ldt', [128, 16], F32)
        t = [aalloc('s5t', [128, 16], F32) for _ in range(12)]
        bR = aalloc('bR', [128, 16, 16], F32); bI = aalloc('bI', [128, 16, 16], F32)
        cR = aalloc('cR', [128, 16, 16], F32); cI = aalloc('cI', [128, 16, 16], F32)
        u1 = aalloc('u1', [128, 16, 16], F32); u2 = aalloc('u2', [128, 16, 16], F32)
        bsrc = [aalloc('bsrc', [128, 16, 32], BF16) for _ in range(2)]
        K = ('s5c', L)
        vec_load(aR[:, :], 's5_a_re', L, [[1, 128], [128, 16]], K)
        vec_load(aI[:, :], 's5_a_im', L, [[1, 128], [128, 16]], K)
        for e in range(2):
            vec_load(ldt[e * 64:(e + 1) * 64, :], 's5_log_dt', L, [[0, 64], [2, 16]], K, off=e)
        vec_load(bR[:, :, :], 's5_b_re', L, [[16, 128], [2048, 16], [1, 16]], K, slow=False)
        vec_load(bI[:, :, :], 's5_b_im', L, [[16, 128], [2048, 16], [1, 16]], K, slow=False)
        for e in range(2):
            for pr in range(16):
                vec_load(cR[e * 64:(e + 1) * 64, pr, :], 's5_c_re', L, [[1, 64], [64, 16]], K, off=e * 1024 + pr * 2048)
                vec_load(cI[e * 64:(e + 1) * 64, pr, :], 's5_c_im', L, [[1, 64], [64, 16]], K, off=e * 1024 + pr * 2048)
        vec_load(Dd[:, :], 's5_d', L, [[1, 32], [32, 16]], K)
        R = [K]
        dt_, dre, dim, mag, sn, cs, A1r, den, zr, fre, fim, tmp = t
        act(dt_[:, :], ldt[:, :], AF.Exp, r=R, w=R)
        tt('dve', dre[:, :], dt_[:, :], aR[:, :], ALU.mult, r=R, w=R)
        tt('dve', dim[:, :], dt_[:, :], aI[:, :], ALU.mult, r=R, w=R)
        act(mag[:, :], dre[:, :], AF.Exp, r=R, w=R)
        cp('dve', sn[:, :], dim[:, :], r=R, w=R)
        ts('dve', cs[:, :], dim[:, :], 0.5 * PI, None, ALU.add, None, r=R, w=R)
        for arr in (sn, cs):
            cp('dve', den[:, :], arr[:, :], r=R, w=R)
            for th in (PI, 3 * PI, 5 * PI, 7 * PI):
                ts('dve', tmp[:, :], den[:, :], th, None, ALU.is_ge, None, r=R, w=R)
                stt('dve', arr[:, :], tmp[:, :], -2 * PI, arr[:, :], ALU.mult, ALU.add, r=R, w=R)
            act(arr[:, :], arr[:, :], AF.Sin, r=R, w=R)
        tt('dve', A1r[:, :], mag[:, :], cs[:, :], ALU.mult, r=R, w=R)
        tt('dve', A1i[:, :], mag[:, :], sn[:, :], ALU.mult, r=R, w=R)
        ts('dve', nA1i[:, :], A1i[:, :], -1.0, None, ALU.mult, None, r=R, w=R)
        cp('dve', AA1[:, 0, :], A1r[:, :], r=R, w=R)
        cp('dve', AA1[:, 1, :], A1r[:, :], r=R, w=R)
        cp('dve', PW[:, 0, 0, :], A1r[:, :], r=R, w=R)
        cp('dve', PW[:, 0, 1, :], A1i[:, :], r=R, w=R)
        for k in range(7):
            tt('dve', tmp[:, :], PW[:, k, 0, :], A1r[:, :], ALU.mult, r=R, w=R)
            tt('dve', den[:, :], PW[:, k, 1, :], A1i[:, :], ALU.mult, r=R, w=R)
            tt('dve', PW[:, k + 1, 0, :], tmp[:, :], den[:, :], ALU.subtract, r=R, w=R)
            tt('dve', tmp[:, :], PW[:, k, 0, :], A1i[:, :], ALU.mult, r=R, w=R)
            tt('dve', den[:, :], PW[:, k, 1, :], A1r[:, :], ALU.mult, r=R, w=R)
            tt('dve', PW[:, k + 1, 1, :], tmp[:, :], den[:, :], ALU.add, r=R, w=R)
        for k in range(8):
            ts('dve', nPWi[:, k, :], PW[:, k, 1, :], -1.0, None, ALU.mult, None, r=R, w=R)
        cp('dve', AA8[:, 0, :], PW[:, 7, 0, :], r=R, w=R)
        cp('dve', AA8[:, 1, :], PW[:, 7, 0, :], r=R, w=R)
        tt('dve', den[:, :], aR[:, :], aR[:, :], ALU.mult, r=R, w=R)
        tt('dve', tmp[:, :], aI[:, :], aI[:, :], ALU.mult, r=R, w=R)
        tt('dve', den[:, :], den[:, :], tmp[:, :], ALU.add, r=R, w=R)
        recip(den[:, :], den[:, :], r=R, w=R)
        ts('dve', zr[:, :], A1r[:, :], -1.0, None, ALU.add, None, r=R, w=R)
        tt('dve', fre[:, :], zr[:, :], aR[:, :], ALU.mult, r=R, w=R)
        tt('dve', tmp[:, :], A1i[:, :], aI[:, :], ALU.mult, r=R, w=R)
        tt('dve', fre[:, :], fre[:, :], tmp[:, :], ALU.add, r=R, w=R)
        tt('dve', fre[:, :], fre[:, :], den[:, :], ALU.mult, r=R, w=R)
        tt('dve', fim[:, :], A1i[:, :], aR[:, :], ALU.mult, r=R, w=R)
        tt('dve', tmp[:, :], zr[:, :], aI[:, :], ALU.mult, r=R, w=R)
        tt('dve', fim[:, :], fim[:, :], tmp[:, :], ALU.subtract, r=R, w=R)
        tt('dve', fim[:, :], fim[:, :], den[:, :], ALU.mult, r=R, w=R)
        frb = fre[:, :].unsqueeze(2).to_broadcast([128, 16, 16])
        fib = fim[:, :].unsqueeze(2).to_broadcast([128, 16, 16])
        tt('dve', u1[:, :, :], bR[:, :, :], frb, ALU.mult, r=R, w=R)
        tt('dve', u2[:, :, :], bI[:, :, :], fib, ALU.mult, r=R, w=R)
        tt('dve', u1[:, :, :], u1[:, :, :], u2[:, :, :], ALU.subtract, r=R, w=R)
        tt('dve', u2[:, :, :], bI[:, :, :], frb, ALU.mult, r=R, w=R)
        tt('dve', bI[:, :, :], bR[:, :, :], fib, ALU.mult, r=R, w=R)
        tt('dve', u2[:, :, :], u2[:, :, :], bI[:, :, :], ALU.add, r=R, w=R)
        ts('dve', cI[:, :, :], cI[:, :, :], -1.0, None, ALU.mult, None, r=R, w=R)
        for ri, (bsrc_, srcb, srcc) in enumerate(((bsrc[0], u1, cR), (bsrc[1], u2, cI))):
            memset('dve', bsrc_[:, :, :], 0.0, R)
            memset('dve', Ec[:, ri, :, :], 0.0, R)
            for e in range(2):
                cp('dve', bsrc_[e * 64:(e + 1) * 64, :, e * 16:(e + 1) * 16], srcb[e * 64:(e + 1) * 64, :, :], r=R, w=R)
                cp('dve', Ec[e * 64:(e + 1) * 64, ri, :, e * 16:(e + 1) * 16], srcc[e * 64:(e + 1) * 64, :, :], r=R, w=R)
        for ri in range(2):
            memset('dve', Et[ri][:, :, :, :], 0.0, R)
        for tau in range(8):
            prb = PW[:, tau, 0, :].unsqueeze(2).to_broadcast([128, 16, 16])
            pib = PW[:, tau, 1, :].unsqueeze(2).to_broadcast([128, 16, 16])
            tt('dve', u1[:, :, :], cR[:, :, :], prb, ALU.mult, r=R, w=R)
            tt('dve', u2[:, :, :], cI[:, :, :], pib, ALU.mult, r=R, w=R)
            tt('dve', u1[:, :, :], u1[:, :, :], u2[:, :, :], ALU.add, r=R, w=R)
            tt('dve', u2[:, :, :], cI[:, :, :], prb, ALU.mult, r=R, w=R)
            tt('dve', bR[:, :, :], cR[:, :, :], pib, ALU.mult, r=R, w=R)
            tt('dve', u2[:, :, :], u2[:, :, :], bR[:, :, :], ALU.subtract, r=R, w=R)
            for ri, srcE in ((0, u1), (1, u2)):
                for e in range(2):
                    cp('dve', Et[ri][e * 64:(e + 1) * 64, tau, :, e * 16:(e + 1) * 16], srcE[e * 64:(e + 1) * 64, :, :],
                       r=R, w=R)
        pbf = psum[:, 7, :].bitcast(BF16)
        for ri in range(2):
            for g4 in range(2):
                for j in range(8):
                    pr = g4 * 8 + j
                    tr(pbf[0:32, j * 128:(j + 1) * 128], bsrc[ri][:, pr, :], ident[:, :], r=R + [('c', 'ident')],
                       w=[('ps', 7)])
                cp('act', BBT[:, g4 * 8:(g4 + 1) * 8, ri, :], pbf[0:32, :].rearrange('p (j q) -> p j q', j=8),
                   r=[('ps', 7)], w=R)

    SBANK = (6, 5)

    def prep_head(nt, hi, qa, qkeys, ka, kkeys, qscale, kscale, B):
        qd2, ki2, kd2, kdT, sT, E1, E2, E3, dec2 = B
        stt('dve', qd2[hi][:, 0:nt], qa, qscale, E1[:, 0:nt], ALU.mult, ALU.mult, r=qkeys + [('E1',)], w=[('qd', hi)])
        stt('dve', ki2[hi][:, 0:nt], ka, kscale, E2[:, 0:nt], ALU.mult, ALU.mult, r=kkeys + [('E2',)], w=[('ki', hi)])
        stt('dve', kd2[hi][:, 0:nt], ka, kscale, E3[:, 0:nt], ALU.mult, ALU.mult, r=kkeys + [('E3',)], w=[('kd', hi)])

    def chunk_pair(nt, heads, vtok, vcol0, stl, bfl, use_n, o_banks, den_banks, B):
        nch = nt // 64
        qd2, ki2, kd2, kdT, sT, E1, E2, E3, dec2 = B
        S32 = stl[0]
        Sbf = bfl[0]
        pbf = psum[:, 7, :].bitcast(BF16)
        def issue_tr(c, hi):
            kd = kd2[hi]
            i2 = hi * 2 + c % 2
            PS = slice((c % 2) * 64, (c % 2) * 64 + 64)
            tr(pbf[PS, i2 * 128:(i2 + 1) * 128], kd[:, c * 64:c * 64 + 64], ident[:, :], r=[('kd', hi), ('c', 'ident')],
               w=[('ps7', i2)])
            cp('act', kdT[i2][PS, :], pbf[PS, i2 * 128:(i2 + 1) * 128], r=[('ps7', i2)], w=[('kdT', i2)])

        for hi in range(len(heads)):
            issue_tr(0, hi)
        for c in range(nch):
            cs = c * 64
            for hi, h in enumerate(heads):
                qd, ki, kd, dec = qd2[hi], ki2[hi], kd2[hi], dec2[hi]
                i2 = hi * 2 + c % 2
                P0 = (c % 2) * 64
                PS = slice(P0, P0 + 64)
                sb = SBANK[hi]
                o_bank = o_banks[hi]
                mm(psum[PS, sb, 0:64], ki[:, cs:cs + 64], qd[:, cs:cs + 64], True, True, r=[('ki', hi), ('qd', hi)],
                   w=[('ps', sb)])
                tt('dve', sT[i2][PS, :], psum[PS, sb, 0:64], maskut[PS, :], ALU.mult, r=[('ps', sb), ('c', 'mask')],
                   w=[('sT', i2)])
                vv = vtok[PS, c // 2, vcol0 + h * 128: vcol0 + (h + 1) * 128]
                mm(psum[:, o_bank, cs:cs + 64], vv, sT[i2][PS, :], True, False, r=[('vtok',), ('sT', i2)],
                   w=[('ps', o_bank)])
                if use_n:
                    Nbf = bfl[1]
                    den_bank = den_banks[hi]
                    mm(psum[:, den_bank, cs:cs + 64], onesbf[PS, :], sT[i2][PS, :], True, False,
                       r=[('c', 'onesbf'), ('sT', i2)], w=[('ps', den_bank)])
                mm(psum[:, sb, 128:256], kdT[i2][PS, :], vv, True, True, r=[('kdT', i2), ('vtok',)], w=[('ps', sb)])
                if use_n:
                    mm(psum[:, sb, 256:384], kdT[i2][PS, :], onesbf[PS, :], True, True,
                       r=[('kdT', i2), ('c', 'onesbf')], w=[('ps', sb)])
                if c + 1 < nch:
                    issue_tr(c + 1, hi)
                mm(psum[:, o_bank, cs:cs + 64], Sbf[:, h, :], qd[:, cs:cs + 64], False, True,
                   r=[('stbf', id(Sbf), h), ('qd', hi)], w=[('ps', o_bank)])
                if use_n:
                    mm(psum[:, den_bank, cs:cs + 64], Nbf[:, h, :], qd[:, cs:cs + 64], False, True,
                       r=[('stbf', id(Nbf), h), ('qd', hi)], w=[('ps', den_bank)])
                stt('dve', S32[:, h, :], S32[:, h, :], dec[:, c:c + 1], psum[:, sb, 128:256], ALU.mult, ALU.add,
                    r=[('st', id(S32), h), ('dec', hi), ('ps', sb)], w=[('st', id(S32), h)])
                cp('act', Sbf[:, h, :], S32[:, h, :], r=[('st', id(S32), h)], w=[('stbf', id(Sbf), h)])
                if use_n:
                    N32 = stl[1]
                    stt('dve', N32[:, h, :], N32[:, h, :], dec[:, c:c + 1], psum[:, sb, 256:384], ALU.mult, ALU.add,
                        r=[('st', id(N32), h), ('dec', hi), ('ps', sb)], w=[('st', id(N32), h)])
                    cp('act', Nbf[:, h, :], N32[:, h, :], r=[('st', id(N32), h)], w=[('stbf', id(Nbf), h)])

    def vproj(tok0, nt, wt, wkey, c0, vtok):
        for blk in range(nt // 128):
            b = nextbank()
            for k in range(8):
                mm(ps(b, 128, 512), uT[:, k, tok0 + blk * 128: tok0 + blk * 128 + 128], wt[:, k, c0:c0 + 512],
                   k == 0, k == 7, r=wkeys(wkey, c0, 512) + [('uT',)], w=[('ps', b)])
            cp('act', vtok[:, blk, :], ps(b, 128, 512), r=[('ps', b)], w=[('vtok',)])

    def merge_out(tok0, nt, oT, okey, wpr, wprkey, wgate, wgkey, gc0, first, gsig, gtmp):
        for co in range(8):
            b = nextbank()
            for k in range(4):
                mm(ps(b, 128, nt), wpr[:, k, co * 128:(co + 1) * 128], oT[:, k, 0:nt], k == 0, k == 3,
                   r=[wprkey, okey], w=[('ps', b)])
            b2 = nextbank()
            proj(b2, wgate, wgkey, gc0 + co * 128, 128, uT, ('uT',), tok0, nt)
            act(gsig[:, 0:nt], ps(b2, 128, nt), AF.Sigmoid, r=[('ps', b2)], w=[('gsig',)])
            if first:
                tt('dve', merged[:, co, tok0:tok0 + nt], ps(b, 128, nt), gsig[:, 0:nt], ALU.mult,
                   r=[('ps', b), ('gsig',)], w=[('mg',)])
            else:
                tt('dve', gtmp[:, 0:nt], ps(b, 128, nt), gsig[:, 0:nt], ALU.mult, r=[('ps', b), ('gsig',)],
                   w=[('gtmp',)])
                tt('pool', merged[:, co, tok0:tok0 + nt], merged[:, co, tok0:tok0 + nt], gtmp[:, 0:nt], ALU.add,
                   r=[('gtmp',), ('mg',)], w=[('mg',)])

    def common_bufs():
        qd2 = [aalloc('qd', [128, 512], BF16) for _ in range(2)]
        ki2 = [aalloc('ki', [128, 512], BF16) for _ in range(2)]
        kd2 = [aalloc('kd', [128, 512], BF16) for _ in range(2)]
        kdT = [aalloc('kdT', [128, 128], BF16) for _ in range(4)]
        sT = [aalloc('sT', [128, 64], BF16) for _ in range(4)]
        E1 = aalloc('E1', [128, 512], F32); E2 = aalloc('E2', [128, 512], F32); E3 = aalloc('E3', [128, 512], F32)
        dec2 = [aalloc('dec', [128, 8], F32) for _ in range(2)]
        return (qd2, ki2, kd2, kdT, sT, E1, E2, E3, dec2)

    def phase_gla(L):
        arena[0] = PH0
        Wg = aalloc('Wg', [128, 8, 3088], BF16)
        Wgk = aalloc('Wgk', [16, 1, 512], BF16)
        Wpr = aalloc('Wpr', [128, 4, D], BF16)
        bgk = aalloc('bgk', [128, 4], F32); gln = aalloc('gln', [128, 1], F32)
        Bx = aalloc('Bx', [128, 513], F32)
        Drel = aalloc('Drel', [128, 512], F32); Drel2 = aalloc('Drel2', [128, 512], F32)
        B = common_bufs()
        qd2, ki2, kd2, kdT, sT, E1, E2, E3, dec2 = B
        vtok = aalloc('vtok', [128, 4, 512], BF16)
        lrT = aalloc('lrT', [16, 512], BF16)
        sq = aalloc('sq', [128, 512], BF16)
        rstd = E1; t1 = E2; sr = E3
        ogT = aalloc('ogT', [128, 4, 512], BF16)
        gsig = aalloc('gsig', [128, 512], F32); gtmp = aalloc('gtmp', [128, 512], F32)
        spb = gtmp
        K = ('W', 'g')
        KW = WK(('Wc', 'g'))
        load_wc(Wg, KW, 'w_in', L, 8, 1024, 1024, 1040)
        load_wc(Wg, KW, 'w_in', L, 8, 0, 0, 1024)
        load_wc(Wg, KW, 'w_in', L, 8, 2064, O_GATE, 1024)
        load_w(Wgk[:, :, :], K, 'gla_w_gk', L, 0, 1, 0, 512, rows=16)
        load_w(Wpr[:, :, :], K, 'gla_w_proj', L, 0, 4, 0, D)
        vec_load(bgk[:, :], 'gla_b_gk', L, [[1, 128], [128, 4]], K)
        ts('dve', bgk[:, :], bgk[:, :], -1.0, None, ALU.mult, None, r=[K], w=[K])
        vec_load(gln[:, :], 'gla_norm', L, [[1, 128], [1, 1]], K)
        memset('pool', Bx[:, 0:1], 0.0, [('Bx',)])
        for tok0 in range(0, SEGT, TILE):
            nt = TILE
            nch = nt // 64
            vproj(tok0, nt, Wg, KW, O_GV, vtok)
            b = nextbank()
            proj(b, Wg, KW, O_LR, 16, uT, ('uT',), tok0, nt)
            cp('act', lrT[:, 0:nt], ps(b, 16, nt), r=[('ps', b)], w=[('lrT',)])
            for pair in ((0, 1), (2, 3)):
                for hi, h in enumerate(pair):
                    dec = dec2[hi]
                    b = nextbank()
                    mm(ps(b, 128, nt), Wgk[:, 0, h * 128:(h + 1) * 128], lrT[:, 0:nt], True, True, r=[K, ('lrT',)],
                       w=[('ps', b)])
                    act(spb[:, 0:nt], ps(b, 128, nt), AF.Exp, r=[('ps', b), K], w=[('gtmp',)], bias=bgk[:, h:h + 1],
                        scale=-1.0)
                    act(spb[:, 0:nt], spb[:, 0:nt], AF.Ln, r=[('gtmp',)], w=[('gtmp',)], bias=1.0)
                    S.add('dve', lambda e, nt=nt: e.tensor_tensor_scan(out=Bx[:, 1:1 + nt], data0=ones32[:, 0:nt],
                                                                       data1=spb[:, 0:nt], initial=0.0,
                                                                       op0=ALU.mult, op1=ALU.add),
                          r=[('gtmp',), ('c', 'ones32')], w=[('Bx',)])
                    d3 = Drel[:, 0:nt].rearrange('p (c l) -> p c l', l=64)
                    d23 = Drel2[:, 0:nt].rearrange('p (c l) -> p c l', l=64)
                    bx3 = Bx[:, 1:1 + nt].rearrange('p (c l) -> p c l', l=64)
                    br3 = Bx[:, 0:nt].rearrange('p (c l) -> p c l', l=64)[:, :, 0:1]
                    tt('dve', d3, bx3, br3.to_broadcast([128, nch, 64]), ALU.subtract, r=[('Bx',)], w=[('Drel',)])
                    tt('dve', d23, d3, d3[:, :, 63:64].to_broadcast([128, nch, 64]), ALU.subtract, r=[('Drel',)],
                       w=[('Drel2',)])
                    act(E1[:, 0:nt], Drel[:, 0:nt], AF.Exp, r=[('Drel',)], w=[('E1',)], scale=-1.0 / 16)
                    act(dec[:, 0:nch], Drel[:, 63:nt:64], AF.Exp, r=[('Drel',)], w=[('dec', hi)], scale=-1.0 / 16)
                    act(E2[:, 0:nt], Drel[:, 0:nt], AF.Exp, r=[('Drel',)], w=[('E2',)], scale=1.0 / 16)
                    act(E3[:, 0:nt], Drel2[:, 0:nt], AF.Exp, r=[('Drel2',)], w=[('E3',)], scale=1.0 / 16)
                    bq = nextbank()
                    proj(bq, Wg, KW, O_GQ + h * 128, 128, uT, ('uT',), tok0, nt)
                    bk = nextbank()
                    proj(bk, Wg, KW, O_GK + h * 128, 128, uT, ('uT',), tok0, nt)
                    prep_head(nt, hi, ps(bq, 128, nt), [('ps', bq)], ps(bk, 128, nt), [('ps', bk)], 128 ** -0.5, 1.0, B)
                obs = [nextbank(), nextbank()]
                chunk_pair(nt, pair, vtok, 0, [St32], [Stbf], False, obs, None, B)
                for hi, h in enumerate(pair):
                    ob = obs[hi]
                    act(sq[:, 0:nt], psum[:, ob, 0:nt], AF.Square, r=[('ps', ob)], w=[('sq',)])
                    b = nextbank()
                    mm(ps(b, 128, nt), onesbf[:, :], sq[:, 0:nt], True, True, r=[('c', 'onesbf'), ('sq',)],
                       w=[('ps', b)])
                    act(rstd[:, 0:nt], ps(b, 128, nt), AF.Sqrt, r=[('ps', b), ('c', 'eps')], w=[('E1',)],
                        bias=eps_t[:, :], scale=1.0 / 128)
                    recip(rstd[:, 0:nt], rstd[:, 0:nt], r=[('E1',)], w=[('E1',)])
                    tt('dve', t1[:, 0:nt], psum[:, ob, 0:nt], rstd[:, 0:nt], ALU.mult, r=[('ps', ob), ('E1',)],
                       w=[('E2',)])
                    b = nextbank()
                    proj(b, Wg, KW, O_GR + h * 128, 128, uT, ('uT',), tok0, nt)
                    act(sr[:, 0:nt], ps(b, 128, nt), AF.Silu, r=[('ps', b)], w=[('E3',)])
                    stt('dve', ogT[:, h, 0:nt], sr[:, 0:nt], gln[:, 0:1], t1[:, 0:nt], ALU.mult, ALU.mult,
                        r=[('E3',), K, ('E2',)], w=[('ogT',)])
            merge_out(tok0, nt, ogT, ('ogT',), Wpr, K, Wg, KW, 2064, True, gsig, gtmp)

    def phase_ml(L):
        arena[0] = PH0
        Wm = aalloc('Wm', [128, 8, 3080], BF16)
        Wpr = aalloc('Wmp', [128, 4, D], BF16)
        B = common_bufs()
        qd2, ki2, kd2, kdT, sT, E1, E2, E3, dec2 = B
        vtok = aalloc('vtok', [128, 4, 512], BF16)
        zc = aalloc('zc', [128, 515], F32)
        qh = aalloc('qh', [128, 512], BF16); kh = aalloc('kh', [128, 512], BF16)
        B8 = aalloc('B8', [8, 513], F32)
        T2 = aalloc('T2', [8, 512], F32); T3 = aalloc('T3', [8, 512], F32)
        bif = aalloc('bif', [8, 1], F32); cw = aalloc('cw', [128, 8, 4], F32); cb = aalloc('cb', [128, 8], F32)
        dabs = E1; t1 = E2; so = E3
        ogT = aalloc('hT', [128, 4, 512], BF16)
        gsig = aalloc('gsig', [128, 512], F32); gtmp = aalloc('gtmp', [128, 512], F32)
        cacc = gtmp; gi8 = gsig[0:8, :]
        selF = aalloc('selF', [8, 4, 128], F32)
        selIF = aalloc('selIF', [8, 4, 128], F32)
        for h in range(4):
            for (dst, rows_) in ((selF, (4 + h,)), (selIF, (h, 4 + h))):
                memset('pool', dst[:, h, :], 0.0, [('c', 'sel')])
                for rr in rows_:
                    memset('pool', zc[0:8, 0:128], 1.0, [('zc',)])
                    S.add('pool', lambda e, rr=rr: e.affine_select(out=zc[0:8, 0:128], in_=zc[0:8, 0:128],
                                                                   pattern=[[0, 128]], compare_op=ALU.is_equal, fill=0.0,
                                                                   base=-rr, channel_multiplier=1),
                          r=[('zc',)], w=[('zc',)])
                    tt('pool', dst[:, h, :], dst[:, h, :], zc[0:8, 0:128], ALU.add, r=[('zc',), ('c', 'sel')],
                       w=[('c', 'sel')])
        K = ('W', 'm')
        KW = WK(('Wc', 'm'))
        load_wc(Wm, KW, 'w_in', L, 8, 1024, O_MQ + 1024, 1032)
        load_wc(Wm, KW, 'w_in', L, 8, 0, O_MQ, 1024)
        load_wc(Wm, KW, 'w_in', L, 8, 2056, O_GATE + 2048, 1024)
        load_w(Wpr[:, :, :], K, 'ml_w_proj', L, 0, 4, 0, D)
        vec_load(bif[0:4, :], 'ml_b_i', L, [[1, 4], [1, 1]], K)
        vec_load(bif[4:8, :], 'ml_b_f', L, [[1, 4], [1, 1]], K)
        for j in range(4):
            vec_load(cw[:, :, j], 'ml_conv_w', L, [[1, 128], [128, 8]], K, off=j * 1024)
        vec_load(cb[:, :], 'ml_conv_b', L, [[1, 128], [128, 8]], K)
        memset('pool', B8[:, 0:1], 0.0, [('B8',)])
        for tok0 in range(0, SEGT, TILE):
            nt = TILE
            nch = nt // 64
            vproj(tok0, nt, Wm, KW, 1024, vtok)
            b = nextbank()
            proj(b, Wm, KW, 1536, 8, uT, ('uT',), tok0, nt)
            act(gi8[:, 0:nt], ps(b, 8, nt), AF.Identity, r=[('ps', b), K], w=[('gsig',)], bias=bif[:, :])
            act(T3[:, 0:nt], gi8[:, 0:nt], AF.Exp, r=[('gsig',)], w=[('T3',)], scale=-1.0)
            act(T3[:, 0:nt], T3[:, 0:nt], AF.Ln, r=[('T3',)], w=[('T3',)], bias=1.0)
            S.add('dve', lambda e, nt=nt: e.tensor_tensor_scan(out=B8[:, 1:1 + nt], data0=ones32[0:8, 0:nt],
                                                               data1=T3[:, 0:nt], initial=0.0, op0=ALU.mult,
                                                               op1=ALU.add),
                  r=[('T3',), ('c', 'ones32')], w=[('B8',)])
            d3 = T2[:, 0:nt].rearrange('p (c l) -> p c l', l=64)
            d23 = T3[:, 0:nt].rearrange('p (c l) -> p c l', l=64)
            bx3 = B8[:, 1:1 + nt].rearrange('p (c l) -> p c l', l=64)
            br3 = B8[:, 0:nt].rearrange('p (c l) -> p c l', l=64)[:, :, 0:1]
            tt('dve', d3, bx3, br3.to_broadcast([8, nch, 64]), ALU.subtract, r=[('B8',)], w=[('T2',)])
            tt('dve', d23, d3, d3[:, :, 63:64].to_broadcast([8, nch, 64]), ALU.subtract, r=[('T2',), ('B8',)],
               w=[('T3',)])
            cp('dve', T2[0:4, 0:nt], gi8[0:4, 0:nt], r=[('gsig',), ('T2',)], w=[('T2',)])
            cp('dve', T3[0:4, 0:nt], gi8[0:4, 0:nt], r=[('gsig',), ('T3',)], w=[('T3',)])
            for pair in ((0, 1), (2, 3)):
                for hi, h in enumerate(pair):
                    dec = dec2[hi]
                    b1 = nextbank()
                    mm(ps(b1, 128, nt), selF[:, h, :], T2[:, 0:nt], True, True, r=[('c', 'sel'), ('T2',)],
                       w=[('ps', b1)])
                    act(E1[:, 0:nt], ps(b1, 128, nt), AF.Exp, r=[('ps', b1)], w=[('E1',)], scale=-1.0)
                    act(dec[:, 0:nch], psum[:, b1, 63:nt:64], AF.Exp, r=[('ps', b1)], w=[('dec', hi)], scale=-1.0)
                    b2 = nextbank()
                    mm(ps(b2, 128, nt), selIF[:, h, :], T2[:, 0:nt], True, True, r=[('c', 'sel'), ('T2',)],
                       w=[('ps', b2)])
                    act(E2[:, 0:nt], ps(b2, 128, nt), AF.Exp, r=[('ps', b2)], w=[('E2',)])
                    b3 = nextbank()
                    mm(ps(b3, 128, nt), selIF[:, h, :], T3[:, 0:nt], True, True, r=[('c', 'sel'), ('T3',)],
                       w=[('ps', b3)])
                    act(E3[:, 0:nt], ps(b3, 128, nt), AF.Exp, r=[('ps', b3)], w=[('E3',)])
                    for which, dstq in ((0, qh), (1, kh)):
                        ct = which * 4 + h
                        b = nextbank()
                        proj(b, Wm, KW, ct * 128, 128, uT, ('uT',), tok0, nt)
                        cp('act', zc[:, 3:3 + nt], ps(b, 128, nt), r=[('ps', b)], w=[('zc',)])
                        cp('pool', zc[:, 0:3], zch[:, ct, :], r=[('zch', ct)], w=[('zc',)])
                        ts('dve', cacc[:, 0:nt], zc[:, 0:nt], cw[:, ct, 0:1], None, ALU.mult, None, r=[('zc',), K],
                           w=[('gtmp',)])
                        for j in range(1, 4):
                            stt('dve', cacc[:, 0:nt], zc[:, j:j + nt], cw[:, ct, j:j + 1], cacc[:, 0:nt], ALU.mult,
                                ALU.add, r=[('zc',), K, ('gtmp',)], w=[('gtmp',)])
                        cp('pool', zch[:, ct, :], zc[:, nt:nt + 3], r=[('zc',)], w=[('zch', ct)])
                        act(dstq[:, 0:nt], cacc[:, 0:nt], AF.Silu, r=[('gtmp',), K], w=[('qk', which)],
                            bias=cb[:, ct:ct + 1])
                    prep_head(nt, hi, qh[:, 0:nt], [('qk', 0)], kh[:, 0:nt], [('qk', 1)], 1.0, 128 ** -0.5, B)
                obs = [nextbank(), nextbank()]
                dbs = [nextbank(), nextbank()]
                chunk_pair(nt, pair, vtok, 0, [Ct32, Nt32], [Ctbf, Ntbf], True, obs, dbs, B)
                for hi, h in enumerate(pair):
                    ob, db = obs[hi], dbs[hi]
                    ts('dve', dabs[:, 0:nt], psum[:, db, 0:nt], -1.0, 1.0, ALU.mult, ALU.max, r=[('ps', db)],
                       w=[('E1',)])
                    tt('dve', dabs[:, 0:nt], dabs[:, 0:nt], psum[:, db, 0:nt], ALU.max, r=[('ps', db), ('E1',)],
                       w=[('E1',)])
                    recip(dabs[:, 0:nt], dabs[:, 0:nt], r=[('E1',)], w=[('E1',)])
                    tt('dve', t1[:, 0:nt], psum[:, ob, 0:nt], dabs[:, 0:nt], ALU.mult, r=[('ps', ob), ('E1',)],
                       w=[('E2',)])
                    b = nextbank()
                    proj(b, Wm, KW, 1544 + h * 128, 128, uT, ('uT',), tok0, nt)
                    act(so[:, 0:nt], ps(b, 128, nt), AF.Sigmoid, r=[('ps', b)], w=[('E3',)])
                    tt('dve', ogT[:, h, 0:nt], t1[:, 0:nt], so[:, 0:nt], ALU.mult, r=[('E2',), ('E3',)], w=[('ogT',)])
            merge_out(tok0, nt, ogT, ('ogT',), Wpr, K, Wm, KW, 2056, False, gsig, gtmp)

    def phase_s5(L):
        arena[0] = PH0
        TB = 128
        NCH = TB // 8
        Ws = aalloc('Ws', [128, 8, 1536], BF16)
        Wglu = aalloc('Wglu', [128, 4, 512], BF16)
        Wsp = aalloc('Wsp', [128, 4, D], BF16)
        bglu = aalloc('bglu', [128, 4], F32)
        su128 = aalloc('su128', [128, 4, 512], BF16)
        sup2 = [aalloc('sup', [32, 16, TB], BF16) for _ in range(2)]
        GU = aalloc('GU', [128, 2, 16, TB], F32)
        XC = aalloc('XC', [128, 2, 16, NCH + 1], F32)
        XXbf = aalloc('XXbf', [128, 2, 16, TB], BF16)
        XCbf = aalloc('XCbf', [128, 2, 16, NCH], BF16)
        P1 = aalloc('P1', [128, 2, 16, NCH], F32); P2 = aalloc('P2', [128, 2, 16, NCH], F32)
        P1s = aalloc('P1s', [128, 2, 16], F32); P2s = aalloc('P2s', [128, 2, 16], F32)
        yp = aalloc('yp', [32, 16, TB], BF16)
        yv = aalloc('yv', [128, 512], F32)
        ga = aalloc('ga', [128, 512], F32)
        yg = aalloc('yg', [128, 4, 512], BF16)
        gsig = aalloc('gsig', [128, 512], F32); gtmp = aalloc('gtmp', [128, 512], F32)
        gb = gtmp
        K = ('W', 's')
        KC = ('s5c', L)
        y2 = yg
        KW = WK(('Wc', 's'))
        load_wc(Ws, KW, 'w_in', L, 8, 0, O_SU, 512)
        load_wc(Ws, KW, 'w_in', L, 8, 512, O_GATE + 1024, 1024)
        load_w(Wglu[:, :, :], K, 's5_w_glu', L, 0, 4, 0, 512)
        load_w(Wsp[:, :, :], K, 's5_w_proj', L, 0, 4, 0, D)
        vec_load(bglu[:, :], 's5_b_glu', L, [[1, 128], [128, 4]], K)
        cp('dve', XC[:, :, :, 0], XXst[:, :, :], r=[('xxst',)], w=[('XC',)])
        GU5 = GU[:, :, :, :].rearrange('p r g (l n) -> p r g l n', l=8)

        def G(tau):
            return GU5[:, :, :, tau, :]

        def cmul_acc(dst, src, k, dkey, skey):
            prb = PW[:, k, 0, :].unsqueeze(1).unsqueeze(3).to_broadcast([128, 2, 16, NCH])
            pib = PW[:, k, 1, :].unsqueeze(2).to_broadcast([128, 16, NCH])
            npib = nPWi[:, k, :].unsqueeze(2).to_broadcast([128, 16, NCH])
            tt('dve', P1[:, :, :, :], src, prb, ALU.mult, r=[skey, KC], w=[('P1',)])
            tt(P2ENG, P2[:, 0, :, :], src[:, 1], npib, ALU.mult, r=[skey, KC], w=[('P2',)])
            tt(P2ENG, P2[:, 1, :, :], src[:, 0], pib, ALU.mult, r=[skey, KC], w=[('P2',)])
            tt('dve', dst, dst, P1[:, :, :, :], ALU.add, r=[dkey, ('P1',)], w=[dkey])
            tt('dve', dst, dst, P2[:, :, :, :], ALU.add, r=[dkey, ('P2',)], w=[dkey])

        for tok0 in range(0, SEGT, TILE):
            nt = TILE
            for blk in range(4):
                b = nextbank()
                proj(b, Ws, KW, blk * 128, 128, uT, ('uT',), tok0, nt)
                cp('act', su128[:, blk, 0:nt], ps(b, 128, nt), r=[('ps', b)], w=[('su128',)])
            def front(tb, si):
                supb = sup2[si]
                for blk in range(4):
                    b = nextbank()
                    for k in range(4):
                        mm(psum[0:32, b, k * TB:(k + 1) * TB], ident[:, 32 * k:32 * k + 32],
                           su128[:, blk, tb:tb + TB].rearrange('p (n l) -> p l n', l=8), True, True,
                           r=[('c', 'ident'), ('su128',)], w=[('ps', b)])
                    cp('act', supb[:, blk * 4:(blk + 1) * 4, :], ps(b, 32, 4 * TB).rearrange('p (k t) -> p k t', k=4),
                       r=[('ps', b)], w=[('sup', si)])
                for ri in range(2):
                    for g in range(4):
                        b = nextbank()
                        for k in range(4):
                            pr = g * 4 + k
                            mm(psum[:, b, k * TB:(k + 1) * TB], BBT[:, pr, ri, :], supb[:, pr, :], True, True,
                               r=[KC, ('sup', si)], w=[('ps', b)])
                        cp('act', GU[:, ri, g * 4:(g + 1) * 4, :], ps(b, 128, 4 * TB).rearrange('p (k t) -> p k t', k=4),
                           r=[('ps', b)], w=[('GU',)])

            def mid(tb):
                for tau in range(1, 8):
                    cmul_acc(G(tau), G(tau - 1), 0, ('GU',), ('GU',))
                cp('act', XXbf[:, :, :, :], GU[:, :, :, :], r=[('GU',)], w=[('XXbf',)])
                for n in range(NCH):
                    tt('dve', P1s[:, :, :], XC[:, :, :, n], AA8[:, :, :], ALU.mult, r=[('XC',), KC], w=[('P1s',)])
                    tt('dve', P2s[:, 0, :], XC[:, 1, :, n], nPWi[:, 7, :], ALU.mult, r=[('XC',), KC], w=[('P2s',)])
                    tt('dve', P2s[:, 1, :], XC[:, 0, :, n], PW[:, 7, 1, :], ALU.mult, r=[('XC',), KC], w=[('P2s',)])
                    tt('dve', P1s[:, :, :], P1s[:, :, :], P2s[:, :, :], ALU.add, r=[('P1s',), ('P2s',)], w=[('P1s',)])
                    tt('dve', XC[:, :, :, n + 1], P1s[:, :, :], GU5[:, :, :, 7, n], ALU.add, r=[('P1s',), ('GU',)],
                       w=[('XC',)])
                cp('act', XCbf[:, :, :, :], XC[:, :, :, 0:NCH], r=[('XC',)], w=[('XCbf',)])
                cp('dve', XC[:, :, :, 0], XC[:, :, :, NCH], r=[('XC',)], w=[('XC',)])

            ybanks = {}

            def tail_y(tb):
                bl = []
                for g in range(4):
                    b = nextbank()
                    bl.append(b)
                    for k in range(4):
                        pr = g * 4 + k
                        mm(psum[0:32, b, k * TB:(k + 1) * TB], Ec[:, 0, pr, :], XXbf[:, 0, pr, :], True, False,
                           r=[KC, ('XXbf',)], w=[('ps', b)])
                        mm(psum[0:32, b, k * TB:(k + 1) * TB], Ec[:, 1, pr, :], XXbf[:, 1, pr, :], False, False,
                           r=[KC, ('XXbf',)], w=[('ps', b)])
                        for tau in range(8):
                            for ri in range(2):
                                mm(psum[0:32, b, k * TB + tau * NCH:k * TB + (tau + 1) * NCH], Et[ri][:, tau, pr, :],
                                   XCbf[:, ri, pr, :], False, tau == 7 and ri == 1, r=[KC, ('XCbf',)], w=[('ps', b)])
                ybanks[tb] = bl

            def tail_rest(tb, si):
                supb = sup2[si]
                for g in range(4):
                    b = ybanks[tb][g]
                    for k in range(4):
                        pr = g * 4 + k
                        stt('dve', yp[:, pr, :], supb[:, pr, :], Dd[:, pr:pr + 1], psum[0:32, b, k * TB:(k + 1) * TB],
                            ALU.mult, ALU.add, r=[('sup', si), KC, ('ps', b)], w=[('yp',)])
                b = nextbank()
                for blk in range(4):
                    for k in range(4):
                        mm(psum[:, b, blk * TB:(blk + 1) * TB], gsel[:, k, :],
                           yp[:, blk * 4 + k, :].rearrange('p (l n) -> p n l', l=8), k == 0, k == 3,
                           r=[('c', 'gsel'), ('yp',)], w=[('ps', b)])
                cp('act', yv[:, :], ps(b, 128, 4 * TB), r=[('ps', b)], w=[('yv',)])
                act(ga[:, :], yv[:, :], AF.Square, r=[('yv',)], w=[('ga',)])
                ts('pool', ga[:, :], ga[:, :], 0.044715, 1.0, ALU.mult, ALU.add, r=[('ga',)], w=[('ga',)])
                tt('pool', ga[:, :], ga[:, :], yv[:, :], ALU.mult, r=[('ga',), ('yv',)], w=[('ga',)])
                act(gb[:, :], ga[:, :], AF.Sigmoid, r=[('ga',)], w=[('gtmp',)], scale=GC)
                tt('pool', yg[:, :, tb:tb + TB], gb[:, :].rearrange('p (k t) -> p k t', k=4),
                   yv[:, :].rearrange('p (k t) -> p k t', k=4), ALU.mult, r=[('gtmp',), ('yv',)], w=[('yg',)])

            tbs = list(range(0, nt, TB))
            front(tbs[0], 0)
            mid(tbs[0])
            for bi_, tb in enumerate(tbs):
                if bi_ + 1 < len(tbs):
                    front(tbs[bi_ + 1], (bi_ + 1) % 2)
                tail_y(tb)
                if bi_ + 1 < len(tbs):
                    mid(tbs[bi_ + 1])
                tail_rest(tb, bi_ % 2)
            gbanks = []
            for co in range(4):
                b = nextbank()
                gbanks.append(b)
                for k in range(4):
                    mm(ps(b, 128, nt), Wglu[:, k, co * 128:(co + 1) * 128], yg[:, k, 0:nt], k == 0, k == 3,
                       r=[K, ('yg',)], w=[('ps', b)])
            for co in range(4):
                b = gbanks[co]
                act(gsig[:, 0:nt], ps(b, 128, nt), AF.Sigmoid, r=[('ps', b), K], w=[('gsig',)], bias=bglu[:, co:co + 1])
                tt('dve', y2[:, co, 0:nt], yg[:, co, 0:nt], gsig[:, 0:nt], ALU.mult, r=[('yg',), ('gsig',)],
                   w=[('yg',)])
            merge_out(tok0, nt, y2, ('yg',), Wsp, K, Ws, KW, 512, False, gsig, gtmp)
        cp('dve', XXst[:, :, :], XC[:, :, :, 0], r=[('XC',)], w=[('xxst',)])

    def phase_out(L, src_t, seg):
        arena[0] = PH0
        Wo = aalloc('Wo', [128, 8, D], BF16)
        t1 = aalloc('t1o', [128, D], F32)
        K = ('W', 'o')
        load_w(Wo[:, :, :], K, 'w_out', L, 0, 8, 0, D)
        dma(gainb[:, :], rawap(W['norm_mix_post'], L * D, [[0, 128], [1, D]]), w=[('c', 'gain')])
        junk_ref[0] = aalloc('junkO', [128, D], BF16)
        xbo = [aalloc('xbo', [128, D], F32) for _ in range(2)]
        if L == 0 and seg == 0:
            dump('mgall', merged[:, :, :], [('mg',)])
        for bi, tb in enumerate(range(0, SEGT, 128)):
            xi = bi % 2
            xb = xbo[xi]
            dma(xb[:, :], rawap(src_t, (seg * SEGT + tb) * D, [[D, 128], [1, D]]), w=[('xbo', xi)])
            banks = []
            for hf in range(2):
                b = nextbank()
                banks.append(b)
                for k in range(8):
                    mm(ps(b, 128, 512), merged[:, k, tb:tb + 128], Wo[:, k, hf * 512:(hf + 1) * 512], k == 0, k == 7,
                       r=[('mg',), K], w=[('ps', b)])
            rowsum_rstd(banks, 128, 8)
            for hf, b in enumerate(banks):
                tt('dve', t1[:, hf * 512:(hf + 1) * 512], ps(b, 128, 512), gains[1][:, hf * 512:(hf + 1) * 512], ALU.mult,
                   r=[('ps', b), ('c', 'gain')], w=[('t1o', hf)])
                stt('dve', xb[:, hf * 512:(hf + 1) * 512], t1[:, hf * 512:(hf + 1) * 512], small[:, 8:9],
                    xb[:, hf * 512:(hf + 1) * 512], ALU.mult, ALU.add, r=[('t1o', hf), ('rsx',), ('xbo', xi)],
                    w=[('xbo', xi)])
            if L == 0 and seg == 0 and tb == 128:
                dump('osmall', small[:, 0:16], [('rsx',), ('ssx', 0), ('ssx', 1), ('ssx', 2), ('ssx', 3)])
                dump('ot1', t1[:, :], [('t1o', 0), ('t1o', 1)])
                dump('ojunk', junk_ref[0][:, :], [('junk',)])
            dma(rawap(xmid, tb * D, [[D, 128], [1, D]]), xb[:, :], r=[('xbo', xi)], w=[('xmid', tb)])
            if L == 0 and seg == 0:
                dump('xmid%d' % tb, xb[:, :], [('xbo', xi)])

    hs = nc.dram_tensor('hs', [22, 128, SEGT], BF16)

    def phase_ffn_a(L, seg):
        arena[0] = UT0
        Wup = aalloc('Wup', [128, 8, DFF], BF16)
        Wga = aalloc('Wga', [128, 8, DFF], BF16)
        NTF = 512
        uF = [aalloc('uF', [128, 8, NTF], BF16) for _ in range(2)]
        NB = 3
        fsets = [(aalloc('ab', [128, NTF + 2], F32), aalloc('vv', [128, NTF], F32), aalloc('gaF', [128, NTF], F32),
                  aalloc('vg', [128, NTF], F32), aalloc('ho', [128, NTF], BF16)) for _ in range(NB)]
        fw = aalloc('fw', [128, 22, 3], F32); fb = aalloc('fb', [128, 22], F32)
        nsets = norm_sets(2)
        nbi = [0]
        K = ('W', 'f')
        KU = WK(('Wc', 'fu'), 704)
        KG = WK(('Wc', 'fg'), 704)
        for q4 in range(4):
            load_wc(Wup, KU, 'ffn_w_up', L, 8, q4 * 704, q4 * 704, 704)
            load_wc(Wga, KG, 'ffn_w_gate', L, 8, q4 * 704, q4 * 704, 704)
        for j in range(3):
            vec_load(fw[:, :, j], 'ffn_conv_w', L, [[1, 128], [128, 22]], K, off=j * DFF)
        vec_load(fb[:, :], 'ffn_conv_b', L, [[1, 128], [128, 22]], K)
        dma(gains2[0][:, :], rawap(W['norm_ffn_pre'], L * D, [[0, 128], [1, D]]), w=[('c', 'gain')])
        gi = 0
        def ffn_norm(ti):
            t0_ = ti * NTF
            for j in range(NTF // 128):
                tb = t0_ + j * 128
                norm_block(rawap(xmid, tb * D, [[D, 128], [1, D]]), 128, gains[2], ('c', 'gain'), uF[ti % 2],
                           ('uF', ti % 2), j * 128, 0, extra_r=[('xmid', tb)], bufs=nsets[nbi[0] % 2])
                nbi[0] += 1

        ntile = SEGT // NTF
        ffn_norm(0)
        for ti, t0 in enumerate(range(0, SEGT, NTF)):
            uFt = uF[ti % 2]
            ukey = ('uF', ti % 2)
            for ct in range(22):
                if ct == 8 and ti + 1 < ntile:
                    ffn_norm(ti + 1)
                si = gi % NB
                gi += 1
                ab, vv, ga, vg, ho = fsets[si]
                b = nextbank()
                proj(b, Wup, KU, ct * 128, 128, uFt, ukey, 0, NTF)
                cp('act', ab[:, 2:2 + NTF], ps(b, 128, NTF), r=[('ps', b)], w=[('ab', si)])
                cp('pool', ab[:, 0:2], ahist[:, ct, :], r=[('ah', ct)], w=[('ab', si)])
                ts('dve', vv[:, :], ab[:, 0:NTF], fw[:, ct, 0:1], fb[:, ct:ct + 1], ALU.mult, ALU.add,
                   r=[('ab', si), K], w=[('vv', si)])
                for j in range(1, 3):
                    stt('dve', vv[:, :], ab[:, j:j + NTF], fw[:, ct, j:j + 1], vv[:, :], ALU.mult, ALU.add,
                        r=[('ab', si), K, ('vv', si)], w=[('vv', si)])
                cp('pool', ahist[:, ct, :], ab[:, NTF:NTF + 2], r=[('ab', si)], w=[('ah', ct)])
                b2 = nextbank()
                proj(b2, Wga, KG, ct * 128, 128, uFt, ukey, 0, NTF)
                tt('dve', vg[:, :], vv[:, :], ps(b2, 128, NTF), ALU.mult, r=[('vv', si), ('ps', b2)], w=[('vg', si)])
                act(ga[:, :], vv[:, :], AF.Square, r=[('vv', si)], w=[('ga', si)], scale=math.sqrt(0.044715))
                stt('dve', ga[:, :], ga[:, :], 1.0, vv[:, :], ALU.add, ALU.mult, r=[('ga', si), ('vv', si)],
                    w=[('ga', si)])
                act(ga[:, :], ga[:, :], AF.Sigmoid, r=[('ga', si)], w=[('ga', si)], scale=GC)
                tt('pool', ho[:, :], ga[:, :], vg[:, :], ALU.mult, r=[('ga', si), ('vg', si)], w=[('ho', si)])
                dma(rawap(hs, ct * 128 * SEGT + t0, [[SEGT, 128], [1, NTF]]), ho[:, :], r=[('ho', si)],
                    w=[('hs', ct, t0)])

    def phase_ffn_b(L, dst_t, seg):
        arena[0] = UT0
        Wd = aalloc('Wd', [128, 22, D], BF16)
        hb = [aalloc('hb', [128, 22, 512], BF16) for _ in range(2)]
        t1 = aalloc('t1f', [128, D], F32)
        xb2 = [aalloc('xb2', [128, D], F32) for _ in range(2)]
        K = ('W', 'fd')
        junk_ref[0] = aalloc('junkF', [128, D], BF16)
        KD = WK(('Wc', 'fd'), 512)
        for hf in range(2):
            load_w(Wd[:, :, hf * 512:(hf + 1) * 512], (KD.base, hf), 'ffn_w_down', L, 0, 22, hf * 512, 512)
        dma(gains2[1][:, :], rawap(W['norm_ffn_post'], L * D, [[0, 128], [1, D]]), w=[('c', 'gain')])
        bi = 0
        for ti, t0 in enumerate(range(0, SEGT, 512)):
            hbt = hb[ti % 2]
            hkey = ('hb', ti % 2)
            for c0 in (0, 11):
                dma(hbt[:, c0:c0 + 11, :], rawap(hs, c0 * 128 * SEGT + t0, [[SEGT, 128], [128 * SEGT, 11], [1, 512]]),
                    r=[('hs', ct, t0) for ct in range(c0, c0 + 11)], w=[hkey])
            for j in range(4):
                tb = t0 + j * 128
                xi = bi % 2
                bi += 1
                xb = xb2[xi]
                dma(xb[:, :], rawap(xmid, tb * D, [[D, 128], [1, D]]), r=[('xmid', tb)], w=[('xb2', xi)])
                banks = []
                for hf in range(2):
                    b = nextbank()
                    banks.append(b)
                    for k in range(22):
                        mm(ps(b, 128, 512), hbt[:, k, j * 128:(j + 1) * 128], Wd[:, k, hf * 512:(hf + 1) * 512], k == 0,
                           k == 21, r=[hkey, (KD.base, hf)], w=[('ps', b)])
                rowsum_rstd(banks, 128, 9)
                for hf, b in enumerate(banks):
                    tt('dve', t1[:, hf * 512:(hf + 1) * 512], ps(b, 128, 512), gains[3][:, hf * 512:(hf + 1) * 512],
                       ALU.mult, r=[('ps', b), ('c', 'gain')], w=[('t1f', hf)])
                    stt('dve', xb[:, hf * 512:(hf + 1) * 512], t1[:, hf * 512:(hf + 1) * 512], small[:, 9:10],
                        xb[:, hf * 512:(hf + 1) * 512], ALU.mult, ALU.add, r=[('t1f', hf), ('rsx',), ('xb2', xi)],
                        w=[('xb2', xi)])
                dma(rawap(dst_t, (seg * SEGT + tb) * D, [[D, 128], [1, D]]), xb[:, :], r=[('xb2', xi)],
                    w=[('dst', seg, tb)])

    for L in range(nlayer):
        src_t = x_in if L == 0 else x1
        dst_t = out_t if L == nlayer - 1 else x1
        for stt_ in (St32, Ct32, Nt32):
            memset('pool', stt_[:, :, :], 0.0, [('st', id(stt_), h) for h in range(4)])
        for bf_, s32 in ((Stbf, St32), (Ctbf, Ct32), (Ntbf, Nt32)):
            for h in range(4):
                cp('pool', bf_[:, h, :], s32[:, h, :], r=[('st', id(s32), h)], w=[('stbf', id(bf_), h)])
        memset('pool', XXst[:, :, :], 0.0, [('xxst',)])
        for ct in range(8):
            memset('pool', zch[:, ct, :], 0.0, [('zch', ct)])
        for ct in range(22):
            memset('pool', ahist[:, ct, :], 0.0, [('ah', ct)])
        S.barrier()
        s5_setup(L)
        for seg in range(nseg):
            S.barrier()
            dma(gainb[:, :], rawap(W['norm_mix_pre'], L * D, [[0, 128], [1, D]]), w=[('c', 'gain')])
            arena[0] = PH0
            nsets = norm_sets(3)
            for bi, tb in enumerate(range(0, SEGT, 128)):
                extra = [('dst', seg, tb)] if L > 0 else []
                norm_block(rawap(src_t, (seg * SEGT + tb) * D, [[D, 128], [1, D]]), 128, gains[0], ('c', 'gain'), uT,
                           ('uT',), tb, bi % 2, extra_r=extra, bufs=nsets[bi % 3])
            S.barrier()
            if 'gla' not in skip:
                phase_gla(L)
                S.barrier()
            if 'ml' not in skip:
                phase_ml(L)
                S.barrier()
            if 's5' not in skip:
                phase_s5(L)
                S.barrier()
            if 'out' not in skip:
                phase_out(L, src_t, seg)
                S.barrier()
            if 'ffn' not in skip:
                phase_ffn_a(L, seg)
                S.barrier()
                phase_ffn_b(L, dst_t, seg)
                S.barrier()

    with nc.semaphore('e_pe') as s0, nc.semaphore('e_act') as s1, nc.semaphore('e_dve') as s2, \
            nc.semaphore('e_pool') as s3, nc.semaphore('e_sp') as s4:
        esem = {'pe': s0, 'act': s1, 'dve': s2, 'pool': s3, 'sp': s4}
        import contextlib
        with contextlib.ExitStack() as es:
            ssem = [es.enter_context(nc.semaphore('dslot%d' % i)) for i in range(S.nslots)]
            with nc.Block() as block:
                S.emit(nc, block, esem, ssem)
    build.dbg_names = dbg_names
    build.amax = amax
    build.ph0 = PH0
    return nc


def kernel(**inputs):
    nc = build()
    m = {'x': np.ascontiguousarray(inputs['x'].reshape(SEQ, D), dtype=np.float32)}
    for n in WNAMES:
        m[n] = np.ascontiguousarray(inputs[n], dtype=np.float32)
    res = run_bass_kernel_spmd(nc, [m], core_ids=[0])
    return res.results[0]['out'].reshape(1, SEQ, D).astype(np.float32)
```

```python
import math
import numpy as np
import concourse.bass as bass
import concourse.mybir as mybir
from concourse.bass_utils import run_bass_kernel_spmd

F32 = mybir.dt.float32
BF16 = mybir.dt.bfloat16
ALU = mybir.AluOpType
AF = mybir.ActivationFunctionType

NCORE = 8
D = 1024
DIN = 7704
DFF = 2816
SEQ = 16384
OWN = SEQ // NCORE
HALO = 64
PRE = 3
NT = OWN + HALO
NW = NT + PRE
EPS = 1e-6
O_GQ, O_GK, O_GV, O_LR, O_GR, O_SU, O_MQ, O_MK, O_MV, O_MI, O_MF, O_MO, O_GATE = (
    0, 512, 1024, 1536, 1552, 2064, 2576, 3088, 3600, 4112, 4116, 4120, 4632)
TILES = [(0, 64), (64, 512), (576, 512), (1088, 512), (1600, 512)]
SNAP_TOK = OWN
LS5 = 8
GC = 1.5957691216057308

WNAMES = ['norm_mix_pre', 'norm_mix_post', 'norm_ffn_pre', 'norm_ffn_post', 'w_in',
          'gla_w_gk', 'gla_b_gk', 'gla_norm', 'gla_w_proj',
          's5_a_re', 's5_a_im', 's5_log_dt', 's5_b_re', 's5_b_im', 's5_c_re', 's5_c_im', 's5_d',
          's5_w_glu', 's5_b_glu', 's5_w_proj', 'ml_conv_w', 'ml_conv_b', 'ml_b_i', 'ml_b_f',
          'ml_w_proj', 'w_out', 'ffn_w_up', 'ffn_w_gate', 'ffn_conv_w', 'ffn_conv_b', 'ffn_w_down']
WSHAPES = {'norm_mix_pre': [D], 'norm_mix_post': [D], 'norm_ffn_pre': [D], 'norm_ffn_post': [D],
           'w_in': [D, DIN], 'gla_w_gk': [16, 512], 'gla_b_gk': [512], 'gla_norm': [128],
           'gla_w_proj': [512, D], 's5_a_re': [32, 64], 's5_a_im': [32, 64], 's5_log_dt': [32],
           's5_b_re': [32, 64, 16], 's5_b_im': [32, 64, 16], 's5_c_re': [32, 16, 64],
           's5_c_im': [32, 16, 64], 's5_d': [32, 16], 's5_w_glu': [512, 512], 's5_b_glu': [512],
           's5_w_proj': [512, D], 'ml_conv_w': [4, D], 'ml_conv_b': [D], 'ml_b_i': [4], 'ml_b_f': [4],
           'ml_w_proj': [512, D], 'w_out': [D, D], 'ffn_w_up': [D, DFF], 'ffn_w_gate': [D, DFF],
           'ffn_conv_w': [3, DFF], 'ffn_conv_b': [DFF], 'ffn_w_down': [DFF, D]}


class Sched:
    ENGS = ('pe', 'act', 'dve', 'pool', 'sp')
    SELF_SYNC = ('act', 'dve', 'pool')

    def __init__(self, nslots=16):
        self.streams = {e: [] for e in self.ENGS}
        self.ops = []
        self.lastw = {}
        self.rd = {}
        self.nslots = nslots
        self.slot_cnt = [0] * nslots
        self.slot_last = [None] * nslots
        self.dma_n = 0

    def add(self, eng, fn, r=(), w=(), dma=False):
        oid = len(self.ops)
        deps = set()
        raw = set()
        for k in r:
            p = self.lastw.get(k)
            if p is not None:
                deps.add(p)
                raw.add(p)
        for k in w:
            p = self.lastw.get(k)
            if p is not None:
                deps.add(p)
            rr = self.rd.get(k)
            if rr:
                deps.update(rr[0].values())
                deps.update(rr[1])
        o = {'id': oid, 'eng': eng, 'fn': fn, 'dma': dma, 'flag': False, 'raw': raw}
        if dma:
            s = self.dma_n % self.nslots
            self.dma_n += 1
            if self.slot_last[s] is not None:
                deps.add(self.slot_last[s])
            self.slot_cnt[s] += 1
            o['slot'] = s
            o['slotval'] = 16 * self.slot_cnt[s]
            self.slot_last[s] = oid
        for k in w:
            self.lastw[k] = oid
            self.rd[k] = ({}, [])
        for k in r:
            rr = self.rd.setdefault(k, ({}, []))
            if dma:
                rr[1].append(oid)
            else:
                rr[0][eng] = oid
        deps.discard(oid)
        o['deps'] = deps
        self.ops.append(o)
        self.streams[eng].append(o)
        return o

    def barrier(self):
        last = {}
        for e in self.ENGS:
            st = self.streams[e]
            for o in reversed(st):
                if o['fn'] is not None and not o['dma']:
                    last[e] = o['id']
                    break
        dmas = [x for x in self.slot_last if x is not None]
        for e in self.ENGS:
            oid = len(self.ops)
            deps = set(v for k, v in last.items() if k != e) | set(dmas)
            o = {'id': oid, 'eng': e, 'fn': None, 'dma': False, 'flag': False, 'deps': deps}
            self.ops.append(o)
            self.streams[e].append(o)

    def emit(self, nc, block, esem, ssem):
        ops = self.ops
        for o in ops:
            for d in o['deps']:
                dd = ops[d]
                if dd['dma']:
                    continue
                if dd['eng'] != o['eng'] or (d in o.get('raw', ()) and o['eng'] in self.SELF_SYNC):
                    dd['flag'] = True
        for e in self.ENGS:
            c = 0
            for o in self.streams[e]:
                if o['flag'] and not o['dma']:
                    c += 1
                    o['fidx'] = c
        sched = self

        def run(ename, eng):
            waited = {}
            for o in sched.streams[ename]:
                need = {}
                for d in o['deps']:
                    dd = ops[d]
                    if dd['dma']:
                        key = ('s', dd['slot'])
                        val = dd['slotval']
                    elif dd['eng'] == ename and not (d in o.get('raw', ()) and ename in sched.SELF_SYNC):
                        continue
                    else:
                        key = ('e', dd['eng'])
                        val = dd['fidx']
                    if val > waited.get(key, 0) and val > need.get(key, 0):
                        need[key] = val
                for key, val in need.items():
                    sem = ssem[key[1]] if key[0] == 's' else esem[key[1]]
                    eng.wait_ge(sem, val)
                    waited[key] = val
                if o['fn'] is None:
                    continue
                ins = o['fn'](eng)
                if o['dma']:
                    ins.then_inc(ssem[o['slot']], 16)
                elif o['flag']:
                    ins.then_inc(esem[ename], 1)
            if ename == 'sp':
                for s in range(sched.nslots):
                    if sched.slot_cnt[s]:
                        eng.wait_ge(ssem[s], 16 * sched.slot_cnt[s])

        @block.tensor
        def _(e):
            run('pe', e)

        @block.scalar
        def _(e):
            run('act', e)

        @block.vector
        def _(e):
            run('dve', e)

        @block.gpsimd
        def _(e):
            run('pool', e)

        @block.sync
        def _(e):
            run('sp', e)


def rawap(t, offset, pat):
    return bass.AP(tensor=t, offset=offset, ap=[list(p) for p in pat])


SEGT = 2048
DBGT = 512
BAR = False
P2ENG = 'dve'
TILE = 512
PI = math.pi


def build(nseg=8, nlayer=2, dbg=False, skip=()):
    nc = bass.Bass("TRN2", target_bir_lowering=False)
    S = Sched()
    ntok = nseg * SEGT
    x_in = nc.dram_tensor('x', [ntok, D], F32, kind='ExternalInput')
    W = {n: nc.dram_tensor(n, [2] + WSHAPES[n], F32, kind='ExternalInput') for n in WNAMES}
    WSZ = {n: int(np.prod(WSHAPES[n])) for n in WNAMES}
    out_t = nc.dram_tensor('out', [ntok, D], F32, kind='ExternalOutput')
    x1 = nc.dram_tensor('x1', [ntok, D], F32)
    xmid = nc.dram_tensor('xmid', [SEGT, D], F32)

    sb_top = [16512]
    SB_END = 229376

    def alloc(name, shape, dt, at=None):
        nbytes = int(np.prod(shape[1:])) * (4 if dt == F32 else 2)
        nbytes = (nbytes + 63) // 64 * 64
        if at is None:
            off = sb_top[0]
            sb_top[0] += nbytes
        else:
            off = at
        assert off + nbytes <= SB_END, (name, off, nbytes)
        return nc.alloc_sbuf_tensor_at(name, list(shape), dt, offset=off)

    psum = nc.alloc_psum_tensor('psum', [128, 8, 512], F32)

    def ps(b, p=128, n=512):
        return psum[0:p, b, 0:n]

    bank_rr = [0]

    def nextbank():
        b = bank_rr[0] % 5
        bank_rr[0] += 1
        return b

    ident = alloc('ident', [128, 128], BF16)
    onesbf = alloc('onesbf', [128, 128], BF16)
    ones32 = alloc('ones32', [128, 512], F32)
    maskut = alloc('maskut', [128, 64], F32)
    eps_t = alloc('eps_t', [128, 1], F32)
    negpi = alloc('negpi', [128, 1], F32)
    gainb = alloc('gainb', [128, D], F32)
    gains2 = [gainb, gainb]
    gains = [gainb, gainb, gainb, gainb]
    junk_ref = [None]
    small = alloc('small', [128, 64], F32)
    stage0 = alloc('stage0', [128, 512], F32)
    stage = [stage0, stage0]
    St32 = alloc('St32', [128, 4, 128], F32)
    Ct32 = alloc('Ct32', [128, 4, 128], F32)
    Nt32 = alloc('Nt32', [128, 4, 128], F32)
    Stbf = alloc('Stbf', [128, 4, 128], BF16)
    Ctbf = alloc('Ctbf', [128, 4, 128], BF16)
    Ntbf = alloc('Ntbf', [128, 4, 128], BF16)
    XXst = alloc('XXst', [128, 2, 16], F32)
    zch = alloc('zch', [128, 8, 3], F32)
    ahist = alloc('ahist', [128, 22, 2], F32)
    BBT = alloc('BBT', [32, 16, 2, 128], BF16)
    Ec = alloc('Ec', [128, 2, 16, 32], BF16)
    AA1 = alloc('AA1', [128, 2, 16], F32)
    A1i = alloc('A1i', [128, 16], F32)
    nA1i = alloc('nA1i', [128, 16], F32)
    Dd = alloc('Dd', [32, 16], F32)
    PW = alloc('PW', [128, 8, 2, 16], F32)
    nPWi = alloc('nPWi', [128, 8, 16], F32)
    AA8 = alloc('AA8', [128, 2, 16], F32)
    gsel = alloc('gsel', [32, 4, 128], BF16)
    Et = [alloc('Et%d' % i, [128, 8, 16, 32], BF16) for i in range(2)]
    sel_ref = [None, None]
    UT0 = sb_top[0]
    uT = alloc('uT', [128, 8, SEGT], BF16)
    merged = alloc('merged', [128, 8, SEGT], BF16)
    PH0 = sb_top[0]
    arena = [PH0]
    an = [0]

    amax = {}

    def aalloc(name, shape, dt):
        an[0] += 1
        t = alloc('%s_%d' % (name, an[0]), shape, dt, at=arena[0])
        nb = int(np.prod(shape[1:])) * (4 if dt == F32 else 2)
        arena[0] += (nb + 63) // 64 * 64
        amax['top'] = max(amax.get('top', 0), arena[0])
        return t

    cnt = {'stage': 0}
    dumped = {}

    def dma(out, in_, r=(), w=(), slow=False):
        if slow:
            return S.add('sp', lambda e: e.dma_start(out=out, in_=in_, allow_slow_non_contiguous=True),
                         r=r, w=w, dma=True)
        return S.add('sp', lambda e: e.dma_start(out=out, in_=in_), r=r, w=w, dma=True)

    def act(out, in_, func, r, w, bias=None, scale=None, accum=None):
        kw = {}
        if bias is not None:
            kw['bias'] = bias
        if scale is not None:
            kw['scale'] = scale
        if accum is not None:
            kw['accum_out'] = accum
        return S.add('act', lambda e: e.activation(out=out, in_=in_, func=func, **kw), r=r, w=w)

    def tt(eng, out, in0, in1, op, r, w):
        return S.add(eng, lambda e: e.tensor_tensor(out=out, in0=in0, in1=in1, op=op), r=r, w=w)

    def ts(eng, out, in0, s1, s2, op0, op1, r, w):
        if s2 is None:
            return S.add(eng, lambda e: e.tensor_single_scalar(out=out, in_=in0, scalar=s1, op=op0), r=r, w=w)
        return S.add(eng, lambda e: e.tensor_scalar(out=out, in0=in0, scalar1=s1, scalar2=s2, op0=op0, op1=op1),
                     r=r, w=w)

    def stt(eng, out, in0, sc, in1, op0, op1, r, w):
        return S.add(eng, lambda e: e.scalar_tensor_tensor(out=out, in0=in0, scalar=sc, in1=in1, op0=op0, op1=op1),
                     r=r, w=w)

    def cp(eng, out, in_, r, w):
        if eng == 'act':
            return S.add('act', lambda e: e.copy(out=out, in_=in_), r=r, w=w)
        return S.add(eng, lambda e: e.tensor_copy(out=out, in_=in_), r=r, w=w)

    def mm(out, lhsT, rhs, start, stop, r, w):
        return S.add('pe', lambda e: e.matmul(out, lhsT, rhs, start=start, stop=stop, skip_group_check=True),
                     r=r, w=w)

    def tr(out, in_, idn, r, w):
        return S.add('pe', lambda e: e.transpose(out, in_, idn), r=r, w=w)

    def memset(eng, ap, val, w):
        return S.add(eng, lambda e: e.memset(ap, val), r=(), w=w)

    dbg_names = []

    def dump(name, ap, rkeys):
        if not dbg:
            return
        t = nc.dram_tensor('dbg_' + name, list(ap.shape), ap.dtype, kind='ExternalOutput')
        dbg_names.append('dbg_' + name)
        full = t.ap()
        dma(full, ap, r=rkeys, w=[('dbg', name)])

    scr = nc.dram_tensor('scr_bar', [64, 64], F32)
    barn = [0]

    def lbar(items):
        for ap_, key in items:
            i = barn[0] % 64
            barn[0] += 1
            if ap_.dtype != F32:
                continue
            dma(scr.ap()[i:i + 1, 0:1], ap_, r=[key], w=[('scr', i)])

    def recip(out, in_, r, w):
        return S.add('dve', lambda e: e.reciprocal(out=out, in_=in_), r=r, w=w)

    def load_w(dst, key, name, L, row0, kt, col0, ncols, rows=128):
        src_t = W[name]
        ld = WSHAPES[name][1] if len(WSHAPES[name]) > 1 else 1
        base = L * WSZ[name]
        kstep = max(1, 8 if kt <= 8 else 11)
        for k0 in range(0, kt, kstep):
            kn = min(kstep, kt - k0)
            src = rawap(src_t, base + (row0 + k0 * 128) * ld + col0, [[ld, rows], [128 * ld, kn], [1, ncols]])
            dd = dst[:, k0:k0 + kn, :]
            S.add('pool', lambda e, dd=dd, src=src: e.dma_start(out=dd, in_=src), r=(), w=[key], dma=True)

    class WK:
        def __init__(self, base, chunk=512):
            self.base = base
            self.chunk = chunk

        def keys(self, c0, n):
            return [(self.base, j) for j in range(c0 // self.chunk, (c0 + n - 1) // self.chunk + 1)]

    def wkeys(wkey, c0, n):
        return wkey.keys(c0, n) if isinstance(wkey, WK) else [wkey]

    def load_wc(dst_t, wk, name, L, kt, dcol0, scol0, ncols):
        c = dcol0
        end = dcol0 + ncols
        while c < end:
            ce = min(end, (c // wk.chunk + 1) * wk.chunk)
            load_w(dst_t[:, :, c:ce], (wk.base, c // wk.chunk), name, L, 0, kt, scol0 + (c - dcol0), ce - c)
            c = ce

    def vec_load(dst, name, L, pat, key, off=0, slow=True):
        dma(dst, rawap(W[name], L * WSZ[name] + off, pat), w=[key], slow=slow)

    memset('pool', ones32[:, :], 1.0, [('c', 'ones32')])
    memset('pool', eps_t[:, :], EPS, [('c', 'eps')])
    memset('pool', negpi[:, :], -PI, [('c', 'negpi')])
    memset('pool', maskut[0:64, :], 1.0, [('c', 'mask')])
    S.add('pool', lambda e: e.affine_select(out=maskut[0:64, :], in_=maskut[0:64, :], pattern=[[1, 64]],
                                            compare_op=ALU.is_ge, fill=0.0, base=0, channel_multiplier=-1),
          r=[('c', 'mask')], w=[('c', 'mask')])
    cp('pool', onesbf[:, :], ones32[:, 0:128], r=[('c', 'ones32')], w=[('c', 'onesbf')])
    dma(maskut[64:128, :], maskut[0:64, :], r=[('c', 'mask')], w=[('c', 'mask')])
    identf = stage[1]
    memset('pool', identf[:, 0:128], 1.0, [('stage', 1)])
    S.add('pool', lambda e: e.affine_select(out=identf[:, 0:128], in_=identf[:, 0:128], pattern=[[1, 128]],
                                            compare_op=ALU.is_equal, fill=0.0, base=0, channel_multiplier=-1),
          r=[('stage', 1)], w=[('stage', 1)])
    cp('pool', ident[:, :], identf[:, 0:128], r=[('stage', 1)], w=[('c', 'ident')])
    for k in range(4):
        memset('pool', identf[0:32, 128:256], 1.0, [('stage', 1)])
        S.add('pool', lambda e, k=k: e.affine_select(out=identf[0:32, 128:256], in_=identf[0:32, 128:256],
                                                     pattern=[[1, 128]], compare_op=ALU.is_equal, fill=0.0,
                                                     base=-32 * k, channel_multiplier=-1),
              r=[('stage', 1)], w=[('stage', 1)])
        cp('pool', gsel[:, k, :], identf[0:32, 128:256], r=[('stage', 1)], w=[('c', 'gsel')])
    def norm_block(src_ap, nb, gain_t, gkey, dstT, dkey, dcol0, xi, extra_r=(), bufs=None):
        xb, un_, junk_, sc, sid = bufs
        kx, ku, kj = ('nbx', sid), ('nbu', sid), ('nbj', sid)
        b7 = nextbank()
        dma(xb[0:nb, :], src_ap, r=extra_r, w=[kx])
        act(junk_[0:nb, :], xb[0:nb, :], AF.Square, r=[kx], w=[kj])
        S.add('dve', lambda e: e.reduce_sum(out=small[0:nb, sc:sc + 1], in_=junk_[0:nb, :], axis=mybir.AxisListType.X),
              r=[kj], w=[('ss', sid)])
        act(small[0:nb, sc + 1:sc + 2], small[0:nb, sc:sc + 1], AF.Sqrt, r=[('ss', sid), ('c', 'eps')], w=[('sd', sid)],
            bias=eps_t[0:nb, :], scale=1.0 / D)
        recip(small[0:nb, sc + 2:sc + 3], small[0:nb, sc + 1:sc + 2], r=[('sd', sid)], w=[('rs', sid)])
        stt('dve', un_[0:nb, :], xb[0:nb, :], small[0:nb, sc + 2:sc + 3], gain_t[0:nb, :], ALU.mult, ALU.mult,
            r=[kx, ('rs', sid), gkey], w=[ku])
        pbf = psum[:, b7, :].bitcast(BF16)
        for k in range(8):
            tr(pbf[:, k * 128:k * 128 + nb], un_[0:nb, k * 128:(k + 1) * 128], ident[0:nb, 0:nb],
               r=[ku, ('c', 'ident')], w=[('ps', b7)])
        cp('act', dstT[:, :, dcol0:dcol0 + nb], pbf.rearrange('p (k t) -> p k t', k=8)[:, :, 0:nb], r=[('ps', b7)],
           w=[dkey])

    def norm_sets(n):
        out = []
        for i in range(n):
            out.append((aalloc('nbx', [128, D], F32), aalloc('nbu', [128, D], BF16), aalloc('nbj', [128, D], BF16),
                        20 + 3 * i, i))
        return out

    def rowsum_rstd(banks, nb, col):
        for hf, b in enumerate(banks):
            junk = junk_ref[0]
            act(junk[0:nb, hf * 512:(hf + 1) * 512], ps(b, nb, 512), AF.Square, r=[('ps', b)], w=[('junk',)])
            S.add('dve', lambda e, hf=hf: e.reduce_sum(out=small[0:nb, 4 + hf:5 + hf],
                                                       in_=junk[0:nb, hf * 512:(hf + 1) * 512],
                                                       axis=mybir.AxisListType.X), r=[('junk',)], w=[('ssx', hf)])
        tt('dve', small[0:nb, 6:7], small[0:nb, 4:5], small[0:nb, 5:6], ALU.add, r=[('ssx', 0), ('ssx', 1)],
           w=[('ssx', 2)])
        act(small[0:nb, 7:8], small[0:nb, 6:7], AF.Sqrt, r=[('ssx', 2), ('c', 'eps')], w=[('ssx', 3)],
            bias=eps_t[0:nb, :], scale=1.0 / D)
        recip(small[0:nb, col:col + 1], small[0:nb, 7:8], r=[('ssx', 3)], w=[('rsx',)])

    def proj(bank, wt, wkey, c0, M, src, skey, tok0, nt, kt=8):
        for k in range(kt):
            mm(ps(bank, M, nt), wt[:, k, c0:c0 + M], src[:, k, tok0:tok0 + nt], k == 0, k == kt - 1,
               r=wkeys(wkey, c0, M) + [skey], w=[('ps', bank)])

    def gelu(dst, v, n, vkey, dkey, tmpa, tmpb, P=128):
        act(tmpa[0:P, 0:n], v, AF.Square, r=[vkey], w=[('ga',)])
        ts('dve', tmpa[0:P, 0:n], tmpa[0:P, 0:n], 0.044715, 1.0, ALU.mult, ALU.add, r=[('ga',)], w=[('ga',)])
        tt('dve', tmpa[0:P, 0:n], tmpa[0:P, 0:n], v, ALU.mult, r=[('ga',), vkey], w=[('ga',)])
        act(tmpb[0:P, 0:n], tmpa[0:P, 0:n], AF.Sigmoid, r=[('ga',)], w=[('gb',)], scale=GC)
        tt('pool', dst, tmpb[0:P, 0:n], v, ALU.mult, r=[('gb',), vkey], w=[dkey])

    def s5_setup(L):
        arena[0] = PH0
        aR = aalloc('aR', [128, 16], F32); aI = aalloc('aI', [128, 16], F32); ldt = aalloc('ldt', [128, 16], F32)
        t = [aalloc('s5t', [128, 16], F32) for _ in range(12)]
        bR = aalloc('bR', [128, 16, 16], F32); bI = aalloc('bI', [128, 16, 16], F32)
        cR = aalloc('cR', [128, 16, 16], F32); cI = aalloc('cI', [128, 16, 16], F32)
        u1 = aalloc('u1', [128, 16, 16], F32); u2 = aalloc('u2', [128, 16, 16], F32)
        bsrc = [aalloc('bsrc', [128, 16, 32], BF16) for _ in range(2)]
        K = ('s5c', L)
        vec_load(aR[:, :], 's5_a_re', L, [[1, 128], [128, 16]], K)
        vec_load(aI[:, :], 's5_a_im', L, [[1, 128], [128, 16]], K)
        for e in range(2):
            vec_load(ldt[e * 64:(e + 1) * 64, :], 's5_log_dt', L, [[0, 64], [2, 16]], K, off=e)
        vec_load(bR[:, :, :], 's5_b_re', L, [[16, 128], [2048, 16], [1, 16]], K, slow=False)
        vec_load(bI[:, :, :], 's5_b_im', L, [[16, 128], [2048, 16], [1, 16]], K, slow=False)
        for e in range(2):
            for pr in range(16):
                vec_load(cR[e * 64:(e + 1) * 64, pr, :], 's5_c_re', L, [[1, 64], [64, 16]], K, off=e * 1024 + pr * 2048)
                vec_load(cI[e * 64:(e + 1) * 64, pr, :], 's5_c_im', L, [[1, 64], [64, 16]], K, off=e * 1024 + pr * 2048)
        vec_load(Dd[:, :], 's5_d', L, [[1, 32], [32, 16]], K)
        R = [K]
        dt_, dre, dim, mag, sn, cs, A1r, den, zr, fre, fim, tmp = t
        act(dt_[:, :], ldt[:, :], AF.Exp, r=R, w=R)
        tt('dve', dre[:, :], dt_[:, :], aR[:, :], ALU.mult, r=R, w=R)
        tt('dve', dim[:, :], dt_[:, :], aI[:, :], ALU.mult, r=R, w=R)
        act(mag[:, :], dre[:, :], AF.Exp, r=R, w=R)
        cp('dve', sn[:, :], dim[:, :], r=R, w=R)
        ts('dve', cs[:, :], dim[:, :], 0.5 * PI, None, ALU.add, None, r=R, w=R)
        for arr in (sn, cs):
            cp('dve', den[:, :], arr[:, :], r=R, w=R)
            for th in (PI, 3 * PI, 5 * PI, 7 * PI):
                ts('dve', tmp[:, :], den[:, :], th, None, ALU.is_ge, None, r=R, w=R)
                stt('dve', arr[:, :], tmp[:, :], -2 * PI, arr[:, :], ALU.mult, ALU.add, r=R, w=R)
            act(arr[:, :], arr[:, :], AF.Sin, r=R, w=R)
        tt('dve', A1r[:, :], mag[:, :], cs[:, :], ALU.mult, r=R, w=R)
        tt('dve', A1i[:, :], mag[:, :], sn[:, :], ALU.mult, r=R, w=R)
        ts('dve', nA1i[:, :], A1i[:, :], -1.0, None, ALU.mult, None, r=R, w=R)
        cp('dve', AA1[:, 0, :], A1r[:, :], r=R, w=R)
        cp('dve', AA1[:, 1, :], A1r[:, :], r=R, w=R)
        cp('dve', PW[:, 0, 0, :], A1r[:, :], r=R, w=R)
        cp('dve', PW[:, 0, 1, :], A1i[:, :], r=R, w=R)
        for k in range(7):
            tt('dve', tmp[:, :], PW[:, k, 0, :], A1r[:, :], ALU.mult, r=R, w=R)
            tt('dve', den[:, :], PW[:, k, 1, :], A1i[:, :], ALU.mult, r=R, w=R)
            tt('dve', PW[:, k + 1, 0, :], tmp[:, :], den[:, :], ALU.subtract, r=R, w=R)
            tt('dve', tmp[:, :], PW[:, k, 0, :], A1i[:, :], ALU.mult, r=R, w=R)
            tt('dve', den[:, :], PW[:, k, 1, :], A1r[:, :], ALU.mult, r=R, w=R)
            tt('dve', PW[:, k + 1, 1, :], tmp[:, :], den[:, :], ALU.add, r=R, w=R)
        for k in range(8):
            ts('dve', nPWi[:, k, :], PW[:, k, 1, :], -1.0, None, ALU.mult, None, r=R, w=R)
        cp('dve', AA8[:, 0, :], PW[:, 7, 0, :], r=R, w=R)
        cp('dve', AA8[:, 1, :], PW[:, 7, 0, :], r=R, w=R)
        tt('dve', den[:, :], aR[:, :], aR[:, :], ALU.mult, r=R, w=R)
        tt('dve', tmp[:, :], aI[:, :], aI[:, :], ALU.mult, r=R, w=R)
        tt('dve', den[:, :], den[:, :], tmp[:, :], ALU.add, r=R, w=R)
        recip(den[:, :], den[:, :], r=R, w=R)
        ts('dve', zr[:, :], A1r[:, :], -1.0, None, ALU.add, None, r=R, w=R)
        tt('dve', fre[:, :], zr[:, :], aR[:, :], ALU.mult, r=R, w=R)
        tt('dve', tmp[:, :], A1i[:, :], aI[:, :], ALU.mult, r=R, w=R)
        tt('dve', fre[:, :], fre[:, :], tmp[:, :], ALU.add, r=R, w=R)
        tt('dve', fre[:, :], fre[:, :], den[:, :], ALU.mult, r=R, w=R)
        tt('dve', fim[:, :], A1i[:, :], aR[:, :], ALU.mult, r=R, w=R)
        tt('dve', tmp[:, :], zr[:, :], aI[:, :], ALU.mult, r=R, w=R)
        tt('dve', fim[:, :], fim[:, :], tmp[:, :], ALU.subtract, r=R, w=R)
        tt('dve', fim[:, :], fim[:, :], den[:, :], ALU.mult, r=R, w=R)
        frb = fre[:, :].unsqueeze(2).to_broadcast([128, 16, 16])
        fib = fim[:, :].unsqueeze(2).to_broadcast([128, 16, 16])
        tt('dve', u1[:, :, :], bR[:, :, :], frb, ALU.mult, r=R, w=R)
        tt('dve', u2[:, :, :], bI[:, :, :], fib, ALU.mult, r=R, w=R)
        tt('dve', u1[:, :, :], u1[:, :, :], u2[:, :, :], ALU.subtract, r=R, w=R)
        tt('dve', u2[:, :, :], bI[:, :, :], frb, ALU.mult, r=R, w=R)
        tt('dve', bI[:, :, :], bR[:, :, :], fib, ALU.mult, r=R, w=R)
        tt('dve', u2[:, :, :], u2[:, :, :], bI[:, :, :], ALU.add, r=R, w=R)
        ts('dve', cI[:, :, :], cI[:, :, :], -1.0, None, ALU.mult, None, r=R, w=R)
        for ri, (bsrc_, srcb, srcc) in enumerate(((bsrc[0], u1, cR), (bsrc[1], u2, cI))):
            memset('dve', bsrc_[:, :, :], 0.0, R)
            memset('dve', Ec[:, ri, :, :], 0.0, R)
            for e in range(2):
                cp('dve', bsrc_[e * 64:(e + 1) * 64, :, e * 16:(e + 1) * 16], srcb[e * 64:(e + 1) * 64, :, :], r=R, w=R)
                cp('dve', Ec[e * 64:(e + 1) * 64, ri, :, e * 16:(e + 1) * 16], srcc[e * 64:(e + 1) * 64, :, :], r=R, w=R)
        for ri in range(2):
            memset('dve', Et[ri][:, :, :, :], 0.0, R)
        for tau in range(8):
            prb = PW[:, tau, 0, :].unsqueeze(2).to_broadcast([128, 16, 16])
            pib = PW[:, tau, 1, :].unsqueeze(2).to_broadcast([128, 16, 16])
            tt('dve', u1[:, :, :], cR[:, :, :], prb, ALU.mult, r=R, w=R)
            tt('dve', u2[:, :, :], cI[:, :, :], pib, ALU.mult, r=R, w=R)
            tt('dve', u1[:, :, :], u1[:, :, :], u2[:, :, :], ALU.add, r=R, w=R)
            tt('dve', u2[:, :, :], cI[:, :, :], prb, ALU.mult, r=R, w=R)
            tt('dve', bR[:, :, :], cR[:, :, :], pib, ALU.mult, r=R, w=R)
            tt('dve', u2[:, :, :], u2[:, :, :], bR[:, :, :], ALU.subtract, r=R, w=R)
            for ri, srcE in ((0, u1), (1, u2)):
                for e in range(2):
                    cp('dve', Et[ri][e * 64:(e + 1) * 64, tau, :, e * 16:(e + 1) * 16], srcE[e * 64:(e + 1) * 64, :, :],
                       r=R, w=R)
        pbf = psum[:, 7, :].bitcast(BF16)
        for ri in range(2):
            for g4 in range(2):
                for j in range(8):
                    pr = g4 * 8 + j
                    tr(pbf[0:32, j * 128:(j + 1) * 128], bsrc[ri][:, pr, :], ident[:, :], r=R + [('c', 'ident')],
                       w=[('ps', 7)])
                cp('act', BBT[:, g4 * 8:(g4 + 1) * 8, ri, :], pbf[0:32, :].rearrange('p (j q) -> p j q', j=8),
                   r=[('ps', 7)], w=R)

    SBANK = (6, 5)

    def prep_head(nt, hi, qa, qkeys, ka, kkeys, qscale, kscale, B):
        qd2, ki2, kd2, kdT, sT, E1, E2, E3, dec2 = B
        stt('dve', qd2[hi][:, 0:nt], qa, qscale, E1[:, 0:nt], ALU.mult, ALU.mult, r=qkeys + [('E1',)], w=[('qd', hi)])
        stt('dve', ki2[hi][:, 0:nt], ka, kscale, E2[:, 0:nt], ALU.mult, ALU.mult, r=kkeys + [('E2',)], w=[('ki', hi)])
        stt('dve', kd2[hi][:, 0:nt], ka, kscale, E3[:, 0:nt], ALU.mult, ALU.mult, r=kkeys + [('E3',)], w=[('kd', hi)])

    def chunk_pair(nt, heads, vtok, vcol0, stl, bfl, use_n, o_banks, den_banks, B):
        nch = nt // 64
        qd2, ki2, kd2, kdT, sT, E1, E2, E3, dec2 = B
        S32 = stl[0]
        Sbf = bfl[0]
        pbf = psum[:, 7, :].bitcast(BF16)
        def issue_tr(c, hi):
            kd = kd2[hi]
            i2 = hi * 2 + c % 2
            PS = slice((c % 2) * 64, (c % 2) * 64 + 64)
            tr(pbf[PS, i2 * 128:(i2 + 1) * 128], kd[:, c * 64:c * 64 + 64], ident[:, :], r=[('kd', hi), ('c', 'ident')],
               w=[('ps7', i2)])
            cp('act', kdT[i2][PS, :], pbf[PS, i2 * 128:(i2 + 1) * 128], r=[('ps7', i2)], w=[('kdT', i2)])

        for hi in range(len(heads)):
            issue_tr(0, hi)
        for c in range(nch):
            cs = c * 64
            for hi, h in enumerate(heads):
                qd, ki, kd, dec = qd2[hi], ki2[hi], kd2[hi], dec2[hi]
                i2 = hi * 2 + c % 2
                P0 = (c % 2) * 64
                PS = slice(P0, P0 + 64)
                sb = SBANK[hi]
                o_bank = o_banks[hi]
                mm(psum[PS, sb, 0:64], ki[:, cs:cs + 64], qd[:, cs:cs + 64], True, True, r=[('ki', hi), ('qd', hi)],
                   w=[('ps', sb)])
                tt('dve', sT[i2][PS, :], psum[PS, sb, 0:64], maskut[PS, :], ALU.mult, r=[('ps', sb), ('c', 'mask')],
                   w=[('sT', i2)])
                vv = vtok[PS, c // 2, vcol0 + h * 128: vcol0 + (h + 1) * 128]
                mm(psum[:, o_bank, cs:cs + 64], vv, sT[i2][PS, :], True, False, r=[('vtok',), ('sT', i2)],
                   w=[('ps', o_bank)])
                if use_n:
                    Nbf = bfl[1]
                    den_bank = den_banks[hi]
                    mm(psum[:, den_bank, cs:cs + 64], onesbf[PS, :], sT[i2][PS, :], True, False,
                       r=[('c', 'onesbf'), ('sT', i2)], w=[('ps', den_bank)])
                mm(psum[:, sb, 128:256], kdT[i2][PS, :], vv, True, True, r=[('kdT', i2), ('vtok',)], w=[('ps', sb)])
                if use_n:
                    mm(psum[:, sb, 256:384], kdT[i2][PS, :], onesbf[PS, :], True, True,
                       r=[('kdT', i2), ('c', 'onesbf')], w=[('ps', sb)])
                if c + 1 < nch:
                    issue_tr(c + 1, hi)
                mm(psum[:, o_bank, cs:cs + 64], Sbf[:, h, :], qd[:, cs:cs + 64], False, True,
                   r=[('stbf', id(Sbf), h), ('qd', hi)], w=[('ps', o_bank)])
                if use_n:
                    mm(psum[:, den_bank, cs:cs + 64], Nbf[:, h, :], qd[:, cs:cs + 64], False, True,
                       r=[('stbf', id(Nbf), h), ('qd', hi)], w=[('ps', den_bank)])
                stt('dve', S32[:, h, :], S32[:, h, :], dec[:, c:c + 1], psum[:, sb, 128:256], ALU.mult, ALU.add,
                    r=[('st', id(S32), h), ('dec', hi), ('ps', sb)], w=[('st', id(S32), h)])
                cp('act', Sbf[:, h, :], S32[:, h, :], r=[('st', id(S32), h)], w=[('stbf', id(Sbf), h)])
                if use_n:
                    N32 = stl[1]
                    stt('dve', N32[:, h, :], N32[:, h, :], dec[:, c:c + 1], psum[:, sb, 256:384], ALU.mult, ALU.add,
                        r=[('st', id(N32), h), ('dec', hi), ('ps', sb)], w=[('st', id(N32), h)])
                    cp('act', Nbf[:, h, :], N32[:, h, :], r=[('st', id(N32), h)], w=[('stbf', id(Nbf), h)])

    def vproj(tok0, nt, wt, wkey, c0, vtok):
        for blk in range(nt // 128):
            b = nextbank()
            for k in range(8):
                mm(ps(b, 128, 512), uT[:, k, tok0 + blk * 128: tok0 + blk * 128 + 128], wt[:, k, c0:c0 + 512],
                   k == 0, k == 7, r=wkeys(wkey, c0, 512) + [('uT',)], w=[('ps', b)])
            cp('act', vtok[:, blk, :], ps(b, 128, 512), r=[('ps', b)], w=[('vtok',)])

    def merge_out(tok0, nt, oT, okey, wpr, wprkey, wgate, wgkey, gc0, first, gsig, gtmp):
        for co in range(8):
            b = nextbank()
            for k in range(4):
                mm(ps(b, 128, nt), wpr[:, k, co * 128:(co + 1) * 128], oT[:, k, 0:nt], k == 0, k == 3,
                   r=[wprkey, okey], w=[('ps', b)])
            b2 = nextbank()
            proj(b2, wgate, wgkey, gc0 + co * 128, 128, uT, ('uT',), tok0, nt)
            act(gsig[:, 0:nt], ps(b2, 128, nt), AF.Sigmoid, r=[('ps', b2)], w=[('gsig',)])
            if first:
                tt('dve', merged[:, co, tok0:tok0 + nt], ps(b, 128, nt), gsig[:, 0:nt], ALU.mult,
                   r=[('ps', b), ('gsig',)], w=[('mg',)])
            else:
                tt('dve', gtmp[:, 0:nt], ps(b, 128, nt), gsig[:, 0:nt], ALU.mult, r=[('ps', b), ('gsig',)],
                   w=[('gtmp',)])
                tt('pool', merged[:, co, tok0:tok0 + nt], merged[:, co, tok0:tok0 + nt], gtmp[:, 0:nt], ALU.add,
                   r=[('gtmp',), ('mg',)], w=[('mg',)])

    def common_bufs():
        qd2 = [aalloc('qd', [128, 512], BF16) for _ in range(2)]
        ki2 = [aalloc('ki', [128, 512], BF16) for _ in range(2)]
        kd2 = [aalloc('kd', [128, 512], BF16) for _ in range(2)]
        kdT = [aalloc('kdT', [128, 128], BF16) for _ in range(4)]
        sT = [aalloc('sT', [128, 64], BF16) for _ in range(4)]
        E1 = aalloc('E1', [128, 512], F32); E2 = aalloc('E2', [128, 512], F32); E3 = aalloc('E3', [128, 512], F32)
        dec2 = [aalloc('dec', [128, 8], F32) for _ in range(2)]
        return (qd2, ki2, kd2, kdT, sT, E1, E2, E3, dec2)

    def phase_gla(L):
        arena[0] = PH0
        Wg = aalloc('Wg', [128, 8, 3088], BF16)
        Wgk = aalloc('Wgk', [16, 1, 512], BF16)
        Wpr = aalloc('Wpr', [128, 4, D], BF16)
        bgk = aalloc('bgk', [128, 4], F32); gln = aalloc('gln', [128, 1], F32)
        Bx = aalloc('Bx', [128, 513], F32)
        Drel = aalloc('Drel', [128, 512], F32); Drel2 = aalloc('Drel2', [128, 512], F32)
        B = common_bufs()
        qd2, ki2, kd2, kdT, sT, E1, E2, E3, dec2 = B
        vtok = aalloc('vtok', [128, 4, 512], BF16)
        lrT = aalloc('lrT', [16, 512], BF16)
        sq = aalloc('sq', [128, 512], BF16)
        rstd = E1; t1 = E2; sr = E3
        ogT = aalloc('ogT', [128, 4, 512], BF16)
        gsig = aalloc('gsig', [128, 512], F32); gtmp = aalloc('gtmp', [128, 512], F32)
        spb = gtmp
        K = ('W', 'g')
        KW = WK(('Wc', 'g'))
        load_wc(Wg, KW, 'w_in', L, 8, 1024, 1024, 1040)
        load_wc(Wg, KW, 'w_in', L, 8, 0, 0, 1024)
        load_wc(Wg, KW, 'w_in', L, 8, 2064, O_GATE, 1024)
        load_w(Wgk[:, :, :], K, 'gla_w_gk', L, 0, 1, 0, 512, rows=16)
        load_w(Wpr[:, :, :], K, 'gla_w_proj', L, 0, 4, 0, D)
        vec_load(bgk[:, :], 'gla_b_gk', L, [[1, 128], [128, 4]], K)
        ts('dve', bgk[:, :], bgk[:, :], -1.0, None, ALU.mult, None, r=[K], w=[K])
        vec_load(gln[:, :], 'gla_norm', L, [[1, 128], [1, 1]], K)
        memset('pool', Bx[:, 0:1], 0.0, [('Bx',)])
        for tok0 in range(0, SEGT, TILE):
            nt = TILE
            nch = nt // 64
            vproj(tok0, nt, Wg, KW, O_GV, vtok)
            b = nextbank()
            proj(b, Wg, KW, O_LR, 16, uT, ('uT',), tok0, nt)
            cp('act', lrT[:, 0:nt], ps(b, 16, nt), r=[('ps', b)], w=[('lrT',)])
            for pair in ((0, 1), (2, 3)):
                for hi, h in enumerate(pair):
                    dec = dec2[hi]
                    b = nextbank()
                    mm(ps(b, 128, nt), Wgk[:, 0, h * 128:(h + 1) * 128], lrT[:, 0:nt], True, True, r=[K, ('lrT',)],
                       w=[('ps', b)])
                    act(spb[:, 0:nt], ps(b, 128, nt), AF.Exp, r=[('ps', b), K], w=[('gtmp',)], bias=bgk[:, h:h + 1],
                        scale=-1.0)
                    act(spb[:, 0:nt], spb[:, 0:nt], AF.Ln, r=[('gtmp',)], w=[('gtmp',)], bias=1.0)
                    S.add('dve', lambda e, nt=nt: e.tensor_tensor_scan(out=Bx[:, 1:1 + nt], data0=ones32[:, 0:nt],
                                                                       data1=spb[:, 0:nt], initial=0.0,
                                                                       op0=ALU.mult, op1=ALU.add),
                          r=[('gtmp',), ('c', 'ones32')], w=[('Bx',)])
                    d3 = Drel[:, 0:nt].rearrange('p (c l) -> p c l', l=64)
                    d23 = Drel2[:, 0:nt].rearrange('p (c l) -> p c l', l=64)
                    bx3 = Bx[:, 1:1 + nt].rearrange('p (c l) -> p c l', l=64)
                    br3 = Bx[:, 0:nt].rearrange('p (c l) -> p c l', l=64)[:, :, 0:1]
                    tt('dve', d3, bx3, br3.to_broadcast([128, nch, 64]), ALU.subtract, r=[('Bx',)], w=[('Drel',)])
                    tt('dve', d23, d3, d3[:, :, 63:64].to_broadcast([128, nch, 64]), ALU.subtract, r=[('Drel',)],
                       w=[('Drel2',)])
                    act(E1[:, 0:nt], Drel[:, 0:nt], AF.Exp, r=[('Drel',)], w=[('E1',)], scale=-1.0 / 16)
                    act(dec[:, 0:nch], Drel[:, 63:nt:64], AF.Exp, r=[('Drel',)], w=[('dec', hi)], scale=-1.0 / 16)
                    act(E2[:, 0:nt], Drel[:, 0:nt], AF.Exp, r=[('Drel',)], w=[('E2',)], scale=1.0 / 16)
                    act(E3[:, 0:nt], Drel2[:, 0:nt], AF.Exp, r=[('Drel2',)], w=[('E3',)], scale=1.0 / 16)
                    bq = nextbank()
                    proj(bq, Wg, KW, O_GQ + h * 128, 128, uT, ('uT',), tok0, nt)
                    bk = nextbank()
                    proj(bk, Wg, KW, O_GK + h * 128, 128, uT, ('uT',), tok0, nt)
                    prep_head(nt, hi, ps(bq, 128, nt), [('ps', bq)], ps(bk, 128, nt), [('ps', bk)], 128 ** -0.5, 1.0, B)
                obs = [nextbank(), nextbank()]
                chunk_pair(nt, pair, vtok, 0, [St32], [Stbf], False, obs, None, B)
                for hi, h in enumerate(pair):
                    ob = obs[hi]
                    act(sq[:, 0:nt], psum[:, ob, 0:nt], AF.Square, r=[('ps', ob)], w=[('sq',)])
                    b = nextbank()
                    mm(ps(b, 128, nt), onesbf[:, :], sq[:, 0:nt], True, True, r=[('c', 'onesbf'), ('sq',)],
                       w=[('ps', b)])
                    act(rstd[:, 0:nt], ps(b, 128, nt), AF.Sqrt, r=[('ps', b), ('c', 'eps')], w=[('E1',)],
                        bias=eps_t[:, :], scale=1.0 / 128)
                    recip(rstd[:, 0:nt], rstd[:, 0:nt], r=[('E1',)], w=[('E1',)])
                    tt('dve', t1[:, 0:nt], psum[:, ob, 0:nt], rstd[:, 0:nt], ALU.mult, r=[('ps', ob), ('E1',)],
                       w=[('E2',)])
                    b = nextbank()
                    proj(b, Wg, KW, O_GR + h * 128, 128, uT, ('uT',), tok0, nt)
                    act(sr[:, 0:nt], ps(b, 128, nt), AF.Silu, r=[('ps', b)], w=[('E3',)])
                    stt('dve', ogT[:, h, 0:nt], sr[:, 0:nt], gln[:, 0:1], t1[:, 0:nt], ALU.mult, ALU.mult,
                        r=[('E3',), K, ('E2',)], w=[('ogT',)])
            merge_out(tok0, nt, ogT, ('ogT',), Wpr, K, Wg, KW, 2064, True, gsig, gtmp)

    def phase_ml(L):
        arena[0] = PH0
        Wm = aalloc('Wm', [128, 8, 3080], BF16)
        Wpr = aalloc('Wmp', [128, 4, D], BF16)
        B = common_bufs()
        qd2, ki2, kd2, kdT, sT, E1, E2, E3, dec2 = B
        vtok = aalloc('vtok', [128, 4, 512], BF16)
        zc = aalloc('zc', [128, 515], F32)
        qh = aalloc('qh', [128, 512], BF16); kh = aalloc('kh', [128, 512], BF16)
        B8 = aalloc('B8', [8, 513], F32)
        T2 = aalloc('T2', [8, 512], F32); T3 = aalloc('T3', [8, 512], F32)
        bif = aalloc('bif', [8, 1], F32); cw = aalloc('cw', [128, 8, 4], F32); cb = aalloc('cb', [128, 8], F32)
        dabs = E1; t1 = E2; so = E3
        ogT = aalloc('hT', [128, 4, 512], BF16)
        gsig = aalloc('gsig', [128, 512], F32); gtmp = aalloc('gtmp', [128, 512], F32)
        cacc = gtmp; gi8 = gsig[0:8, :]
        selF = aalloc('selF', [8, 4, 128], F32)
        selIF = aalloc('selIF', [8, 4, 128], F32)
        for h in range(4):
            for (dst, rows_) in ((selF, (4 + h,)), (selIF, (h, 4 + h))):
                memset('pool', dst[:, h, :], 0.0, [('c', 'sel')])
                for rr in rows_:
                    memset('pool', zc[0:8, 0:128], 1.0, [('zc',)])
                    S.add('pool', lambda e, rr=rr: e.affine_select(out=zc[0:8, 0:128], in_=zc[0:8, 0:128],
                                                                   pattern=[[0, 128]], compare_op=ALU.is_equal, fill=0.0,
                                                                   base=-rr, channel_multiplier=1),
                          r=[('zc',)], w=[('zc',)])
                    tt('pool', dst[:, h, :], dst[:, h, :], zc[0:8, 0:128], ALU.add, r=[('zc',), ('c', 'sel')],
                       w=[('c', 'sel')])
        K = ('W', 'm')
        KW = WK(('Wc', 'm'))
        load_wc(Wm, KW, 'w_in', L, 8, 1024, O_MQ + 1024, 1032)
        load_wc(Wm, KW, 'w_in', L, 8, 0, O_MQ, 1024)
        load_wc(Wm, KW, 'w_in', L, 8, 2056, O_GATE + 2048, 1024)
        load_w(Wpr[:, :, :], K, 'ml_w_proj', L, 0, 4, 0, D)
        vec_load(bif[0:4, :], 'ml_b_i', L, [[1, 4], [1, 1]], K)
        vec_load(bif[4:8, :], 'ml_b_f', L, [[1, 4], [1, 1]], K)
        for j in range(4):
            vec_load(cw[:, :, j], 'ml_conv_w', L, [[1, 128], [128, 8]], K, off=j * 1024)
        vec_load(cb[:, :], 'ml_conv_b', L, [[1, 128], [128, 8]], K)
        memset('pool', B8[:, 0:1], 0.0, [('B8',)])
        for tok0 in range(0, SEGT, TILE):
            nt = TILE
            nch = nt // 64
            vproj(tok0, nt, Wm, KW, 1024, vtok)
            b = nextbank()
            proj(b, Wm, KW, 1536, 8, uT, ('uT',), tok0, nt)
            act(gi8[:, 0:nt], ps(b, 8, nt), AF.Identity, r=[('ps', b), K], w=[('gsig',)], bias=bif[:, :])
            act(T3[:, 0:nt], gi8[:, 0:nt], AF.Exp, r=[('gsig',)], w=[('T3',)], scale=-1.0)
            act(T3[:, 0:nt], T3[:, 0:nt], AF.Ln, r=[('T3',)], w=[('T3',)], bias=1.0)
            S.add('dve', lambda e, nt=nt: e.tensor_tensor_scan(out=B8[:, 1:1 + nt], data0=ones32[0:8, 0:nt],
                                                               data1=T3[:, 0:nt], initial=0.0, op0=ALU.mult,
                                                               op1=ALU.add),
                  r=[('T3',), ('c', 'ones32')], w=[('B8',)])
            d3 = T2[:, 0:nt].rearrange('p (c l) -> p c l', l=64)
            d23 = T3[:, 0:nt].rearrange('p (c l) -> p c l', l=64)
            bx3 = B8[:, 1:1 + nt].rearrange('p (c l) -> p c l', l=64)
            br3 = B8[:, 0:nt].rearrange('p (c l) -> p c l', l=64)[:, :, 0:1]
            tt('dve', d3, bx3, br3.to_broadcast([8, nch, 64]), ALU.subtract, r=[('B8',)], w=[('T2',)])
            tt('dve', d23, d3, d3[:, :, 63:64].to_broadcast([8, nch, 64]), ALU.subtract, r=[('T2',), ('B8',)],
               w=[('T3',)])
            cp('dve', T2[0:4, 0:nt], gi8[0:4, 0:nt], r=[('gsig',), ('T2',)], w=[('T2',)])
            cp('dve', T3[0:4, 0:nt], gi8[0:4, 0:nt], r=[('gsig',), ('T3',)], w=[('T3',)])
            for pair in ((0, 1), (2, 3)):
                for hi, h in enumerate(pair):
                    dec = dec2[hi]
                    b1 = nextbank()
                    mm(ps(b1, 128, nt), selF[:, h, :], T2[:, 0:nt], True, True, r=[('c', 'sel'), ('T2',)],
                       w=[('ps', b1)])
                    act(E1[:, 0:nt], ps(b1, 128, nt), AF.Exp, r=[('ps', b1)], w=[('E1',)], scale=-1.0)
                    act(dec[:, 0:nch], psum[:, b1, 63:nt:64], AF.Exp, r=[('ps', b1)], w=[('dec', hi)], scale=-1.0)
                    b2 = nextbank()
                    mm(ps(b2, 128, nt), selIF[:, h, :], T2[:, 0:nt], True, True, r=[('c', 'sel'), ('T2',)],
                       w=[('ps', b2)])
                    act(E2[:, 0:nt], ps(b2, 128, nt), AF.Exp, r=[('ps', b2)], w=[('E2',)])
                    b3 = nextbank()
                    mm(ps(b3, 128, nt), selIF[:, h, :], T3[:, 0:nt], True, True, r=[('c', 'sel'), ('T3',)],
                       w=[('ps', b3)])
                    act(E3[:, 0:nt], ps(b3, 128, nt), AF.Exp, r=[('ps', b3)], w=[('E3',)])
                    for which, dstq in ((0, qh), (1, kh)):
                        ct = which * 4 + h
                        b = nextbank()
                        proj(b, Wm, KW, ct * 128, 128, uT, ('uT',), tok0, nt)
                        cp('act', zc[:, 3:3 + nt], ps(b, 128, nt), r=[('ps', b)], w=[('zc',)])
                        cp('pool', zc[:, 0:3], zch[:, ct, :], r=[('zch', ct)], w=[('zc',)])
                        ts('dve', cacc[:, 0:nt], zc[:, 0:nt], cw[:, ct, 0:1], None, ALU.mult, None, r=[('zc',), K],
                           w=[('gtmp',)])
                        for j in range(1, 4):
                            stt('dve', cacc[:, 0:nt], zc[:, j:j + nt], cw[:, ct, j:j + 1], cacc[:, 0:nt], ALU.mult,
                                ALU.add, r=[('zc',), K, ('gtmp',)], w=[('gtmp',)])
                        cp('pool', zch[:, ct, :], zc[:, nt:nt + 3], r=[('zc',)], w=[('zch', ct)])
                        act(dstq[:, 0:nt], cacc[:, 0:nt], AF.Silu, r=[('gtmp',), K], w=[('qk', which)],
                            bias=cb[:, ct:ct + 1])
                    prep_head(nt, hi, qh[:, 0:nt], [('qk', 0)], kh[:, 0:nt], [('qk', 1)], 1.0, 128 ** -0.5, B)
                obs = [nextbank(), nextbank()]
                dbs = [nextbank(), nextbank()]
                chunk_pair(nt, pair, vtok, 0, [Ct32, Nt32], [Ctbf, Ntbf], True, obs, dbs, B)
                for hi, h in enumerate(pair):
                    ob, db = obs[hi], dbs[hi]
                    ts('dve', dabs[:, 0:nt], psum[:, db, 0:nt], -1.0, 1.0, ALU.mult, ALU.max, r=[('ps', db)],
                       w=[('E1',)])
                    tt('dve', dabs[:, 0:nt], dabs[:, 0:nt], psum[:, db, 0:nt], ALU.max, r=[('ps', db), ('E1',)],
                       w=[('E1',)])
                    recip(dabs[:, 0:nt], dabs[:, 0:nt], r=[('E1',)], w=[('E1',)])
                    tt('dve', t1[:, 0:nt], psum[:, ob, 0:nt], dabs[:, 0:nt], ALU.mult, r=[('ps', ob), ('E1',)],
                       w=[('E2',)])
                    b = nextbank()
                    proj(b, Wm, KW, 1544 + h * 128, 128, uT, ('uT',), tok0, nt)
                    act(so[:, 0:nt], ps(b, 128, nt), AF.Sigmoid, r=[('ps', b)], w=[('E3',)])
                    tt('dve', ogT[:, h, 0:nt], t1[:, 0:nt], so[:, 0:nt], ALU.mult, r=[('E2',), ('E3',)], w=[('ogT',)])
            merge_out(tok0, nt, ogT, ('ogT',), Wpr, K, Wm, KW, 2056, False, gsig, gtmp)

    def phase_s5(L):
        arena[0] = PH0
        TB = 128
        NCH = TB // 8
        Ws = aalloc('Ws', [128, 8, 1536], BF16)
        Wglu = aalloc('Wglu', [128, 4, 512], BF16)
        Wsp = aalloc('Wsp', [128, 4, D], BF16)
        bglu = aalloc('bglu', [128, 4], F32)
        su128 = aalloc('su128', [128, 4, 512], BF16)
        sup2 = [aalloc('sup', [32, 16, TB], BF16) for _ in range(2)]
        GU = aalloc('GU', [128, 2, 16, TB], F32)
        XC = aalloc('XC', [128, 2, 16, NCH + 1], F32)
        XXbf = aalloc('XXbf', [128, 2, 16, TB], BF16)
        XCbf = aalloc('XCbf', [128, 2, 16, NCH], BF16)
        P1 = aalloc('P1', [128, 2, 16, NCH], F32); P2 = aalloc('P2', [128, 2, 16, NCH], F32)
        P1s = aalloc('P1s', [128, 2, 16], F32); P2s = aalloc('P2s', [128, 2, 16], F32)
        yp = aalloc('yp', [32, 16, TB], BF16)
        yv = aalloc('yv', [128, 512], F32)
        ga = aalloc('ga', [128, 512], F32)
        yg = aalloc('yg', [128, 4, 512], BF16)
        gsig = aalloc('gsig', [128, 512], F32); gtmp = aalloc('gtmp', [128, 512], F32)
        gb = gtmp
        K = ('W', 's')
        KC = ('s5c', L)
        y2 = yg
        KW = WK(('Wc', 's'))
        load_wc(Ws, KW, 'w_in', L, 8, 0, O_SU, 512)
        load_wc(Ws, KW, 'w_in', L, 8, 512, O_GATE + 1024, 1024)
        load_w(Wglu[:, :, :], K, 's5_w_glu', L, 0, 4, 0, 512)
        load_w(Wsp[:, :, :], K, 's5_w_proj', L, 0, 4, 0, D)
        vec_load(bglu[:, :], 's5_b_glu', L, [[1, 128], [128, 4]], K)
        cp('dve', XC[:, :, :, 0], XXst[:, :, :], r=[('xxst',)], w=[('XC',)])
        GU5 = GU[:, :, :, :].rearrange('p r g (l n) -> p r g l n', l=8)

        def G(tau):
            return GU5[:, :, :, tau, :]

        def cmul_acc(dst, src, k, dkey, skey):
            prb = PW[:, k, 0, :].unsqueeze(1).unsqueeze(3).to_broadcast([128, 2, 16, NCH])
            pib = PW[:, k, 1, :].unsqueeze(2).to_broadcast([128, 16, NCH])
            npib = nPWi[:, k, :].unsqueeze(2).to_broadcast([128, 16, NCH])
            tt('dve', P1[:, :, :, :], src, prb, ALU.mult, r=[skey, KC], w=[('P1',)])
            tt(P2ENG, P2[:, 0, :, :], src[:, 1], npib, ALU.mult, r=[skey, KC], w=[('P2',)])
            tt(P2ENG, P2[:, 1, :, :], src[:, 0], pib, ALU.mult, r=[skey, KC], w=[('P2',)])
            tt('dve', dst, dst, P1[:, :, :, :], ALU.add, r=[dkey, ('P1',)], w=[dkey])
            tt('dve', dst, dst, P2[:, :, :, :], ALU.add, r=[dkey, ('P2',)], w=[dkey])

        for tok0 in range(0, SEGT, TILE):
            nt = TILE
            for blk in range(4):
                b = nextbank()
                proj(b, Ws, KW, blk * 128, 128, uT, ('uT',), tok0, nt)
                cp('act', su128[:, blk, 0:nt], ps(b, 128, nt), r=[('ps', b)], w=[('su128',)])
            def front(tb, si):
                supb = sup2[si]
                for blk in range(4):
                    b = nextbank()
                    for k in range(4):
                        mm(psum[0:32, b, k * TB:(k + 1) * TB], ident[:, 32 * k:32 * k + 32],
                           su128[:, blk, tb:tb + TB].rearrange('p (n l) -> p l n', l=8), True, True,
                           r=[('c', 'ident'), ('su128',)], w=[('ps', b)])
                    cp('act', supb[:, blk * 4:(blk + 1) * 4, :], ps(b, 32, 4 * TB).rearrange('p (k t) -> p k t', k=4),
                       r=[('ps', b)], w=[('sup', si)])
                for ri in range(2):
                    for g in range(4):
                        b = nextbank()
                        for k in range(4):
                            pr = g * 4 + k
                            mm(psum[:, b, k * TB:(k + 1) * TB], BBT[:, pr, ri, :], supb[:, pr, :], True, True,
                               r=[KC, ('sup', si)], w=[('ps', b)])
                        cp('act', GU[:, ri, g * 4:(g + 1) * 4, :], ps(b, 128, 4 * TB).rearrange('p (k t) -> p k t', k=4),
                           r=[('ps', b)], w=[('GU',)])

            def mid(tb):
                for tau in range(1, 8):
                    cmul_acc(G(tau), G(tau - 1), 0, ('GU',), ('GU',))
                cp('act', XXbf[:, :, :, :], GU[:, :, :, :], r=[('GU',)], w=[('XXbf',)])
                for n in range(NCH):
                    tt('dve', P1s[:, :, :], XC[:, :, :, n], AA8[:, :, :], ALU.mult, r=[('XC',), KC], w=[('P1s',)])
                    tt('dve', P2s[:, 0, :], XC[:, 1, :, n], nPWi[:, 7, :], ALU.mult, r=[('XC',), KC], w=[('P2s',)])
                    tt('dve', P2s[:, 1, :], XC[:, 0, :, n], PW[:, 7, 1, :], ALU.mult, r=[('XC',), KC], w=[('P2s',)])
                    tt('dve', P1s[:, :, :], P1s[:, :, :], P2s[:, :, :], ALU.add, r=[('P1s',), ('P2s',)], w=[('P1s',)])
                    tt('dve', XC[:, :, :, n + 1], P1s[:, :, :], GU5[:, :, :, 7, n], ALU.add, r=[('P1s',), ('GU',)],
                       w=[('XC',)])
                cp('act', XCbf[:, :, :, :], XC[:, :, :, 0:NCH], r=[('XC',)], w=[('XCbf',)])
                cp('dve', XC[:, :, :, 0], XC[:, :, :, NCH], r=[('XC',)], w=[('XC',)])

            ybanks = {}

            def tail_y(tb):
                bl = []
                for g in range(4):
                    b = nextbank()
                    bl.append(b)
                    for k in range(4):
                        pr = g * 4 + k
                        mm(psum[0:32, b, k * TB:(k + 1) * TB], Ec[:, 0, pr, :], XXbf[:, 0, pr, :], True, False,
                           r=[KC, ('XXbf',)], w=[('ps', b)])
                        mm(psum[0:32, b, k * TB:(k + 1) * TB], Ec[:, 1, pr, :], XXbf[:, 1, pr, :], False, False,
                           r=[KC, ('XXbf',)], w=[('ps', b)])
                        for tau in range(8):
                            for ri in range(2):
                                mm(psum[0:32, b, k * TB + tau * NCH:k * TB + (tau + 1) * NCH], Et[ri][:, tau, pr, :],
                                   XCbf[:, ri, pr, :], False, tau == 7 and ri == 1, r=[KC, ('XCbf',)], w=[('ps', b)])
                ybanks[tb] = bl

            def tail_rest(tb, si):
                supb = sup2[si]
                for g in range(4):
                    b = ybanks[tb][g]
                    for k in range(4):
                        pr = g * 4 + k
                        stt('dve', yp[:, pr, :], supb[:, pr, :], Dd[:, pr:pr + 1], psum[0:32, b, k * TB:(k + 1) * TB],
                            ALU.mult, ALU.add, r=[('sup', si), KC, ('ps', b)], w=[('yp',)])
                b = nextbank()
                for blk in range(4):
                    for k in range(4):
                        mm(psum[:, b, blk * TB:(blk + 1) * TB], gsel[:, k, :],
                           yp[:, blk * 4 + k, :].rearrange('p (l n) -> p n l', l=8), k == 0, k == 3,
                           r=[('c', 'gsel'), ('yp',)], w=[('ps', b)])
                cp('act', yv[:, :], ps(b, 128, 4 * TB), r=[('ps', b)], w=[('yv',)])
                act(ga[:, :], yv[:, :], AF.Square, r=[('yv',)], w=[('ga',)])
                ts('pool', ga[:, :], ga[:, :], 0.044715, 1.0, ALU.mult, ALU.add, r=[('ga',)], w=[('ga',)])
                tt('pool', ga[:, :], ga[:, :], yv[:, :], ALU.mult, r=[('ga',), ('yv',)], w=[('ga',)])
                act(gb[:, :], ga[:, :], AF.Sigmoid, r=[('ga',)], w=[('gtmp',)], scale=GC)
                tt('pool', yg[:, :, tb:tb + TB], gb[:, :].rearrange('p (k t) -> p k t', k=4),
                   yv[:, :].rearrange('p (k t) -> p k t', k=4), ALU.mult, r=[('gtmp',), ('yv',)], w=[('yg',)])

            tbs = list(range(0, nt, TB))
            front(tbs[0], 0)
            mid(tbs[0])
            for bi_, tb in enumerate(tbs):
                if bi_ + 1 < len(tbs):
                    front(tbs[bi_ + 1], (bi_ + 1) % 2)
                tail_y(tb)
                if bi_ + 1 < len(tbs):
                    mid(tbs[bi_ + 1])
                tail_rest(tb, bi_ % 2)
            gbanks = []
            for co in range(4):
                b = nextbank()
                gbanks.append(b)
                for k in range(4):
                    mm(ps(b, 128, nt), Wglu[:, k, co * 128:(co + 1) * 128], yg[:, k, 0:nt], k == 0, k == 3,
                       r=[K, ('yg',)], w=[('ps', b)])
            for co in range(4):
                b = gbanks[co]
                act(gsig[:, 0:nt], ps(b, 128, nt), AF.Sigmoid, r=[('ps', b), K], w=[('gsig',)], bias=bglu[:, co:co + 1])
                tt('dve', y2[:, co, 0:nt], yg[:, co, 0:nt], gsig[:, 0:nt], ALU.mult, r=[('yg',), ('gsig',)],
                   w=[('yg',)])
            merge_out(tok0, nt, y2, ('yg',), Wsp, K, Ws, KW, 512, False, gsig, gtmp)
        cp('dve', XXst[:, :, :], XC[:, :, :, 0], r=[('XC',)], w=[('xxst',)])

    def phase_out(L, src_t, seg):
        arena[0] = PH0
        Wo = aalloc('Wo', [128, 8, D], BF16)
        t1 = aalloc('t1o', [128, D], F32)
        K = ('W', 'o')
        load_w(Wo[:, :, :], K, 'w_out', L, 0, 8, 0, D)
        dma(gainb[:, :], rawap(W['norm_mix_post'], L * D, [[0, 128], [1, D]]), w=[('c', 'gain')])
        junk_ref[0] = aalloc('junkO', [128, D], BF16)
        xbo = [aalloc('xbo', [128, D], F32) for _ in range(2)]
        if L == 0 and seg == 0:
            dump('mgall', merged[:, :, :], [('mg',)])
        for bi, tb in enumerate(range(0, SEGT, 128)):
            xi = bi % 2
            xb = xbo[xi]
            dma(xb[:, :], rawap(src_t, (seg * SEGT + tb) * D, [[D, 128], [1, D]]), w=[('xbo', xi)])
            banks = []
            for hf in range(2):
                b = nextbank()
                banks.append(b)
                for k in range(8):
                    mm(ps(b, 128, 512), merged[:, k, tb:tb + 128], Wo[:, k, hf * 512:(hf + 1) * 512], k == 0, k == 7,
                       r=[('mg',), K], w=[('ps', b)])
            rowsum_rstd(banks, 128, 8)
            for hf, b in enumerate(banks):
                tt('dve', t1[:, hf * 512:(hf + 1) * 512], ps(b, 128, 512), gains[1][:, hf * 512:(hf + 1) * 512], ALU.mult,
                   r=[('ps', b), ('c', 'gain')], w=[('t1o', hf)])
                stt('dve', xb[:, hf * 512:(hf + 1) * 512], t1[:, hf * 512:(hf + 1) * 512], small[:, 8:9],
                    xb[:, hf * 512:(hf + 1) * 512], ALU.mult, ALU.add, r=[('t1o', hf), ('rsx',), ('xbo', xi)],
                    w=[('xbo', xi)])
            if L == 0 and seg == 0 and tb == 128:
                dump('osmall', small[:, 0:16], [('rsx',), ('ssx', 0), ('ssx', 1), ('ssx', 2), ('ssx', 3)])
                dump('ot1', t1[:, :], [('t1o', 0), ('t1o', 1)])
                dump('ojunk', junk_ref[0][:, :], [('junk',)])
            dma(rawap(xmid, tb * D, [[D, 128], [1, D]]), xb[:, :], r=[('xbo', xi)], w=[('xmid', tb)])
            if L == 0 and seg == 0:
                dump('xmid%d' % tb, xb[:, :], [('xbo', xi)])

    hs = nc.dram_tensor('hs', [22, 128, SEGT], BF16)

    def phase_ffn_a(L, seg):
        arena[0] = UT0
        Wup = aalloc('Wup', [128, 8, DFF], BF16)
        Wga = aalloc('Wga', [128, 8, DFF], BF16)
        NTF = 512
        uF = [aalloc('uF', [128, 8, NTF], BF16) for _ in range(2)]
        NB = 3
        fsets = [(aalloc('ab', [128, NTF + 2], F32), aalloc('vv', [128, NTF], F32), aalloc('gaF', [128, NTF], F32),
                  aalloc('vg', [128, NTF], F32), aalloc('ho', [128, NTF], BF16)) for _ in range(NB)]
        fw = aalloc('fw', [128, 22, 3], F32); fb = aalloc('fb', [128, 22], F32)
        nsets = norm_sets(2)
        nbi = [0]
        K = ('W', 'f')
        KU = WK(('Wc', 'fu'), 704)
        KG = WK(('Wc', 'fg'), 704)
        for q4 in range(4):
            load_wc(Wup, KU, 'ffn_w_up', L, 8, q4 * 704, q4 * 704, 704)
            load_wc(Wga, KG, 'ffn_w_gate', L, 8, q4 * 704, q4 * 704, 704)
        for j in range(3):
            vec_load(fw[:, :, j], 'ffn_conv_w', L, [[1, 128], [128, 22]], K, off=j * DFF)
        vec_load(fb[:, :], 'ffn_conv_b', L, [[1, 128], [128, 22]], K)
        dma(gains2[0][:, :], rawap(W['norm_ffn_pre'], L * D, [[0, 128], [1, D]]), w=[('c', 'gain')])
        gi = 0
        def ffn_norm(ti):
            t0_ = ti * NTF
            for j in range(NTF // 128):
                tb = t0_ + j * 128
                norm_block(rawap(xmid, tb * D, [[D, 128], [1, D]]), 128, gains[2], ('c', 'gain'), uF[ti % 2],
                           ('uF', ti % 2), j * 128, 0, extra_r=[('xmid', tb)], bufs=nsets[nbi[0] % 2])
                nbi[0] += 1

        ntile = SEGT // NTF
        ffn_norm(0)
        for ti, t0 in enumerate(range(0, SEGT, NTF)):
            uFt = uF[ti % 2]
            ukey = ('uF', ti % 2)
            def stage_a(ct, si):
                ab, vv, ga, vg, ho = fsets[si]
                b = nextbank()
                proj(b, Wup, KU, ct * 128, 128, uFt, ukey, 0, NTF)
                cp('act', ab[:, 2:2 + NTF], ps(b, 128, NTF), r=[('ps', b)], w=[('ab', si)])
                cp('pool', ab[:, 0:2], ahist[:, ct, :], r=[('ah', ct)], w=[('ab', si)])
                ts('dve', vv[:, :], ab[:, 0:NTF], fw[:, ct, 0:1], fb[:, ct:ct + 1], ALU.mult, ALU.add,
                   r=[('ab', si), K], w=[('vv', si)])
                for j in range(1, 3):
                    stt('dve', vv[:, :], ab[:, j:j + NTF], fw[:, ct, j:j + 1], vv[:, :], ALU.mult, ALU.add,
                        r=[('ab', si), K, ('vv', si)], w=[('vv', si)])
                cp('pool', ahist[:, ct, :], ab[:, NTF:NTF + 2], r=[('ab', si)], w=[('ah', ct)])

            def stage_b(ct, si):
                ab, vv, ga, vg, ho = fsets[si]
                b2 = nextbank()
                proj(b2, Wga, KG, ct * 128, 128, uFt, ukey, 0, NTF)
                tt('dve', vg[:, :], vv[:, :], ps(b2, 128, NTF), ALU.mult, r=[('vv', si), ('ps', b2)], w=[('vg', si)])
                act(ga[:, :], vv[:, :], AF.Square, r=[('vv', si)], w=[('ga', si)], scale=math.sqrt(0.044715))
                stt('dve', ga[:, :], ga[:, :], 1.0, vv[:, :], ALU.add, ALU.mult, r=[('ga', si), ('vv', si)],
                    w=[('ga', si)])
                act(ga[:, :], ga[:, :], AF.Sigmoid, r=[('ga', si)], w=[('ga', si)], scale=GC)
                tt('pool', ho[:, :], ga[:, :], vg[:, :], ALU.mult, r=[('ga', si), ('vg', si)], w=[('ho', si)])
                dma(rawap(hs, ct * 128 * SEGT + t0, [[SEGT, 128], [1, NTF]]), ho[:, :], r=[('ho', si)],
                    w=[('hs', ct, t0)])

            sis = [(gi + c_) % NB for c_ in range(22)]
            gi += 22
            stage_a(0, sis[0])
            for ct in range(22):
                if ct == 8 and ti + 1 < ntile:
                    ffn_norm(ti + 1)
                if ct + 1 < 22:
                    stage_a(ct + 1, sis[ct + 1])
                stage_b(ct, sis[ct])

    def phase_ffn_b(L, dst_t, seg):
        arena[0] = UT0
        Wd = aalloc('Wd', [128, 22, D], BF16)
        hb = [aalloc('hb', [128, 22, 512], BF16) for _ in range(2)]
        t1 = aalloc('t1f', [128, D], F32)
        xb2 = [aalloc('xb2', [128, D], F32) for _ in range(2)]
        K = ('W', 'fd')
        junk_ref[0] = aalloc('junkF', [128, D], BF16)
        KD = WK(('Wc', 'fd'), 512)
        for hf in range(2):
            load_w(Wd[:, :, hf * 512:(hf + 1) * 512], (KD.base, hf), 'ffn_w_down', L, 0, 22, hf * 512, 512)
        dma(gains2[1][:, :], rawap(W['norm_ffn_post'], L * D, [[0, 128], [1, D]]), w=[('c', 'gain')])
        bi = 0
        for ti, t0 in enumerate(range(0, SEGT, 512)):
            hbt = hb[ti % 2]
            hkey = ('hb', ti % 2)
            for c0 in (0, 11):
                dma(hbt[:, c0:c0 + 11, :], rawap(hs, c0 * 128 * SEGT + t0, [[SEGT, 128], [128 * SEGT, 11], [1, 512]]),
                    r=[('hs', ct, t0) for ct in range(c0, c0 + 11)], w=[hkey])
            for j in range(4):
                tb = t0 + j * 128
                xi = bi % 2
                bi += 1
                xb = xb2[xi]
                dma(xb[:, :], rawap(xmid, tb * D, [[D, 128], [1, D]]), r=[('xmid', tb)], w=[('xb2', xi)])
                banks = []
                for hf in range(2):
                    b = nextbank()
                    banks.append(b)
                    for k in range(22):
                        mm(ps(b, 128, 512), hbt[:, k, j * 128:(j + 1) * 128], Wd[:, k, hf * 512:(hf + 1) * 512], k == 0,
                           k == 21, r=[hkey, (KD.base, hf)], w=[('ps', b)])
                rowsum_rstd(banks, 128, 9)
                for hf, b in enumerate(banks):
                    tt('dve', t1[:, hf * 512:(hf + 1) * 512], ps(b, 128, 512), gains[3][:, hf * 512:(hf + 1) * 512],
                       ALU.mult, r=[('ps', b), ('c', 'gain')], w=[('t1f', hf)])
                    stt('dve', xb[:, hf * 512:(hf + 1) * 512], t1[:, hf * 512:(hf + 1) * 512], small[:, 9:10],
                        xb[:, hf * 512:(hf + 1) * 512], ALU.mult, ALU.add, r=[('t1f', hf), ('rsx',), ('xb2', xi)],
                        w=[('xb2', xi)])
                dma(rawap(dst_t, (seg * SEGT + tb) * D, [[D, 128], [1, D]]), xb[:, :], r=[('xb2', xi)],
                    w=[('dst', seg, tb)])

    for L in range(nlayer):
        src_t = x_in if L == 0 else x1
        dst_t = out_t if L == nlayer - 1 else x1
        for stt_ in (St32, Ct32, Nt32):
            memset('pool', stt_[:, :, :], 0.0, [('st', id(stt_), h) for h in range(4)])
        for bf_, s32 in ((Stbf, St32), (Ctbf, Ct32), (Ntbf, Nt32)):
            for h in range(4):
                cp('pool', bf_[:, h, :], s32[:, h, :], r=[('st', id(s32), h)], w=[('stbf', id(bf_), h)])
        memset('pool', XXst[:, :, :], 0.0, [('xxst',)])
        for ct in range(8):
            memset('pool', zch[:, ct, :], 0.0, [('zch', ct)])
        for ct in range(22):
            memset('pool', ahist[:, ct, :], 0.0, [('ah', ct)])
        S.barrier()
        s5_setup(L)
        for seg in range(nseg):
            S.barrier()
            dma(gainb[:, :], rawap(W['norm_mix_pre'], L * D, [[0, 128], [1, D]]), w=[('c', 'gain')])
            arena[0] = PH0
            nsets = norm_sets(3)
            for bi, tb in enumerate(range(0, SEGT, 128)):
                extra = [('dst', seg, tb)] if L > 0 else []
                norm_block(rawap(src_t, (seg * SEGT + tb) * D, [[D, 128], [1, D]]), 128, gains[0], ('c', 'gain'), uT,
                           ('uT',), tb, bi % 2, extra_r=extra, bufs=nsets[bi % 3])
            S.barrier()
            if 'gla' not in skip:
                phase_gla(L)
                S.barrier()
            if 'ml' not in skip:
                phase_ml(L)
                S.barrier()
            if 's5' not in skip:
                phase_s5(L)
                S.barrier()
            if 'out' not in skip:
                phase_out(L, src_t, seg)
                S.barrier()
            if 'ffn' not in skip:
                phase_ffn_a(L, seg)
                S.barrier()
                phase_ffn_b(L, dst_t, seg)
                S.barrier()

    with nc.semaphore('e_pe') as s0, nc.semaphore('e_act') as s1, nc.semaphore('e_dve') as s2, \
            nc.semaphore('e_pool') as s3, nc.semaphore('e_sp') as s4:
        esem = {'pe': s0, 'act': s1, 'dve': s2, 'pool': s3, 'sp': s4}
        import contextlib
        with contextlib.ExitStack() as es:
            ssem = [es.enter_context(nc.semaphore('dslot%d' % i)) for i in range(S.nslots)]
            with nc.Block() as block:
                S.emit(nc, block, esem, ssem)
    build.dbg_names = dbg_names
    build.amax = amax
    build.ph0 = PH0
    return nc


def kernel(**inputs):
    nc = build()
    m = {'x': np.ascontiguousarray(inputs['x'].reshape(SEQ, D), dtype=np.float32)}
    for n in WNAMES:
        m[n] = np.ascontiguousarray(inputs[n], dtype=np.float32)
    res = run_bass_kernel_spmd(nc, [m], core_ids=[0])
    return res.results[0]['out'].reshape(1, SEQ, D).astype(np.float32)
```
